# Optimizing a Trainium2 kernel written in Bass

```python
import math
import jax, jax.numpy as jnp
from jax import lax
import numpy as np

D_MODEL = 2048
BATCH = 4
SEQ = 2048
DEPTH = 1
DEC_BATCH = 128
DEC_SEQ = 8
PAST_LEN = 16384
PAGE_SIZE = 128

MIX_WIDTH = D_MODEL
RET_WIDTH = MIX_WIDTH // 2
CONV_WIDTH = MIX_WIDTH - RET_WIDTH
RET_HEADS = 8
RET_DIM = RET_WIDTH // RET_HEADS
RET_CHUNK = 128
CONV_K = 3
ROPE_BASE = 10000.0
N_MEM = 256
XA_HEADS = 4
XA_DIM = D_MODEL // XA_HEADS
D_FF = -(-8 * D_MODEL // (3 * 256)) * 256
IN_COLS = 4 * RET_WIDTH + 3 * CONV_WIDTH
RMS_EPS = 1e-6
GN_EPS = 1e-5

kernel_name = "hybrid_retention_shortconv_memxattn_step"


def rmsnorm(x, g):
    xf = x.astype(jnp.float32)
    y = xf * lax.rsqrt(jnp.mean(xf * xf, axis=-1, keepdims=True) + RMS_EPS)
    return (y * g.astype(jnp.float32)).astype(x.dtype)


def rotary(x, pos):
    half = x.shape[-1] // 2
    inv = ROPE_BASE ** (-jnp.arange(half, dtype=jnp.float32) / half)
    ang = pos.astype(jnp.float32)[:, None] * inv[None, :]
    cos = jnp.cos(ang)[None, :, None, :]
    sin = jnp.sin(ang)[None, :, None, :]
    x1, x2 = x[..., :half], x[..., half:]
    return jnp.concatenate([x1 * cos - x2 * sin, x1 * sin + x2 * cos], axis=-1)


def retention(q, k, v, s0):
    B, T, H, D = q.shape
    L = RET_CHUNK if T % RET_CHUNK == 0 else T
    n = T // L
    lg = jnp.log(1.0 - 2.0 ** (-5.0 - jnp.arange(H, dtype=jnp.float32)))
    idx = jnp.arange(L, dtype=jnp.float32)
    rel = idx[:, None] - idx[None, :]
    dmask = jnp.exp(jnp.where(rel[None] >= 0, rel[None] * lg[:, None, None], -jnp.inf))
    q_dec = jnp.exp((idx + 1.0)[:, None] * lg[None, :])
    k_dec = jnp.exp((L - 1.0 - idx)[:, None] * lg[None, :])
    chunk_dec = jnp.exp(L * lg)

    def to_chunks(t):
        return jnp.moveaxis(t.reshape(B, n, L, H, D), 1, 0)

    def step(s, inp):
        qc, kc, vc = inp
        scores = jnp.einsum('blhd,bmhd->bhlm', qc, kc) * dmask[None]
        inner = jnp.einsum('bhlm,bmhd->blhd', scores, vc)
        cross = jnp.einsum('blhd,bhde->blhe', qc, s) * q_dec[None, :, :, None]
        s_new = s * chunk_dec[None, :, None, None] + jnp.einsum(
            'blhd,blhe->bhde', kc * k_dec[None, :, :, None], vc)
        return s_new, inner + cross

    s_fin, out = lax.scan(step, s0, (to_chunks(q), to_chunks(k), to_chunks(v)))
    out = jnp.moveaxis(out, 0, 1).reshape(B, T, H, D)
    return out, s_fin


def mixer_sublayer(x, pos, s0, conv_buf, ln_g, w_in, conv_w, gn_g, w_out):
    B, T, _ = x.shape
    h = rmsnorm(x, ln_g)
    z = h @ w_in
    R, C = RET_WIDTH, CONV_WIDTH
    q, k, v, g, bg, cg, hin = jnp.split(
        z, [R, 2 * R, 3 * R, 4 * R, 4 * R + C, 4 * R + 2 * C], axis=-1)
    q = rotary(q.astype(jnp.float32).reshape(B, T, RET_HEADS, RET_DIM), pos)
    k = rotary(k.astype(jnp.float32).reshape(B, T, RET_HEADS, RET_DIM), pos) * (RET_DIM ** -0.5)
    v = v.astype(jnp.float32).reshape(B, T, RET_HEADS, RET_DIM)
    o, s_new = retention(q, k, v, s0.astype(jnp.float32))
    mu = jnp.mean(o, axis=-1, keepdims=True)
    var = jnp.mean(jnp.square(o - mu), axis=-1, keepdims=True)
    on = ((o - mu) * lax.rsqrt(var + GN_EPS)).reshape(B, T, R) * gn_g.astype(jnp.float32)
    ret_out = (jax.nn.silu(g.astype(jnp.float32)) * on).astype(x.dtype)
    u = cg * hin
    u_pad = jnp.concatenate([conv_buf.astype(u.dtype), u], axis=1)
    conv = (conv_w[0] * u_pad[:, :T] + conv_w[1] * u_pad[:, 1:T + 1]
            + conv_w[2] * u_pad[:, 2:T + 2])
    conv_out = bg * conv
    new_buf = u_pad[:, -(CONV_K - 1):]
    mix = jnp.concatenate([ret_out, conv_out], axis=-1) @ w_out
    return x + mix, s_new, new_buf


def memory_kv(mem, ln_mem_g, w_mk, w_mv):
    B, N, _ = mem.shape
    hm = rmsnorm(mem, ln_mem_g)
    mk = (hm @ w_mk).reshape(B, N, XA_HEADS, XA_DIM)
    mv = (hm @ w_mv).reshape(B, N, XA_HEADS, XA_DIM)
    return mk, mv


def memory_xattn_sublayer(x, ln_g, w_xq, w_xo, mk, mv):
    B, T, _ = x.shape
    h = rmsnorm(x, ln_g)
    q = (h @ w_xq).reshape(B, T, XA_HEADS, XA_DIM).astype(jnp.float32)
    s = jnp.einsum('bthd,bnhd->bhtn', q, mk.astype(jnp.float32)) * (XA_DIM ** -0.5)
    p = jax.nn.softmax(s, axis=-1)
    o = jnp.einsum('bhtn,bnhd->bthd', p, mv.astype(jnp.float32)).reshape(B, T, D_MODEL)
    return x + o.astype(x.dtype) @ w_xo


def ffn_sublayer(x, ln_g, w_gate, w_up, w_down):
    h = rmsnorm(x, ln_g)
    return x + (jax.nn.silu(h @ w_gate) * (h @ w_up)) @ w_down


def setup_inputs(seed: int = 0) -> dict:
    key = jax.random.key(seed)
    ks = jax.random.split(key, 24)
    f32 = jnp.float32

    def nrm(k, shape, scale=1.0):
        return jax.random.normal(k, shape, f32) * scale

    def gain(k, shape):
        return 1.0 + 0.05 * jax.random.normal(k, shape, f32)

    return {
        "x_prompt": nrm(ks[0], (BATCH, SEQ, D_MODEL)),
        "x_sample": nrm(ks[1], (DEC_BATCH, DEC_SEQ, D_MODEL)),
        "mem_prompt": nrm(ks[2], (BATCH, N_MEM, D_MODEL)),
        "state_ret": nrm(ks[3], (DEPTH, DEC_BATCH, RET_HEADS, RET_DIM, RET_DIM)),
        "state_conv": nrm(ks[4], (DEPTH, DEC_BATCH, CONV_K - 1, CONV_WIDTH)),
        "cache_mem_k": nrm(ks[5], (DEPTH, DEC_BATCH, N_MEM, XA_HEADS, XA_DIM)),
        "cache_mem_v": nrm(ks[6], (DEPTH, DEC_BATCH, N_MEM, XA_HEADS, XA_DIM)),
        "ln_mix_g": gain(ks[7], (DEPTH, D_MODEL)),
        "w_in": nrm(ks[8], (DEPTH, D_MODEL, IN_COLS), D_MODEL ** -0.5),
        "conv_w": nrm(ks[9], (DEPTH, CONV_K, CONV_WIDTH), CONV_K ** -0.5),
        "ret_gn_g": gain(ks[10], (DEPTH, RET_WIDTH)),
        "w_mix_out": nrm(ks[11], (DEPTH, MIX_WIDTH, D_MODEL), MIX_WIDTH ** -0.5),
        "ln_mem_g": gain(ks[12], (DEPTH, D_MODEL)),
        "ln_xa_g": gain(ks[13], (DEPTH, D_MODEL)),
        "w_xq": nrm(ks[14], (DEPTH, D_MODEL, D_MODEL), D_MODEL ** -0.5),
        "w_mk": nrm(ks[15], (DEPTH, D_MODEL, D_MODEL), D_MODEL ** -0.5),
        "w_mv": nrm(ks[16], (DEPTH, D_MODEL, D_MODEL), D_MODEL ** -0.5),
        "w_xo": nrm(ks[17], (DEPTH, D_MODEL, D_MODEL), D_MODEL ** -0.5),
        "ln_ffn_g": gain(ks[18], (DEPTH, D_MODEL)),
        "w_gate": nrm(ks[19], (DEPTH, D_MODEL, D_FF), D_MODEL ** -0.5),
        "w_up": nrm(ks[20], (DEPTH, D_MODEL, D_FF), D_MODEL ** -0.5),
        "w_down": nrm(ks[21], (DEPTH, D_FF, D_MODEL), D_FF ** -0.5),
        "final_g": gain(ks[22], (D_MODEL,)),
    }


def reference(x_prompt, x_sample, mem_prompt, state_ret, state_conv, cache_mem_k, cache_mem_v,
              ln_mix_g, w_in, conv_w, ret_gn_g, w_mix_out, ln_mem_g, ln_xa_g, w_xq, w_mk, w_mv,
              w_xo, ln_ffn_g, w_gate, w_up, w_down, final_g):
    Bp, Tp, _ = x_prompt.shape
    Bs, Ts, _ = x_sample.shape
    pos_p = jnp.arange(Tp, dtype=jnp.int32)
    pos_s = PAST_LEN + jnp.arange(Ts, dtype=jnp.int32)
    yp, ys = x_prompt, x_sample
    ret_p, ret_s, conv_p, conv_s, mk_p_all, mv_p_all = [], [], [], [], [], []
    for l in range(DEPTH):
        s0_p = jnp.zeros((Bp, RET_HEADS, RET_DIM, RET_DIM), jnp.float32)
        buf0_p = jnp.zeros((Bp, CONV_K - 1, CONV_WIDTH), x_prompt.dtype)
        yp, sp, cp = mixer_sublayer(yp, pos_p, s0_p, buf0_p, ln_mix_g[l], w_in[l], conv_w[l],
                                    ret_gn_g[l], w_mix_out[l])
        ys, ss, cs = mixer_sublayer(ys, pos_s, state_ret[l], state_conv[l], ln_mix_g[l], w_in[l],
                                    conv_w[l], ret_gn_g[l], w_mix_out[l])
        mk_p, mv_p = memory_kv(mem_prompt, ln_mem_g[l], w_mk[l], w_mv[l])
        yp = memory_xattn_sublayer(yp, ln_xa_g[l], w_xq[l], w_xo[l], mk_p, mv_p)
        ys = memory_xattn_sublayer(ys, ln_xa_g[l], w_xq[l], w_xo[l], cache_mem_k[l], cache_mem_v[l])
        yp = ffn_sublayer(yp, ln_ffn_g[l], w_gate[l], w_up[l], w_down[l])
        ys = ffn_sublayer(ys, ln_ffn_g[l], w_gate[l], w_up[l], w_down[l])
        ret_p.append(sp)
        ret_s.append(ss)
        conv_p.append(cp)
        conv_s.append(cs)
        mk_p_all.append(mk_p)
        mv_p_all.append(mv_p)
    y_prompt = rmsnorm(yp, final_g)
    y_sample = rmsnorm(ys, final_g)
    state_ret_prompt = jnp.stack(ret_p)
    state_ret_sample = jnp.stack(ret_s)
    state_conv_prompt = jnp.stack(conv_p)
    state_conv_sample = jnp.stack(conv_s)
    cache_mem_k_prompt = jnp.stack(mk_p_all)
    cache_mem_v_prompt = jnp.stack(mv_p_all)
    return (y_prompt, y_sample, state_ret_prompt, state_ret_sample, state_conv_prompt,
            state_conv_sample, cache_mem_k_prompt, cache_mem_v_prompt)
```

```python
import math
import os
from contextlib import ExitStack

import numpy as np
import concourse.bass as bass
import concourse.mybir as mybir
from concourse.bass_utils import run_bass_kernel_spmd

F32 = mybir.dt.float32
BF16 = mybir.dt.bfloat16
ALU = mybir.AluOpType
AF = mybir.ActivationFunctionType
AX = mybir.AxisListType

D = 2048
KC = 16
TP = 1024
TS = 128
T = TP + TS
NT = T // 128
NSEQ = 16
H = 8
DFF = 5632
NG = DFF // 512
NMEM = 256
RMS_EPS = 1e-6
GN_EPS = 1e-5
NSLOT = 4
TB = [(0, 512), (512, 512), (1024, 128)]
TBP = [(0, 512), (512, 512)]
LG = [math.log(1.0 - 2.0 ** (-5.0 - h)) for h in range(H)]

ENGS = ("pe", "act", "dve", "pool", "sp")
_PE_LABELS = [] if os.environ.get("KLABELS") else None


class Prog:
    def __init__(self, nc, sem_alloc):
        self.nc = nc
        self.sem_alloc = sem_alloc
        self.q = {e: [] for e in ENGS}
        self.cnt = {e: 0 for e in ENGS}
        self.epoch = {e: 0 for e in ENGS}
        self.sems = {}
        self.seen = {e: {} for e in ENGS}
        self.lastw = {}
        self.readers = {}
        self.nds = 12
        self.dq = {q: {"i": 0, "uses": [0] * self.nds} for q in ("sp", "pool", "act")}
        self.dma_tokens = []
        self.weight_tokens = set()
        self.EPOCH_MAX = 4000
        self.dead = False
        self.stop = float(os.environ.get("KSTOP", "99"))

    def phase(self, k):
        if k > self.stop:
            self.dead = True

    def sem(self, key):
        if key not in self.sems:
            self.sems[key] = self.sem_alloc("s_" + "_".join(str(k) for k in key))
        return self.sems[key]

    def _collect(self, eng, reads, writes):
        deps = {}

        def need(tok):
            if tok is None:
                return
            k, v = tok
            if k[0] == "e" and k[1] == "pe" and eng == "pe":
                return
            if v > deps.get(k, 0):
                deps[k] = v

        for t in reads:
            need(self.lastw.get(t))
        for t in writes:
            need(self.lastw.get(t))
            for r in self.readers.get(t, ()):
                need(r)
        waits = []
        for k, v in deps.items():
            if v > self.seen[eng].get(k, 0):
                self.seen[eng][k] = v
                waits.append((k, v))
        return waits

    def _record(self, tok, reads, writes):
        for t in reads:
            self.readers.setdefault(t, []).append(tok)
        for t in writes:
            self.lastw[t] = tok
            self.readers[t] = []

    def op(self, eng, fn, reads=(), writes=()):
        if self.dead:
            return None
        ps_reads = [t for t in reads if isinstance(t, tuple) and t[0] == "ps" and t not in writes]
        if ps_reads:
            writes = list(writes) + ps_reads
        waits = self._collect(eng, reads, writes)
        if self.cnt[eng] >= self.EPOCH_MAX:
            self.epoch[eng] += 1
            self.cnt[eng] = 0
        self.cnt[eng] += 1
        key = ("e", eng, self.epoch[eng])
        tok = (key, self.cnt[eng])
        self.q[eng].append((waits, fn, (key, 1)))
        self._record(tok, reads, writes)
        return tok

    def dma(self, queue, out, in_, reads=(), writes=(), track_out=False, slow=False):
        if self.dead:
            return None
        st = self.dq[queue]
        slot = st["i"] % self.nds
        st["i"] += 1
        key = ("d", queue, slot)
        waits = self._collect(queue, reads, writes)
        prev = st["uses"][slot] * 16
        if prev > self.seen[queue].get(key, 0):
            self.seen[queue][key] = prev
            waits.append((key, prev))
        st["uses"][slot] += 1
        tok = (key, st["uses"][slot] * 16)
        if slow:
            self.q[queue].append((waits, lambda e, o=out, i=in_: e.dma_start(out=o, in_=i, allow_slow_non_contiguous=True), (key, 16)))
        else:
            self.q[queue].append((waits, lambda e, o=out, i=in_: e.dma_start(out=o, in_=i), (key, 16)))
        self._record(tok, reads, writes)
        if track_out:
            self.dma_tokens.append(tok)
        return tok

    def fence(self):
        if self.dead:
            return
        toks = []
        for e in ENGS:
            if self.cnt[e] > 0:
                toks.append((("e", e, self.epoch[e]), self.cnt[e]))
        for q, st in self.dq.items():
            for s in range(self.nds):
                if st["uses"][s] > 0:
                    tk = (("d", q, s), st["uses"][s] * 16)
                    if tk in self.weight_tokens:
                        continue
                    toks.append(tk)
        for e in ENGS:
            waits = []
            for k, v in toks:
                if k[0] == "e" and k[1] == e and e == "pe":
                    continue
                if v > self.seen[e].get(k, 0):
                    self.seen[e][k] = v
                    waits.append((k, v))
            if waits:
                self.q[e].append((waits, None, None))
        keep_w = {k: v for k, v in self.lastw.items() if isinstance(k, tuple) and k[0] == "slot"}
        keep_r = {k: v for k, v in self.readers.items() if isinstance(k, tuple) and k[0] == "slot"}
        self.lastw = keep_w
        self.readers = keep_r

    def emit(self, block):
        nc = self.nc

        def run(eng_name):
            def body(e):
                for waits, fn, inc in self.q[eng_name]:
                    for k, v in waits:
                        e.wait_ge(self.sem(k), v)
                    if fn is not None:
                        ins = fn(e)
                        ins.then_inc(self.sem(inc[0]), inc[1])
            return body

        block.tensor(run("pe"))
        block.scalar(run("act"))
        block.vector(run("dve"))
        block.gpsimd(run("pool"))
        block.sync(run("sp"))


def build_program():
    nc = bass.Bass("TRN2", target_bir_lowering=False)

    def din(name, shape):
        return nc.dram_tensor(name, list(shape), F32, kind="ExternalInput").ap()

    def dout(name, shape):
        return nc.dram_tensor(name, list(shape), F32, kind="ExternalOutput").ap()

    xp = din("xp", [TP, D]); xprev = din("xprev", [TP, D]); xs = din("xs", [TS, D])
    mem = din("mem", [NMEM, D])
    sret = din("sret", [NSEQ, H, 128, 128]); sconv = din("sconv", [NSEQ * 2, 1024])
    ck = din("ck", [NSEQ, NMEM, D]); cv = din("cv", [NSEQ, NMEM, D])
    ln_mix_g = din("ln_mix_g", [D]); w_in = din("w_in", [D, 7168]); conv_w = din("conv_w", [3, 1024])
    ret_gn_g = din("ret_gn_g", [1024]); w_mix_out = din("w_mix_out", [D, D])
    ln_mem_g = din("ln_mem_g", [D]); ln_xa_g = din("ln_xa_g", [D])
    w_xq = din("w_xq", [D, D]); w_mk = din("w_mk", [D, D]); w_mv = din("w_mv", [D, D]); w_xo = din("w_xo", [D, D])
    ln_ffn_g = din("ln_ffn_g", [D]); w_gate = din("w_gate", [D, DFF]); w_up = din("w_up", [D, DFF])
    w_down = din("w_down", [DFF, D]); final_g = din("final_g", [D])
    c_ident = din("c_ident", [128, 128]); c_perm = din("c_perm", [128, 128])
    c_rot = din("c_rot", [128, 2, T]); c_rotp = din("c_rotp", [128, 2, TP])
    c_dmask = din("c_dmask", [128, H, 128]); c_smask = din("c_smask", [128, H, 128])
    c_qdec = din("c_qdec", [128, H, 128]); c_qdecs = din("c_qdecs", [128, H, 128])
    c_kdec = din("c_kdec", [128, H]); c_kdecp = din("c_kdecp", [128, H, 8]); c_rmd = din("c_rmd", [128, H, NSEQ + 1])

    yp = dout("yp", [TP, D]); ys = dout("ys", [TS, D])
    srp = dout("srp", [H, 128, 128]); srs = dout("srs", [NSEQ, H, 128, 128])
    scp = dout("scp", [2, 1024]); scs = dout("scs", [NSEQ * 2, 1024])
    mko = dout("mko", [128, D]); mvo = dout("mvo", [128, D])

    es = ExitStack()
    with es:
        def sb(name, shape, dt):
            return es.enter_context(nc.sbuf_tensor(name, list(shape), dt))

        RX = sb("RX", [128, NT * D], F32)
        RH = sb("RH", [128, KC * T], BF16)
        RA = sb("RA", [128, KC * T], BF16)
        SL = sb("SL", [128, NSLOT, 4096], BF16)
        identf = sb("identf", [128, 128], F32)
        identb = sb("identb", [128, 128], BF16)
        permb = sb("permb", [128, 128], BF16)
        onesf = sb("onesf", [128, 128], F32)
        gT = sb("gT", [128, 4, KC], F32)
        gnT = sb("gnT", [128, 8], F32)
        cwT = sb("cwT", [128, 3, 8], F32)
        stat = sb("stat", [128, 64], F32)
        Sf = sb("Sf", [128, H, 128], F32)
        MKV = sb("MKV", [128, 2, KC * 256], BF16)
        oo = sb("oo", [128, 2, T], F32)
        oall = oo[:, 0, :]
        osq = oo[:, 1, :]
        uprev = sb("uprev", [128, 8, 2], F32)
        PS = es.enter_context(nc.psum_tensor("PS", [128, 8, 512], F32))

        sem_list = []

        def sem_alloc(name):
            s = es.enter_context(nc.semaphore(name))
            sem_list.append(s)
            return s

        P = Prog(nc, sem_alloc)

        def psf(bank, n=512, off=0):
            return PS[:, bank, off:off + n]

        def psb(bank):
            return PS[:, bank, :].bitcast(BF16)

        def pe(mms, reads, writes):
            if _PE_LABELS is not None and not P.dead:
                import inspect
                fr = inspect.stack()[1]
                _PE_LABELS.append(("%s:%d" % (fr.function, fr.lineno), len(mms)))
            def fn(e, mms=mms):
                ins = None
                for m in mms:
                    if m[0] == "m":
                        ins = e.matmul(m[1], m[2], m[3], start=m[4], stop=m[5])
                    else:
                        ins = e.transpose(m[1], m[2], m[3])
                return ins
            return P.op("pe", fn, reads, writes)

        def dve(f, reads, writes):
            return P.op("dve", f, reads, writes)

        def act(f, reads, writes):
            return P.op("act", f, reads, writes)

        wv = lambda w: w.rearrange("(kc p) n -> p kc n", p=128)
        blocks = []

        def blk_std(w, col0):
            blocks.append((wv(w)[:, :, col0:col0 + 256], (16, 256)))
            return len(blocks) - 1

        def blk_a(w, kh, cb):
            blocks.append((wv(w)[:, 8 * kh:8 * kh + 8, 512 * cb:512 * cb + 512], (8, 512)))
            return len(blocks) - 1

        def blk_down(g, half):
            blocks.append((wv(w_down)[:, 4 * g:4 * g + 4, 1024 * half:1024 * half + 1024], (4, 1024)))
            return len(blocks) - 1

        CQ, CK, CV, CG, CBG, CCG, CHIN = 0, 1024, 2048, 3072, 4096, 5120, 6144
        W = {}
        W["pre_k"] = []; W["pre_v"] = []; W["pre_cg"] = []; W["pre_hin"] = []
        for i in range(4):
            W["pre_k"].append(blk_std(w_in, CK + 256 * i)); W["pre_v"].append(blk_std(w_in, CV + 256 * i))
            W["pre_cg"].append(blk_std(w_in, CCG + 256 * i)); W["pre_hin"].append(blk_std(w_in, CHIN + 256 * i))
        W["q"] = []; W["k"] = []; W["v"] = []; W["g"] = []; W["mkv"] = []
        for i in range(4):
            W["v"].append(blk_std(w_in, CV + 256 * i)); W["k"].append(blk_std(w_in, CK + 256 * i))
            W["mkv"].append((0, 2 * i, blk_std(w_mk, 256 * (2 * i))))
            W["q"].append(blk_std(w_in, CQ + 256 * i))
            W["mkv"].append((0, 2 * i + 1, blk_std(w_mk, 256 * (2 * i + 1))))
            W["g"].append(blk_std(w_in, CG + 256 * i))
            W["mkv"].append((1, 2 * i, blk_std(w_mv, 256 * (2 * i))))
            W["mkv"].append((1, 2 * i + 1, blk_std(w_mv, 256 * (2 * i + 1))))
        W["cg"] = []; W["hin"] = []; W["bg"] = []
        for i in range(4):
            W["cg"].append(blk_std(w_in, CCG + 256 * i)); W["hin"].append(blk_std(w_in, CHIN + 256 * i))
            W["bg"].append(blk_std(w_in, CBG + 256 * i))
        W["mixo"] = [[blk_a(w_mix_out, kh, cb) for kh in range(2)] for cb in range(4)]
        W["xq"] = [blk_std(w_xq, 256 * i) for i in range(8)]
        W["xo"] = [[blk_a(w_xo, kh, cb) for kh in range(2)] for cb in range(4)]
        W["gate"] = []; W["up"] = []; W["down"] = []
        for g in range(NG):
            gl, ul = [], []
            for j in range(2):
                gl.append(blk_std(w_gate, 512 * g + 256 * j)); ul.append(blk_std(w_up, 512 * g + 256 * j))
            W["gate"].append(gl); W["up"].append(ul)
            W["down"].append([blk_down(g, j) for j in range(2)])
        wstate = {"issued": 0, "cur": -1}

        def wslot(i):
            assert i == wstate["cur"] + 1 or i == wstate["cur"], (i, wstate)
            wstate["cur"] = i
            while wstate["issued"] < len(blocks) and wstate["issued"] <= i + NSLOT - 2:
                j = wstate["issued"]
                ap, shp = blocks[j]
                dst = SL[:, j % NSLOT, :].rearrange("p (a b) -> p a b", a=shp[0])
                wt = P.dma("pool", dst, ap, reads=([("hprev", 1)] if j < NSLOT - 1 else ()), writes=[("slot", j % NSLOT)])
                if wt is not None:
                    P.weight_tokens.add(wt)
                wstate["issued"] += 1
            shp = blocks[i][1]
            return SL[:, i % NSLOT, :].rearrange("p (a b) -> p a b", a=shp[0]), ("slot", i % NSLOT)

        P.dma("sp", identf[:], c_ident, writes=["identf"])
        P.dma("pool", identb[:], c_ident, writes=["identb"])
        P.dma("pool", permb[:], c_perm, writes=["permb"])
        gst = oo[:, :, :].rearrange("p a t -> p (a t)")[:, 0:128]
        for i, g in enumerate((ln_mix_g, ln_mem_g, ln_xa_g, ln_ffn_g)):
            P.dma("sp", gst[16 * i:16 * i + 16, :], g.rearrange("(kc p) -> kc p", p=128), writes=["gst"])
        P.dma("sp", gst[64:72, :], ret_gn_g.rearrange("(kc p) -> kc p", p=128), writes=["gst"])
        P.dma("sp", gst[72:96, :], conv_w.rearrange("k (kc p) -> (k kc) p", p=128), writes=["gst"])
        pe([("t", psf(0, 96), gst[0:96, :], identf[0:96, 0:96])], ["gst", "identf"], [("ps", 0)])
        act(lambda e: e.activation(out=gT[:, :, :].rearrange("p a k -> p (a k)"), in_=psf(0, 64), func=AF.Copy), [("ps", 0)], ["gT"])
        act(lambda e: e.activation(out=gnT[:], in_=psf(0, 8, 64), func=AF.Copy), [("ps", 0)], ["gnT"])
        act(lambda e: e.activation(out=cwT[:, :, :].rearrange("p k c -> p (k c)"), in_=psf(0, 24, 72), func=AF.Copy), [("ps", 0)], ["cwT"])
        dve(lambda e: e.memset(onesf[:], 1.0 / 128.0), [], ["onesf"])

        rx_off = [0]

        def rx(n, dt=F32, shape=None):
            nf = n if dt == F32 else (n + 1) // 2
            o = rx_off[0]
            rx_off[0] += nf
            assert rx_off[0] <= NT * D, rx_off[0]
            v = RX[:, o:o + nf]
            if dt == BF16:
                v = v.bitcast(BF16)[:, 0:n]
            return v

        rot = rx(2 * T).rearrange("p (a t) -> p a t", a=2)
        dmask = rx(2 * 128).rearrange("p (h l) -> p h l", h=2)
        smask = rx(2 * 128).rearrange("p (h l) -> p h l", h=2)
        qdec = rx(2 * 128).rearrange("p (h l) -> p h l", h=2)
        qdecs = rx(2 * 128).rearrange("p (h l) -> p h l", h=2)
        kdec = rx(H)
        kdecp = rx(H * 8).rearrange("p (h c) -> p h c", h=H)
        rmd = rx(H * (NSEQ + 1)).rearrange("p (h s) -> p h s", h=H)
        qT = rx(2 * T, BF16).rearrange("p (a t) -> p a t", a=2)
        kT = rx(2 * T, BF16).rearrange("p (a t) -> p a t", a=2)
        sgT = rx(2 * T, BF16).rearrange("p (a t) -> p a t", a=2)
        vtok = rx(NT * 256, BF16).rearrange("p (t c) -> p t c", t=NT)
        raw = rx(T, BF16)
        t1 = rx(T)
        t2 = rx(T)
        stm = rx(2 * 128, BF16).rearrange("p (a l) -> p a l", a=2)
        kdt = rx(2 * 128, BF16).rearrange("p (a l) -> p a l", a=2)
        qds_ = rx(128, BF16)
        inn = rx(128)
        un0 = rx_off[0]
        xload = rx(2 * D).rearrange("p (a d) -> p a d", a=2)
        sqj = rx(D, BF16)
        enda = rx_off[0]
        rx_off[0] = un0
        SH = 8
        ssf2 = rx(2 * SH * 128).rearrange("p (a s e) -> p a s e", a=2, s=SH)
        ssb = rx(SH * 128, BF16).rearrange("p (s e) -> p s e", s=SH)
        Sprev = rx(2 * 8 * 128, BF16).rearrange("p (a c e) -> p a c e", a=2, c=8)
        Stmp = rx(2 * 128).rearrange("p (a e) -> p a e", a=2)
        qdall = rx(TP, BF16)
        vexp_off = rx_off[0]
        vexp = rx(NSEQ * 128, BF16).rearrange("p (s e) -> p s e", s=NSEQ)
        ktkd = rx(128, BF16)
        kdta = rx(8 * 128, BF16).rearrange("p (c d) -> p c d", c=8)
        osq_alt = RX[:, vexp_off:vexp_off + T]
        stma = rx(9 * 128, BF16).rearrange("p (c l) -> p c l", c=9)
        endb = rx_off[0]
        rx_off[0] = un0
        upad = rx(2 * (TP + 2))
        upads = rx(NSEQ * 10).rearrange("p (s k) -> p s k", s=NSEQ)
        cacc = rx(T)
        scT = rx(8 * 32).rearrange("p (c k) -> p c k", c=8)
        sc_in = rx(1024)
        sc_out = rx(1024)
        nbt = rx(32)
        endc = rx_off[0]
        rx_off[0] = max(enda, endb, endc)
        assert rx_off[0] <= NT * D, rx_off[0]

        P.dma("sp", kdec, c_kdec, writes=["kdec"])
        P.dma("sp", kdecp, c_kdecp, writes=["kdecp"])
        P.dma("sp", rmd, c_rmd, writes=["rmd"])

        hT = RH[:, :].rearrange("p (k t) -> p k t", k=KC)
        hprevT = RA[:, 0:KC * TP].rearrange("p (k t) -> p k t", k=KC)
        catT = RA[:, :].rearrange("p (k t) -> p k t", k=KC)
        xres = RX[:, :].rearrange("p (t d) -> p t d", t=NT)

        xn2 = oo[:, :, :].rearrange("p a t -> p (a t)").bitcast(BF16)[:, 0:2 * D].rearrange("p (a d) -> p a d", a=2)

        def norm_tile(i, xt, xr, gidx, dstT, dkey):
            j = i % 2
            so = 4 * j
            act(lambda e: e.activation(out=xn2[:, j, :], in_=xt, func=AF.Square, accum_out=stat[:, so:so + 1]), xr, [("xn", j), ("ss", j), "gst"])
            act(lambda e: e.activation(out=stat[:, so + 1:so + 2], in_=stat[:, so:so + 1], func=AF.Sqrt, bias=RMS_EPS, scale=1.0 / D), [("ss", j)], [("ms", j)])
            dve(lambda e: e.reciprocal(out=stat[:, so + 2:so + 3], in_=stat[:, so + 1:so + 2]), [("ms", j)], [("rstd", j)])
            if j == 0:
                act(lambda e: e.activation(out=xn2[:, j, :], in_=xt, func=AF.Copy, scale=stat[:, so + 2:so + 3]), xr + [("rstd", j)], [("xn", j)])
            else:
                dve(lambda e: e.tensor_scalar(out=xn2[:, j, :], in0=xt, scalar1=stat[:, so + 2:so + 3], scalar2=None, op0=ALU.mult),
                    xr + [("rstd", j)], [("xn", j)])
            b0 = 4 + 2 * j
            pbb = PS[:, b0:b0 + 2, :].rearrange("p b n -> p (b n)").bitcast(BF16)
            pe([("t", pbb[:, 128 * kc:128 * kc + 128], xn2[:, j, 128 * kc:128 * kc + 128], identb[:]) for kc in range(KC)],
               [("xn", j), "identb"], [("ps", b0), ("ps", b0 + 1)])
            dve(lambda e: e.tensor_tensor(out=dstT[:, :, 128 * i:128 * i + 128], in0=pbb.rearrange("p (k t) -> p k t", k=KC),
                                          in1=gT[:, gidx, :].unsqueeze(2).broadcast_to([128, KC, 128]), op=ALU.mult),
                [("ps", b0), ("ps", b0 + 1), "gT"], [(dkey, i)])

        def norm_tiles(src_aps, gidx, dstT, dkey):
            for i in range(len(src_aps)):
                xt = xload[:, i % 2, :]
                P.dma("sp", xt, src_aps[i], writes=[("xload", i % 2)])
                norm_tile(i, xt, [("xload", i % 2)], gidx, dstT, dkey)

        def proj_b(slot, slot_key, c, src, src_reads, tbs, banks):
            mms = []
            for (t0, n), b in zip(tbs, banks):
                pass
            for kc in range(KC):
                for (t0, n), b in zip(tbs, banks):
                    mms.append(("m", psf(b, n), slot[:, kc, 128 * c:128 * c + 128], src[:, kc, t0:t0 + n],
                                kc == 0, kc == KC - 1))
            return pe(mms, [slot_key] + src_reads, [("ps", b) for b in banks[:len(tbs)]])

        bank_sets = [[0, 1, 2], [3, 4, 5]]
        bs_i = [0]

        def next_banks():
            b = bank_sets[bs_i[0] % 2]
            bs_i[0] += 1
            return b

        def pview(banks, ntok):
            b0 = banks[0]
            return PS[:, b0:b0 + 3, :].rearrange("p b n -> p (b n)")[:, 0:ntok]

        def pkeys(banks, tbs):
            return [("ps", b) for b in banks[:len(tbs)]]

        def rotary_pre(banks, tbs, tab, scale):
            ntok = tbs[-1][0] + tbs[-1][1]
            pv = pview(banks, ntok)
            pk = pkeys(banks, tbs)
            act(lambda e: e.activation(out=raw[:, 0:ntok], in_=pv, func=AF.Copy), pk, ["raw"])
            dve(lambda e: e.scalar_tensor_tensor(out=t1[:, 0:ntok], in0=pv, scalar=scale, in1=tab[:, 0, 0:ntok],
                                                 op0=ALU.mult, op1=ALU.mult), pk + ["rot"], ["t1"])

        def rotary_post(banks, tbs, tab, scale, dst, dkey):
            ntok = tbs[-1][0] + tbs[-1][1]
            pv = pview(banks, ntok)
            pk = pkeys(banks, tbs)
            pe([("m", psf(b, n), permb[:], raw[:, t0:t0 + n], True, True) for (t0, n), b in zip(tbs, banks)],
               ["raw", "permb"], pk)
            dve(lambda e: e.scalar_tensor_tensor(out=t2[:, 0:ntok], in0=pv, scalar=scale, in1=tab[:, 1, 0:ntok],
                                                 op0=ALU.mult, op1=ALU.mult), pk + ["rot"], ["t2"])
            dve(lambda e: e.tensor_tensor(out=dst[:, 0:ntok], in0=t1[:, 0:ntok], in1=t2[:, 0:ntok], op=ALU.add),
                ["t1", "t2"], [("rotout", dkey)])

        def rotary(banks, tbs, tab, scale, dst, dkey):
            rotary_pre(banks, tbs, tab, scale)
            rotary_post(banks, tbs, tab, scale, dst, dkey)

        P.phase(1)
        P.dma("sp", rot[:, :, 0:TP], c_rotp, writes=["rot"])
        RAxn = RA
        xn_pre = RH[:, 0:8 * D].rearrange("p (t d) -> p t d", t=8)
        norm_tiles([xprev[128 * i:128 * i + 128, :] for i in range(8)], 0, hprevT, "hprev")
        P.phase(1.2)
        dve(lambda e: e.memset(Sf[:], 0.0), [], ["Sf"])
        kprevT = kT
        srcs = [xp[128 * i:128 * i + 128, :] for i in range(8)] + [xs[:, :]]
        main_i = [0]

        def main_norm_tiles(k):
            for _ in range(k):
                i = main_i[0]
                if i >= NT:
                    return
                main_i[0] += 1
                xt = xload[:, i % 2, :]
                P.dma("sp", xt, srcs[i], writes=[("xload", i % 2)])
                norm_tile(i, xt, [("xload", i % 2)], 0, hT, "hmain")

        P.phase(1.22)
        XB0, YB0 = [0, 1, 2], [3, 4, 5]
        cgp = stat[:, 8:8 + 16].rearrange("p (c k) -> p c k", c=8)
        for hp in range(4):
            slot, sk = wslot(W["pre_k"][hp])
            P.phase(1.25)
            hpr = [("hprev", i_) for i_ in range(8)]
            proj_b(slot, sk, 0, hprevT, hpr, TBP, XB0)
            proj_b(slot, sk, 1, hprevT, hpr, TBP, YB0)
            rotary(XB0, TBP, rot, 128.0 ** -0.5, kprevT[:, 0, :], ("k", 0))
            rotary_pre(YB0, TBP, rot, 128.0 ** -0.5)
            slot, sk = wslot(W["pre_v"][hp])
            for t in range(8):
                b = 6 + (t % 2)
                mms = [("m", psf(b, 256), hprevT[:, kc, 128 * t:128 * t + 128], slot[:, kc, :], kc == 0, kc == KC - 1)
                       for kc in range(KC)]
                pe(mms, [sk] + hpr, [("ps", b)])
                act(lambda e, b=b, t=t: e.activation(out=vtok[:, t, :], in_=psf(b, 256), func=AF.Copy),
                    [("ps", b)], [("vtok", t)])
            rotary_post(YB0, TBP, rot, 128.0 ** -0.5, kprevT[:, 1, :], ("k", 1))
            P.phase(1.6)
            main_norm_tiles(2)
            kdas = []
            for c in range(2):
                h = 2 * hp + c
                b = 6 + c
                pb = psb(b)
                pe([("t", pb[:, 128 * ch:128 * ch + 128], kprevT[:, c, 128 * ch:128 * ch + 128], identb[:]) for ch in range(8)],
                   [("rotout", ("k", c)), "identb"], [("ps", b)])
                kda = qT[:, c, 0:TP].rearrange("p (a d) -> p a d", a=8)
                kdas.append(kda)
                dve(lambda e, pb=pb, h=h, kda=kda: e.tensor_tensor(
                    out=kda, in0=pb[:, :].rearrange("p (a d) -> p a d", a=8),
                    in1=kdecp[:, h, :].unsqueeze(2).broadcast_to([128, 8, 128]), op=ALU.mult), [("ps", b), "kdecp"], [("kda", c)])

            def conv_tail(which, name):
                slot, sk = wslot(W[name][hp])
                for c in range(2):
                    ch = 2 * hp + c
                    b = 4 + (ch % 2)
                    mms = [("m", psf(b, 2), slot[:, kc, 128 * c:128 * c + 128], hprevT[:, kc, TP - 2:TP], kc == 0, kc == KC - 1)
                           for kc in range(KC)]
                    pe(mms, [sk] + [("hprev", i_) for i_ in range(8)], [("ps", b)])
                    if which == 0:
                        act(lambda e, b=b, ch=ch: e.activation(out=cgp[:, ch, :], in_=psf(b, 2), func=AF.Copy),
                            [("ps", b)], [("cgp", ch)])
                    else:
                        dve(lambda e, b=b, ch=ch: e.tensor_tensor(out=uprev[:, ch, :], in0=cgp[:, ch, :], in1=psf(b, 2),
                                                                  op=ALU.mult), [("ps", b), ("cgp", ch)], [("uprev", ch)])

            conv_tail(0, "pre_cg")
            for c in range(2):
                h = 2 * hp + c
                sb_bank = 2 + c
                kda = kdas[c]
                pe([("m", psf(sb_bank, 128), kda[:, ch, :], vtok[:, ch, 128 * c:128 * c + 128], ch == 0, ch == 7) for ch in range(8)],
                   [("kda", c)] + [("vtok", ch) for ch in range(8)], [("ps", sb_bank)])
                act(lambda e, h=h, sb_bank=sb_bank: e.activation(out=Sf[:, h, :], in_=psf(sb_bank, 128), func=AF.Copy),
                    [("ps", sb_bank)], [("Sf", h)])
            conv_tail(1, "pre_hin")
        main_norm_tiles(NT)
        slot_v0, sk_v0 = wslot(W["v"][0])
        for t in range(NT):
            b = 6 + (t % 2)
            mms = [("m", psf(b, 256), hT[:, kc, 128 * t:128 * t + 128], slot_v0[:, kc, :], kc == 0, kc == KC - 1)
                   for kc in range(KC)]
            pe(mms, [sk_v0] + [("hmain", i_) for i_ in range(NT)], [("ps", b)])
            act(lambda e, b=b, t=t: e.activation(out=vtok[:, t, :], in_=psf(b, 256), func=AF.Copy),
                [("ps", b)], [("vtok", t)])
        slot_k0, sk_k0 = wslot(W["k"][0])
        hmk = [("hmain", i_) for i_ in range(NT)]
        proj_b(slot_k0, sk_k0, 0, hT, hmk, TB, [0, 1, 2])
        proj_b(slot_k0, sk_k0, 1, hT, hmk, TB, [3, 4, 5])
        P.fence()

        P.phase(2)
        P.dma("sp", rot, c_rot, writes=["rot"])

        P.phase(3)
        def proj_v(slot, sk, extra=()):
            for t in range(NT):
                b = 6 + (t % 2)
                mms = [("m", psf(b, 256), hT[:, kc, 128 * t:128 * t + 128], slot[:, kc, :], kc == 0, kc == KC - 1)
                       for kc in range(KC)]
                pe(mms, [sk] + list(extra), [("ps", b)])
                act(lambda e, b=b, t=t: e.activation(out=vtok[:, t, :], in_=psf(b, 256), func=AF.Copy),
                    [("ps", b)], [("vtok", t)])

        def stage1_t(c, h):
            kk = [("rotout", ("k", c))]
            pb = psb(6)
            pe([("t", pb[:, 128 * ch:128 * ch + 128], kT[:, c, 128 * ch:128 * ch + 128], identb[:]) for ch in range(8)],
               kk + ["identb"], [("ps", 6)])
            act(lambda e, pb=pb, h=h: e.activation(out=kdta.rearrange("p c d -> p (c d)"), in_=pb[:, :], func=AF.Copy,
                                                   scale=kdec[:, h:h + 1]), [("ps", 6), "kdec"], ["kdta"])

        def stage1_a(c, h):
            pe([("m", psf(6 + ch // 4, 128, 128 * (ch % 4)), kdta[:, ch, :], vtok[:, ch, 128 * c:128 * c + 128], True, True) for ch in range(8)],
               ["kdta"] + [("vtok", ch) for ch in range(8)], [("ps", 6), ("ps", 7)])
            g128 = math.exp(128.0 * LG[h])
            act(lambda e: e.activation(out=Sprev[:, c, 0, :], in_=Sf[:, h, :], func=AF.Copy), [("Sf", h)], [("Sprev", c, 0)])
            for ch in range(8):
                ab = 6 + ch // 4
                src = Sf[:, h, :] if ch % 2 == 0 else Stmp[:, c, :]
                dst = Stmp[:, c, :] if ch % 2 == 0 else Sf[:, h, :]
                sk_, dk_ = (("Sf", h), ("Stmp", c)) if ch % 2 == 0 else (("Stmp", c), ("Sf", h))
                dve(lambda e, src=src, dst=dst, ab=ab, ch=ch: e.scalar_tensor_tensor(
                    out=dst, in0=src, scalar=g128, in1=psf(ab, 128, 128 * (ch % 4)), op0=ALU.mult, op1=ALU.add),
                    [("ps", ab), sk_], [dk_])
                if ch < 7:
                    act(lambda e, dst=dst, ch=ch: e.activation(out=Sprev[:, c, ch + 1, :], in_=dst, func=AF.Copy),
                        [dk_], [("Sprev", c, ch + 1)])
            P.dma("sp", srp[h], Sf[:, h, :], reads=[("Sf", h)], track_out=True)

        def stage3(c, h):
            oall = oo[:, c, :]
            qk = [("rotout", ("q", c))]
            kk = [("rotout", ("k", c))]
            for hf in range(2):
                P.dma("sp", ssf2[:, hf], sret[SH * hf:SH * hf + SH, h].rearrange("s d e -> d s e"), writes=[("ssf", hf)])
            pb = psb(3)
            pe([("t", pb[:, 0:128], kT[:, c, TP:T], identb[:])], kk + ["identb"], [("ps", 3)])
            act(lambda e, pb=pb: e.activation(out=ktkd, in_=pb[:, 0:128], func=AF.Copy, scale=rmd[:, h, NSEQ:NSEQ + 1]),
                [("ps", 3), "rmd"], ["ktkd"])
            dve(lambda e: e.tensor_tensor(
                out=vexp, in0=vtok[:, 8, 128 * c:128 * c + 128].unsqueeze(1).broadcast_to([128, NSEQ, 128]),
                in1=rmd[:, 0, 0:NSEQ].unsqueeze(2).broadcast_to([128, NSEQ, 128]), op=ALU.mult), [("vtok", 8), "rmd"], ["vexp"])
            dve(lambda e: e.tensor_tensor(
                out=qdall.rearrange("p (a l) -> p a l", l=128), in0=qT[:, c, 0:TP].rearrange("p (a l) -> p a l", l=128),
                in1=qdec[:, c, :].unsqueeze(1).broadcast_to([128, 8, 128]), op=ALU.mult), qk + ["qdec"], ["qdall"])
            dve(lambda e: e.tensor_tensor(out=qds_, in0=qT[:, c, TP:T], in1=qdecs[:, c, :], op=ALU.mult), qk + ["qdecs"], ["qds"])
            pe([("m", psf(6 + ch // 4, 128, 128 * (ch % 4)), kT[:, c, 128 * ch:128 * ch + 128], qT[:, c, 128 * ch:128 * ch + 128], True, True) for ch in range(8)]
               + [("m", psf(2, 128), kT[:, c, TP:T], qT[:, c, TP:T], True, True)], qk + kk, [("ps", 6), ("ps", 7), ("ps", 2)])
            for half in range(2):
                dve(lambda e, half=half: e.tensor_tensor(
                    out=stma[:, 4 * half:4 * half + 4, :], in0=psf(6 + half).rearrange("p (a l) -> p a l", l=128),
                    in1=dmask[:, c, :].unsqueeze(1).broadcast_to([128, 4, 128]), op=ALU.mult), [("ps", 6 + half), "dmask"], [("stma", half)])
            dve(lambda e: e.tensor_tensor(out=stma[:, 8, :], in0=psf(2, 128), in1=smask[:, c, :], op=ALU.mult), [("ps", 2), "smask"], [("stma", 2)])
            mms = []
            for ch in range(8):
                o_ = psf(ch // 4, 128, 128 * (ch % 4))
                mms.append(("m", o_, vtok[:, ch, 128 * c:128 * c + 128], stma[:, ch, :], True, False))
                mms.append(("m", o_, Sprev[:, c, ch, :], qdall[:, 128 * ch:128 * ch + 128], False, True))
            pe(mms, [("vtok", ch) for ch in range(8)] + [("stma", 0), ("stma", 1), "qdall"] + [("Sprev", c, ch) for ch in range(8)],
               [("ps", 0), ("ps", 1)])
            act(lambda e: e.activation(out=oall[:, 0:TP], in_=PS[:, 0:2, :].rearrange("p b n -> p (b n)"), func=AF.Copy),
                [("ps", 0), ("ps", 1)], [("oall", c, 0), ("xl2", 0)])
            pe([("m", psf(4, 128), vtok[:, 8, 128 * c:128 * c + 128], stma[:, 8, :], True, True)], [("vtok", 8), ("stma", 2)], [("ps", 4)])
            act(lambda e: e.activation(out=inn, in_=psf(4, 128), func=AF.Copy), [("ps", 4)], ["inn"])
            g8 = math.exp(8.0 * LG[h])
            for hf in range(2):
                s0 = SH * hf
                ssf = ssf2[:, hf]
                act(lambda e, ssf=ssf: e.activation(out=ssb.rearrange("p s e -> p (s e)"), in_=ssf.rearrange("p s e -> p (s e)"), func=AF.Copy),
                    [("ssf", hf)], ["ssb"])
                pe([("m", psf(5, 8, 8 * (s0 + s_)), ssb[:, s_, :], qds_[:, 8 * (s0 + s_):8 * (s0 + s_) + 8], True, True) for s_ in range(SH)],
                   ["ssb", "qds"], [("ps", 5)])
                for q4 in range(2):
                    b = 2 + q4
                    sq0 = s0 + 4 * q4
                    pe([("m", psf(b), ktkd, vexp[:, sq0:sq0 + 4, :].rearrange("p s e -> p (s e)"), True, True)], ["ktkd", "vexp"], [("ps", b)])
                    dve(lambda e, b=b, q4=q4, ssf=ssf: e.scalar_tensor_tensor(
                        out=ssf[:, 4 * q4:4 * q4 + 4, :].rearrange("p s e -> p (s e)"),
                        in0=ssf[:, 4 * q4:4 * q4 + 4, :].rearrange("p s e -> p (s e)"), scalar=g8, in1=psf(b),
                        op0=ALU.mult, op1=ALU.add), [("ps", b), ("ssf", hf)], [("ssf", hf)])
                P.dma("sp", srs[s0:s0 + SH, h].rearrange("s d e -> d s e"), ssf, reads=[("ssf", hf)], track_out=True)
            dve(lambda e: e.tensor_tensor(out=oall[:, TP:T], in0=psf(5, 128), in1=inn, op=ALU.add), [("ps", 5), "inn"], [("oall", c, 1), ("xl2", 0)])
            osq_, okeys = gn_bufs(c)
            dve(lambda e: e.tensor_tensor(out=osq_, in0=oall, in1=oall, op=ALU.mult), [("oall", c, 0), ("oall", c, 1)], okeys)

        def gn_bufs(c):
            if c == 0:
                return t2, ["t2"]
            return osq_alt, ["vexp", "ktkd", "kdta"]

        GXB, GYB = [0, 1, 2], [3, 4, 5]

        def gn_mm(c):
            oall = oo[:, c, :]
            osq_, okeys = gn_bufs(c)
            oa = [("oall", c, 0), ("oall", c, 1)]
            pe([("m", psf(b, n), onesf[:], oall[:, t0:t0 + n], True, True) for (t0, n), b in zip(TB, GXB)], oa + ["onesf"], pkeys(GXB, TB))
            pe([("m", psf(b, n), onesf[:], osq_[:, t0:t0 + n], True, True) for (t0, n), b in zip(TB, GYB)], okeys + ["onesf"], pkeys(GYB, TB))

        def gn_head(c):
            mv_, qv_ = pview(GXB, T), pview(GYB, T)
            act(lambda e: e.activation(out=t1, in_=mv_, func=AF.Copy), pkeys(GXB, TB), ["t1"])
            dve(lambda e: e.tensor_tensor(out=t2, in0=t1, in1=t1, op=ALU.mult), ["t1"], ["t2"])
            dve(lambda e: e.tensor_tensor(out=t2, in0=qv_, in1=t2, op=ALU.subtract), pkeys(GYB, TB) + ["t2"], ["t2"])

        def gn_tail(c, h):
            oall = oo[:, c, :]
            oa = [("oall", c, 0), ("oall", c, 1)]
            act(lambda e: e.activation(out=t2, in_=t2, func=AF.Ln, bias=GN_EPS, scale=1.0), ["t2"], ["t2"])
            act(lambda e: e.activation(out=t2, in_=t2, func=AF.Exp, scale=-0.5), ["t2"], ["t2"])
            dve(lambda e: e.tensor_tensor(out=t1, in0=oall, in1=t1, op=ALU.subtract), oa + ["t1"], ["t1"])
            dve(lambda e: e.scalar_tensor_tensor(out=t1, in0=t1, scalar=gnT[:, h:h + 1], in1=t2, op0=ALU.mult, op1=ALU.mult),
                ["t1", "t2", "gnT"], ["t1"])
            dve(lambda e: e.tensor_tensor(out=catT[:, h, :], in0=t1, in1=sgT[:, c, :], op=ALU.mult), ["t1", ("sgT", c)], [("cat", h)])

        mkT = MKV[:, 0, :].rearrange("p (k t) -> p k t", k=KC)
        mvb = MKV[:, 1, :].rearrange("p (t d) -> p t d", t=2)
        csc = RA[:, 8 * T:16 * T]
        hmT = csc[:, 0:4096].rearrange("p (k t) -> p k t", k=KC)
        xn_mem = csc[:, 4096:8192].rearrange("p (t d) -> p t d", t=2)
        mst = csc[:, 8192:9216].bitcast(F32).rearrange("p (a c) -> p a c", a=2)
        xload2 = oo[:, :, :].rearrange("p a t -> p (a t)")[:, 0:D]

        def norm_mem():
            for i in range(2):
                xt = xload2
                P.dma("sp", xt, mem[128 * i:128 * i + 128, :], writes=[("xl2", 0)])
                act(lambda e, xt=xt, i=i: e.activation(out=xn_mem[:, i, :], in_=xt, func=AF.Square, accum_out=stat[:, 0:1]), [("xl2", 0)], [("xnm", i), "ss"])
                act(lambda e: e.activation(out=stat[:, 1:2], in_=stat[:, 0:1], func=AF.Sqrt, bias=RMS_EPS, scale=1.0 / D), ["ss"], ["ms"])
                dve(lambda e: e.reciprocal(out=stat[:, 2:3], in_=stat[:, 1:2]), ["ms"], ["rstd"])
                dve(lambda e, xt=xt, i=i: e.tensor_scalar(out=xn_mem[:, i, :], in0=xt, scalar1=stat[:, 2:3], scalar2=None,
                                                          op0=ALU.mult), [("xl2", 0), "rstd"], [("xnm", i)])
            for kc in range(KC):
                bank = 6 + (kc % 2)
                pb = psb(bank)
                pe([("t", pb[:, 128 * j:128 * j + 128], xn_mem[:, j, 128 * kc:128 * kc + 128], identb[:]) for j in range(2)],
                   [("xnm", 0), ("xnm", 1), "identb"], [("ps", bank)])
                act(lambda e, pb=pb, kc=kc: e.activation(out=hmT[:, kc, :], in_=pb[:, 0:256], func=AF.Copy, scale=gT[:, 1, kc:kc + 1]),
                    [("ps", bank), "gT"], [("hmT", kc)])

        hm_reads = [("hmT", kc) for kc in range(KC)]

        def memkv_block(which, i, blk):
            dst = mko if which == 0 else mvo
            slot, sk = wslot(blk)
            for t in range(2):
                b = 6 + t
                mms = [("m", psf(b, 256), hmT[:, kc, 128 * t:128 * t + 128], slot[:, kc, :], kc == 0, kc == KC - 1) for kc in range(KC)]
                pe(mms, [sk] + hm_reads, [("ps", b)])
                if t == 0:
                    j = i % 2
                    act(lambda e, b=b, j=j: e.activation(out=mst[:, j, :], in_=psf(b, 256), func=AF.Copy), [("ps", b)], [("mst", j)])
                    P.dma("sp", dst[:, 256 * i:256 * i + 256], mst[:, j, :], reads=[("mst", j)], track_out=True)
                if which == 1:
                    dve(lambda e, b=b, t=t, i=i: e.tensor_copy(out=mvb[:, t, 256 * i:256 * i + 256], in_=psf(b, 256)),
                        [("ps", b)], [("mvb", t, i)])
            if which == 0:
                for c in range(2):
                    b = 6 + c
                    mms = [("m", psf(b, 256), slot[:, kc, 128 * c:128 * c + 128], hmT[:, kc, :], kc == 0, kc == KC - 1) for kc in range(KC)]
                    pe(mms, [sk] + hm_reads, [("ps", b)])
                    act(lambda e, b=b, i=i, c=c: e.activation(out=mkT[:, 2 * i + c, :], in_=psf(b, 256), func=AF.Copy),
                        [("ps", b)], [("mkT", 2 * i + c)])

        XB_, YB_ = [0, 1, 2], [3, 4, 5]
        for hp in range(4):
            P.dma("sp", dmask, c_dmask[:, 2 * hp:2 * hp + 2, :], writes=["dmask"])
            P.dma("sp", smask, c_smask[:, 2 * hp:2 * hp + 2, :], writes=["smask"])
            P.dma("sp", qdec, c_qdec[:, 2 * hp:2 * hp + 2, :], writes=["qdec"])
            P.dma("sp", qdecs, c_qdecs[:, 2 * hp:2 * hp + 2, :], writes=["qdecs"])
            KS = 128.0 ** -0.5
            if hp > 0:
                slot, sk = wslot(W["k"][hp])
                proj_b(slot, sk, 0, hT, [], TB, XB_)
                proj_b(slot, sk, 1, hT, [], TB, YB_)
            rotary(XB_, TB, rot, KS, kT[:, 0, :], ("k", 0))
            rotary_pre(YB_, TB, rot, KS)
            if hp == 0:
                norm_mem()
            memkv_block(*W["mkv"][4 * hp + 0])
            rotary_post(YB_, TB, rot, KS, kT[:, 1, :], ("k", 1))
            slot, sk = wslot(W["q"][hp])
            stage1_t(0, 2 * hp)
            proj_b(slot, sk, 0, hT, [], TB, XB_)
            stage1_a(0, 2 * hp)
            proj_b(slot, sk, 1, hT, [], TB, YB_)
            rotary(XB_, TB, rot, 1.0, qT[:, 0, :], ("q", 0))
            rotary_pre(YB_, TB, rot, 1.0)
            stage1_t(1, 2 * hp + 1)
            memkv_block(*W["mkv"][4 * hp + 1])
            slot_g, sk_g = wslot(W["g"][hp])
            proj_b(slot_g, sk_g, 0, hT, [], TB, XB_)
            pv_ = pview(XB_, T)
            act(lambda e, pv_=pv_: e.activation(out=sgT[:, 0, :], in_=pv_, func=AF.Silu), pkeys(XB_, TB), [("sgT", 0)])
            stage1_a(1, 2 * hp + 1)
            rotary_post(YB_, TB, rot, 1.0, qT[:, 1, :], ("q", 1))
            proj_b(slot_g, sk_g, 1, hT, [], TB, YB_)
            pv_ = pview(YB_, T)
            act(lambda e, pv_=pv_: e.activation(out=sgT[:, 1, :], in_=pv_, func=AF.Silu), pkeys(YB_, TB), [("sgT", 1)])
            stage3(0, 2 * hp)
            memkv_block(*W["mkv"][4 * hp + 2])
            stage3(1, 2 * hp + 1)
            memkv_block(*W["mkv"][4 * hp + 3])
            gn_mm(0)
            gn_head(0)
            gn_mm(1)
            if hp < 3:
                slot, sk = wslot(W["v"][hp + 1])
                proj_v(slot, sk)
            gn_tail(0, 2 * hp)
            gn_head(1)
            gn_tail(1, 2 * hp + 1)
        slot_cg0, sk_cg0 = wslot(W["cg"][0])
        cg0_banks = [next_banks(), next_banks()]
        for c in range(2):
            proj_b(slot_cg0, sk_cg0, c, hT, [], TB, cg0_banks[c])
        P.fence()

        P.phase(4)
        P.dma("sp", sc_in[0:32, :], sconv, writes=["sc_in"])

        def sconv_transposes():
            for chk in range(8):
                pe([("t", psf(6, 32), sc_in[0:32, 128 * chk:128 * chk + 128], identf[0:32, 0:32])], ["sc_in", "identf"], [("ps", 6)])
                act(lambda e, chk=chk: e.activation(out=scT[:, chk, :], in_=psf(6, 32), func=AF.Copy), [("ps", 6)], [("scT", chk)])

        for cp in range(4):
            if cp > 0:
                slot_cg, sk_cg = wslot(W["cg"][cp])
            cg_sb = [t1, t2]
            for c in range(2):
                if cp == 0:
                    banks = cg0_banks[c]
                else:
                    banks = next_banks()
                    proj_b(slot_cg, sk_cg, c, hT, [], TB, banks)
                for (t0, n), b in zip(TB, banks):
                    act(lambda e, b=b, n=n, t0=t0, c=c: e.activation(out=cg_sb[c][:, t0:t0 + n], in_=psf(b, n), func=AF.Copy),
                        [("ps", b)], [("cgsb", c, t0)])
            if cp == 0:
                sconv_transposes()
            slot_h, sk_h = wslot(W["hin"][cp])
            for c in range(2):
                chk = 2 * cp + c
                banks = next_banks()
                proj_b(slot_h, sk_h, c, hT, [], TB, banks)
                up = upad[:, c * (TP + 2):(c + 1) * (TP + 2)]
                dve(lambda e, up=up, chk=chk: e.tensor_copy(out=up[:, 0:2], in_=uprev[:, chk, :]), [("uprev", chk)], [("upad", c, -1)])
                for (t0, n), b in zip(TBP, banks):
                    dve(lambda e, b=b, n=n, t0=t0, c=c, up=up: e.tensor_tensor(
                        out=up[:, 2 + t0:2 + t0 + n], in0=cg_sb[c][:, t0:t0 + n], in1=psf(b, n), op=ALU.mult),
                        [("ps", b), ("cgsb", c, t0)], [("upad", c, t0)])
                ups = upads
                dve(lambda e, chk=chk: e.tensor_copy(out=upads[:, :, 0:2], in_=scT[:, chk, :].rearrange("p (s k) -> p s k", k=2)),
                    [("scT", chk)], [("upads", 0)])
                b = banks[2]
                dve(lambda e, b=b, c=c: e.tensor_tensor(
                    out=upads[:, :, 2:10], in0=cg_sb[c][:, TP:T].rearrange("p (s k) -> p s k", k=8),
                    in1=psf(b, 128).rearrange("p (s k) -> p s k", k=8), op=ALU.mult),
                    [("ps", b), ("cgsb", c, 1024)], [("upads", 1)])
                upr = [("upad", c, -1), ("upad", c, 0), ("upad", c, 512)]
                dve(lambda e, up=up, chk=chk: e.tensor_scalar(out=cacc[:, 0:TP], in0=up[:, 0:TP], scalar1=cwT[:, 0, chk:chk + 1],
                                                              scalar2=None, op0=ALU.mult), upr + ["cwT"], [("cacc", 0)])
                for k in (1, 2):
                    dve(lambda e, up=up, chk=chk, k=k: e.scalar_tensor_tensor(
                        out=cacc[:, 0:TP], in0=up[:, k:k + TP], scalar=cwT[:, k, chk:chk + 1], in1=cacc[:, 0:TP],
                        op0=ALU.mult, op1=ALU.add), upr + [("cacc", 0)], [("cacc", 0)])
                ca_s = cacc[:, TP:T].rearrange("p (s k) -> p s k", k=8)
                usr = [("upads", 0), ("upads", 1)]
                dve(lambda e, chk=chk: e.tensor_scalar(out=ca_s, in0=upads[:, :, 0:8], scalar1=cwT[:, 0, chk:chk + 1],
                                                       scalar2=None, op0=ALU.mult), usr + ["cwT"], [("cacc", 1)])
                for k in (1, 2):
                    dve(lambda e, chk=chk, k=k: e.scalar_tensor_tensor(
                        out=ca_s, in0=upads[:, :, k:k + 8], scalar=cwT[:, k, chk:chk + 1], in1=ca_s,
                        op0=ALU.mult, op1=ALU.add), usr + [("cacc", 1)], [("cacc", 1)])
                pe([("t", psf(7, 128)[0:2, :], up[:, TP:TP + 2], identf[:])], [("upad", c, 512), "identf"], [("ps", 7)])
                act(lambda e, chk=chk: e.activation(out=sc_out[0:2, 128 * chk:128 * chk + 128], in_=psf(7, 128)[0:2, :], func=AF.Copy),
                    [("ps", 7)], [("sc_out", chk)])
                dve(lambda e: e.tensor_copy(out=nbt.rearrange("p (s k) -> p s k", k=2), in_=upads[:, :, 8:10]), [("upads", 1)], ["nbt"])
                pe([("t", psf(7, 128)[0:32, :], nbt, identf[:])], ["nbt", "identf"], [("ps", 7)])
                act(lambda e, chk=chk: e.activation(out=sc_in[0:32, 128 * chk:128 * chk + 128], in_=psf(7, 128)[0:32, :], func=AF.Copy),
                    [("ps", 7)], [("sc_in2", chk), "sc_in"])
                dve(lambda e, c=c: e.tensor_copy(out=cg_sb[c][:, :], in_=cacc[:, :]), [("cacc", 0), ("cacc", 1), ("cgsb", c, 0), ("cgsb", c, 512), ("cgsb", c, 1024)],
                    [("cgsb", c, 0), ("cgsb", c, 512), ("cgsb", c, 1024)])
            slot_b, sk_b = wslot(W["bg"][cp])
            for c in range(2):
                chk = 2 * cp + c
                banks = next_banks()
                proj_b(slot_b, sk_b, c, hT, [], TB, banks)
                for (t0, n), b in zip(TB, banks):
                    dve(lambda e, b=b, n=n, t0=t0, c=c, chk=chk: e.tensor_tensor(
                        out=catT[:, 8 + chk, t0:t0 + n], in0=cg_sb[c][:, t0:t0 + n], in1=psf(b, n), op=ALU.mult),
                        [("ps", b), ("cgsb", c, t0)], [("cat", 8 + chk, t0)])
        P.dma("sp", scp, sc_out[0:2, :], reads=[("sc_out", i) for i in range(8)], track_out=True)
        P.dma("sp", scs, sc_in[0:32, :], reads=[("sc_in2", i) for i in range(8)], track_out=True)
        P.fence()

        P.phase(5)
        def out_proj(wkey, srcT, accumulate_first_from_dram, after_tile=None):
            if accumulate_first_from_dram is not None:
                for t in range(NT):
                    P.dma("sp", xres[:, t, :], accumulate_first_from_dram[t], writes=[("x", t, cb) for cb in range(4)])
            for cb in range(4):
                s0, k0 = wslot(W[wkey][cb][0])
                s1, k1 = wslot(W[wkey][cb][1])
                for t in range(NT):
                    b = t % 4
                    mms = []
                    for kc in range(KC):
                        s_ = s0 if kc < 8 else s1
                        mms.append(("m", psf(b), srcT[:, kc, 128 * t:128 * t + 128], s_[:, kc % 8, :], kc == 0, kc == KC - 1))
                    pe(mms, [k0, k1], [("ps", b)])
                    dve(lambda e, b=b, t=t, cb=cb: e.tensor_tensor(out=xres[:, t, 512 * cb:512 * cb + 512],
                                                                   in0=xres[:, t, 512 * cb:512 * cb + 512], in1=psf(b), op=ALU.add),
                        [("ps", b), ("x", t, cb)], [("x", t, cb)])
                    if cb == 3 and after_tile is not None and t >= 1:
                        after_tile(t - 1)
            if after_tile is not None:
                after_tile(NT - 1)

        def norm_hook(gidx):
            return lambda t: norm_tile(t, xres[:, t, :], [("x", t, cb) for cb in range(4)], gidx, hT, "hres")

        out_proj("mixo", catT, srcs, after_tile=norm_hook(2))
        P.fence()

        P.phase(6)
        qxT = RA[:, :].rearrange("p (k t) -> p k t", k=KC)
        for i in range(8):
            slot, sk = wslot(W["xq"][i])
            for c in range(2):
                banks = next_banks()
                proj_b(slot, sk, c, hT, [], TB, banks)
                for (t0, n), b in zip(TB, banks):
                    act(lambda e, b=b, n=n, t0=t0, i=i, c=c: e.activation(out=qxT[:, 2 * i + c, t0:t0 + n], in_=psf(b, n), func=AF.Copy),
                        [("ps", b)], [("qx", 2 * i + c, t0)])
        P.fence()

        P.phase(7)
        rh_off = [0]

        def rh(n, dt=BF16):
            nb = n if dt == BF16 else 2 * n
            o = rh_off[0]
            rh_off[0] += nb
            assert rh_off[0] <= KC * T, rh_off[0]
            v = RH[:, o:o + nb]
            if dt == F32:
                v = v.bitcast(F32)
            return v

        rh_xn0 = 0
        P.phase(8)
        rh_off[0] = rh_xn0
        pn = rh(2 * 1024).rearrange("p (a t n) -> p a t n", a=2, t=4)
        pT = rh(2 * 1024).rearrange("p (a c l) -> p a c l", a=2, c=2)
        XS = 512.0 ** -0.5
        oT = qxT
        combos = [(hd, g4) for hd in range(4) for g4 in range(2)]

        def h2_a(i):
            hd, g4 = combos[i]
            b0 = 2 * (i % 2)
            mms = []
            for tt in range(4):
                t = 4 * g4 + tt
                o_ = psf(b0 + tt // 2, 256, 256 * (tt % 2))
                for dc in range(4):
                    mms.append(("m", o_, qxT[:, 4 * hd + dc, 128 * t:128 * t + 128], mkT[:, 4 * hd + dc, :], dc == 0, dc == 3))
            pe(mms, [], [("ps", b0), ("ps", b0 + 1)])

        def h2_b(i):
            hd, g4 = combos[i]
            pj = i % 2
            b0 = 2 * pj
            skeys = [("ps", b0), ("ps", b0 + 1)]
            so = 32 + 16 * pj
            sc4 = PS[:, b0:b0 + 2, :].rearrange("p b (t n) -> p (b t) n", t=2)
            dve(lambda e: e.reduce_max(out=stat[:, so:so + 4], in_=sc4, axis=AX.X), skeys, [("mx", pj)])
            dve(lambda e: e.tensor_scalar(out=stat[:, so + 4:so + 8], in0=stat[:, so:so + 4], scalar1=-XS, scalar2=None, op0=ALU.mult),
                [("mx", pj)], [("nmx", pj)])
            for tt in range(4):
                act(lambda e, tt=tt: e.activation(out=pn[:, pj, tt, :], in_=psf(b0 + tt // 2, 256, 256 * (tt % 2)), func=AF.Exp,
                                                  bias=stat[:, so + 4 + tt:so + 5 + tt], scale=XS, accum_out=stat[:, so + 8 + tt:so + 9 + tt]),
                    [("ps", b0 + tt // 2), ("nmx", pj)], [("pn", pj, tt), ("sm", pj, tt)])
            dve(lambda e: e.reciprocal(out=stat[:, so + 12:so + 16], in_=stat[:, so + 8:so + 12]), [("sm", pj, tt) for tt in range(4)], [("rs", pj)])
            dve(lambda e: e.tensor_tensor(out=pn[:, pj], in0=pn[:, pj], in1=stat[:, so + 12:so + 16].unsqueeze(2).broadcast_to([128, 4, 256]), op=ALU.mult),
                [("pn", pj, tt) for tt in range(4)] + [("rs", pj)], [("pn", pj, tt) for tt in range(4)])
            pb = psb(4 + pj)
            pe([("t", pb[:, 512 * nc_ + 128 * tt:512 * nc_ + 128 * tt + 128], pn[:, pj, tt, 128 * nc_:128 * nc_ + 128], identb[:]) for tt in range(4) for nc_ in range(2)],
               [("pn", pj, tt) for tt in range(4)] + ["identb"], [("ps", 4 + pj)])
            act(lambda e: e.activation(out=pT[:, pj].rearrange("p c l -> p (c l)"), in_=pb[:, :], func=AF.Copy), [("ps", 4 + pj)], [("pT", pj)])
            for e2 in range(2):
                mms = []
                for k_ in range(2):
                    ec = 2 * e2 + k_
                    for nc_ in range(2):
                        mms.append(("m", psf(6 + k_), mvb[:, nc_, 512 * hd + 128 * ec:512 * hd + 128 * ec + 128], pT[:, pj, nc_, :], nc_ == 0, nc_ == 1))
                pe(mms, [("pT", pj)], [("ps", 6), ("ps", 7)])
                act(lambda e, e2=e2: e.activation(out=oT[:, 4 * hd + 2 * e2:4 * hd + 2 * e2 + 2, 512 * g4:512 * g4 + 512],
                                                  in_=PS[:, 6:8, :], func=AF.Copy), [("ps", 6), ("ps", 7)], [("oT", hd, g4, e2)])

        h2_a(0)
        for i in range(len(combos)):
            if i + 1 < len(combos):
                h2_a(i + 1)
            h2_b(i)
        P.fence()

        P.phase(9)
        rh_off[0] = 0
        kraw = rh(2 * 2 * D).rearrange("p (a t d) -> p a t d", a=2, t=2)
        vraw = rh(2 * 2 * D).rearrange("p (a t d) -> p a t d", a=2, t=2)
        kTs = rh(2 * 1024).rearrange("p (a x) -> p a x", a=2)
        oob = oo[:, :, :].rearrange("p a t -> p (a t)").bitcast(BF16)
        ps_s = oob[:, 0:1024].rearrange("p (h n) -> p h n", h=4)
        pTs = oob[:, 1024:1088].rearrange("p (h c l) -> p h c l", h=4, c=2)
        scs_ = oo[:, :, :].rearrange("p a t -> p (a t)")[:, 1024:2048].rearrange("p (h n) -> p h n", h=4)
        def h3_load_k(s):
            a = s % 2
            P.dma("pool", kraw[:, a], ck[s].rearrange("(t p) d -> p t d", p=128), writes=[("kraw", a)])

        def h3_load_v(s):
            a = s % 2
            P.dma("pool", vraw[:, a], cv[s].rearrange("(t p) d -> p t d", p=128), writes=[("vraw", a)])

        def h3_load(s):
            h3_load_k(s)
            h3_load_v(s)

        def h3_a(s):
            a = s % 2
            sb0 = 2 + 2 * a

            def tr(hd):
                j = hd % 2
                pb = psb(j)
                mms = [("t", pb[:, 256 * dc + 128 * nt_:256 * dc + 128 * nt_ + 128], kraw[:, a, nt_, 512 * hd + 128 * dc:512 * hd + 128 * dc + 128], identb[:])
                       for dc in range(4) for nt_ in range(2)]
                pe(mms, [("kraw", a), "identb"], [("ps", j)])
                act(lambda e, pb=pb, j=j: e.activation(out=kTs[:, j, :], in_=pb[:, :], func=AF.Copy), [("ps", j)], [("kTs", j)])

            def sc_(hd):
                j = hd % 2
                sbk = sb0 + hd // 2
                so = 256 * (hd % 2)
                mms = [("m", psf(sbk, 256, so)[0:8, :], qxT[:, 4 * hd + dc, TP + 8 * s:TP + 8 * s + 8], kTs[:, j, 256 * dc:256 * dc + 256], dc == 0, dc == 3) for dc in range(4)]
                pe(mms, [("kTs", j)], [("ps", sbk)])

            tr(0); tr(1); sc_(0); tr(2); sc_(1); tr(3); sc_(2); sc_(3)

        def h3_b(s):
            a = s % 2
            sb0 = 2 + 2 * a
            skeys = [("ps", sb0), ("ps", sb0 + 1)]
            sc = PS[0:8, sb0:sb0 + 2, :].rearrange("p b n -> p (b n)")
            sc4 = PS[0:8, sb0:sb0 + 2, :].rearrange("p b (t n) -> p (b t) n", t=2)
            dve(lambda e: e.reduce_max(out=stat[0:8, 16:20], in_=sc4, axis=AX.X), skeys, ["mx"])
            dve(lambda e: e.tensor_tensor(out=scs_[0:8], in0=sc4, in1=stat[0:8, 16:20].unsqueeze(2).broadcast_to([8, 4, 256]), op=ALU.subtract),
                skeys + ["mx"], ["scs"])
            act(lambda e: e.activation(out=ps_s[0:8].rearrange("p h n -> p (h n)"), in_=scs_[0:8].rearrange("p h n -> p (h n)"), func=AF.Exp, scale=XS),
                ["scs"], ["pss"])
            dve(lambda e: e.reduce_sum(out=stat[0:8, 24:28], in_=ps_s[0:8], axis=AX.X), ["pss"], ["sm"])
            dve(lambda e: e.reciprocal(out=stat[0:8, 28:32], in_=stat[0:8, 24:28]), ["sm"], ["rs"])
            dve(lambda e: e.tensor_tensor(out=ps_s[0:8], in0=ps_s[0:8], in1=stat[0:8, 28:32].unsqueeze(2).broadcast_to([8, 4, 256]), op=ALU.mult),
                ["pss", "rs"], ["pss"])
            pb2 = psb(6)
            pe([("t", pb2[:, 16 * hd + 8 * nc_:16 * hd + 8 * nc_ + 8], ps_s[0:8, hd, 128 * nc_:128 * nc_ + 128], identb[0:8, 0:8]) for hd in range(4) for nc_ in range(2)],
               ["pss", "identb"], [("ps", 6)])
            act(lambda e: e.activation(out=pTs.rearrange("p h c l -> p (h c l)"), in_=pb2[:, 0:64], func=AF.Copy), [("ps", 6)], ["pTs"])
            mms = []
            for hd in range(4):
                for ec in range(4):
                    for nc_ in range(2):
                        mms.append(("m", psf(7, 8, 8 * (4 * hd + ec)), vraw[:, a, nc_, 512 * hd + 128 * ec:512 * hd + 128 * ec + 128], pTs[:, hd, nc_, :], nc_ == 0, nc_ == 1))
            pe(mms, [("vraw", a), "pTs"], [("ps", 7)])
            act(lambda e: e.activation(out=oT[:, :, TP + 8 * s:TP + 8 * s + 8], in_=psf(7, 128).rearrange("p (c l) -> p c l", c=16), func=AF.Copy),
                [("ps", 7)], [("oTs", s)])

        h3_load(0); h3_load(1)
        h3_a(0)
        for s in range(NSEQ):
            if s + 1 < NSEQ:
                h3_a(s + 1)
            if s + 2 < NSEQ:
                h3_load_k(s + 2)
            h3_b(s)
            if s + 2 < NSEQ:
                h3_load_v(s + 2)
        P.fence()

        P.phase(10)
        out_proj("xo", oT, None, after_tile=norm_hook(3))
        P.fence()

        P.phase(11)
        hid = RA[:, 0:2 * 4 * T].rearrange("p (a k t) -> p a k t", a=2, k=4)
        gsb = RA[:, 2 * 4 * T:2 * 4 * T + 2 * T].bitcast(F32)
        for g in range(NG):
            a = g % 2
            for j in range(2):
                sgj, kgj = wslot(W["gate"][g][j])
                suj, kuj = wslot(W["up"][g][j])
                for c in range(2):
                    kk_ = 2 * j + c
                    bg_ = next_banks()
                    hrd = ["hT"] if g == NG - 1 else []
                    proj_b(sgj, kgj, c, hT, hrd, TB, bg_)
                    for (t0, n), b in zip(TB, bg_):
                        act(lambda e, b=b, n=n, t0=t0: e.activation(out=gsb[:, t0:t0 + n], in_=psf(b, n), func=AF.Silu),
                            [("ps", b)], [("gsb", t0)])
                    bu_ = next_banks()
                    proj_b(suj, kuj, c, hT, hrd, TB, bu_)
                    for (t0, n), b in zip(TB, bu_):
                        dve(lambda e, b=b, n=n, t0=t0, a=a, kk_=kk_: e.tensor_tensor(out=hid[:, a, kk_, t0:t0 + n], in0=gsb[:, t0:t0 + n], in1=psf(b, n), op=ALU.mult),
                            [("ps", b), ("gsb", t0)], [("hid", a, kk_, t0)])
            hr = [("hid", a, k_, t0) for k_ in range(4) for t0, _ in TB]

            def down_tile(t, cb, sd, kd_, b):
                co = 512 * (cb % 2)
                mms = [("m", psf(b), hid[:, a, k_, 128 * t:128 * t + 128], sd[:, k_, co:co + 512], k_ == 0, k_ == 3) for k_ in range(4)]
                pe(mms, [kd_] + hr, [("ps", b)])
                dve(lambda e, b=b, t=t, cb=cb: e.tensor_tensor(out=xres[:, t, 512 * cb:512 * cb + 512],
                                                               in0=xres[:, t, 512 * cb:512 * cb + 512], in1=psf(b), op=ALU.add),
                    [("ps", b), ("x", t, cb)], [("x", t, cb)])

            if g < NG - 1:
                for cb in range(4):
                    if cb % 2 == 0:
                        sd, kd_ = wslot(W["down"][g][cb // 2])
                    for t in range(NT):
                        down_tile(t, cb, sd, kd_, 6 + (t % 2))
            else:
                sd0, kd0 = wslot(W["down"][g][0])
                sd1, kd1 = wslot(W["down"][g][1])
                for t in range(NT):
                    for cb in range(4):
                        down_tile(t, cb, (sd0, sd1)[cb // 2], (kd0, kd1)[cb // 2], 4 + cb)

        P.phase(12)
        gbc = RH[:, 0:2 * D].bitcast(F32)
        P.dma("sp", gbc, final_g.partition_broadcast(128), reads=[], writes=["gbc", "hT"])
        sq3 = RH[:, 2 * D:3 * D]
        yst = RH[:, 3 * D:7 * D].bitcast(F32).rearrange("p (a d) -> p a d", a=2)
        for t in range(NT):
            a = t % 2
            act(lambda e, t=t: e.activation(out=sq3, in_=xres[:, t, :], func=AF.Square, accum_out=stat[:, 0:1]), [("x", t, cb) for cb in range(4)] + ["gbc"], ["sq3", "ss"])
            act(lambda e: e.activation(out=stat[:, 1:2], in_=stat[:, 0:1], func=AF.Sqrt, bias=RMS_EPS, scale=1.0 / D), ["ss"], ["ms"])
            dve(lambda e: e.reciprocal(out=stat[:, 2:3], in_=stat[:, 1:2]), ["ms"], ["rstd"])
            dve(lambda e, t=t, a=a: e.scalar_tensor_tensor(out=yst[:, a, :], in0=xres[:, t, :], scalar=stat[:, 2:3], in1=gbc, op0=ALU.mult, op1=ALU.mult),
                ["rstd", "gbc"] + [("x", t, cb) for cb in range(4)], [("yst", a)])
            dst = yp[128 * t:128 * t + 128, :] if t < 8 else ys[:, :]
            P.dma("sp", dst, yst[:, a, :], reads=[("yst", a)], track_out=True)
        P.dead = False
        final_waits = []
        for k, v in P.dma_tokens:
            if v > P.seen["sp"].get(k, 0):
                P.seen["sp"][k] = v
                final_waits.append((k, v))
        P.q["sp"].append((final_waits, None, None))

        print("engine op counts", {e: (P.epoch[e], P.cnt[e]) for e in ENGS}, "n_sems", len(P.sems), flush=True)
        with nc.Block() as block:
            P.emit(block)
        print("n_sems", len(P.sems), flush=True)
    return nc


def _consts(half):
    c = {}
    c["c_ident"] = np.eye(128, dtype=np.float32)
    perm = np.zeros((128, 128), np.float32)
    for m in range(128):
        perm[(m + 64) % 128, m] = 1.0
    c["c_perm"] = perm
    inv = 10000.0 ** (-np.arange(64, dtype=np.float64) / 64.0)

    def rot_table(pos):
        ang = pos.astype(np.float64)[None, :] * inv[:, None]
        cos = np.cos(ang).astype(np.float32); sin = np.sin(ang).astype(np.float32)
        tab = np.zeros((128, 2, pos.shape[0]), np.float32)
        tab[:64, 0] = cos; tab[64:, 0] = cos
        tab[:64, 1] = -sin; tab[64:, 1] = sin
        return tab
    pos_main = np.concatenate([half * TP + np.arange(TP), np.tile(16384 + np.arange(8), NSEQ)])
    c["c_rot"] = rot_table(pos_main)
    c["c_rotp"] = rot_table(np.arange(TP))
    lg = np.array(LG, dtype=np.float64)
    m = np.arange(128)[:, None]; l = np.arange(128)[None, :]
    dm = np.zeros((128, H, 128)); sm = np.zeros((128, H, 128))
    for h in range(H):
        rel = l - m
        dm[:, h, :] = np.where(rel >= 0, np.exp(rel * lg[h]), 0.0)
        same = (m // 8) == (l // 8)
        sm[:, h, :] = np.where((rel >= 0) & same, np.exp(rel * lg[h]), 0.0)
    c["c_dmask"] = dm.astype(np.float32); c["c_smask"] = sm.astype(np.float32)
    qd = np.zeros((128, H, 128)); qds = np.zeros((128, H, 128))
    for h in range(H):
        qd[:, h, :] = np.exp((np.arange(128) + 1.0) * lg[h])[None, :]
        qds[:, h, :] = np.exp(((np.arange(128) % 8) + 1.0) * lg[h])[None, :]
    c["c_qdec"] = qd.astype(np.float32); c["c_qdecs"] = qds.astype(np.float32)
    kd = np.zeros((128, H)); kdp = np.zeros((128, H, 8)); rmd = np.zeros((128, H, NSEQ + 1))
    for h in range(H):
        kd[:, h] = np.exp((127.0 - np.arange(128)) * lg[h])
        for ch in range(8):
            kdp[:, h, ch] = np.exp((1023.0 - (128 * ch + np.arange(128))) * lg[h])
        for s in range(NSEQ):
            mm_ = np.arange(128)
            rmd[:, h, s] = np.where(mm_ // 8 == s, 1.0, 0.0)
        rmd[:, h, NSEQ] = np.exp((7.0 - (np.arange(128) % 8)) * lg[h])
    c["c_kdec"] = kd.astype(np.float32); c["c_kdecp"] = kdp.astype(np.float32); c["c_rmd"] = rmd.astype(np.float32)
    return c


def _core_inputs(c, shared, x_prompt, x_sample, mem_prompt, state_ret, state_conv, cache_mem_k, cache_mem_v):
    b, half = c // 2, c % 2
    m = dict(shared)
    m["xp"] = np.ascontiguousarray(x_prompt[b, half * TP:(half + 1) * TP])
    m["xprev"] = np.ascontiguousarray(x_prompt[b, 0:TP]) if half == 1 else np.zeros((TP, D), np.float32)
    m["xs"] = np.ascontiguousarray(x_sample[NSEQ * c:NSEQ * (c + 1)].reshape(TS, D))
    own = mem_prompt[b, 128 * half:128 * half + 128]
    oth = mem_prompt[b, 128 * (1 - half):128 * (1 - half) + 128]
    m["mem"] = np.ascontiguousarray(np.concatenate([own, oth], axis=0))
    m["sret"] = np.ascontiguousarray(state_ret[0, NSEQ * c:NSEQ * (c + 1)])
    m["sconv"] = np.ascontiguousarray(state_conv[0, NSEQ * c:NSEQ * (c + 1)].reshape(NSEQ * 2, 1024))
    m["ck"] = np.ascontiguousarray(cache_mem_k[0, NSEQ * c:NSEQ * (c + 1)].reshape(NSEQ, NMEM, D))
    m["cv"] = np.ascontiguousarray(cache_mem_v[0, NSEQ * c:NSEQ * (c + 1)].reshape(NSEQ, NMEM, D))
    m.update(_consts(half))
    return m


_NC_CACHE = {}


def kernel(x_prompt, x_sample, mem_prompt, state_ret, state_conv, cache_mem_k, cache_mem_v,
           ln_mix_g, w_in, conv_w, ret_gn_g, w_mix_out, ln_mem_g, ln_xa_g, w_xq, w_mk, w_mv,
           w_xo, ln_ffn_g, w_gate, w_up, w_down, final_g):
    f = lambda a: np.ascontiguousarray(np.asarray(a, dtype=np.float32))
    x_prompt, x_sample, mem_prompt = f(x_prompt), f(x_sample), f(mem_prompt)
    state_ret, state_conv, cache_mem_k, cache_mem_v = f(state_ret), f(state_conv), f(cache_mem_k), f(cache_mem_v)
    shared = {
        "ln_mix_g": f(ln_mix_g)[0], "w_in": f(w_in)[0], "conv_w": f(conv_w)[0], "ret_gn_g": f(ret_gn_g)[0],
        "w_mix_out": f(w_mix_out)[0], "ln_mem_g": f(ln_mem_g)[0], "ln_xa_g": f(ln_xa_g)[0], "w_xq": f(w_xq)[0],
        "w_mk": f(w_mk)[0], "w_mv": f(w_mv)[0], "w_xo": f(w_xo)[0], "ln_ffn_g": f(ln_ffn_g)[0],
        "w_gate": f(w_gate)[0], "w_up": f(w_up)[0], "w_down": f(w_down)[0], "final_g": f(final_g),
    }
    if "nc" not in _NC_CACHE:
        _NC_CACHE["nc"] = build_program()
    nc = _NC_CACHE["nc"]
    in_maps = [_core_inputs(c, shared, x_prompt, x_sample, mem_prompt, state_ret, state_conv, cache_mem_k, cache_mem_v)
               for c in range(8)]
    res = run_bass_kernel_spmd(nc, in_maps, core_ids=list(range(8)))
    R = res.results
    y_prompt = np.zeros((4, 2048, D), np.float32)
    y_sample = np.zeros((128, 8, D), np.float32)
    srp_o = np.zeros((1, 4, H, 128, 128), np.float32)
    srs_o = np.zeros((1, 128, H, 128, 128), np.float32)
    scp_o = np.zeros((1, 4, 2, 1024), np.float32)
    scs_o = np.zeros((1, 128, 2, 1024), np.float32)
    mk_o = np.zeros((1, 4, NMEM, 4, 512), np.float32)
    mv_o = np.zeros((1, 4, NMEM, 4, 512), np.float32)
    for c in range(8):
        b, half = c // 2, c % 2
        r = R[c]
        y_prompt[b, half * TP:(half + 1) * TP] = r["yp"]
        y_sample[NSEQ * c:NSEQ * (c + 1)] = r["ys"].reshape(NSEQ, 8, D)
        if half == 1:
            srp_o[0, b] = r["srp"]
            scp_o[0, b] = r["scp"]
        srs_o[0, NSEQ * c:NSEQ * (c + 1)] = r["srs"]
        scs_o[0, NSEQ * c:NSEQ * (c + 1)] = r["scs"].reshape(NSEQ, 2, 1024)
        mk_o[0, b, 128 * half:128 * half + 128] = r["mko"].reshape(128, 4, 512)
        mv_o[0, b, 128 * half:128 * half + 128] = r["mvo"].reshape(128, 4, 512)
    return (y_prompt, y_sample, srp_o, srs_o, scp_o, scs_o, mk_o, mv_o)
```

```python
import math
import os
from contextlib import ExitStack

import numpy as np
import concourse.bass as bass
import concourse.mybir as mybir
from concourse.bass_utils import run_bass_kernel_spmd

F32 = mybir.dt.float32
BF16 = mybir.dt.bfloat16
ALU = mybir.AluOpType
AF = mybir.ActivationFunctionType
AX = mybir.AxisListType

D = 2048
KC = 16
TP = 1024
TS = 128
T = TP + TS
NT = T // 128
NSEQ = 16
H = 8
DFF = 5632
NG = DFF // 512
NMEM = 256
RMS_EPS = 1e-6
GN_EPS = 1e-5
NSLOT = 4
TB = [(0, 512), (512, 512), (1024, 128)]
TBP = [(0, 512), (512, 512)]
LG = [math.log(1.0 - 2.0 ** (-5.0 - h)) for h in range(H)]

ENGS = ("pe", "act", "dve", "pool", "sp")
_PE_LABELS = [] if os.environ.get("KLABELS") else None


class Prog:
    def __init__(self, nc, sem_alloc):
        self.nc = nc
        self.sem_alloc = sem_alloc
        self.q = {e: [] for e in ENGS}
        self.cnt = {e: 0 for e in ENGS}
        self.epoch = {e: 0 for e in ENGS}
        self.sems = {}
        self.seen = {e: {} for e in ENGS}
        self.lastw = {}
        self.readers = {}
        self.nds = 12
        self.dq = {q: {"i": 0, "uses": [0] * self.nds} for q in ("sp", "pool", "act")}
        self.dma_tokens = []
        self.weight_tokens = set()
        self.EPOCH_MAX = 4000
        self.dead = False
        self.stop = float(os.environ.get("KSTOP", "99"))

    def phase(self, k):
        if k > self.stop:
            self.dead = True

    def sem(self, key):
        if key not in self.sems:
            self.sems[key] = self.sem_alloc("s_" + "_".join(str(k) for k in key))
        return self.sems[key]

    def _collect(self, eng, reads, writes):
        deps = {}

        def need(tok):
            if tok is None:
                return
            k, v = tok
            if k[0] == "e" and k[1] == "pe" and eng == "pe":
                return
            if v > deps.get(k, 0):
                deps[k] = v

        for t in reads:
            need(self.lastw.get(t))
        for t in writes:
            need(self.lastw.get(t))
            for r in self.readers.get(t, ()):
                need(r)
        waits = []
        for k, v in deps.items():
            if v > self.seen[eng].get(k, 0):
                self.seen[eng][k] = v
                waits.append((k, v))
        return waits

    def _record(self, tok, reads, writes):
        for t in reads:
            self.readers.setdefault(t, []).append(tok)
        for t in writes:
            self.lastw[t] = tok
            self.readers[t] = []

    def op(self, eng, fn, reads=(), writes=()):
        if self.dead:
            return None
        ps_reads = [t for t in reads if isinstance(t, tuple) and t[0] == "ps" and t not in writes]
        if ps_reads:
            writes = list(writes) + ps_reads
        waits = self._collect(eng, reads, writes)
        if self.cnt[eng] >= self.EPOCH_MAX:
            self.epoch[eng] += 1
            self.cnt[eng] = 0
        self.cnt[eng] += 1
        key = ("e", eng, self.epoch[eng])
        tok = (key, self.cnt[eng])
        self.q[eng].append((waits, fn, (key, 1)))
        self._record(tok, reads, writes)
        return tok

    def dma(self, queue, out, in_, reads=(), writes=(), track_out=False, slow=False):
        if self.dead:
            return None
        st = self.dq[queue]
        slot = st["i"] % self.nds
        st["i"] += 1
        key = ("d", queue, slot)
        waits = self._collect(queue, reads, writes)
        prev = st["uses"][slot] * 16
        if prev > self.seen[queue].get(key, 0):
            self.seen[queue][key] = prev
            waits.append((key, prev))
        st["uses"][slot] += 1
        tok = (key, st["uses"][slot] * 16)
        if slow:
            self.q[queue].append((waits, lambda e, o=out, i=in_: e.dma_start(out=o, in_=i, allow_slow_non_contiguous=True), (key, 16)))
        else:
            self.q[queue].append((waits, lambda e, o=out, i=in_: e.dma_start(out=o, in_=i), (key, 16)))
        self._record(tok, reads, writes)
        if track_out:
            self.dma_tokens.append(tok)
        return tok

    def fence(self):
        if self.dead:
            return
        toks = []
        for e in ENGS:
            if self.cnt[e] > 0:
                toks.append((("e", e, self.epoch[e]), self.cnt[e]))
        for q, st in self.dq.items():
            for s in range(self.nds):
                if st["uses"][s] > 0:
                    tk = (("d", q, s), st["uses"][s] * 16)
                    if tk in self.weight_tokens:
                        continue
                    toks.append(tk)
        for e in ENGS:
            waits = []
            for k, v in toks:
                if k[0] == "e" and k[1] == e and e == "pe":
                    continue
                if v > self.seen[e].get(k, 0):
                    self.seen[e][k] = v
                    waits.append((k, v))
            if waits:
                self.q[e].append((waits, None, None))
        keep_w = {k: v for k, v in self.lastw.items() if isinstance(k, tuple) and k[0] == "slot"}
        keep_r = {k: v for k, v in self.readers.items() if isinstance(k, tuple) and k[0] == "slot"}
        self.lastw = keep_w
        self.readers = keep_r

    def emit(self, block):
        nc = self.nc

        def run(eng_name):
            def body(e):
                for waits, fn, inc in self.q[eng_name]:
                    for k, v in waits:
                        e.wait_ge(self.sem(k), v)
                    if fn is not None:
                        ins = fn(e)
                        ins.then_inc(self.sem(inc[0]), inc[1])
            return body

        block.tensor(run("pe"))
        block.scalar(run("act"))
        block.vector(run("dve"))
        block.gpsimd(run("pool"))
        block.sync(run("sp"))


def build_program():
    nc = bass.Bass("TRN2", target_bir_lowering=False)

    def din(name, shape):
        return nc.dram_tensor(name, list(shape), F32, kind="ExternalInput").ap()

    def dout(name, shape):
        return nc.dram_tensor(name, list(shape), F32, kind="ExternalOutput").ap()

    xp = din("xp", [TP, D]); xprev = din("xprev", [TP, D]); xs = din("xs", [TS, D])
    mem = din("mem", [NMEM, D])
    sret = din("sret", [NSEQ, H, 128, 128]); sconv = din("sconv", [NSEQ * 2, 1024])
    ck = din("ck", [NSEQ, NMEM, D]); cv = din("cv", [NSEQ, NMEM, D])
    ln_mix_g = din("ln_mix_g", [D]); w_in = din("w_in", [D, 7168]); conv_w = din("conv_w", [3, 1024])
    ret_gn_g = din("ret_gn_g", [1024]); w_mix_out = din("w_mix_out", [D, D])
    ln_mem_g = din("ln_mem_g", [D]); ln_xa_g = din("ln_xa_g", [D])
    w_xq = din("w_xq", [D, D]); w_mk = din("w_mk", [D, D]); w_mv = din("w_mv", [D, D]); w_xo = din("w_xo", [D, D])
    ln_ffn_g = din("ln_ffn_g", [D]); w_gate = din("w_gate", [D, DFF]); w_up = din("w_up", [D, DFF])
    w_down = din("w_down", [DFF, D]); final_g = din("final_g", [D])
    c_ident = din("c_ident", [128, 128]); c_perm = din("c_perm", [128, 128])
    c_rot = din("c_rot", [128, 2, T]); c_rotp = din("c_rotp", [128, 2, TP])
    c_dmask = din("c_dmask", [128, H, 128]); c_smask = din("c_smask", [128, H, 128])
    c_qdec = din("c_qdec", [128, H, 128]); c_qdecs = din("c_qdecs", [128, H, 128])
    c_kdec = din("c_kdec", [128, H]); c_kdecp = din("c_kdecp", [128, H, 8]); c_rmd = din("c_rmd", [128, H, NSEQ + 1])

    yp = dout("yp", [TP, D]); ys = dout("ys", [TS, D])
    srp = dout("srp", [H, 128, 128]); srs = dout("srs", [NSEQ, H, 128, 128])
    scp = dout("scp", [2, 1024]); scs = dout("scs", [NSEQ * 2, 1024])
    mko = dout("mko", [128, D]); mvo = dout("mvo", [128, D])

    es = ExitStack()
    with es:
        def sb(name, shape, dt):
            return es.enter_context(nc.sbuf_tensor(name, list(shape), dt))

        RX = sb("RX", [128, NT * D], F32)
        RH = sb("RH", [128, KC * T], BF16)
        RA = sb("RA", [128, KC * T], BF16)
        SL = sb("SL", [128, NSLOT, 4096], BF16)
        identf = sb("identf", [128, 128], F32)
        identb = sb("identb", [128, 128], BF16)
        permb = sb("permb", [128, 128], BF16)
        onesf = sb("onesf", [128, 128], F32)
        gT = sb("gT", [128, 4, KC], F32)
        gnT = sb("gnT", [128, 8], F32)
        cwT = sb("cwT", [128, 3, 8], F32)
        stat = sb("stat", [128, 64], F32)
        Sf = sb("Sf", [128, H, 128], F32)
        MKV = sb("MKV", [128, 2, KC * 256], BF16)
        oo = sb("oo", [128, 2, T], F32)
        oall = oo[:, 0, :]
        osq = oo[:, 1, :]
        uprev = sb("uprev", [128, 8, 2], F32)
        PS = es.enter_context(nc.psum_tensor("PS", [128, 8, 512], F32))

        sem_list = []

        def sem_alloc(name):
            s = es.enter_context(nc.semaphore(name))
            sem_list.append(s)
            return s

        P = Prog(nc, sem_alloc)

        def psf(bank, n=512, off=0):
            return PS[:, bank, off:off + n]

        def psb(bank):
            return PS[:, bank, :].bitcast(BF16)

        def pe(mms, reads, writes):
            if _PE_LABELS is not None and not P.dead:
                import inspect
                fr = inspect.stack()[1]
                _PE_LABELS.append(("%s:%d" % (fr.function, fr.lineno), len(mms)))
            def fn(e, mms=mms):
                ins = None
                for m in mms:
                    if m[0] == "m":
                        ins = e.matmul(m[1], m[2], m[3], start=m[4], stop=m[5])
                    else:
                        ins = e.transpose(m[1], m[2], m[3])
                return ins
            return P.op("pe", fn, reads, writes)

        def dve(f, reads, writes):
            return P.op("dve", f, reads, writes)

        def act(f, reads, writes):
            return P.op("act", f, reads, writes)

        wv = lambda w: w.rearrange("(kc p) n -> p kc n", p=128)
        blocks = []

        def blk_std(w, col0):
            blocks.append((wv(w)[:, :, col0:col0 + 256], (16, 256)))
            return len(blocks) - 1

        def blk_a(w, kh, cb):
            blocks.append((wv(w)[:, 8 * kh:8 * kh + 8, 512 * cb:512 * cb + 512], (8, 512)))
            return len(blocks) - 1

        def blk_down(g, half):
            blocks.append((wv(w_down)[:, 4 * g:4 * g + 4, 1024 * half:1024 * half + 1024], (4, 1024)))
            return len(blocks) - 1

        CQ, CK, CV, CG, CBG, CCG, CHIN = 0, 1024, 2048, 3072, 4096, 5120, 6144
        W = {}
        W["pre_k"] = []; W["pre_v"] = []; W["pre_cg"] = []; W["pre_hin"] = []
        for i in range(4):
            W["pre_k"].append(blk_std(w_in, CK + 256 * i)); W["pre_v"].append(blk_std(w_in, CV + 256 * i))
            W["pre_cg"].append(blk_std(w_in, CCG + 256 * i)); W["pre_hin"].append(blk_std(w_in, CHIN + 256 * i))
        W["q"] = []; W["k"] = []; W["v"] = []; W["g"] = []; W["mkv"] = []
        for i in range(4):
            W["v"].append(blk_std(w_in, CV + 256 * i)); W["k"].append(blk_std(w_in, CK + 256 * i))
            W["mkv"].append((0, 2 * i, blk_std(w_mk, 256 * (2 * i))))
            W["q"].append(blk_std(w_in, CQ + 256 * i))
            W["mkv"].append((0, 2 * i + 1, blk_std(w_mk, 256 * (2 * i + 1))))
            W["g"].append(blk_std(w_in, CG + 256 * i))
            W["mkv"].append((1, 2 * i, blk_std(w_mv, 256 * (2 * i))))
            W["mkv"].append((1, 2 * i + 1, blk_std(w_mv, 256 * (2 * i + 1))))
        W["cg"] = []; W["hin"] = []; W["bg"] = []
        for i in range(4):
            W["cg"].append(blk_std(w_in, CCG + 256 * i)); W["hin"].append(blk_std(w_in, CHIN + 256 * i))
            W["bg"].append(blk_std(w_in, CBG + 256 * i))
        W["mixo"] = [[blk_a(w_mix_out, kh, cb) for kh in range(2)] for cb in range(4)]
        W["xq"] = [blk_std(w_xq, 256 * i) for i in range(8)]
        W["xo"] = [[blk_a(w_xo, kh, cb) for kh in range(2)] for cb in range(4)]
        W["gate"] = []; W["up"] = []; W["down"] = []
        for g in range(NG):
            gl, ul = [], []
            for j in range(2):
                gl.append(blk_std(w_gate, 512 * g + 256 * j)); ul.append(blk_std(w_up, 512 * g + 256 * j))
            W["gate"].append(gl); W["up"].append(ul)
            W["down"].append([blk_down(g, j) for j in range(2)])
        wstate = {"issued": 0, "cur": -1}

        def wslot(i):
            assert i == wstate["cur"] + 1 or i == wstate["cur"], (i, wstate)
            wstate["cur"] = i
            while wstate["issued"] < len(blocks) and wstate["issued"] <= i + NSLOT - 2:
                j = wstate["issued"]
                ap, shp = blocks[j]
                dst = SL[:, j % NSLOT, :].rearrange("p (a b) -> p a b", a=shp[0])
                wt = P.dma("pool", dst, ap, reads=([("hprev", 1)] if j < NSLOT - 1 else ()), writes=[("slot", j % NSLOT)])
                if wt is not None:
                    P.weight_tokens.add(wt)
                wstate["issued"] += 1
            shp = blocks[i][1]
            return SL[:, i % NSLOT, :].rearrange("p (a b) -> p a b", a=shp[0]), ("slot", i % NSLOT)

        P.dma("sp", identf[:], c_ident, writes=["identf"])
        P.dma("pool", identb[:], c_ident, writes=["identb"])
        P.dma("pool", permb[:], c_perm, writes=["permb"])
        gst = oo[:, :, :].rearrange("p a t -> p (a t)")[:, 0:128]
        for i, g in enumerate((ln_mix_g, ln_mem_g, ln_xa_g, ln_ffn_g)):
            P.dma("sp", gst[16 * i:16 * i + 16, :], g.rearrange("(kc p) -> kc p", p=128), writes=["gst"])
        P.dma("sp", gst[64:72, :], ret_gn_g.rearrange("(kc p) -> kc p", p=128), writes=["gst"])
        P.dma("sp", gst[72:96, :], conv_w.rearrange("k (kc p) -> (k kc) p", p=128), writes=["gst"])
        pe([("t", psf(0, 96), gst[0:96, :], identf[0:96, 0:96])], ["gst", "identf"], [("ps", 0)])
        act(lambda e: e.activation(out=gT[:, :, :].rearrange("p a k -> p (a k)"), in_=psf(0, 64), func=AF.Copy), [("ps", 0)], ["gT"])
        act(lambda e: e.activation(out=gnT[:], in_=psf(0, 8, 64), func=AF.Copy), [("ps", 0)], ["gnT"])
        act(lambda e: e.activation(out=cwT[:, :, :].rearrange("p k c -> p (k c)"), in_=psf(0, 24, 72), func=AF.Copy), [("ps", 0)], ["cwT"])
        dve(lambda e: e.memset(onesf[:], 1.0 / 128.0), [], ["onesf"])

        rx_off = [0]

        def rx(n, dt=F32, shape=None):
            nf = n if dt == F32 else (n + 1) // 2
            o = rx_off[0]
            rx_off[0] += nf
            assert rx_off[0] <= NT * D, rx_off[0]
            v = RX[:, o:o + nf]
            if dt == BF16:
                v = v.bitcast(BF16)[:, 0:n]
            return v

        rot = rx(2 * T).rearrange("p (a t) -> p a t", a=2)
        dmask = rx(2 * 128).rearrange("p (h l) -> p h l", h=2)
        smask = rx(2 * 128).rearrange("p (h l) -> p h l", h=2)
        qdec = rx(2 * 128).rearrange("p (h l) -> p h l", h=2)
        qdecs = rx(2 * 128).rearrange("p (h l) -> p h l", h=2)
        kdec = rx(H)
        kdecp = rx(H * 8).rearrange("p (h c) -> p h c", h=H)
        rmd = rx(H * (NSEQ + 1)).rearrange("p (h s) -> p h s", h=H)
        qT = rx(2 * T, BF16).rearrange("p (a t) -> p a t", a=2)
        kT = rx(2 * T, BF16).rearrange("p (a t) -> p a t", a=2)
        sgT = rx(2 * T, BF16).rearrange("p (a t) -> p a t", a=2)
        vtok = rx(NT * 256, BF16).rearrange("p (t c) -> p t c", t=NT)
        raw = rx(T, BF16)
        t1 = rx(T)
        t2 = rx(T)
        stm = rx(2 * 128, BF16).rearrange("p (a l) -> p a l", a=2)
        kdt = rx(2 * 128, BF16).rearrange("p (a l) -> p a l", a=2)
        qds_ = rx(128, BF16)
        inn = rx(128)
        un0 = rx_off[0]
        xload = rx(2 * D).rearrange("p (a d) -> p a d", a=2)
        sqj = rx(D, BF16)
        enda = rx_off[0]
        rx_off[0] = un0
        SH = 8
        ssf2 = rx(2 * SH * 128).rearrange("p (a s e) -> p a s e", a=2, s=SH)
        ssb = rx(SH * 128, BF16).rearrange("p (s e) -> p s e", s=SH)
        Sprev = rx(2 * 8 * 128, BF16).rearrange("p (a c e) -> p a c e", a=2, c=8)
        Stmp = rx(2 * 128).rearrange("p (a e) -> p a e", a=2)
        qdall = rx(TP, BF16)
        vexp_off = rx_off[0]
        vexp = rx(NSEQ * 128, BF16).rearrange("p (s e) -> p s e", s=NSEQ)
        ktkd = rx(128, BF16)
        kdta = rx(8 * 128, BF16).rearrange("p (c d) -> p c d", c=8)
        osq_alt = RX[:, vexp_off:vexp_off + T]
        stma = rx(9 * 128, BF16).rearrange("p (c l) -> p c l", c=9)
        endb = rx_off[0]
        rx_off[0] = un0
        upad = rx(2 * (TP + 2))
        upads = rx(NSEQ * 10).rearrange("p (s k) -> p s k", s=NSEQ)
        cacc = rx(T)
        scT = rx(8 * 32).rearrange("p (c k) -> p c k", c=8)
        sc_in = rx(1024)
        sc_out = rx(1024)
        nbt = rx(32)
        endc = rx_off[0]
        rx_off[0] = max(enda, endb, endc)
        assert rx_off[0] <= NT * D, rx_off[0]

        P.dma("sp", kdec, c_kdec, writes=["kdec"])
        P.dma("sp", kdecp, c_kdecp, writes=["kdecp"])
        P.dma("sp", rmd, c_rmd, writes=["rmd"])

        hT = RH[:, :].rearrange("p (k t) -> p k t", k=KC)
        hprevT = RA[:, 0:KC * TP].rearrange("p (k t) -> p k t", k=KC)
        catT = RA[:, :].rearrange("p (k t) -> p k t", k=KC)
        xres = RX[:, :].rearrange("p (t d) -> p t d", t=NT)

        xn2 = oo[:, :, :].rearrange("p a t -> p (a t)").bitcast(BF16)[:, 0:2 * D].rearrange("p (a d) -> p a d", a=2)

        def norm_tile(i, xt, xr, gidx, dstT, dkey):
            j = i % 2
            so = 4 * j
            act(lambda e: e.activation(out=xn2[:, j, :], in_=xt, func=AF.Square, accum_out=stat[:, so:so + 1]), xr, [("xn", j), ("ss", j), "gst"])
            act(lambda e: e.activation(out=stat[:, so + 1:so + 2], in_=stat[:, so:so + 1], func=AF.Sqrt, bias=RMS_EPS, scale=1.0 / D), [("ss", j)], [("ms", j)])
            dve(lambda e: e.reciprocal(out=stat[:, so + 2:so + 3], in_=stat[:, so + 1:so + 2]), [("ms", j)], [("rstd", j)])
            if j == 0:
                act(lambda e: e.activation(out=xn2[:, j, :], in_=xt, func=AF.Copy, scale=stat[:, so + 2:so + 3]), xr + [("rstd", j)], [("xn", j)])
            else:
                dve(lambda e: e.tensor_scalar(out=xn2[:, j, :], in0=xt, scalar1=stat[:, so + 2:so + 3], scalar2=None, op0=ALU.mult),
                    xr + [("rstd", j)], [("xn", j)])
            b0 = 4 + 2 * j
            pbb = PS[:, b0:b0 + 2, :].rearrange("p b n -> p (b n)").bitcast(BF16)
            pe([("t", pbb[:, 128 * kc:128 * kc + 128], xn2[:, j, 128 * kc:128 * kc + 128], identb[:]) for kc in range(KC)],
               [("xn", j), "identb"], [("ps", b0), ("ps", b0 + 1)])
            dve(lambda e: e.tensor_tensor(out=dstT[:, :, 128 * i:128 * i + 128], in0=pbb.rearrange("p (k t) -> p k t", k=KC),
                                          in1=gT[:, gidx, :].unsqueeze(2).broadcast_to([128, KC, 128]), op=ALU.mult),
                [("ps", b0), ("ps", b0 + 1), "gT"], [(dkey, i)])

        def norm_tiles(src_aps, gidx, dstT, dkey):
            for i in range(len(src_aps)):
                xt = xload[:, i % 2, :]
                P.dma("sp", xt, src_aps[i], writes=[("xload", i % 2)])
                norm_tile(i, xt, [("xload", i % 2)], gidx, dstT, dkey)

        def proj_b(slot, slot_key, c, src, src_reads, tbs, banks):
            mms = []
            for (t0, n), b in zip(tbs, banks):
                pass
            for kc in range(KC):
                for (t0, n), b in zip(tbs, banks):
                    mms.append(("m", psf(b, n), slot[:, kc, 128 * c:128 * c + 128], src[:, kc, t0:t0 + n],
                                kc == 0, kc == KC - 1))
            return pe(mms, [slot_key] + src_reads, [("ps", b) for b in banks[:len(tbs)]])

        bank_sets = [[0, 1, 2], [3, 4, 5]]
        bs_i = [0]

        def next_banks():
            b = bank_sets[bs_i[0] % 2]
            bs_i[0] += 1
            return b

        def pview(banks, ntok):
            b0 = banks[0]
            return PS[:, b0:b0 + 3, :].rearrange("p b n -> p (b n)")[:, 0:ntok]

        def pkeys(banks, tbs):
            return [("ps", b) for b in banks[:len(tbs)]]

        def rotary_pre(banks, tbs, tab, scale):
            ntok = tbs[-1][0] + tbs[-1][1]
            pv = pview(banks, ntok)
            pk = pkeys(banks, tbs)
            act(lambda e: e.activation(out=raw[:, 0:ntok], in_=pv, func=AF.Copy), pk, ["raw"])
            dve(lambda e: e.scalar_tensor_tensor(out=t1[:, 0:ntok], in0=pv, scalar=scale, in1=tab[:, 0, 0:ntok],
                                                 op0=ALU.mult, op1=ALU.mult), pk + ["rot"], ["t1"])

        def rotary_post(banks, tbs, tab, scale, dst, dkey):
            ntok = tbs[-1][0] + tbs[-1][1]
            pv = pview(banks, ntok)
            pk = pkeys(banks, tbs)
            pe([("m", psf(b, n), permb[:], raw[:, t0:t0 + n], True, True) for (t0, n), b in zip(tbs, banks)],
               ["raw", "permb"], pk)
            dve(lambda e: e.scalar_tensor_tensor(out=t2[:, 0:ntok], in0=pv, scalar=scale, in1=tab[:, 1, 0:ntok],
                                                 op0=ALU.mult, op1=ALU.mult), pk + ["rot"], ["t2"])
            dve(lambda e: e.tensor_tensor(out=dst[:, 0:ntok], in0=t1[:, 0:ntok], in1=t2[:, 0:ntok], op=ALU.add),
                ["t1", "t2"], [("rotout", dkey)])

        def rotary(banks, tbs, tab, scale, dst, dkey):
            rotary_pre(banks, tbs, tab, scale)
            rotary_post(banks, tbs, tab, scale, dst, dkey)

        P.phase(1)
        P.dma("sp", rot[:, :, 0:TP], c_rotp, writes=["rot"])
        RAxn = RA
        xn_pre = RH[:, 0:8 * D].rearrange("p (t d) -> p t d", t=8)
        norm_tiles([xprev[128 * i:128 * i + 128, :] for i in range(8)], 0, hprevT, "hprev")
        P.phase(1.2)
        dve(lambda e: e.memset(Sf[:], 0.0), [], ["Sf"])
        kprevT = kT
        srcs = [xp[128 * i:128 * i + 128, :] for i in range(8)] + [xs[:, :]]
        main_i = [0]

        def main_norm_tiles(k):
            for _ in range(k):
                i = main_i[0]
                if i >= NT:
                    return
                main_i[0] += 1
                xt = xload[:, i % 2, :]
                P.dma("sp", xt, srcs[i], writes=[("xload", i % 2)])
                norm_tile(i, xt, [("xload", i % 2)], 0, hT, "hmain")

        P.phase(1.22)
        XB0, YB0 = [0, 1, 2], [3, 4, 5]
        cgp = stat[:, 8:8 + 16].rearrange("p (c k) -> p c k", c=8)
        for hp in range(4):
            slot, sk = wslot(W["pre_k"][hp])
            P.phase(1.25)
            hpr = [("hprev", i_) for i_ in range(8)]
            proj_b(slot, sk, 0, hprevT, hpr, TBP, XB0)
            proj_b(slot, sk, 1, hprevT, hpr, TBP, YB0)
            rotary(XB0, TBP, rot, 128.0 ** -0.5, kprevT[:, 0, :], ("k", 0))
            rotary_pre(YB0, TBP, rot, 128.0 ** -0.5)
            slot, sk = wslot(W["pre_v"][hp])
            for t in range(8):
                b = 6 + (t % 2)
                mms = [("m", psf(b, 256), hprevT[:, kc, 128 * t:128 * t + 128], slot[:, kc, :], kc == 0, kc == KC - 1)
                       for kc in range(KC)]
                pe(mms, [sk] + hpr, [("ps", b)])
                act(lambda e, b=b, t=t: e.activation(out=vtok[:, t, :], in_=psf(b, 256), func=AF.Copy),
                    [("ps", b)], [("vtok", t)])
            rotary_post(YB0, TBP, rot, 128.0 ** -0.5, kprevT[:, 1, :], ("k", 1))
            P.phase(1.6)
            main_norm_tiles(2)
            kdas = []
            for c in range(2):
                h = 2 * hp + c
                b = 6 + c
                pb = psb(b)
                pe([("t", pb[:, 128 * ch:128 * ch + 128], kprevT[:, c, 128 * ch:128 * ch + 128], identb[:]) for ch in range(8)],
                   [("rotout", ("k", c)), "identb"], [("ps", b)])
                kda = qT[:, c, 0:TP].rearrange("p (a d) -> p a d", a=8)
                kdas.append(kda)
                dve(lambda e, pb=pb, h=h, kda=kda: e.tensor_tensor(
                    out=kda, in0=pb[:, :].rearrange("p (a d) -> p a d", a=8),
                    in1=kdecp[:, h, :].unsqueeze(2).broadcast_to([128, 8, 128]), op=ALU.mult), [("ps", b), "kdecp"], [("kda", c)])

            def conv_tail(which, name):
                slot, sk = wslot(W[name][hp])
                for c in range(2):
                    ch = 2 * hp + c
                    b = 4 + (ch % 2)
                    mms = [("m", psf(b, 2), slot[:, kc, 128 * c:128 * c + 128], hprevT[:, kc, TP - 2:TP], kc == 0, kc == KC - 1)
                           for kc in range(KC)]
                    pe(mms, [sk] + [("hprev", i_) for i_ in range(8)], [("ps", b)])
                    if which == 0:
                        act(lambda e, b=b, ch=ch: e.activation(out=cgp[:, ch, :], in_=psf(b, 2), func=AF.Copy),
                            [("ps", b)], [("cgp", ch)])
                    else:
                        dve(lambda e, b=b, ch=ch: e.tensor_tensor(out=uprev[:, ch, :], in0=cgp[:, ch, :], in1=psf(b, 2),
                                                                  op=ALU.mult), [("ps", b), ("cgp", ch)], [("uprev", ch)])

            conv_tail(0, "pre_cg")
            for c in range(2):
                h = 2 * hp + c
                sb_bank = 2 + c
                kda = kdas[c]
                pe([("m", psf(sb_bank, 128), kda[:, ch, :], vtok[:, ch, 128 * c:128 * c + 128], ch == 0, ch == 7) for ch in range(8)],
                   [("kda", c)] + [("vtok", ch) for ch in range(8)], [("ps", sb_bank)])
                act(lambda e, h=h, sb_bank=sb_bank: e.activation(out=Sf[:, h, :], in_=psf(sb_bank, 128), func=AF.Copy),
                    [("ps", sb_bank)], [("Sf", h)])
            conv_tail(1, "pre_hin")
        main_norm_tiles(NT)
        slot_v0, sk_v0 = wslot(W["v"][0])
        for t in range(NT):
            b = 6 + (t % 2)
            mms = [("m", psf(b, 256), hT[:, kc, 128 * t:128 * t + 128], slot_v0[:, kc, :], kc == 0, kc == KC - 1)
                   for kc in range(KC)]
            pe(mms, [sk_v0] + [("hmain", i_) for i_ in range(NT)], [("ps", b)])
            act(lambda e, b=b, t=t: e.activation(out=vtok[:, t, :], in_=psf(b, 256), func=AF.Copy),
                [("ps", b)], [("vtok", t)])
        slot_k0, sk_k0 = wslot(W["k"][0])
        hmk = [("hmain", i_) for i_ in range(NT)]
        proj_b(slot_k0, sk_k0, 0, hT, hmk, TB, [0, 1, 2])
        proj_b(slot_k0, sk_k0, 1, hT, hmk, TB, [3, 4, 5])
        P.fence()

        P.phase(2)
        P.dma("sp", rot, c_rot, writes=["rot"])

        P.phase(3)
        def proj_v(slot, sk, extra=()):
            for t in range(NT):
                b = 6 + (t % 2)
                mms = [("m", psf(b, 256), hT[:, kc, 128 * t:128 * t + 128], slot[:, kc, :], kc == 0, kc == KC - 1)
                       for kc in range(KC)]
                pe(mms, [sk] + list(extra), [("ps", b)])
                act(lambda e, b=b, t=t: e.activation(out=vtok[:, t, :], in_=psf(b, 256), func=AF.Copy),
                    [("ps", b)], [("vtok", t)])

        def stage1_t(c, h):
            kk = [("rotout", ("k", c))]
            pb = psb(6)
            pe([("t", pb[:, 128 * ch:128 * ch + 128], kT[:, c, 128 * ch:128 * ch + 128], identb[:]) for ch in range(8)],
               kk + ["identb"], [("ps", 6)])
            act(lambda e, pb=pb, h=h: e.activation(out=kdta.rearrange("p c d -> p (c d)"), in_=pb[:, :], func=AF.Copy,
                                                   scale=kdec[:, h:h + 1]), [("ps", 6), "kdec"], ["kdta"])

        def stage1_a(c, h):
            pe([("m", psf(6 + ch // 4, 128, 128 * (ch % 4)), kdta[:, ch, :], vtok[:, ch, 128 * c:128 * c + 128], True, True) for ch in range(8)],
               ["kdta"] + [("vtok", ch) for ch in range(8)], [("ps", 6), ("ps", 7)])
            g128 = math.exp(128.0 * LG[h])
            act(lambda e: e.activation(out=Sprev[:, c, 0, :], in_=Sf[:, h, :], func=AF.Copy), [("Sf", h)], [("Sprev", c, 0)])
            for ch in range(8):
                ab = 6 + ch // 4
                src = Sf[:, h, :] if ch % 2 == 0 else Stmp[:, c, :]
                dst = Stmp[:, c, :] if ch % 2 == 0 else Sf[:, h, :]
                sk_, dk_ = (("Sf", h), ("Stmp", c)) if ch % 2 == 0 else (("Stmp", c), ("Sf", h))
                dve(lambda e, src=src, dst=dst, ab=ab, ch=ch: e.scalar_tensor_tensor(
                    out=dst, in0=src, scalar=g128, in1=psf(ab, 128, 128 * (ch % 4)), op0=ALU.mult, op1=ALU.add),
                    [("ps", ab), sk_], [dk_])
                if ch < 7:
                    act(lambda e, dst=dst, ch=ch: e.activation(out=Sprev[:, c, ch + 1, :], in_=dst, func=AF.Copy),
                        [dk_], [("Sprev", c, ch + 1)])
            P.dma("sp", srp[h], Sf[:, h, :], reads=[("Sf", h)], track_out=True)

        def stage3(c, h):
            oall = oo[:, c, :]
            qk = [("rotout", ("q", c))]
            kk = [("rotout", ("k", c))]
            for hf in range(2):
                P.dma("sp", ssf2[:, hf], sret[SH * hf:SH * hf + SH, h].rearrange("s d e -> d s e"), writes=[("ssf", hf)])
            pb = psb(3)
            pe([("t", pb[:, 0:128], kT[:, c, TP:T], identb[:])], kk + ["identb"], [("ps", 3)])
            act(lambda e, pb=pb: e.activation(out=ktkd, in_=pb[:, 0:128], func=AF.Copy, scale=rmd[:, h, NSEQ:NSEQ + 1]),
                [("ps", 3), "rmd"], ["ktkd"])
            dve(lambda e: e.tensor_tensor(
                out=vexp, in0=vtok[:, 8, 128 * c:128 * c + 128].unsqueeze(1).broadcast_to([128, NSEQ, 128]),
                in1=rmd[:, 0, 0:NSEQ].unsqueeze(2).broadcast_to([128, NSEQ, 128]), op=ALU.mult), [("vtok", 8), "rmd"], ["vexp"])
            dve(lambda e: e.tensor_tensor(
                out=qdall.rearrange("p (a l) -> p a l", l=128), in0=qT[:, c, 0:TP].rearrange("p (a l) -> p a l", l=128),
                in1=qdec[:, c, :].unsqueeze(1).broadcast_to([128, 8, 128]), op=ALU.mult), qk + ["qdec"], ["qdall"])
            dve(lambda e: e.tensor_tensor(out=qds_, in0=qT[:, c, TP:T], in1=qdecs[:, c, :], op=ALU.mult), qk + ["qdecs"], ["qds"])
            pe([("m", psf(6 + ch // 4, 128, 128 * (ch % 4)), kT[:, c, 128 * ch:128 * ch + 128], qT[:, c, 128 * ch:128 * ch + 128], True, True) for ch in range(8)]
               + [("m", psf(2, 128), kT[:, c, TP:T], qT[:, c, TP:T], True, True)], qk + kk, [("ps", 6), ("ps", 7), ("ps", 2)])
            for half in range(2):
                dve(lambda e, half=half: e.tensor_tensor(
                    out=stma[:, 4 * half:4 * half + 4, :], in0=psf(6 + half).rearrange("p (a l) -> p a l", l=128),
                    in1=dmask[:, c, :].unsqueeze(1).broadcast_to([128, 4, 128]), op=ALU.mult), [("ps", 6 + half), "dmask"], [("stma", half)])
            dve(lambda e: e.tensor_tensor(out=stma[:, 8, :], in0=psf(2, 128), in1=smask[:, c, :], op=ALU.mult), [("ps", 2), "smask"], [("stma", 2)])
            mms = []
            for ch in range(8):
                o_ = psf(ch // 4, 128, 128 * (ch % 4))
                mms.append(("m", o_, vtok[:, ch, 128 * c:128 * c + 128], stma[:, ch, :], True, False))
                mms.append(("m", o_, Sprev[:, c, ch, :], qdall[:, 128 * ch:128 * ch + 128], False, True))
            pe(mms, [("vtok", ch) for ch in range(8)] + [("stma", 0), ("stma", 1), "qdall"] + [("Sprev", c, ch) for ch in range(8)],
               [("ps", 0), ("ps", 1)])
            act(lambda e: e.activation(out=oall[:, 0:TP], in_=PS[:, 0:2, :].rearrange("p b n -> p (b n)"), func=AF.Copy),
                [("ps", 0), ("ps", 1)], [("oall", c, 0), ("xl2", 0)])
            pe([("m", psf(4, 128), vtok[:, 8, 128 * c:128 * c + 128], stma[:, 8, :], True, True)], [("vtok", 8), ("stma", 2)], [("ps", 4)])
            act(lambda e: e.activation(out=inn, in_=psf(4, 128), func=AF.Copy), [("ps", 4)], ["inn"])
            g8 = math.exp(8.0 * LG[h])
            for hf in range(2):
                s0 = SH * hf
                ssf = ssf2[:, hf]
                act(lambda e, ssf=ssf: e.activation(out=ssb.rearrange("p s e -> p (s e)"), in_=ssf.rearrange("p s e -> p (s e)"), func=AF.Copy),
                    [("ssf", hf)], ["ssb"])
                pe([("m", psf(5, 8, 8 * (s0 + s_)), ssb[:, s_, :], qds_[:, 8 * (s0 + s_):8 * (s0 + s_) + 8], True, True) for s_ in range(SH)],
                   ["ssb", "qds"], [("ps", 5)])
                for q4 in range(2):
                    b = 2 + q4
                    sq0 = s0 + 4 * q4
                    pe([("m", psf(b), ktkd, vexp[:, sq0:sq0 + 4, :].rearrange("p s e -> p (s e)"), True, True)], ["ktkd", "vexp"], [("ps", b)])
                    dve(lambda e, b=b, q4=q4, ssf=ssf: e.scalar_tensor_tensor(
                        out=ssf[:, 4 * q4:4 * q4 + 4, :].rearrange("p s e -> p (s e)"),
                        in0=ssf[:, 4 * q4:4 * q4 + 4, :].rearrange("p s e -> p (s e)"), scalar=g8, in1=psf(b),
                        op0=ALU.mult, op1=ALU.add), [("ps", b), ("ssf", hf)], [("ssf", hf)])
                P.dma("sp", srs[s0:s0 + SH, h].rearrange("s d e -> d s e"), ssf, reads=[("ssf", hf)], track_out=True)
            dve(lambda e: e.tensor_tensor(out=oall[:, TP:T], in0=psf(5, 128), in1=inn, op=ALU.add), [("ps", 5), "inn"], [("oall", c, 1), ("xl2", 0)])
            osq_, okeys = gn_bufs(c)
            dve(lambda e: e.tensor_tensor(out=osq_, in0=oall, in1=oall, op=ALU.mult), [("oall", c, 0), ("oall", c, 1)], okeys)

        def gn_bufs(c):
            if c == 0:
                return t2, ["t2"]
            return osq_alt, ["vexp", "ktkd", "kdta"]

        GXB, GYB = [0, 1, 2], [3, 4, 5]

        def gn_mm(c):
            oall = oo[:, c, :]
            osq_, okeys = gn_bufs(c)
            oa = [("oall", c, 0), ("oall", c, 1)]
            pe([("m", psf(b, n), onesf[:], oall[:, t0:t0 + n], True, True) for (t0, n), b in zip(TB, GXB)], oa + ["onesf"], pkeys(GXB, TB))
            pe([("m", psf(b, n), onesf[:], osq_[:, t0:t0 + n], True, True) for (t0, n), b in zip(TB, GYB)], okeys + ["onesf"], pkeys(GYB, TB))

        def gn_head(c):
            mv_, qv_ = pview(GXB, T), pview(GYB, T)
            act(lambda e: e.activation(out=t1, in_=mv_, func=AF.Copy), pkeys(GXB, TB), ["t1"])
            dve(lambda e: e.tensor_tensor(out=t2, in0=t1, in1=t1, op=ALU.mult), ["t1"], ["t2"])
            dve(lambda e: e.tensor_tensor(out=t2, in0=qv_, in1=t2, op=ALU.subtract), pkeys(GYB, TB) + ["t2"], ["t2"])

        def gn_tail(c, h):
            oall = oo[:, c, :]
            oa = [("oall", c, 0), ("oall", c, 1)]
            act(lambda e: e.activation(out=t2, in_=t2, func=AF.Ln, bias=GN_EPS, scale=1.0), ["t2"], ["t2"])
            act(lambda e: e.activation(out=t2, in_=t2, func=AF.Exp, scale=-0.5), ["t2"], ["t2"])
            dve(lambda e: e.tensor_tensor(out=t1, in0=oall, in1=t1, op=ALU.subtract), oa + ["t1"], ["t1"])
            dve(lambda e: e.scalar_tensor_tensor(out=t1, in0=t1, scalar=gnT[:, h:h + 1], in1=t2, op0=ALU.mult, op1=ALU.mult),
                ["t1", "t2", "gnT"], ["t1"])
            dve(lambda e: e.tensor_tensor(out=catT[:, h, :], in0=t1, in1=sgT[:, c, :], op=ALU.mult), ["t1", ("sgT", c)], [("cat", h)])

        mkT = MKV[:, 0, :].rearrange("p (k t) -> p k t", k=KC)
        mvb = MKV[:, 1, :].rearrange("p (t d) -> p t d", t=2)
        csc = RA[:, 8 * T:16 * T]
        hmT = csc[:, 0:4096].rearrange("p (k t) -> p k t", k=KC)
        xn_mem = csc[:, 4096:8192].rearrange("p (t d) -> p t d", t=2)
        mst = csc[:, 8192:9216].bitcast(F32).rearrange("p (a c) -> p a c", a=2)
        xload2 = oo[:, :, :].rearrange("p a t -> p (a t)")[:, 0:D]

        def norm_mem():
            for i in range(2):
                xt = xload2
                P.dma("sp", xt, mem[128 * i:128 * i + 128, :], writes=[("xl2", 0)])
                act(lambda e, xt=xt, i=i: e.activation(out=xn_mem[:, i, :], in_=xt, func=AF.Square, accum_out=stat[:, 0:1]), [("xl2", 0)], [("xnm", i), "ss"])
                act(lambda e: e.activation(out=stat[:, 1:2], in_=stat[:, 0:1], func=AF.Sqrt, bias=RMS_EPS, scale=1.0 / D), ["ss"], ["ms"])
                dve(lambda e: e.reciprocal(out=stat[:, 2:3], in_=stat[:, 1:2]), ["ms"], ["rstd"])
                dve(lambda e, xt=xt, i=i: e.tensor_scalar(out=xn_mem[:, i, :], in0=xt, scalar1=stat[:, 2:3], scalar2=None,
                                                          op0=ALU.mult), [("xl2", 0), "rstd"], [("xnm", i)])
            for kc in range(KC):
                bank = 6 + (kc % 2)
                pb = psb(bank)
                pe([("t", pb[:, 128 * j:128 * j + 128], xn_mem[:, j, 128 * kc:128 * kc + 128], identb[:]) for j in range(2)],
                   [("xnm", 0), ("xnm", 1), "identb"], [("ps", bank)])
                act(lambda e, pb=pb, kc=kc: e.activation(out=hmT[:, kc, :], in_=pb[:, 0:256], func=AF.Copy, scale=gT[:, 1, kc:kc + 1]),
                    [("ps", bank), "gT"], [("hmT", kc)])

        hm_reads = [("hmT", kc) for kc in range(KC)]

        def memkv_block(which, i, blk):
            dst = mko if which == 0 else mvo
            slot, sk = wslot(blk)
            for t in range(2):
                b = 6 + t
                mms = [("m", psf(b, 256), hmT[:, kc, 128 * t:128 * t + 128], slot[:, kc, :], kc == 0, kc == KC - 1) for kc in range(KC)]
                pe(mms, [sk] + hm_reads, [("ps", b)])
                if t == 0:
                    j = i % 2
                    act(lambda e, b=b, j=j: e.activation(out=mst[:, j, :], in_=psf(b, 256), func=AF.Copy), [("ps", b)], [("mst", j)])
                    P.dma("sp", dst[:, 256 * i:256 * i + 256], mst[:, j, :], reads=[("mst", j)], track_out=True)
                if which == 1:
                    dve(lambda e, b=b, t=t, i=i: e.tensor_copy(out=mvb[:, t, 256 * i:256 * i + 256], in_=psf(b, 256)),
                        [("ps", b)], [("mvb", t, i)])
            if which == 0:
                for c in range(2):
                    b = 6 + c
                    mms = [("m", psf(b, 256), slot[:, kc, 128 * c:128 * c + 128], hmT[:, kc, :], kc == 0, kc == KC - 1) for kc in range(KC)]
                    pe(mms, [sk] + hm_reads, [("ps", b)])
                    act(lambda e, b=b, i=i, c=c: e.activation(out=mkT[:, 2 * i + c, :], in_=psf(b, 256), func=AF.Copy),
                        [("ps", b)], [("mkT", 2 * i + c)])

        XB_, YB_ = [0, 1, 2], [3, 4, 5]
        for hp in range(4):
            P.dma("sp", dmask, c_dmask[:, 2 * hp:2 * hp + 2, :], writes=["dmask"])
            P.dma("sp", smask, c_smask[:, 2 * hp:2 * hp + 2, :], writes=["smask"])
            P.dma("sp", qdec, c_qdec[:, 2 * hp:2 * hp + 2, :], writes=["qdec"])
            P.dma("sp", qdecs, c_qdecs[:, 2 * hp:2 * hp + 2, :], writes=["qdecs"])
            KS = 128.0 ** -0.5
            if hp > 0:
                slot, sk = wslot(W["k"][hp])
                proj_b(slot, sk, 0, hT, [], TB, XB_)
                proj_b(slot, sk, 1, hT, [], TB, YB_)
            rotary(XB_, TB, rot, KS, kT[:, 0, :], ("k", 0))
            rotary_pre(YB_, TB, rot, KS)
            if hp == 0:
                norm_mem()
            memkv_block(*W["mkv"][4 * hp + 0])
            rotary_post(YB_, TB, rot, KS, kT[:, 1, :], ("k", 1))
            slot, sk = wslot(W["q"][hp])
            stage1_t(0, 2 * hp)
            proj_b(slot, sk, 0, hT, [], TB, XB_)
            stage1_a(0, 2 * hp)
            proj_b(slot, sk, 1, hT, [], TB, YB_)
            rotary(XB_, TB, rot, 1.0, qT[:, 0, :], ("q", 0))
            rotary_pre(YB_, TB, rot, 1.0)
            stage1_t(1, 2 * hp + 1)
            memkv_block(*W["mkv"][4 * hp + 1])
            slot_g, sk_g = wslot(W["g"][hp])
            proj_b(slot_g, sk_g, 0, hT, [], TB, XB_)
            pv_ = pview(XB_, T)
            act(lambda e, pv_=pv_: e.activation(out=sgT[:, 0, :], in_=pv_, func=AF.Silu), pkeys(XB_, TB), [("sgT", 0)])
            stage1_a(1, 2 * hp + 1)
            rotary_post(YB_, TB, rot, 1.0, qT[:, 1, :], ("q", 1))
            proj_b(slot_g, sk_g, 1, hT, [], TB, YB_)
            pv_ = pview(YB_, T)
            act(lambda e, pv_=pv_: e.activation(out=sgT[:, 1, :], in_=pv_, func=AF.Silu), pkeys(YB_, TB), [("sgT", 1)])
            stage3(0, 2 * hp)
            memkv_block(*W["mkv"][4 * hp + 2])
            stage3(1, 2 * hp + 1)
            memkv_block(*W["mkv"][4 * hp + 3])
            gn_mm(0)
            gn_head(0)
            gn_mm(1)
            if hp < 3:
                slot, sk = wslot(W["v"][hp + 1])
                proj_v(slot, sk)
            gn_tail(0, 2 * hp)
            gn_head(1)
            gn_tail(1, 2 * hp + 1)
        slot_cg0, sk_cg0 = wslot(W["cg"][0])
        cg0_banks = [next_banks(), next_banks()]
        for c in range(2):
            proj_b(slot_cg0, sk_cg0, c, hT, [], TB, cg0_banks[c])
        P.fence()

        P.phase(4)
        P.dma("sp", sc_in[0:32, :], sconv, writes=["sc_in"])

        def sconv_transposes():
            for chk in range(8):
                pe([("t", psf(6, 32), sc_in[0:32, 128 * chk:128 * chk + 128], identf[0:32, 0:32])], ["sc_in", "identf"], [("ps", 6)])
                act(lambda e, chk=chk: e.activation(out=scT[:, chk, :], in_=psf(6, 32), func=AF.Copy), [("ps", 6)], [("scT", chk)])

        for cp in range(4):
            if cp > 0:
                slot_cg, sk_cg = wslot(W["cg"][cp])
            cg_sb = [t1, t2]
            for c in range(2):
                if cp == 0:
                    banks = cg0_banks[c]
                else:
                    banks = next_banks()
                    proj_b(slot_cg, sk_cg, c, hT, [], TB, banks)
                for (t0, n), b in zip(TB, banks):
                    act(lambda e, b=b, n=n, t0=t0, c=c: e.activation(out=cg_sb[c][:, t0:t0 + n], in_=psf(b, n), func=AF.Copy),
                        [("ps", b)], [("cgsb", c, t0)])
            if cp == 0:
                sconv_transposes()
            slot_h, sk_h = wslot(W["hin"][cp])
            for c in range(2):
                chk = 2 * cp + c
                banks = next_banks()
                proj_b(slot_h, sk_h, c, hT, [], TB, banks)
                up = upad[:, c * (TP + 2):(c + 1) * (TP + 2)]
                dve(lambda e, up=up, chk=chk: e.tensor_copy(out=up[:, 0:2], in_=uprev[:, chk, :]), [("uprev", chk)], [("upad", c, -1)])
                for (t0, n), b in zip(TBP, banks):
                    dve(lambda e, b=b, n=n, t0=t0, c=c, up=up: e.tensor_tensor(
                        out=up[:, 2 + t0:2 + t0 + n], in0=cg_sb[c][:, t0:t0 + n], in1=psf(b, n), op=ALU.mult),
                        [("ps", b), ("cgsb", c, t0)], [("upad", c, t0)])
                ups = upads
                dve(lambda e, chk=chk: e.tensor_copy(out=upads[:, :, 0:2], in_=scT[:, chk, :].rearrange("p (s k) -> p s k", k=2)),
                    [("scT", chk)], [("upads", 0)])
                b = banks[2]
                dve(lambda e, b=b, c=c: e.tensor_tensor(
                    out=upads[:, :, 2:10], in0=cg_sb[c][:, TP:T].rearrange("p (s k) -> p s k", k=8),
                    in1=psf(b, 128).rearrange("p (s k) -> p s k", k=8), op=ALU.mult),
                    [("ps", b), ("cgsb", c, 1024)], [("upads", 1)])
                upr = [("upad", c, -1), ("upad", c, 0), ("upad", c, 512)]
                dve(lambda e, up=up, chk=chk: e.tensor_scalar(out=cacc[:, 0:TP], in0=up[:, 0:TP], scalar1=cwT[:, 0, chk:chk + 1],
                                                              scalar2=None, op0=ALU.mult), upr + ["cwT"], [("cacc", 0)])
                for k in (1, 2):
                    dve(lambda e, up=up, chk=chk, k=k: e.scalar_tensor_tensor(
                        out=cacc[:, 0:TP], in0=up[:, k:k + TP], scalar=cwT[:, k, chk:chk + 1], in1=cacc[:, 0:TP],
                        op0=ALU.mult, op1=ALU.add), upr + [("cacc", 0)], [("cacc", 0)])
                ca_s = cacc[:, TP:T].rearrange("p (s k) -> p s k", k=8)
                usr = [("upads", 0), ("upads", 1)]
                dve(lambda e, chk=chk: e.tensor_scalar(out=ca_s, in0=upads[:, :, 0:8], scalar1=cwT[:, 0, chk:chk + 1],
                                                       scalar2=None, op0=ALU.mult), usr + ["cwT"], [("cacc", 1)])
                for k in (1, 2):
                    dve(lambda e, chk=chk, k=k: e.scalar_tensor_tensor(
                        out=ca_s, in0=upads[:, :, k:k + 8], scalar=cwT[:, k, chk:chk + 1], in1=ca_s,
                        op0=ALU.mult, op1=ALU.add), usr + [("cacc", 1)], [("cacc", 1)])
                pe([("t", psf(7, 128)[0:2, :], up[:, TP:TP + 2], identf[:])], [("upad", c, 512), "identf"], [("ps", 7)])
                act(lambda e, chk=chk: e.activation(out=sc_out[0:2, 128 * chk:128 * chk + 128], in_=psf(7, 128)[0:2, :], func=AF.Copy),
                    [("ps", 7)], [("sc_out", chk)])
                dve(lambda e: e.tensor_copy(out=nbt.rearrange("p (s k) -> p s k", k=2), in_=upads[:, :, 8:10]), [("upads", 1)], ["nbt"])
                pe([("t", psf(7, 128)[0:32, :], nbt, identf[:])], ["nbt", "identf"], [("ps", 7)])
                act(lambda e, chk=chk: e.activation(out=sc_in[0:32, 128 * chk:128 * chk + 128], in_=psf(7, 128)[0:32, :], func=AF.Copy),
                    [("ps", 7)], [("sc_in2", chk), "sc_in"])
                dve(lambda e, c=c: e.tensor_copy(out=cg_sb[c][:, :], in_=cacc[:, :]), [("cacc", 0), ("cacc", 1), ("cgsb", c, 0), ("cgsb", c, 512), ("cgsb", c, 1024)],
                    [("cgsb", c, 0), ("cgsb", c, 512), ("cgsb", c, 1024)])
            slot_b, sk_b = wslot(W["bg"][cp])
            for c in range(2):
                chk = 2 * cp + c
                banks = next_banks()
                proj_b(slot_b, sk_b, c, hT, [], TB, banks)
                for (t0, n), b in zip(TB, banks):
                    dve(lambda e, b=b, n=n, t0=t0, c=c, chk=chk: e.tensor_tensor(
                        out=catT[:, 8 + chk, t0:t0 + n], in0=cg_sb[c][:, t0:t0 + n], in1=psf(b, n), op=ALU.mult),
                        [("ps", b), ("cgsb", c, t0)], [("cat", 8 + chk, t0)])
        P.dma("sp", scp, sc_out[0:2, :], reads=[("sc_out", i) for i in range(8)], track_out=True)
        P.dma("sp", scs, sc_in[0:32, :], reads=[("sc_in2", i) for i in range(8)], track_out=True)
        P.fence()

        P.phase(5)
        def out_proj(wkey, srcT, accumulate_first_from_dram, after_tile=None):
            if accumulate_first_from_dram is not None:
                for t in range(NT):
                    P.dma("sp", xres[:, t, :], accumulate_first_from_dram[t], writes=[("x", t, cb) for cb in range(4)])
            for cb in range(4):
                s0, k0 = wslot(W[wkey][cb][0])
                s1, k1 = wslot(W[wkey][cb][1])
                for t in range(NT):
                    b = t % 4
                    mms = []
                    for kc in range(KC):
                        s_ = s0 if kc < 8 else s1
                        mms.append(("m", psf(b), srcT[:, kc, 128 * t:128 * t + 128], s_[:, kc % 8, :], kc == 0, kc == KC - 1))
                    pe(mms, [k0, k1], [("ps", b)])
                    dve(lambda e, b=b, t=t, cb=cb: e.tensor_tensor(out=xres[:, t, 512 * cb:512 * cb + 512],
                                                                   in0=xres[:, t, 512 * cb:512 * cb + 512], in1=psf(b), op=ALU.add),
                        [("ps", b), ("x", t, cb)], [("x", t, cb)])
                    if cb == 3 and after_tile is not None and t >= 1:
                        after_tile(t - 1)
            if after_tile is not None:
                after_tile(NT - 1)

        def norm_hook(gidx):
            return lambda t: norm_tile(t, xres[:, t, :], [("x", t, cb) for cb in range(4)], gidx, hT, "hres")

        out_proj("mixo", catT, srcs, after_tile=norm_hook(2))
        P.fence()

        P.phase(6)
        qxT = RA[:, :].rearrange("p (k t) -> p k t", k=KC)
        for i in range(8):
            slot, sk = wslot(W["xq"][i])
            for c in range(2):
                banks = next_banks()
                proj_b(slot, sk, c, hT, [], TB, banks)
                for (t0, n), b in zip(TB, banks):
                    act(lambda e, b=b, n=n, t0=t0, i=i, c=c: e.activation(out=qxT[:, 2 * i + c, t0:t0 + n], in_=psf(b, n), func=AF.Copy),
                        [("ps", b)], [("qx", 2 * i + c, t0)])
        P.fence()

        P.phase(7)
        rh_off = [0]

        def rh(n, dt=BF16):
            nb = n if dt == BF16 else 2 * n
            o = rh_off[0]
            rh_off[0] += nb
            assert rh_off[0] <= KC * T, rh_off[0]
            v = RH[:, o:o + nb]
            if dt == F32:
                v = v.bitcast(F32)
            return v

        rh_xn0 = 0
        P.phase(8)
        oob2 = oo[:, :, :].rearrange("p a t -> p (a t)").bitcast(BF16)
        pn = oob2[:, 0:2048].rearrange("p (a t n) -> p a t n", a=2, t=4)
        sfb = Sf[:, :, :].rearrange("p h e -> p (h e)").bitcast(BF16)
        pT = sfb[:, 0:2048].rearrange("p (a c l) -> p a c l", a=2, c=2)
        rh_off[0] = 0
        kraw = rh(2 * 2 * D).rearrange("p (a t d) -> p a t d", a=2, t=2)
        vraw = rh(2 * 2 * D).rearrange("p (a t d) -> p a t d", a=2, t=2)
        kTs = rh(2 * 1024).rearrange("p (a x) -> p a x", a=2)
        for s_ in range(2):
            P.dma("pool", kraw[:, s_], ck[s_].rearrange("(t p) d -> p t d", p=128), writes=[("kraw", s_)])
            P.dma("pool", vraw[:, s_], cv[s_].rearrange("(t p) d -> p t d", p=128), writes=[("vraw", s_)])
        XS = 512.0 ** -0.5
        oT = qxT
        combos = [(hd, g4) for hd in range(4) for g4 in range(2)]

        def h2_a(i):
            hd, g4 = combos[i]
            b0 = 2 * (i % 2)
            mms = []
            for tt in range(4):
                t = 4 * g4 + tt
                o_ = psf(b0 + tt // 2, 256, 256 * (tt % 2))
                for dc in range(4):
                    mms.append(("m", o_, qxT[:, 4 * hd + dc, 128 * t:128 * t + 128], mkT[:, 4 * hd + dc, :], dc == 0, dc == 3))
            pe(mms, [], [("ps", b0), ("ps", b0 + 1)])

        def h2_b(i):
            hd, g4 = combos[i]
            pj = i % 2
            b0 = 2 * pj
            skeys = [("ps", b0), ("ps", b0 + 1)]
            so = 32 + 16 * pj
            sc4 = PS[:, b0:b0 + 2, :].rearrange("p b (t n) -> p (b t) n", t=2)
            dve(lambda e: e.reduce_max(out=stat[:, so:so + 4], in_=sc4, axis=AX.X), skeys, [("mx", pj)])
            dve(lambda e: e.tensor_scalar(out=stat[:, so + 4:so + 8], in0=stat[:, so:so + 4], scalar1=-XS, scalar2=None, op0=ALU.mult),
                [("mx", pj)], [("nmx", pj)])
            for tt in range(4):
                act(lambda e, tt=tt: e.activation(out=pn[:, pj, tt, :], in_=psf(b0 + tt // 2, 256, 256 * (tt % 2)), func=AF.Exp,
                                                  bias=stat[:, so + 4 + tt:so + 5 + tt], scale=XS, accum_out=stat[:, so + 8 + tt:so + 9 + tt]),
                    [("ps", b0 + tt // 2), ("nmx", pj)], [("pn", pj, tt), ("sm", pj, tt)])
            dve(lambda e: e.reciprocal(out=stat[:, so + 12:so + 16], in_=stat[:, so + 8:so + 12]), [("sm", pj, tt) for tt in range(4)], [("rs", pj)])
            dve(lambda e: e.tensor_tensor(out=pn[:, pj], in0=pn[:, pj], in1=stat[:, so + 12:so + 16].unsqueeze(2).broadcast_to([128, 4, 256]), op=ALU.mult),
                [("pn", pj, tt) for tt in range(4)] + [("rs", pj)], [("pn", pj, tt) for tt in range(4)])
            pb = psb(4 + pj)
            pe([("t", pb[:, 512 * nc_ + 128 * tt:512 * nc_ + 128 * tt + 128], pn[:, pj, tt, 128 * nc_:128 * nc_ + 128], identb[:]) for tt in range(4) for nc_ in range(2)],
               [("pn", pj, tt) for tt in range(4)] + ["identb"], [("ps", 4 + pj)])
            act(lambda e: e.activation(out=pT[:, pj].rearrange("p c l -> p (c l)"), in_=pb[:, :], func=AF.Copy), [("ps", 4 + pj)], [("pT", pj)])
            for e2 in range(2):
                mms = []
                for k_ in range(2):
                    ec = 2 * e2 + k_
                    for nc_ in range(2):
                        mms.append(("m", psf(6 + k_), mvb[:, nc_, 512 * hd + 128 * ec:512 * hd + 128 * ec + 128], pT[:, pj, nc_, :], nc_ == 0, nc_ == 1))
                pe(mms, [("pT", pj)], [("ps", 6), ("ps", 7)])
                act(lambda e, e2=e2: e.activation(out=oT[:, 4 * hd + 2 * e2:4 * hd + 2 * e2 + 2, 512 * g4:512 * g4 + 512],
                                                  in_=PS[:, 6:8, :], func=AF.Copy), [("ps", 6), ("ps", 7)], [("oT", hd, g4, e2)])

        h2_a(0)
        for i in range(len(combos)):
            if i + 1 < len(combos):
                h2_a(i + 1)
            h2_b(i)
        P.fence()

        P.phase(9)
        oob = oo[:, :, :].rearrange("p a t -> p (a t)").bitcast(BF16)
        ps_s = oob[:, 0:1024].rearrange("p (h n) -> p h n", h=4)
        pTs = oob[:, 1024:1088].rearrange("p (h c l) -> p h c l", h=4, c=2)
        scs_ = oo[:, :, :].rearrange("p a t -> p (a t)")[:, 1024:2048].rearrange("p (h n) -> p h n", h=4)
        def h3_load_k(s):
            a = s % 2
            P.dma("pool", kraw[:, a], ck[s].rearrange("(t p) d -> p t d", p=128), writes=[("kraw", a)])

        def h3_load_v(s):
            a = s % 2
            P.dma("pool", vraw[:, a], cv[s].rearrange("(t p) d -> p t d", p=128), writes=[("vraw", a)])

        def h3_load(s):
            h3_load_k(s)
            h3_load_v(s)

        def h3_a(s):
            a = s % 2
            sb0 = 2 + 2 * a

            def tr(hd):
                j = hd % 2
                pb = psb(j)
                mms = [("t", pb[:, 256 * dc + 128 * nt_:256 * dc + 128 * nt_ + 128], kraw[:, a, nt_, 512 * hd + 128 * dc:512 * hd + 128 * dc + 128], identb[:])
                       for dc in range(4) for nt_ in range(2)]
                pe(mms, [("kraw", a), "identb"], [("ps", j)])
                act(lambda e, pb=pb, j=j: e.activation(out=kTs[:, j, :], in_=pb[:, :], func=AF.Copy), [("ps", j)], [("kTs", j)])

            def sc_(hd):
                j = hd % 2
                sbk = sb0 + hd // 2
                so = 256 * (hd % 2)
                mms = [("m", psf(sbk, 256, so)[0:8, :], qxT[:, 4 * hd + dc, TP + 8 * s:TP + 8 * s + 8], kTs[:, j, 256 * dc:256 * dc + 256], dc == 0, dc == 3) for dc in range(4)]
                pe(mms, [("kTs", j)], [("ps", sbk)])

            tr(0); tr(1); sc_(0); tr(2); sc_(1); tr(3); sc_(2); sc_(3)

        def h3_b(s):
            a = s % 2
            sb0 = 2 + 2 * a
            skeys = [("ps", sb0), ("ps", sb0 + 1)]
            sc = PS[0:8, sb0:sb0 + 2, :].rearrange("p b n -> p (b n)")
            sc4 = PS[0:8, sb0:sb0 + 2, :].rearrange("p b (t n) -> p (b t) n", t=2)
            dve(lambda e: e.reduce_max(out=stat[0:8, 16:20], in_=sc4, axis=AX.X), skeys, ["mx"])
            dve(lambda e: e.tensor_tensor(out=scs_[0:8], in0=sc4, in1=stat[0:8, 16:20].unsqueeze(2).broadcast_to([8, 4, 256]), op=ALU.subtract),
                skeys + ["mx"], ["scs"])
            act(lambda e: e.activation(out=ps_s[0:8].rearrange("p h n -> p (h n)"), in_=scs_[0:8].rearrange("p h n -> p (h n)"), func=AF.Exp, scale=XS),
                ["scs"], ["pss"])
            dve(lambda e: e.reduce_sum(out=stat[0:8, 24:28], in_=ps_s[0:8], axis=AX.X), ["pss"], ["sm"])
            dve(lambda e: e.reciprocal(out=stat[0:8, 28:32], in_=stat[0:8, 24:28]), ["sm"], ["rs"])
            dve(lambda e: e.tensor_tensor(out=ps_s[0:8], in0=ps_s[0:8], in1=stat[0:8, 28:32].unsqueeze(2).broadcast_to([8, 4, 256]), op=ALU.mult),
                ["pss", "rs"], ["pss"])
            pb2 = psb(6)
            pe([("t", pb2[:, 16 * hd + 8 * nc_:16 * hd + 8 * nc_ + 8], ps_s[0:8, hd, 128 * nc_:128 * nc_ + 128], identb[0:8, 0:8]) for hd in range(4) for nc_ in range(2)],
               ["pss", "identb"], [("ps", 6)])
            act(lambda e: e.activation(out=pTs.rearrange("p h c l -> p (h c l)"), in_=pb2[:, 0:64], func=AF.Copy), [("ps", 6)], ["pTs"])
            mms = []
            for hd in range(4):
                for ec in range(4):
                    for nc_ in range(2):
                        mms.append(("m", psf(7, 8, 8 * (4 * hd + ec)), vraw[:, a, nc_, 512 * hd + 128 * ec:512 * hd + 128 * ec + 128], pTs[:, hd, nc_, :], nc_ == 0, nc_ == 1))
            pe(mms, [("vraw", a), "pTs"], [("ps", 7)])
            act(lambda e: e.activation(out=oT[:, :, TP + 8 * s:TP + 8 * s + 8], in_=psf(7, 128).rearrange("p (c l) -> p c l", c=16), func=AF.Copy),
                [("ps", 7)], [("oTs", s)])

        h3_a(0)
        for s in range(NSEQ):
            if s + 1 < NSEQ:
                h3_a(s + 1)
            if s + 2 < NSEQ:
                h3_load_k(s + 2)
            h3_b(s)
            if s + 2 < NSEQ:
                h3_load_v(s + 2)
        P.fence()

        P.phase(10)
        out_proj("xo", oT, None, after_tile=norm_hook(3))
        P.fence()

        P.phase(11)
        hid = RA[:, 0:2 * 4 * T].rearrange("p (a k t) -> p a k t", a=2, k=4)
        gsb = RA[:, 2 * 4 * T:2 * 4 * T + 2 * T].bitcast(F32)
        for g in range(NG):
            a = g % 2
            for j in range(2):
                sgj, kgj = wslot(W["gate"][g][j])
                suj, kuj = wslot(W["up"][g][j])
                for c in range(2):
                    kk_ = 2 * j + c
                    bg_ = next_banks()
                    hrd = ["hT"] if g == NG - 1 else []
                    proj_b(sgj, kgj, c, hT, hrd, TB, bg_)
                    for (t0, n), b in zip(TB, bg_):
                        act(lambda e, b=b, n=n, t0=t0: e.activation(out=gsb[:, t0:t0 + n], in_=psf(b, n), func=AF.Silu),
                            [("ps", b)], [("gsb", t0)])
                    bu_ = next_banks()
                    proj_b(suj, kuj, c, hT, hrd, TB, bu_)
                    for (t0, n), b in zip(TB, bu_):
                        dve(lambda e, b=b, n=n, t0=t0, a=a, kk_=kk_: e.tensor_tensor(out=hid[:, a, kk_, t0:t0 + n], in0=gsb[:, t0:t0 + n], in1=psf(b, n), op=ALU.mult),
                            [("ps", b), ("gsb", t0)], [("hid", a, kk_, t0)])
            hr = [("hid", a, k_, t0) for k_ in range(4) for t0, _ in TB]

            def down_tile(t, cb, sd, kd_, b):
                co = 512 * (cb % 2)
                mms = [("m", psf(b), hid[:, a, k_, 128 * t:128 * t + 128], sd[:, k_, co:co + 512], k_ == 0, k_ == 3) for k_ in range(4)]
                pe(mms, [kd_] + hr, [("ps", b)])
                dve(lambda e, b=b, t=t, cb=cb: e.tensor_tensor(out=xres[:, t, 512 * cb:512 * cb + 512],
                                                               in0=xres[:, t, 512 * cb:512 * cb + 512], in1=psf(b), op=ALU.add),
                    [("ps", b), ("x", t, cb)], [("x", t, cb)])

            if g < NG - 1:
                for cb in range(4):
                    if cb % 2 == 0:
                        sd, kd_ = wslot(W["down"][g][cb // 2])
                    for t in range(NT):
                        down_tile(t, cb, sd, kd_, 6 + (t % 2))
            else:
                sd0, kd0 = wslot(W["down"][g][0])
                sd1, kd1 = wslot(W["down"][g][1])
                for t in range(NT):
                    for cb in range(4):
                        down_tile(t, cb, (sd0, sd1)[cb // 2], (kd0, kd1)[cb // 2], 4 + cb)

        P.phase(12)
        gbc = RH[:, 0:2 * D].bitcast(F32)
        P.dma("sp", gbc, final_g.partition_broadcast(128), reads=[], writes=["gbc", "hT"])
        sq3 = RH[:, 2 * D:3 * D]
        yst = RH[:, 3 * D:7 * D].bitcast(F32).rearrange("p (a d) -> p a d", a=2)
        for t in range(NT):
            a = t % 2
            act(lambda e, t=t: e.activation(out=sq3, in_=xres[:, t, :], func=AF.Square, accum_out=stat[:, 0:1]), [("x", t, cb) for cb in range(4)] + ["gbc"], ["sq3", "ss"])
            act(lambda e: e.activation(out=stat[:, 1:2], in_=stat[:, 0:1], func=AF.Sqrt, bias=RMS_EPS, scale=1.0 / D), ["ss"], ["ms"])
            dve(lambda e: e.reciprocal(out=stat[:, 2:3], in_=stat[:, 1:2]), ["ms"], ["rstd"])
            dve(lambda e, t=t, a=a: e.scalar_tensor_tensor(out=yst[:, a, :], in0=xres[:, t, :], scalar=stat[:, 2:3], in1=gbc, op0=ALU.mult, op1=ALU.mult),
                ["rstd", "gbc"] + [("x", t, cb) for cb in range(4)], [("yst", a)])
            dst = yp[128 * t:128 * t + 128, :] if t < 8 else ys[:, :]
            P.dma("sp", dst, yst[:, a, :], reads=[("yst", a)], track_out=True)
        P.dead = False
        final_waits = []
        for k, v in P.dma_tokens:
            if v > P.seen["sp"].get(k, 0):
                P.seen["sp"][k] = v
                final_waits.append((k, v))
        P.q["sp"].append((final_waits, None, None))

        print("engine op counts", {e: (P.epoch[e], P.cnt[e]) for e in ENGS}, "n_sems", len(P.sems), flush=True)
        with nc.Block() as block:
            P.emit(block)
        print("n_sems", len(P.sems), flush=True)
    return nc


def _consts(half):
    c = {}
    c["c_ident"] = np.eye(128, dtype=np.float32)
    perm = np.zeros((128, 128), np.float32)
    for m in range(128):
        perm[(m + 64) % 128, m] = 1.0
    c["c_perm"] = perm
    inv = 10000.0 ** (-np.arange(64, dtype=np.float64) / 64.0)

    def rot_table(pos):
        ang = pos.astype(np.float64)[None, :] * inv[:, None]
        cos = np.cos(ang).astype(np.float32); sin = np.sin(ang).astype(np.float32)
        tab = np.zeros((128, 2, pos.shape[0]), np.float32)
        tab[:64, 0] = cos; tab[64:, 0] = cos
        tab[:64, 1] = -sin; tab[64:, 1] = sin
        return tab
    pos_main = np.concatenate([half * TP + np.arange(TP), np.tile(16384 + np.arange(8), NSEQ)])
    c["c_rot"] = rot_table(pos_main)
    c["c_rotp"] = rot_table(np.arange(TP))
    lg = np.array(LG, dtype=np.float64)
    m = np.arange(128)[:, None]; l = np.arange(128)[None, :]
    dm = np.zeros((128, H, 128)); sm = np.zeros((128, H, 128))
    for h in range(H):
        rel = l - m
        dm[:, h, :] = np.where(rel >= 0, np.exp(rel * lg[h]), 0.0)
        same = (m // 8) == (l // 8)
        sm[:, h, :] = np.where((rel >= 0) & same, np.exp(rel * lg[h]), 0.0)
    c["c_dmask"] = dm.astype(np.float32); c["c_smask"] = sm.astype(np.float32)
    qd = np.zeros((128, H, 128)); qds = np.zeros((128, H, 128))
    for h in range(H):
        qd[:, h, :] = np.exp((np.arange(128) + 1.0) * lg[h])[None, :]
        qds[:, h, :] = np.exp(((np.arange(128) % 8) + 1.0) * lg[h])[None, :]
    c["c_qdec"] = qd.astype(np.float32); c["c_qdecs"] = qds.astype(np.float32)
    kd = np.zeros((128, H)); kdp = np.zeros((128, H, 8)); rmd = np.zeros((128, H, NSEQ + 1))
    for h in range(H):
        kd[:, h] = np.exp((127.0 - np.arange(128)) * lg[h])
        for ch in range(8):
            kdp[:, h, ch] = np.exp((1023.0 - (128 * ch + np.arange(128))) * lg[h])
        for s in range(NSEQ):
            mm_ = np.arange(128)
            rmd[:, h, s] = np.where(mm_ // 8 == s, 1.0, 0.0)
        rmd[:, h, NSEQ] = np.exp((7.0 - (np.arange(128) % 8)) * lg[h])
    c["c_kdec"] = kd.astype(np.float32); c["c_kdecp"] = kdp.astype(np.float32); c["c_rmd"] = rmd.astype(np.float32)
    return c


def _core_inputs(c, shared, x_prompt, x_sample, mem_prompt, state_ret, state_conv, cache_mem_k, cache_mem_v):
    b, half = c // 2, c % 2
    m = dict(shared)
    m["xp"] = np.ascontiguousarray(x_prompt[b, half * TP:(half + 1) * TP])
    m["xprev"] = np.ascontiguousarray(x_prompt[b, 0:TP]) if half == 1 else np.zeros((TP, D), np.float32)
    m["xs"] = np.ascontiguousarray(x_sample[NSEQ * c:NSEQ * (c + 1)].reshape(TS, D))
    own = mem_prompt[b, 128 * half:128 * half + 128]
    oth = mem_prompt[b, 128 * (1 - half):128 * (1 - half) + 128]
    m["mem"] = np.ascontiguousarray(np.concatenate([own, oth], axis=0))
    m["sret"] = np.ascontiguousarray(state_ret[0, NSEQ * c:NSEQ * (c + 1)])
    m["sconv"] = np.ascontiguousarray(state_conv[0, NSEQ * c:NSEQ * (c + 1)].reshape(NSEQ * 2, 1024))
    m["ck"] = np.ascontiguousarray(cache_mem_k[0, NSEQ * c:NSEQ * (c + 1)].reshape(NSEQ, NMEM, D))
    m["cv"] = np.ascontiguousarray(cache_mem_v[0, NSEQ * c:NSEQ * (c + 1)].reshape(NSEQ, NMEM, D))
    m.update(_consts(half))
    return m


_NC_CACHE = {}


def kernel(x_prompt, x_sample, mem_prompt, state_ret, state_conv, cache_mem_k, cache_mem_v,
           ln_mix_g, w_in, conv_w, ret_gn_g, w_mix_out, ln_mem_g, ln_xa_g, w_xq, w_mk, w_mv,
           w_xo, ln_ffn_g, w_gate, w_up, w_down, final_g):
    f = lambda a: np.ascontiguousarray(np.asarray(a, dtype=np.float32))
    x_prompt, x_sample, mem_prompt = f(x_prompt), f(x_sample), f(mem_prompt)
    state_ret, state_conv, cache_mem_k, cache_mem_v = f(state_ret), f(state_conv), f(cache_mem_k), f(cache_mem_v)
    shared = {
        "ln_mix_g": f(ln_mix_g)[0], "w_in": f(w_in)[0], "conv_w": f(conv_w)[0], "ret_gn_g": f(ret_gn_g)[0],
        "w_mix_out": f(w_mix_out)[0], "ln_mem_g": f(ln_mem_g)[0], "ln_xa_g": f(ln_xa_g)[0], "w_xq": f(w_xq)[0],
        "w_mk": f(w_mk)[0], "w_mv": f(w_mv)[0], "w_xo": f(w_xo)[0], "ln_ffn_g": f(ln_ffn_g)[0],
        "w_gate": f(w_gate)[0], "w_up": f(w_up)[0], "w_down": f(w_down)[0], "final_g": f(final_g),
    }
    if "nc" not in _NC_CACHE:
        _NC_CACHE["nc"] = build_program()
    nc = _NC_CACHE["nc"]
    in_maps = [_core_inputs(c, shared, x_prompt, x_sample, mem_prompt, state_ret, state_conv, cache_mem_k, cache_mem_v)
               for c in range(8)]
    res = run_bass_kernel_spmd(nc, in_maps, core_ids=list(range(8)))
    R = res.results
    y_prompt = np.zeros((4, 2048, D), np.float32)
    y_sample = np.zeros((128, 8, D), np.float32)
    srp_o = np.zeros((1, 4, H, 128, 128), np.float32)
    srs_o = np.zeros((1, 128, H, 128, 128), np.float32)
    scp_o = np.zeros((1, 4, 2, 1024), np.float32)
    scs_o = np.zeros((1, 128, 2, 1024), np.float32)
    mk_o = np.zeros((1, 4, NMEM, 4, 512), np.float32)
    mv_o = np.zeros((1, 4, NMEM, 4, 512), np.float32)
    for c in range(8):
        b, half = c // 2, c % 2
        r = R[c]
        y_prompt[b, half * TP:(half + 1) * TP] = r["yp"]
        y_sample[NSEQ * c:NSEQ * (c + 1)] = r["ys"].reshape(NSEQ, 8, D)
        if half == 1:
            srp_o[0, b] = r["srp"]
            scp_o[0, b] = r["scp"]
        srs_o[0, NSEQ * c:NSEQ * (c + 1)] = r["srs"]
        scs_o[0, NSEQ * c:NSEQ * (c + 1)] = r["scs"].reshape(NSEQ, 2, 1024)
        mk_o[0, b, 128 * half:128 * half + 128] = r["mko"].reshape(128, 4, 512)
        mv_o[0, b, 128 * half:128 * half + 128] = r["mvo"].reshape(128, 4, 512)
    return (y_prompt, y_sample, srp_o, srs_o, scp_o, scs_o, mk_o, mv_o)
```

```python
import math
import os
from contextlib import ExitStack

import numpy as np
import concourse.bass as bass
import concourse.mybir as mybir
from concourse.bass_utils import run_bass_kernel_spmd

F32 = mybir.dt.float32
BF16 = mybir.dt.bfloat16
ALU = mybir.AluOpType
AF = mybir.ActivationFunctionType
AX = mybir.AxisListType

D = 2048
KC = 16
TP = 1024
TS = 128
T = TP + TS
NT = T // 128
NSEQ = 16
H = 8
DFF = 5632
NG = DFF // 512
NMEM = 256
RMS_EPS = 1e-6
GN_EPS = 1e-5
NSLOT = 4
TB = [(0, 512), (512, 512), (1024, 128)]
TBP = [(0, 512), (512, 512)]
LG = [math.log(1.0 - 2.0 ** (-5.0 - h)) for h in range(H)]

ENGS = ("pe", "act", "dve", "pool", "sp")
_PE_LABELS = [] if os.environ.get("KLABELS") else None


class Prog:
    def __init__(self, nc, sem_alloc):
        self.nc = nc
        self.sem_alloc = sem_alloc
        self.q = {e: [] for e in ENGS}
        self.cnt = {e: 0 for e in ENGS}
        self.epoch = {e: 0 for e in ENGS}
        self.sems = {}
        self.seen = {e: {} for e in ENGS}
        self.lastw = {}
        self.readers = {}
        self.nds = 12
        self.dq = {q: {"i": 0, "uses": [0] * self.nds} for q in ("sp", "pool", "act")}
        self.dma_tokens = []
        self.weight_tokens = set()
        self.EPOCH_MAX = 4000
        self.dead = False
        self.stop = float(os.environ.get("KSTOP", "99"))

    def phase(self, k):
        if k > self.stop:
            self.dead = True

    def sem(self, key):
        if key not in self.sems:
            self.sems[key] = self.sem_alloc("s_" + "_".join(str(k) for k in key))
        return self.sems[key]

    def _collect(self, eng, reads, writes):
        deps = {}

        def need(tok):
            if tok is None:
                return
            k, v = tok
            if k[0] == "e" and k[1] == "pe" and eng == "pe":
                return
            if v > deps.get(k, 0):
                deps[k] = v

        for t in reads:
            need(self.lastw.get(t))
        for t in writes:
            need(self.lastw.get(t))
            for r in self.readers.get(t, ()):
                need(r)
        waits = []
        for k, v in deps.items():
            if v > self.seen[eng].get(k, 0):
                self.seen[eng][k] = v
                waits.append((k, v))
        return waits

    def _record(self, tok, reads, writes):
        for t in reads:
            self.readers.setdefault(t, []).append(tok)
        for t in writes:
            self.lastw[t] = tok
            self.readers[t] = []

    def op(self, eng, fn, reads=(), writes=()):
        if self.dead:
            return None
        ps_reads = [t for t in reads if isinstance(t, tuple) and t[0] == "ps" and t not in writes]
        if ps_reads:
            writes = list(writes) + ps_reads
        waits = self._collect(eng, reads, writes)
        if self.cnt[eng] >= self.EPOCH_MAX:
            self.epoch[eng] += 1
            self.cnt[eng] = 0
        self.cnt[eng] += 1
        key = ("e", eng, self.epoch[eng])
        tok = (key, self.cnt[eng])
        self.q[eng].append((waits, fn, (key, 1)))
        self._record(tok, reads, writes)
        return tok

    def dma(self, queue, out, in_, reads=(), writes=(), track_out=False, slow=False):
        if self.dead:
            return None
        st = self.dq[queue]
        slot = st["i"] % self.nds
        st["i"] += 1
        key = ("d", queue, slot)
        waits = self._collect(queue, reads, writes)
        prev = st["uses"][slot] * 16
        if prev > self.seen[queue].get(key, 0):
            self.seen[queue][key] = prev
            waits.append((key, prev))
        st["uses"][slot] += 1
        tok = (key, st["uses"][slot] * 16)
        if slow:
            self.q[queue].append((waits, lambda e, o=out, i=in_: e.dma_start(out=o, in_=i, allow_slow_non_contiguous=True), (key, 16)))
        else:
            self.q[queue].append((waits, lambda e, o=out, i=in_: e.dma_start(out=o, in_=i), (key, 16)))
        self._record(tok, reads, writes)
        if track_out:
            self.dma_tokens.append(tok)
        return tok

    def fence(self):
        if self.dead:
            return
        toks = []
        for e in ENGS:
            if self.cnt[e] > 0:
                toks.append((("e", e, self.epoch[e]), self.cnt[e]))
        for q, st in self.dq.items():
            for s in range(self.nds):
                if st["uses"][s] > 0:
                    tk = (("d", q, s), st["uses"][s] * 16)
                    if tk in self.weight_tokens:
                        continue
                    toks.append(tk)
        for e in ENGS:
            waits = []
            for k, v in toks:
                if k[0] == "e" and k[1] == e and e == "pe":
                    continue
                if v > self.seen[e].get(k, 0):
                    self.seen[e][k] = v
                    waits.append((k, v))
            if waits:
                self.q[e].append((waits, None, None))
        keep_w = {k: v for k, v in self.lastw.items() if isinstance(k, tuple) and k[0] == "slot"}
        keep_r = {k: v for k, v in self.readers.items() if isinstance(k, tuple) and k[0] == "slot"}
        self.lastw = keep_w
        self.readers = keep_r

    def emit(self, block):
        nc = self.nc

        def run(eng_name):
            def body(e):
                for waits, fn, inc in self.q[eng_name]:
                    for k, v in waits:
                        e.wait_ge(self.sem(k), v)
                    if fn is not None:
                        ins = fn(e)
                        ins.then_inc(self.sem(inc[0]), inc[1])
            return body

        block.tensor(run("pe"))
        block.scalar(run("act"))
        block.vector(run("dve"))
        block.gpsimd(run("pool"))
        block.sync(run("sp"))


def build_program():
    nc = bass.Bass("TRN2", target_bir_lowering=False)

    def din(name, shape):
        return nc.dram_tensor(name, list(shape), F32, kind="ExternalInput").ap()

    def dout(name, shape):
        return nc.dram_tensor(name, list(shape), F32, kind="ExternalOutput").ap()

    xp = din("xp", [TP, D]); xprev = din("xprev", [TP, D]); xs = din("xs", [TS, D])
    mem = din("mem", [NMEM, D])
    sret = din("sret", [NSEQ, H, 128, 128]); sconv = din("sconv", [NSEQ * 2, 1024])
    ck = din("ck", [NSEQ, NMEM, D]); cv = din("cv", [NSEQ, NMEM, D])
    ln_mix_g = din("ln_mix_g", [D]); w_in = din("w_in", [D, 7168]); conv_w = din("conv_w", [3, 1024])
    ret_gn_g = din("ret_gn_g", [1024]); w_mix_out = din("w_mix_out", [D, D])
    ln_mem_g = din("ln_mem_g", [D]); ln_xa_g = din("ln_xa_g", [D])
    w_xq = din("w_xq", [D, D]); w_mk = din("w_mk", [D, D]); w_mv = din("w_mv", [D, D]); w_xo = din("w_xo", [D, D])
    ln_ffn_g = din("ln_ffn_g", [D]); w_gate = din("w_gate", [D, DFF]); w_up = din("w_up", [D, DFF])
    w_down = din("w_down", [DFF, D]); final_g = din("final_g", [D])
    c_ident = din("c_ident", [128, 128]); c_perm = din("c_perm", [128, 128])
    c_rot = din("c_rot", [128, 2, T]); c_rotp = din("c_rotp", [128, 2, TP])
    c_dmask = din("c_dmask", [128, H, 128]); c_smask = din("c_smask", [128, H, 128])
    c_qdec = din("c_qdec", [128, H, 128]); c_qdecs = din("c_qdecs", [128, H, 128])
    c_kdec = din("c_kdec", [128, H]); c_kdecp = din("c_kdecp", [128, H, 8]); c_rmd = din("c_rmd", [128, H, NSEQ + 1])

    yp = dout("yp", [TP, D]); ys = dout("ys", [TS, D])
    srp = dout("srp", [H, 128, 128]); srs = dout("srs", [NSEQ, H, 128, 128])
    scp = dout("scp", [2, 1024]); scs = dout("scs", [NSEQ * 2, 1024])
    mko = dout("mko", [128, D]); mvo = dout("mvo", [128, D])

    es = ExitStack()
    with es:
        def sb(name, shape, dt):
            return es.enter_context(nc.sbuf_tensor(name, list(shape), dt))

        RX = sb("RX", [128, NT * D], F32)
        RH = sb("RH", [128, KC * T], BF16)
        RA = sb("RA", [128, KC * T], BF16)
        SL = sb("SL", [128, NSLOT, 4096], BF16)
        identf = sb("identf", [128, 128], F32)
        identb = sb("identb", [128, 128], BF16)
        permb = sb("permb", [128, 128], BF16)
        onesf = sb("onesf", [128, 128], F32)
        gT = sb("gT", [128, 4, KC], F32)
        gnT = sb("gnT", [128, 8], F32)
        cwT = sb("cwT", [128, 3, 8], F32)
        stat = sb("stat", [128, 64], F32)
        Sf = sb("Sf", [128, H, 128], F32)
        MKV = sb("MKV", [128, 2, KC * 256], BF16)
        oo = sb("oo", [128, 2, T], F32)
        oall = oo[:, 0, :]
        osq = oo[:, 1, :]
        uprev = sb("uprev", [128, 8, 2], F32)
        PS = es.enter_context(nc.psum_tensor("PS", [128, 8, 512], F32))

        sem_list = []

        def sem_alloc(name):
            s = es.enter_context(nc.semaphore(name))
            sem_list.append(s)
            return s

        P = Prog(nc, sem_alloc)

        def psf(bank, n=512, off=0):
            return PS[:, bank, off:off + n]

        def psb(bank):
            return PS[:, bank, :].bitcast(BF16)

        def pe(mms, reads, writes):
            if _PE_LABELS is not None and not P.dead:
                import inspect
                fr = inspect.stack()[1]
                _PE_LABELS.append(("%s:%d" % (fr.function, fr.lineno), len(mms)))
            def fn(e, mms=mms):
                ins = None
                for m in mms:
                    if m[0] == "m":
                        ins = e.matmul(m[1], m[2], m[3], start=m[4], stop=m[5])
                    else:
                        ins = e.transpose(m[1], m[2], m[3])
                return ins
            return P.op("pe", fn, reads, writes)

        def dve(f, reads, writes):
            return P.op("dve", f, reads, writes)

        def act(f, reads, writes):
            return P.op("act", f, reads, writes)

        wv = lambda w: w.rearrange("(kc p) n -> p kc n", p=128)
        blocks = []

        def blk_std(w, col0):
            blocks.append((wv(w)[:, :, col0:col0 + 256], (16, 256)))
            return len(blocks) - 1

        def blk_a(w, kh, cb):
            blocks.append((wv(w)[:, 8 * kh:8 * kh + 8, 512 * cb:512 * cb + 512], (8, 512)))
            return len(blocks) - 1

        def blk_down(g, half):
            blocks.append((wv(w_down)[:, 4 * g:4 * g + 4, 1024 * half:1024 * half + 1024], (4, 1024)))
            return len(blocks) - 1

        CQ, CK, CV, CG, CBG, CCG, CHIN = 0, 1024, 2048, 3072, 4096, 5120, 6144
        W = {}
        W["pre_k"] = []; W["pre_v"] = []; W["pre_cg"] = []; W["pre_hin"] = []
        for i in range(4):
            W["pre_k"].append(blk_std(w_in, CK + 256 * i)); W["pre_v"].append(blk_std(w_in, CV + 256 * i))
            W["pre_cg"].append(blk_std(w_in, CCG + 256 * i)); W["pre_hin"].append(blk_std(w_in, CHIN + 256 * i))
        W["q"] = []; W["k"] = []; W["v"] = []; W["g"] = []; W["mkv"] = []
        for i in range(4):
            W["v"].append(blk_std(w_in, CV + 256 * i)); W["k"].append(blk_std(w_in, CK + 256 * i))
            W["mkv"].append((0, 2 * i, blk_std(w_mk, 256 * (2 * i))))
            W["q"].append(blk_std(w_in, CQ + 256 * i))
            W["mkv"].append((0, 2 * i + 1, blk_std(w_mk, 256 * (2 * i + 1))))
            W["g"].append(blk_std(w_in, CG + 256 * i))
            W["mkv"].append((1, 2 * i, blk_std(w_mv, 256 * (2 * i))))
            W["mkv"].append((1, 2 * i + 1, blk_std(w_mv, 256 * (2 * i + 1))))
        W["cg"] = []; W["hin"] = []; W["bg"] = []
        for i in range(4):
            W["cg"].append(blk_std(w_in, CCG + 256 * i)); W["hin"].append(blk_std(w_in, CHIN + 256 * i))
            W["bg"].append(blk_std(w_in, CBG + 256 * i))
        W["mixo"] = [[blk_a(w_mix_out, kh, cb) for kh in range(2)] for cb in range(4)]
        W["xq"] = [blk_std(w_xq, 256 * i) for i in range(8)]
        W["xo"] = [[blk_a(w_xo, kh, cb) for kh in range(2)] for cb in range(4)]
        W["gate"] = []; W["up"] = []; W["down"] = []
        for g in range(NG):
            gl, ul = [], []
            for j in range(2):
                gl.append(blk_std(w_gate, 512 * g + 256 * j)); ul.append(blk_std(w_up, 512 * g + 256 * j))
            W["gate"].append(gl); W["up"].append(ul)
            W["down"].append([blk_down(g, j) for j in range(2)])
        wstate = {"issued": 0, "cur": -1}

        def wslot(i):
            assert i == wstate["cur"] + 1 or i == wstate["cur"], (i, wstate)
            wstate["cur"] = i
            while wstate["issued"] < len(blocks) and wstate["issued"] <= i + NSLOT - 2:
                j = wstate["issued"]
                ap, shp = blocks[j]
                dst = SL[:, j % NSLOT, :].rearrange("p (a b) -> p a b", a=shp[0])
                wt = P.dma("pool", dst, ap, reads=([("hprev", 1)] if j < NSLOT - 1 else ()), writes=[("slot", j % NSLOT)])
                if wt is not None:
                    P.weight_tokens.add(wt)
                wstate["issued"] += 1
            shp = blocks[i][1]
            return SL[:, i % NSLOT, :].rearrange("p (a b) -> p a b", a=shp[0]), ("slot", i % NSLOT)

        P.dma("sp", identf[:], c_ident, writes=["identf"])
        P.dma("pool", identb[:], c_ident, writes=["identb"])
        P.dma("pool", permb[:], c_perm, writes=["permb"])
        gst = oo[:, :, :].rearrange("p a t -> p (a t)")[:, 0:128]
        for i, g in enumerate((ln_mix_g, ln_mem_g, ln_xa_g, ln_ffn_g)):
            P.dma("sp", gst[16 * i:16 * i + 16, :], g.rearrange("(kc p) -> kc p", p=128), writes=["gst"])
        P.dma("sp", gst[64:72, :], ret_gn_g.rearrange("(kc p) -> kc p", p=128), writes=["gst"])
        P.dma("sp", gst[72:96, :], conv_w.rearrange("k (kc p) -> (k kc) p", p=128), writes=["gst"])
        pe([("t", psf(0, 96), gst[0:96, :], identf[0:96, 0:96])], ["gst", "identf"], [("ps", 0)])
        act(lambda e: e.activation(out=gT[:, :, :].rearrange("p a k -> p (a k)"), in_=psf(0, 64), func=AF.Copy), [("ps", 0)], ["gT"])
        act(lambda e: e.activation(out=gnT[:], in_=psf(0, 8, 64), func=AF.Copy), [("ps", 0)], ["gnT"])
        act(lambda e: e.activation(out=cwT[:, :, :].rearrange("p k c -> p (k c)"), in_=psf(0, 24, 72), func=AF.Copy), [("ps", 0)], ["cwT"])
        dve(lambda e: e.memset(onesf[:], 1.0 / 128.0), [], ["onesf"])

        rx_off = [0]

        def rx(n, dt=F32, shape=None):
            nf = n if dt == F32 else (n + 1) // 2
            o = rx_off[0]
            rx_off[0] += nf
            assert rx_off[0] <= NT * D, rx_off[0]
            v = RX[:, o:o + nf]
            if dt == BF16:
                v = v.bitcast(BF16)[:, 0:n]
            return v

        rot = rx(2 * T).rearrange("p (a t) -> p a t", a=2)
        dmask = rx(2 * 128).rearrange("p (h l) -> p h l", h=2)
        smask = rx(2 * 128).rearrange("p (h l) -> p h l", h=2)
        qdec = rx(2 * 128).rearrange("p (h l) -> p h l", h=2)
        qdecs = rx(2 * 128).rearrange("p (h l) -> p h l", h=2)
        kdec = rx(H)
        kdecp = rx(H * 8).rearrange("p (h c) -> p h c", h=H)
        rmd = rx(H * (NSEQ + 1)).rearrange("p (h s) -> p h s", h=H)
        qT = rx(2 * T, BF16).rearrange("p (a t) -> p a t", a=2)
        kT = rx(2 * T, BF16).rearrange("p (a t) -> p a t", a=2)
        sgT = rx(2 * T, BF16).rearrange("p (a t) -> p a t", a=2)
        vtok = rx(NT * 256, BF16).rearrange("p (t c) -> p t c", t=NT)
        raw = rx(T, BF16)
        t1 = rx(T)
        t2 = rx(T)
        stm = rx(2 * 128, BF16).rearrange("p (a l) -> p a l", a=2)
        kdt = rx(2 * 128, BF16).rearrange("p (a l) -> p a l", a=2)
        qds_ = rx(128, BF16)
        inn = rx(128)
        un0 = rx_off[0]
        xload = rx(2 * D).rearrange("p (a d) -> p a d", a=2)
        sqj = rx(D, BF16)
        enda = rx_off[0]
        rx_off[0] = un0
        SH = 8
        ssf2 = rx(2 * SH * 128).rearrange("p (a s e) -> p a s e", a=2, s=SH)
        ssb = rx(SH * 128, BF16).rearrange("p (s e) -> p s e", s=SH)
        Sprev = rx(2 * 8 * 128, BF16).rearrange("p (a c e) -> p a c e", a=2, c=8)
        Stmp = rx(2 * 128).rearrange("p (a e) -> p a e", a=2)
        qdall = rx(TP, BF16)
        vexp_off = rx_off[0]
        vexp = rx(NSEQ * 128, BF16).rearrange("p (s e) -> p s e", s=NSEQ)
        ktkd = rx(128, BF16)
        kdta = rx(8 * 128, BF16).rearrange("p (c d) -> p c d", c=8)
        osq_alt = RX[:, vexp_off:vexp_off + T]
        stma = rx(9 * 128, BF16).rearrange("p (c l) -> p c l", c=9)
        endb = rx_off[0]
        rx_off[0] = un0
        upad = rx(2 * (TP + 2))
        upads = rx(NSEQ * 10).rearrange("p (s k) -> p s k", s=NSEQ)
        cacc = rx(T)
        scT = rx(8 * 32).rearrange("p (c k) -> p c k", c=8)
        sc_in = rx(1024)
        sc_out = rx(1024)
        nbt = rx(32)
        endc = rx_off[0]
        rx_off[0] = max(enda, endb, endc)
        assert rx_off[0] <= NT * D, rx_off[0]

        P.dma("sp", kdec, c_kdec, writes=["kdec"])
        P.dma("sp", kdecp, c_kdecp, writes=["kdecp"])
        P.dma("sp", rmd, c_rmd, writes=["rmd"])

        hT = RH[:, :].rearrange("p (k t) -> p k t", k=KC)
        hprevT = RA[:, 0:KC * TP].rearrange("p (k t) -> p k t", k=KC)
        catT = RA[:, :].rearrange("p (k t) -> p k t", k=KC)
        xres = RX[:, :].rearrange("p (t d) -> p t d", t=NT)

        xn2 = oo[:, :, :].rearrange("p a t -> p (a t)").bitcast(BF16)[:, 0:2 * D].rearrange("p (a d) -> p a d", a=2)

        def norm_tile(i, xt, xr, gidx, dstT, dkey):
            j = i % 2
            so = 4 * j
            act(lambda e: e.activation(out=xn2[:, j, :], in_=xt, func=AF.Square, accum_out=stat[:, so:so + 1]), xr, [("xn", j), ("ss", j), "gst"])
            act(lambda e: e.activation(out=stat[:, so + 1:so + 2], in_=stat[:, so:so + 1], func=AF.Sqrt, bias=RMS_EPS, scale=1.0 / D), [("ss", j)], [("ms", j)])
            dve(lambda e: e.reciprocal(out=stat[:, so + 2:so + 3], in_=stat[:, so + 1:so + 2]), [("ms", j)], [("rstd", j)])
            if j == 0:
                act(lambda e: e.activation(out=xn2[:, j, :], in_=xt, func=AF.Copy, scale=stat[:, so + 2:so + 3]), xr + [("rstd", j)], [("xn", j)])
            else:
                dve(lambda e: e.tensor_scalar(out=xn2[:, j, :], in0=xt, scalar1=stat[:, so + 2:so + 3], scalar2=None, op0=ALU.mult),
                    xr + [("rstd", j)], [("xn", j)])
            b0 = 4 + 2 * j
            pbb = PS[:, b0:b0 + 2, :].rearrange("p b n -> p (b n)").bitcast(BF16)
            pe([("t", pbb[:, 128 * kc:128 * kc + 128], xn2[:, j, 128 * kc:128 * kc + 128], identb[:]) for kc in range(KC)],
               [("xn", j), "identb"], [("ps", b0), ("ps", b0 + 1)])
            dve(lambda e: e.tensor_tensor(out=dstT[:, :, 128 * i:128 * i + 128], in0=pbb.rearrange("p (k t) -> p k t", k=KC),
                                          in1=gT[:, gidx, :].unsqueeze(2).broadcast_to([128, KC, 128]), op=ALU.mult),
                [("ps", b0), ("ps", b0 + 1), "gT"], [(dkey, i)])

        def norm_tiles(src_aps, gidx, dstT, dkey):
            for i in range(len(src_aps)):
                xt = xload[:, i % 2, :]
                P.dma("sp", xt, src_aps[i], writes=[("xload", i % 2)])
                norm_tile(i, xt, [("xload", i % 2)], gidx, dstT, dkey)

        def proj_b(slot, slot_key, c, src, src_reads, tbs, banks):
            mms = []
            for (t0, n), b in zip(tbs, banks):
                pass
            for kc in range(KC):
                for (t0, n), b in zip(tbs, banks):
                    mms.append(("m", psf(b, n), slot[:, kc, 128 * c:128 * c + 128], src[:, kc, t0:t0 + n],
                                kc == 0, kc == KC - 1))
            return pe(mms, [slot_key] + src_reads, [("ps", b) for b in banks[:len(tbs)]])

        bank_sets = [[0, 1, 2], [3, 4, 5]]
        bs_i = [0]

        def next_banks():
            b = bank_sets[bs_i[0] % 2]
            bs_i[0] += 1
            return b

        def pview(banks, ntok):
            b0 = banks[0]
            return PS[:, b0:b0 + 3, :].rearrange("p b n -> p (b n)")[:, 0:ntok]

        def pkeys(banks, tbs):
            return [("ps", b) for b in banks[:len(tbs)]]

        def rotary_pre(banks, tbs, tab, scale):
            ntok = tbs[-1][0] + tbs[-1][1]
            pv = pview(banks, ntok)
            pk = pkeys(banks, tbs)
            act(lambda e: e.activation(out=raw[:, 0:ntok], in_=pv, func=AF.Copy), pk, ["raw"])
            dve(lambda e: e.scalar_tensor_tensor(out=t1[:, 0:ntok], in0=pv, scalar=scale, in1=tab[:, 0, 0:ntok],
                                                 op0=ALU.mult, op1=ALU.mult), pk + ["rot"], ["t1"])

        def rotary_post(banks, tbs, tab, scale, dst, dkey):
            ntok = tbs[-1][0] + tbs[-1][1]
            pv = pview(banks, ntok)
            pk = pkeys(banks, tbs)
            pe([("m", psf(b, n), permb[:], raw[:, t0:t0 + n], True, True) for (t0, n), b in zip(tbs, banks)],
               ["raw", "permb"], pk)
            dve(lambda e: e.scalar_tensor_tensor(out=t2[:, 0:ntok], in0=pv, scalar=scale, in1=tab[:, 1, 0:ntok],
                                                 op0=ALU.mult, op1=ALU.mult), pk + ["rot"], ["t2"])
            dve(lambda e: e.tensor_tensor(out=dst[:, 0:ntok], in0=t1[:, 0:ntok], in1=t2[:, 0:ntok], op=ALU.add),
                ["t1", "t2"], [("rotout", dkey)])

        def rotary(banks, tbs, tab, scale, dst, dkey):
            rotary_pre(banks, tbs, tab, scale)
            rotary_post(banks, tbs, tab, scale, dst, dkey)

        P.phase(1)
        P.dma("sp", rot[:, :, 0:TP], c_rotp, writes=["rot"])
        RAxn = RA
        xn_pre = RH[:, 0:8 * D].rearrange("p (t d) -> p t d", t=8)
        norm_tiles([xprev[128 * i:128 * i + 128, :] for i in range(8)], 0, hprevT, "hprev")
        P.phase(1.2)
        dve(lambda e: e.memset(Sf[:], 0.0), [], ["Sf"])
        kprevT = kT
        srcs = [xp[128 * i:128 * i + 128, :] for i in range(8)] + [xs[:, :]]
        main_i = [0]

        def main_norm_tiles(k):
            for _ in range(k):
                i = main_i[0]
                if i >= NT:
                    return
                main_i[0] += 1
                xt = xload[:, i % 2, :]
                P.dma("sp", xt, srcs[i], writes=[("xload", i % 2)])
                norm_tile(i, xt, [("xload", i % 2)], 0, hT, "hmain")

        P.phase(1.22)
        XB0, YB0 = [0, 1, 2], [3, 4, 5]
        cgp = stat[:, 8:8 + 16].rearrange("p (c k) -> p c k", c=8)
        for hp in range(4):
            slot, sk = wslot(W["pre_k"][hp])
            P.phase(1.25)
            hpr = [("hprev", i_) for i_ in range(8)]
            proj_b(slot, sk, 0, hprevT, hpr, TBP, XB0)
            proj_b(slot, sk, 1, hprevT, hpr, TBP, YB0)
            rotary(XB0, TBP, rot, 128.0 ** -0.5, kprevT[:, 0, :], ("k", 0))
            rotary_pre(YB0, TBP, rot, 128.0 ** -0.5)
            slot, sk = wslot(W["pre_v"][hp])
            for t in range(8):
                b = 6 + (t % 2)
                mms = [("m", psf(b, 256), hprevT[:, kc, 128 * t:128 * t + 128], slot[:, kc, :], kc == 0, kc == KC - 1)
                       for kc in range(KC)]
                pe(mms, [sk] + hpr, [("ps", b)])
                act(lambda e, b=b, t=t: e.activation(out=vtok[:, t, :], in_=psf(b, 256), func=AF.Copy),
                    [("ps", b)], [("vtok", t)])
            rotary_post(YB0, TBP, rot, 128.0 ** -0.5, kprevT[:, 1, :], ("k", 1))
            P.phase(1.6)
            main_norm_tiles(2)
            kdas = []
            for c in range(2):
                h = 2 * hp + c
                b = 6 + c
                pb = psb(b)
                pe([("t", pb[:, 128 * ch:128 * ch + 128], kprevT[:, c, 128 * ch:128 * ch + 128], identb[:]) for ch in range(8)],
                   [("rotout", ("k", c)), "identb"], [("ps", b)])
                kda = qT[:, c, 0:TP].rearrange("p (a d) -> p a d", a=8)
                kdas.append(kda)
                dve(lambda e, pb=pb, h=h, kda=kda: e.tensor_tensor(
                    out=kda, in0=pb[:, :].rearrange("p (a d) -> p a d", a=8),
                    in1=kdecp[:, h, :].unsqueeze(2).broadcast_to([128, 8, 128]), op=ALU.mult), [("ps", b), "kdecp"], [("kda", c)])

            def conv_tail(which, name):
                slot, sk = wslot(W[name][hp])
                for c in range(2):
                    ch = 2 * hp + c
                    b = 4 + (ch % 2)
                    mms = [("m", psf(b, 2), slot[:, kc, 128 * c:128 * c + 128], hprevT[:, kc, TP - 2:TP], kc == 0, kc == KC - 1)
                           for kc in range(KC)]
                    pe(mms, [sk] + [("hprev", i_) for i_ in range(8)], [("ps", b)])
                    if which == 0:
                        act(lambda e, b=b, ch=ch: e.activation(out=cgp[:, ch, :], in_=psf(b, 2), func=AF.Copy),
                            [("ps", b)], [("cgp", ch)])
                    else:
                        dve(lambda e, b=b, ch=ch: e.tensor_tensor(out=uprev[:, ch, :], in0=cgp[:, ch, :], in1=psf(b, 2),
                                                                  op=ALU.mult), [("ps", b), ("cgp", ch)], [("uprev", ch)])

            conv_tail(0, "pre_cg")
            for c in range(2):
                h = 2 * hp + c
                sb_bank = 2 + c
                kda = kdas[c]
                pe([("m", psf(sb_bank, 128), kda[:, ch, :], vtok[:, ch, 128 * c:128 * c + 128], ch == 0, ch == 7) for ch in range(8)],
                   [("kda", c)] + [("vtok", ch) for ch in range(8)], [("ps", sb_bank)])
                act(lambda e, h=h, sb_bank=sb_bank: e.activation(out=Sf[:, h, :], in_=psf(sb_bank, 128), func=AF.Copy),
                    [("ps", sb_bank)], [("Sf", h)])
            conv_tail(1, "pre_hin")
        main_norm_tiles(NT)
        slot_v0, sk_v0 = wslot(W["v"][0])
        for t in range(NT):
            b = 6 + (t % 2)
            mms = [("m", psf(b, 256), hT[:, kc, 128 * t:128 * t + 128], slot_v0[:, kc, :], kc == 0, kc == KC - 1)
                   for kc in range(KC)]
            pe(mms, [sk_v0] + [("hmain", i_) for i_ in range(NT)], [("ps", b)])
            act(lambda e, b=b, t=t: e.activation(out=vtok[:, t, :], in_=psf(b, 256), func=AF.Copy),
                [("ps", b)], [("vtok", t)])
        slot_k0, sk_k0 = wslot(W["k"][0])
        hmk = [("hmain", i_) for i_ in range(NT)]
        proj_b(slot_k0, sk_k0, 0, hT, hmk, TB, [0, 1, 2])
        proj_b(slot_k0, sk_k0, 1, hT, hmk, TB, [3, 4, 5])
        P.fence()

        P.phase(2)
        P.dma("sp", rot, c_rot, writes=["rot"])

        P.phase(3)
        def proj_v(slot, sk, extra=()):
            for t in range(NT):
                b = 6 + (t % 2)
                mms = [("m", psf(b, 256), hT[:, kc, 128 * t:128 * t + 128], slot[:, kc, :], kc == 0, kc == KC - 1)
                       for kc in range(KC)]
                pe(mms, [sk] + list(extra), [("ps", b)])
                act(lambda e, b=b, t=t: e.activation(out=vtok[:, t, :], in_=psf(b, 256), func=AF.Copy),
                    [("ps", b)], [("vtok", t)])

        def stage1_t(c, h):
            kk = [("rotout", ("k", c))]
            pb = psb(6)
            pe([("t", pb[:, 128 * ch:128 * ch + 128], kT[:, c, 128 * ch:128 * ch + 128], identb[:]) for ch in range(8)],
               kk + ["identb"], [("ps", 6)])
            act(lambda e, pb=pb, h=h: e.activation(out=kdta.rearrange("p c d -> p (c d)"), in_=pb[:, :], func=AF.Copy,
                                                   scale=kdec[:, h:h + 1]), [("ps", 6), "kdec"], ["kdta"])

        def stage1_a(c, h):
            pe([("m", psf(6 + ch // 4, 128, 128 * (ch % 4)), kdta[:, ch, :], vtok[:, ch, 128 * c:128 * c + 128], True, True) for ch in range(8)],
               ["kdta"] + [("vtok", ch) for ch in range(8)], [("ps", 6), ("ps", 7)])
            g128 = math.exp(128.0 * LG[h])
            act(lambda e: e.activation(out=Sprev[:, c, 0, :], in_=Sf[:, h, :], func=AF.Copy), [("Sf", h)], [("Sprev", c, 0)])
            for ch in range(8):
                ab = 6 + ch // 4
                src = Sf[:, h, :] if ch % 2 == 0 else Stmp[:, c, :]
                dst = Stmp[:, c, :] if ch % 2 == 0 else Sf[:, h, :]
                sk_, dk_ = (("Sf", h), ("Stmp", c)) if ch % 2 == 0 else (("Stmp", c), ("Sf", h))
                dve(lambda e, src=src, dst=dst, ab=ab, ch=ch: e.scalar_tensor_tensor(
                    out=dst, in0=src, scalar=g128, in1=psf(ab, 128, 128 * (ch % 4)), op0=ALU.mult, op1=ALU.add),
                    [("ps", ab), sk_], [dk_])
                if ch < 7:
                    act(lambda e, dst=dst, ch=ch: e.activation(out=Sprev[:, c, ch + 1, :], in_=dst, func=AF.Copy),
                        [dk_], [("Sprev", c, ch + 1)])
            P.dma("sp", srp[h], Sf[:, h, :], reads=[("Sf", h)], track_out=True)

        def stage3(c, h):
            oall = oo[:, c, :]
            qk = [("rotout", ("q", c))]
            kk = [("rotout", ("k", c))]
            for hf in range(2):
                P.dma("sp", ssf2[:, hf], sret[SH * hf:SH * hf + SH, h].rearrange("s d e -> d s e"), writes=[("ssf", hf)])
            pb = psb(3)
            pe([("t", pb[:, 0:128], kT[:, c, TP:T], identb[:])], kk + ["identb"], [("ps", 3)])
            act(lambda e, pb=pb: e.activation(out=ktkd, in_=pb[:, 0:128], func=AF.Copy, scale=rmd[:, h, NSEQ:NSEQ + 1]),
                [("ps", 3), "rmd"], ["ktkd"])
            dve(lambda e: e.tensor_tensor(
                out=vexp, in0=vtok[:, 8, 128 * c:128 * c + 128].unsqueeze(1).broadcast_to([128, NSEQ, 128]),
                in1=rmd[:, 0, 0:NSEQ].unsqueeze(2).broadcast_to([128, NSEQ, 128]), op=ALU.mult), [("vtok", 8), "rmd"], ["vexp"])
            dve(lambda e: e.tensor_tensor(
                out=qdall.rearrange("p (a l) -> p a l", l=128), in0=qT[:, c, 0:TP].rearrange("p (a l) -> p a l", l=128),
                in1=qdec[:, c, :].unsqueeze(1).broadcast_to([128, 8, 128]), op=ALU.mult), qk + ["qdec"], ["qdall"])
            dve(lambda e: e.tensor_tensor(out=qds_, in0=qT[:, c, TP:T], in1=qdecs[:, c, :], op=ALU.mult), qk + ["qdecs"], ["qds"])
            pe([("m", psf(6 + ch // 4, 128, 128 * (ch % 4)), kT[:, c, 128 * ch:128 * ch + 128], qT[:, c, 128 * ch:128 * ch + 128], True, True) for ch in range(8)]
               + [("m", psf(2, 128), kT[:, c, TP:T], qT[:, c, TP:T], True, True)], qk + kk, [("ps", 6), ("ps", 7), ("ps", 2)])
            for half in range(2):
                dve(lambda e, half=half: e.tensor_tensor(
                    out=stma[:, 4 * half:4 * half + 4, :], in0=psf(6 + half).rearrange("p (a l) -> p a l", l=128),
                    in1=dmask[:, c, :].unsqueeze(1).broadcast_to([128, 4, 128]), op=ALU.mult), [("ps", 6 + half), "dmask"], [("stma", half)])
            dve(lambda e: e.tensor_tensor(out=stma[:, 8, :], in0=psf(2, 128), in1=smask[:, c, :], op=ALU.mult), [("ps", 2), "smask"], [("stma", 2)])
            mms = []
            for ch in range(8):
                o_ = psf(ch // 4, 128, 128 * (ch % 4))
                mms.append(("m", o_, vtok[:, ch, 128 * c:128 * c + 128], stma[:, ch, :], True, False))
                mms.append(("m", o_, Sprev[:, c, ch, :], qdall[:, 128 * ch:128 * ch + 128], False, True))
            pe(mms, [("vtok", ch) for ch in range(8)] + [("stma", 0), ("stma", 1), "qdall"] + [("Sprev", c, ch) for ch in range(8)],
               [("ps", 0), ("ps", 1)])
            act(lambda e: e.activation(out=oall[:, 0:TP], in_=PS[:, 0:2, :].rearrange("p b n -> p (b n)"), func=AF.Copy),
                [("ps", 0), ("ps", 1)], [("oall", c, 0), ("xl2", 0)])
            pe([("m", psf(4, 128), vtok[:, 8, 128 * c:128 * c + 128], stma[:, 8, :], True, True)], [("vtok", 8), ("stma", 2)], [("ps", 4)])
            act(lambda e: e.activation(out=inn, in_=psf(4, 128), func=AF.Copy), [("ps", 4)], ["inn"])
            g8 = math.exp(8.0 * LG[h])
            for hf in range(2):
                s0 = SH * hf
                ssf = ssf2[:, hf]
                act(lambda e, ssf=ssf: e.activation(out=ssb.rearrange("p s e -> p (s e)"), in_=ssf.rearrange("p s e -> p (s e)"), func=AF.Copy),
                    [("ssf", hf)], ["ssb"])
                pe([("m", psf(5, 8, 8 * (s0 + s_)), ssb[:, s_, :], qds_[:, 8 * (s0 + s_):8 * (s0 + s_) + 8], True, True) for s_ in range(SH)],
                   ["ssb", "qds"], [("ps", 5)])
                for q4 in range(2):
                    b = 2 + q4
                    sq0 = s0 + 4 * q4
                    pe([("m", psf(b), ktkd, vexp[:, sq0:sq0 + 4, :].rearrange("p s e -> p (s e)"), True, True)], ["ktkd", "vexp"], [("ps", b)])
                    dve(lambda e, b=b, q4=q4, ssf=ssf: e.scalar_tensor_tensor(
                        out=ssf[:, 4 * q4:4 * q4 + 4, :].rearrange("p s e -> p (s e)"),
                        in0=ssf[:, 4 * q4:4 * q4 + 4, :].rearrange("p s e -> p (s e)"), scalar=g8, in1=psf(b),
                        op0=ALU.mult, op1=ALU.add), [("ps", b), ("ssf", hf)], [("ssf", hf)])
                P.dma("sp", srs[s0:s0 + SH, h].rearrange("s d e -> d s e"), ssf, reads=[("ssf", hf)], track_out=True)
            dve(lambda e: e.tensor_tensor(out=oall[:, TP:T], in0=psf(5, 128), in1=inn, op=ALU.add), [("ps", 5), "inn"], [("oall", c, 1), ("xl2", 0)])
            osq_, okeys = gn_bufs(c)
            dve(lambda e: e.tensor_tensor(out=osq_, in0=oall, in1=oall, op=ALU.mult), [("oall", c, 0), ("oall", c, 1)], okeys)

        def gn_bufs(c):
            if c == 0:
                return t2, ["t2"]
            return osq_alt, ["vexp", "ktkd", "kdta"]

        GXB, GYB = [0, 1, 2], [3, 4, 5]

        def gn_mm(c):
            oall = oo[:, c, :]
            osq_, okeys = gn_bufs(c)
            oa = [("oall", c, 0), ("oall", c, 1)]
            pe([("m", psf(b, n), onesf[:], oall[:, t0:t0 + n], True, True) for (t0, n), b in zip(TB, GXB)], oa + ["onesf"], pkeys(GXB, TB))
            pe([("m", psf(b, n), onesf[:], osq_[:, t0:t0 + n], True, True) for (t0, n), b in zip(TB, GYB)], okeys + ["onesf"], pkeys(GYB, TB))

        OAK = ["vexp", "ktkd", "kdta"]

        def gn_head(c):
            mv_, qv_ = pview(GXB, T), pview(GYB, T)
            act(lambda e: e.activation(out=t1, in_=mv_, func=AF.Copy), pkeys(GXB, TB), ["t1"])
            dve(lambda e: e.tensor_tensor(out=t2, in0=t1, in1=t1, op=ALU.mult), ["t1"], ["t2"])
            dve(lambda e: e.tensor_tensor(out=t2, in0=qv_, in1=t2, op=ALU.subtract), pkeys(GYB, TB) + ["t2"], ["t2"])

        def gn_head1_a():
            mv_ = pview(GXB, T)
            act(lambda e: e.activation(out=osq_alt, in_=mv_, func=AF.Copy), pkeys(GXB, TB), OAK)

        def gn_head1_b():
            qv_ = pview(GYB, T)
            dve(lambda e: e.tensor_tensor(out=t2, in0=osq_alt, in1=osq_alt, op=ALU.mult), OAK, ["t2"])
            dve(lambda e: e.tensor_tensor(out=t2, in0=qv_, in1=t2, op=ALU.subtract), pkeys(GYB, TB) + ["t2"], ["t2"])

        def gn_tail(c, h, mbuf=None, mkeys=None):
            oall = oo[:, c, :]
            oa = [("oall", c, 0), ("oall", c, 1)]
            mb = t1 if mbuf is None else mbuf
            mk_ = ["t1"] if mkeys is None else mkeys
            act(lambda e: e.activation(out=t2, in_=t2, func=AF.Ln, bias=GN_EPS, scale=1.0), ["t2"], ["t2"])
            act(lambda e: e.activation(out=t2, in_=t2, func=AF.Exp, scale=-0.5), ["t2"], ["t2"])
            dve(lambda e: e.tensor_tensor(out=t1, in0=oall, in1=mb, op=ALU.subtract), oa + mk_ + ["t1"], ["t1"])
            dve(lambda e: e.scalar_tensor_tensor(out=t1, in0=t1, scalar=gnT[:, h:h + 1], in1=t2, op0=ALU.mult, op1=ALU.mult),
                ["t1", "t2", "gnT"], ["t1"])
            dve(lambda e: e.tensor_tensor(out=catT[:, h, :], in0=t1, in1=sgT[:, c, :], op=ALU.mult), ["t1", ("sgT", c)], [("cat", h)])

        mkT = MKV[:, 0, :].rearrange("p (k t) -> p k t", k=KC)
        mvb = MKV[:, 1, :].rearrange("p (t d) -> p t d", t=2)
        csc = RA[:, 8 * T:16 * T]
        hmT = csc[:, 0:4096].rearrange("p (k t) -> p k t", k=KC)
        xn_mem = csc[:, 4096:8192].rearrange("p (t d) -> p t d", t=2)
        mst = csc[:, 8192:9216].bitcast(F32).rearrange("p (a c) -> p a c", a=2)
        xload2 = oo[:, :, :].rearrange("p a t -> p (a t)")[:, 0:D]

        def norm_mem():
            for i in range(2):
                xt = xload2
                P.dma("sp", xt, mem[128 * i:128 * i + 128, :], writes=[("xl2", 0)])
                act(lambda e, xt=xt, i=i: e.activation(out=xn_mem[:, i, :], in_=xt, func=AF.Square, accum_out=stat[:, 0:1]), [("xl2", 0)], [("xnm", i), "ss"])
                act(lambda e: e.activation(out=stat[:, 1:2], in_=stat[:, 0:1], func=AF.Sqrt, bias=RMS_EPS, scale=1.0 / D), ["ss"], ["ms"])
                dve(lambda e: e.reciprocal(out=stat[:, 2:3], in_=stat[:, 1:2]), ["ms"], ["rstd"])
                dve(lambda e, xt=xt, i=i: e.tensor_scalar(out=xn_mem[:, i, :], in0=xt, scalar1=stat[:, 2:3], scalar2=None,
                                                          op0=ALU.mult), [("xl2", 0), "rstd"], [("xnm", i)])
            for kc in range(KC):
                bank = 6 + (kc % 2)
                pb = psb(bank)
                pe([("t", pb[:, 128 * j:128 * j + 128], xn_mem[:, j, 128 * kc:128 * kc + 128], identb[:]) for j in range(2)],
                   [("xnm", 0), ("xnm", 1), "identb"], [("ps", bank)])
                act(lambda e, pb=pb, kc=kc: e.activation(out=hmT[:, kc, :], in_=pb[:, 0:256], func=AF.Copy, scale=gT[:, 1, kc:kc + 1]),
                    [("ps", bank), "gT"], [("hmT", kc)])

        hm_reads = [("hmT", kc) for kc in range(KC)]

        def memkv_block(which, i, blk):
            dst = mko if which == 0 else mvo
            slot, sk = wslot(blk)
            for t in range(2):
                b = 6 + t
                mms = [("m", psf(b, 256), hmT[:, kc, 128 * t:128 * t + 128], slot[:, kc, :], kc == 0, kc == KC - 1) for kc in range(KC)]
                pe(mms, [sk] + hm_reads, [("ps", b)])
                if t == 0:
                    j = i % 2
                    act(lambda e, b=b, j=j: e.activation(out=mst[:, j, :], in_=psf(b, 256), func=AF.Copy), [("ps", b)], [("mst", j)])
                    P.dma("sp", dst[:, 256 * i:256 * i + 256], mst[:, j, :], reads=[("mst", j)], track_out=True)
                if which == 1:
                    dve(lambda e, b=b, t=t, i=i: e.tensor_copy(out=mvb[:, t, 256 * i:256 * i + 256], in_=psf(b, 256)),
                        [("ps", b)], [("mvb", t, i)])
            if which == 0:
                for c in range(2):
                    b = 6 + c
                    mms = [("m", psf(b, 256), slot[:, kc, 128 * c:128 * c + 128], hmT[:, kc, :], kc == 0, kc == KC - 1) for kc in range(KC)]
                    pe(mms, [sk] + hm_reads, [("ps", b)])
                    act(lambda e, b=b, i=i, c=c: e.activation(out=mkT[:, 2 * i + c, :], in_=psf(b, 256), func=AF.Copy),
                        [("ps", b)], [("mkT", 2 * i + c)])

        XB_, YB_ = [0, 1, 2], [3, 4, 5]
        for hp in range(4):
            P.dma("sp", dmask, c_dmask[:, 2 * hp:2 * hp + 2, :], writes=["dmask"])
            P.dma("sp", smask, c_smask[:, 2 * hp:2 * hp + 2, :], writes=["smask"])
            P.dma("sp", qdec, c_qdec[:, 2 * hp:2 * hp + 2, :], writes=["qdec"])
            P.dma("sp", qdecs, c_qdecs[:, 2 * hp:2 * hp + 2, :], writes=["qdecs"])
            KS = 128.0 ** -0.5
            if hp > 0:
                slot, sk = wslot(W["k"][hp])
                proj_b(slot, sk, 0, hT, [], TB, XB_)
                proj_b(slot, sk, 1, hT, [], TB, YB_)
            rotary(XB_, TB, rot, KS, kT[:, 0, :], ("k", 0))
            rotary_pre(YB_, TB, rot, KS)
            if hp == 0:
                norm_mem()
            memkv_block(*W["mkv"][4 * hp + 0])
            rotary_post(YB_, TB, rot, KS, kT[:, 1, :], ("k", 1))
            slot, sk = wslot(W["q"][hp])
            stage1_t(0, 2 * hp)
            proj_b(slot, sk, 0, hT, [], TB, XB_)
            stage1_a(0, 2 * hp)
            proj_b(slot, sk, 1, hT, [], TB, YB_)
            rotary(XB_, TB, rot, 1.0, qT[:, 0, :], ("q", 0))
            rotary_pre(YB_, TB, rot, 1.0)
            stage1_t(1, 2 * hp + 1)
            memkv_block(*W["mkv"][4 * hp + 1])
            slot_g, sk_g = wslot(W["g"][hp])
            proj_b(slot_g, sk_g, 0, hT, [], TB, XB_)
            pv_ = pview(XB_, T)
            act(lambda e, pv_=pv_: e.activation(out=sgT[:, 0, :], in_=pv_, func=AF.Silu), pkeys(XB_, TB), [("sgT", 0)])
            stage1_a(1, 2 * hp + 1)
            rotary_post(YB_, TB, rot, 1.0, qT[:, 1, :], ("q", 1))
            proj_b(slot_g, sk_g, 1, hT, [], TB, YB_)
            pv_ = pview(YB_, T)
            act(lambda e, pv_=pv_: e.activation(out=sgT[:, 1, :], in_=pv_, func=AF.Silu), pkeys(YB_, TB), [("sgT", 1)])
            stage3(0, 2 * hp)
            memkv_block(*W["mkv"][4 * hp + 2])
            stage3(1, 2 * hp + 1)
            memkv_block(*W["mkv"][4 * hp + 3])
            gn_mm(0)
            gn_head(0)
            gn_mm(1)
            gn_head1_a()
            if hp < 3:
                slot, sk = wslot(W["v"][hp + 1])
                proj_v(slot, sk)
            gn_tail(0, 2 * hp)
            gn_head1_b()
            gn_tail(1, 2 * hp + 1, mbuf=osq_alt, mkeys=OAK)
        slot_cg0, sk_cg0 = wslot(W["cg"][0])
        cg0_banks = [next_banks(), next_banks()]
        for c in range(2):
            proj_b(slot_cg0, sk_cg0, c, hT, [], TB, cg0_banks[c])
        P.fence()

        P.phase(4)
        P.dma("sp", sc_in[0:32, :], sconv, writes=["sc_in"])

        def sconv_transposes():
            for chk in range(8):
                pe([("t", psf(6, 32), sc_in[0:32, 128 * chk:128 * chk + 128], identf[0:32, 0:32])], ["sc_in", "identf"], [("ps", 6)])
                act(lambda e, chk=chk: e.activation(out=scT[:, chk, :], in_=psf(6, 32), func=AF.Copy), [("ps", 6)], [("scT", chk)])

        for cp in range(4):
            if cp > 0:
                slot_cg, sk_cg = wslot(W["cg"][cp])
            cg_sb = [t1, t2]
            for c in range(2):
                if cp == 0:
                    banks = cg0_banks[c]
                else:
                    banks = next_banks()
                    proj_b(slot_cg, sk_cg, c, hT, [], TB, banks)
                for (t0, n), b in zip(TB, banks):
                    act(lambda e, b=b, n=n, t0=t0, c=c: e.activation(out=cg_sb[c][:, t0:t0 + n], in_=psf(b, n), func=AF.Copy),
                        [("ps", b)], [("cgsb", c, t0)])
            if cp == 0:
                sconv_transposes()
            slot_h, sk_h = wslot(W["hin"][cp])
            for c in range(2):
                chk = 2 * cp + c
                banks = next_banks()
                proj_b(slot_h, sk_h, c, hT, [], TB, banks)
                up = upad[:, c * (TP + 2):(c + 1) * (TP + 2)]
                dve(lambda e, up=up, chk=chk: e.tensor_copy(out=up[:, 0:2], in_=uprev[:, chk, :]), [("uprev", chk)], [("upad", c, -1)])
                for (t0, n), b in zip(TBP, banks):
                    dve(lambda e, b=b, n=n, t0=t0, c=c, up=up: e.tensor_tensor(
                        out=up[:, 2 + t0:2 + t0 + n], in0=cg_sb[c][:, t0:t0 + n], in1=psf(b, n), op=ALU.mult),
                        [("ps", b), ("cgsb", c, t0)], [("upad", c, t0)])
                ups = upads
                dve(lambda e, chk=chk: e.tensor_copy(out=upads[:, :, 0:2], in_=scT[:, chk, :].rearrange("p (s k) -> p s k", k=2)),
                    [("scT", chk)], [("upads", 0)])
                b = banks[2]
                dve(lambda e, b=b, c=c: e.tensor_tensor(
                    out=upads[:, :, 2:10], in0=cg_sb[c][:, TP:T].rearrange("p (s k) -> p s k", k=8),
                    in1=psf(b, 128).rearrange("p (s k) -> p s k", k=8), op=ALU.mult),
                    [("ps", b), ("cgsb", c, 1024)], [("upads", 1)])
                upr = [("upad", c, -1), ("upad", c, 0), ("upad", c, 512)]
                dve(lambda e, up=up, chk=chk: e.tensor_scalar(out=cacc[:, 0:TP], in0=up[:, 0:TP], scalar1=cwT[:, 0, chk:chk + 1],
                                                              scalar2=None, op0=ALU.mult), upr + ["cwT"], [("cacc", 0)])
                for k in (1, 2):
                    dve(lambda e, up=up, chk=chk, k=k: e.scalar_tensor_tensor(
                        out=cacc[:, 0:TP], in0=up[:, k:k + TP], scalar=cwT[:, k, chk:chk + 1], in1=cacc[:, 0:TP],
                        op0=ALU.mult, op1=ALU.add), upr + [("cacc", 0)], [("cacc", 0)])
                ca_s = cacc[:, TP:T].rearrange("p (s k) -> p s k", k=8)
                usr = [("upads", 0), ("upads", 1)]
                dve(lambda e, chk=chk: e.tensor_scalar(out=ca_s, in0=upads[:, :, 0:8], scalar1=cwT[:, 0, chk:chk + 1],
                                                       scalar2=None, op0=ALU.mult), usr + ["cwT"], [("cacc", 1)])
                for k in (1, 2):
                    dve(lambda e, chk=chk, k=k: e.scalar_tensor_tensor(
                        out=ca_s, in0=upads[:, :, k:k + 8], scalar=cwT[:, k, chk:chk + 1], in1=ca_s,
                        op0=ALU.mult, op1=ALU.add), usr + [("cacc", 1)], [("cacc", 1)])
                pe([("t", psf(7, 128)[0:2, :], up[:, TP:TP + 2], identf[:])], [("upad", c, 512), "identf"], [("ps", 7)])
                act(lambda e, chk=chk: e.activation(out=sc_out[0:2, 128 * chk:128 * chk + 128], in_=psf(7, 128)[0:2, :], func=AF.Copy),
                    [("ps", 7)], [("sc_out", chk)])
                dve(lambda e: e.tensor_copy(out=nbt.rearrange("p (s k) -> p s k", k=2), in_=upads[:, :, 8:10]), [("upads", 1)], ["nbt"])
                pe([("t", psf(7, 128)[0:32, :], nbt, identf[:])], ["nbt", "identf"], [("ps", 7)])
                act(lambda e, chk=chk: e.activation(out=sc_in[0:32, 128 * chk:128 * chk + 128], in_=psf(7, 128)[0:32, :], func=AF.Copy),
                    [("ps", 7)], [("sc_in2", chk), "sc_in"])
                dve(lambda e, c=c: e.tensor_copy(out=cg_sb[c][:, :], in_=cacc[:, :]), [("cacc", 0), ("cacc", 1), ("cgsb", c, 0), ("cgsb", c, 512), ("cgsb", c, 1024)],
                    [("cgsb", c, 0), ("cgsb", c, 512), ("cgsb", c, 1024)])
            slot_b, sk_b = wslot(W["bg"][cp])
            for c in range(2):
                chk = 2 * cp + c
                banks = next_banks()
                proj_b(slot_b, sk_b, c, hT, [], TB, banks)
                for (t0, n), b in zip(TB, banks):
                    dve(lambda e, b=b, n=n, t0=t0, c=c, chk=chk: e.tensor_tensor(
                        out=catT[:, 8 + chk, t0:t0 + n], in0=cg_sb[c][:, t0:t0 + n], in1=psf(b, n), op=ALU.mult),
                        [("ps", b), ("cgsb", c, t0)], [("cat", 8 + chk, t0)])
        P.dma("sp", scp, sc_out[0:2, :], reads=[("sc_out", i) for i in range(8)], track_out=True)
        P.dma("sp", scs, sc_in[0:32, :], reads=[("sc_in2", i) for i in range(8)], track_out=True)
        P.fence()

        P.phase(5)
        def out_proj(wkey, srcT, accumulate_first_from_dram, after_tile=None):
            if accumulate_first_from_dram is not None:
                for t in range(NT):
                    P.dma("sp", xres[:, t, :], accumulate_first_from_dram[t], writes=[("x", t, cb) for cb in range(4)])
            for cb in range(4):
                s0, k0 = wslot(W[wkey][cb][0])
                s1, k1 = wslot(W[wkey][cb][1])
                for t in range(NT):
                    b = t % 4
                    mms = []
                    for kc in range(KC):
                        s_ = s0 if kc < 8 else s1
                        mms.append(("m", psf(b), srcT[:, kc, 128 * t:128 * t + 128], s_[:, kc % 8, :], kc == 0, kc == KC - 1))
                    pe(mms, [k0, k1], [("ps", b)])
                    dve(lambda e, b=b, t=t, cb=cb: e.tensor_tensor(out=xres[:, t, 512 * cb:512 * cb + 512],
                                                                   in0=xres[:, t, 512 * cb:512 * cb + 512], in1=psf(b), op=ALU.add),
                        [("ps", b), ("x", t, cb)], [("x", t, cb)])
                    if cb == 3 and after_tile is not None and t >= 1:
                        after_tile(t - 1)
            if after_tile is not None:
                after_tile(NT - 1)

        def norm_hook(gidx):
            return lambda t: norm_tile(t, xres[:, t, :], [("x", t, cb) for cb in range(4)], gidx, hT, "hres")

        out_proj("mixo", catT, srcs, after_tile=norm_hook(2))
        P.fence()

        P.phase(6)
        qxT = RA[:, :].rearrange("p (k t) -> p k t", k=KC)
        for i in range(8):
            slot, sk = wslot(W["xq"][i])
            for c in range(2):
                banks = next_banks()
                proj_b(slot, sk, c, hT, [], TB, banks)
                for (t0, n), b in zip(TB, banks):
                    act(lambda e, b=b, n=n, t0=t0, i=i, c=c: e.activation(out=qxT[:, 2 * i + c, t0:t0 + n], in_=psf(b, n), func=AF.Copy),
                        [("ps", b)], [("qx", 2 * i + c, t0)])
        P.fence()

        P.phase(7)
        rh_off = [0]

        def rh(n, dt=BF16):
            nb = n if dt == BF16 else 2 * n
            o = rh_off[0]
            rh_off[0] += nb
            assert rh_off[0] <= KC * T, rh_off[0]
            v = RH[:, o:o + nb]
            if dt == F32:
                v = v.bitcast(F32)
            return v

        rh_xn0 = 0
        P.phase(8)
        oob2 = oo[:, :, :].rearrange("p a t -> p (a t)").bitcast(BF16)
        pn = oob2[:, 0:2048].rearrange("p (a t n) -> p a t n", a=2, t=4)
        sfb = Sf[:, :, :].rearrange("p h e -> p (h e)").bitcast(BF16)
        pT = sfb[:, 0:2048].rearrange("p (a c l) -> p a c l", a=2, c=2)
        rh_off[0] = 0
        kraw = rh(2 * 2 * D).rearrange("p (a t d) -> p a t d", a=2, t=2)
        vraw = rh(2 * 2 * D).rearrange("p (a t d) -> p a t d", a=2, t=2)
        kTs = rh(2 * 1024).rearrange("p (a x) -> p a x", a=2)
        for s_ in range(2):
            P.dma("pool", kraw[:, s_], ck[s_].rearrange("(t p) d -> p t d", p=128), writes=[("kraw", s_)])
            P.dma("pool", vraw[:, s_], cv[s_].rearrange("(t p) d -> p t d", p=128), writes=[("vraw", s_)])
        XS = 512.0 ** -0.5
        oT = qxT
        combos = [(hd, g4) for hd in range(4) for g4 in range(2)]

        def h2_a(i):
            hd, g4 = combos[i]
            b0 = 2 * (i % 2)
            mms = []
            for tt in range(4):
                t = 4 * g4 + tt
                o_ = psf(b0 + tt // 2, 256, 256 * (tt % 2))
                for dc in range(4):
                    mms.append(("m", o_, qxT[:, 4 * hd + dc, 128 * t:128 * t + 128], mkT[:, 4 * hd + dc, :], dc == 0, dc == 3))
            pe(mms, [], [("ps", b0), ("ps", b0 + 1)])

        def h2_b(i):
            hd, g4 = combos[i]
            pj = i % 2
            b0 = 2 * pj
            skeys = [("ps", b0), ("ps", b0 + 1)]
            so = 32 + 16 * pj
            sc4 = PS[:, b0:b0 + 2, :].rearrange("p b (t n) -> p (b t) n", t=2)
            dve(lambda e: e.reduce_max(out=stat[:, so:so + 4], in_=sc4, axis=AX.X), skeys, [("mx", pj)])
            dve(lambda e: e.tensor_scalar(out=stat[:, so + 4:so + 8], in0=stat[:, so:so + 4], scalar1=-XS, scalar2=None, op0=ALU.mult),
                [("mx", pj)], [("nmx", pj)])
            for tt in range(4):
                act(lambda e, tt=tt: e.activation(out=pn[:, pj, tt, :], in_=psf(b0 + tt // 2, 256, 256 * (tt % 2)), func=AF.Exp,
                                                  bias=stat[:, so + 4 + tt:so + 5 + tt], scale=XS, accum_out=stat[:, so + 8 + tt:so + 9 + tt]),
                    [("ps", b0 + tt // 2), ("nmx", pj)], [("pn", pj, tt), ("sm", pj, tt)])
            dve(lambda e: e.reciprocal(out=stat[:, so + 12:so + 16], in_=stat[:, so + 8:so + 12]), [("sm", pj, tt) for tt in range(4)], [("rs", pj)])
            dve(lambda e: e.tensor_tensor(out=pn[:, pj], in0=pn[:, pj], in1=stat[:, so + 12:so + 16].unsqueeze(2).broadcast_to([128, 4, 256]), op=ALU.mult),
                [("pn", pj, tt) for tt in range(4)] + [("rs", pj)], [("pn", pj, tt) for tt in range(4)])
            pb = psb(4 + pj)
            pe([("t", pb[:, 512 * nc_ + 128 * tt:512 * nc_ + 128 * tt + 128], pn[:, pj, tt, 128 * nc_:128 * nc_ + 128], identb[:]) for tt in range(4) for nc_ in range(2)],
               [("pn", pj, tt) for tt in range(4)] + ["identb"], [("ps", 4 + pj)])
            act(lambda e: e.activation(out=pT[:, pj].rearrange("p c l -> p (c l)"), in_=pb[:, :], func=AF.Copy), [("ps", 4 + pj)], [("pT", pj)])
            for e2 in range(2):
                mms = []
                for k_ in range(2):
                    ec = 2 * e2 + k_
                    for nc_ in range(2):
                        mms.append(("m", psf(6 + k_), mvb[:, nc_, 512 * hd + 128 * ec:512 * hd + 128 * ec + 128], pT[:, pj, nc_, :], nc_ == 0, nc_ == 1))
                pe(mms, [("pT", pj)], [("ps", 6), ("ps", 7)])
                act(lambda e, e2=e2: e.activation(out=oT[:, 4 * hd + 2 * e2:4 * hd + 2 * e2 + 2, 512 * g4:512 * g4 + 512],
                                                  in_=PS[:, 6:8, :], func=AF.Copy), [("ps", 6), ("ps", 7)], [("oT", hd, g4, e2)])

        h2_a(0)
        for i in range(len(combos)):
            if i + 1 < len(combos):
                h2_a(i + 1)
            h2_b(i)
        P.fence()

        P.phase(9)
        oob = oo[:, :, :].rearrange("p a t -> p (a t)").bitcast(BF16)
        ps_s = oob[:, 0:1024].rearrange("p (h n) -> p h n", h=4)
        pTs = oob[:, 1024:1088].rearrange("p (h c l) -> p h c l", h=4, c=2)
        scs_ = oo[:, :, :].rearrange("p a t -> p (a t)")[:, 1024:2048].rearrange("p (h n) -> p h n", h=4)
        def h3_load_k(s):
            a = s % 2
            P.dma("pool", kraw[:, a], ck[s].rearrange("(t p) d -> p t d", p=128), writes=[("kraw", a)])

        def h3_load_v(s):
            a = s % 2
            P.dma("pool", vraw[:, a], cv[s].rearrange("(t p) d -> p t d", p=128), writes=[("vraw", a)])

        def h3_load(s):
            h3_load_k(s)
            h3_load_v(s)

        def h3_a(s):
            a = s % 2
            sb0 = 2 + 2 * a

            def tr(hd):
                j = hd % 2
                pb = psb(j)
                mms = [("t", pb[:, 256 * dc + 128 * nt_:256 * dc + 128 * nt_ + 128], kraw[:, a, nt_, 512 * hd + 128 * dc:512 * hd + 128 * dc + 128], identb[:])
                       for dc in range(4) for nt_ in range(2)]
                pe(mms, [("kraw", a), "identb"], [("ps", j)])
                act(lambda e, pb=pb, j=j: e.activation(out=kTs[:, j, :], in_=pb[:, :], func=AF.Copy), [("ps", j)], [("kTs", j)])

            def sc_(hd):
                j = hd % 2
                sbk = sb0 + hd // 2
                so = 256 * (hd % 2)
                mms = [("m", psf(sbk, 256, so)[0:8, :], qxT[:, 4 * hd + dc, TP + 8 * s:TP + 8 * s + 8], kTs[:, j, 256 * dc:256 * dc + 256], dc == 0, dc == 3) for dc in range(4)]
                pe(mms, [("kTs", j)], [("ps", sbk)])

            tr(0); tr(1); sc_(0); tr(2); sc_(1); tr(3); sc_(2); sc_(3)

        def h3_b(s):
            a = s % 2
            sb0 = 2 + 2 * a
            skeys = [("ps", sb0), ("ps", sb0 + 1)]
            sc = PS[0:8, sb0:sb0 + 2, :].rearrange("p b n -> p (b n)")
            sc4 = PS[0:8, sb0:sb0 + 2, :].rearrange("p b (t n) -> p (b t) n", t=2)
            dve(lambda e: e.reduce_max(out=stat[0:8, 16:20], in_=sc4, axis=AX.X), skeys, ["mx"])
            dve(lambda e: e.tensor_tensor(out=scs_[0:8], in0=sc4, in1=stat[0:8, 16:20].unsqueeze(2).broadcast_to([8, 4, 256]), op=ALU.subtract),
                skeys + ["mx"], ["scs"])
            act(lambda e: e.activation(out=ps_s[0:8].rearrange("p h n -> p (h n)"), in_=scs_[0:8].rearrange("p h n -> p (h n)"), func=AF.Exp, scale=XS),
                ["scs"], ["pss"])
            dve(lambda e: e.reduce_sum(out=stat[0:8, 24:28], in_=ps_s[0:8], axis=AX.X), ["pss"], ["sm"])
            dve(lambda e: e.reciprocal(out=stat[0:8, 28:32], in_=stat[0:8, 24:28]), ["sm"], ["rs"])
            dve(lambda e: e.tensor_tensor(out=ps_s[0:8], in0=ps_s[0:8], in1=stat[0:8, 28:32].unsqueeze(2).broadcast_to([8, 4, 256]), op=ALU.mult),
                ["pss", "rs"], ["pss"])
            pb2 = psb(6)
            pe([("t", pb2[:, 16 * hd + 8 * nc_:16 * hd + 8 * nc_ + 8], ps_s[0:8, hd, 128 * nc_:128 * nc_ + 128], identb[0:8, 0:8]) for hd in range(4) for nc_ in range(2)],
               ["pss", "identb"], [("ps", 6)])
            act(lambda e: e.activation(out=pTs.rearrange("p h c l -> p (h c l)"), in_=pb2[:, 0:64], func=AF.Copy), [("ps", 6)], ["pTs"])
            mms = []
            for hd in range(4):
                for ec in range(4):
                    for nc_ in range(2):
                        mms.append(("m", psf(7, 8, 8 * (4 * hd + ec)), vraw[:, a, nc_, 512 * hd + 128 * ec:512 * hd + 128 * ec + 128], pTs[:, hd, nc_, :], nc_ == 0, nc_ == 1))
            pe(mms, [("vraw", a), "pTs"], [("ps", 7)])
            act(lambda e: e.activation(out=oT[:, :, TP + 8 * s:TP + 8 * s + 8], in_=psf(7, 128).rearrange("p (c l) -> p c l", c=16), func=AF.Copy),
                [("ps", 7)], [("oTs", s)])

        h3_a(0)
        for s in range(NSEQ):
            if s + 1 < NSEQ:
                h3_a(s + 1)
            if s + 2 < NSEQ:
                h3_load_k(s + 2)
            h3_b(s)
            if s + 2 < NSEQ:
                h3_load_v(s + 2)
        P.fence()

        P.phase(10)
        out_proj("xo", oT, None, after_tile=norm_hook(3))
        P.fence()

        P.phase(11)
        hid = RA[:, 0:2 * 4 * T].rearrange("p (a k t) -> p a k t", a=2, k=4)
        gsb = RA[:, 2 * 4 * T:2 * 4 * T + 2 * T].bitcast(F32)
        for g in range(NG):
            a = g % 2
            for j in range(2):
                sgj, kgj = wslot(W["gate"][g][j])
                suj, kuj = wslot(W["up"][g][j])
                for c in range(2):
                    kk_ = 2 * j + c
                    bg_ = next_banks()
                    hrd = ["hT"] if g == NG - 1 else []
                    proj_b(sgj, kgj, c, hT, hrd, TB, bg_)
                    for (t0, n), b in zip(TB, bg_):
                        act(lambda e, b=b, n=n, t0=t0: e.activation(out=gsb[:, t0:t0 + n], in_=psf(b, n), func=AF.Silu),
                            [("ps", b)], [("gsb", t0)])
                    bu_ = next_banks()
                    proj_b(suj, kuj, c, hT, hrd, TB, bu_)
                    for (t0, n), b in zip(TB, bu_):
                        dve(lambda e, b=b, n=n, t0=t0, a=a, kk_=kk_: e.tensor_tensor(out=hid[:, a, kk_, t0:t0 + n], in0=gsb[:, t0:t0 + n], in1=psf(b, n), op=ALU.mult),
                            [("ps", b), ("gsb", t0)], [("hid", a, kk_, t0)])
            hr = [("hid", a, k_, t0) for k_ in range(4) for t0, _ in TB]

            def down_tile(t, cb, sd, kd_, b):
                co = 512 * (cb % 2)
                mms = [("m", psf(b), hid[:, a, k_, 128 * t:128 * t + 128], sd[:, k_, co:co + 512], k_ == 0, k_ == 3) for k_ in range(4)]
                pe(mms, [kd_] + hr, [("ps", b)])
                dve(lambda e, b=b, t=t, cb=cb: e.tensor_tensor(out=xres[:, t, 512 * cb:512 * cb + 512],
                                                               in0=xres[:, t, 512 * cb:512 * cb + 512], in1=psf(b), op=ALU.add),
                    [("ps", b), ("x", t, cb)], [("x", t, cb)])

            if g < NG - 1:
                for cb in range(4):
                    if cb % 2 == 0:
                        sd, kd_ = wslot(W["down"][g][cb // 2])
                    for t in range(NT):
                        down_tile(t, cb, sd, kd_, 6 + (t % 2))
            else:
                sd0, kd0 = wslot(W["down"][g][0])
                sd1, kd1 = wslot(W["down"][g][1])
                for t in range(NT):
                    for cb in range(4):
                        down_tile(t, cb, (sd0, sd1)[cb // 2], (kd0, kd1)[cb // 2], 4 + cb)

        P.phase(12)
        gbc = RH[:, 0:2 * D].bitcast(F32)
        P.dma("sp", gbc, final_g.partition_broadcast(128), reads=[], writes=["gbc", "hT"])
        sq3 = RH[:, 2 * D:3 * D]
        yst = RH[:, 3 * D:7 * D].bitcast(F32).rearrange("p (a d) -> p a d", a=2)
        for t in range(NT):
            a = t % 2
            act(lambda e, t=t: e.activation(out=sq3, in_=xres[:, t, :], func=AF.Square, accum_out=stat[:, 0:1]), [("x", t, cb) for cb in range(4)] + ["gbc"], ["sq3", "ss"])
            act(lambda e: e.activation(out=stat[:, 1:2], in_=stat[:, 0:1], func=AF.Sqrt, bias=RMS_EPS, scale=1.0 / D), ["ss"], ["ms"])
            dve(lambda e: e.reciprocal(out=stat[:, 2:3], in_=stat[:, 1:2]), ["ms"], ["rstd"])
            dve(lambda e, t=t, a=a: e.scalar_tensor_tensor(out=yst[:, a, :], in0=xres[:, t, :], scalar=stat[:, 2:3], in1=gbc, op0=ALU.mult, op1=ALU.mult),
                ["rstd", "gbc"] + [("x", t, cb) for cb in range(4)], [("yst", a)])
            dst = yp[128 * t:128 * t + 128, :] if t < 8 else ys[:, :]
            P.dma("sp", dst, yst[:, a, :], reads=[("yst", a)], track_out=True)
        P.dead = False
        final_waits = []
        for k, v in P.dma_tokens:
            if v > P.seen["sp"].get(k, 0):
                P.seen["sp"][k] = v
                final_waits.append((k, v))
        P.q["sp"].append((final_waits, None, None))

        print("engine op counts", {e: (P.epoch[e], P.cnt[e]) for e in ENGS}, "n_sems", len(P.sems), flush=True)
        with nc.Block() as block:
            P.emit(block)
        print("n_sems", len(P.sems), flush=True)
    return nc


def _consts(half):
    c = {}
    c["c_ident"] = np.eye(128, dtype=np.float32)
    perm = np.zeros((128, 128), np.float32)
    for m in range(128):
        perm[(m + 64) % 128, m] = 1.0
    c["c_perm"] = perm
    inv = 10000.0 ** (-np.arange(64, dtype=np.float64) / 64.0)

    def rot_table(pos):
        ang = pos.astype(np.float64)[None, :] * inv[:, None]
        cos = np.cos(ang).astype(np.float32); sin = np.sin(ang).astype(np.float32)
        tab = np.zeros((128, 2, pos.shape[0]), np.float32)
        tab[:64, 0] = cos; tab[64:, 0] = cos
        tab[:64, 1] = -sin; tab[64:, 1] = sin
        return tab
    pos_main = np.concatenate([half * TP + np.arange(TP), np.tile(16384 + np.arange(8), NSEQ)])
    c["c_rot"] = rot_table(pos_main)
    c["c_rotp"] = rot_table(np.arange(TP))
    lg = np.array(LG, dtype=np.float64)
    m = np.arange(128)[:, None]; l = np.arange(128)[None, :]
    dm = np.zeros((128, H, 128)); sm = np.zeros((128, H, 128))
    for h in range(H):
        rel = l - m
        dm[:, h, :] = np.where(rel >= 0, np.exp(rel * lg[h]), 0.0)
        same = (m // 8) == (l // 8)
        sm[:, h, :] = np.where((rel >= 0) & same, np.exp(rel * lg[h]), 0.0)
    c["c_dmask"] = dm.astype(np.float32); c["c_smask"] = sm.astype(np.float32)
    qd = np.zeros((128, H, 128)); qds = np.zeros((128, H, 128))
    for h in range(H):
        qd[:, h, :] = np.exp((np.arange(128) + 1.0) * lg[h])[None, :]
        qds[:, h, :] = np.exp(((np.arange(128) % 8) + 1.0) * lg[h])[None, :]
    c["c_qdec"] = qd.astype(np.float32); c["c_qdecs"] = qds.astype(np.float32)
    kd = np.zeros((128, H)); kdp = np.zeros((128, H, 8)); rmd = np.zeros((128, H, NSEQ + 1))
    for h in range(H):
        kd[:, h] = np.exp((127.0 - np.arange(128)) * lg[h])
        for ch in range(8):
            kdp[:, h, ch] = np.exp((1023.0 - (128 * ch + np.arange(128))) * lg[h])
        for s in range(NSEQ):
            mm_ = np.arange(128)
            rmd[:, h, s] = np.where(mm_ // 8 == s, 1.0, 0.0)
        rmd[:, h, NSEQ] = np.exp((7.0 - (np.arange(128) % 8)) * lg[h])
    c["c_kdec"] = kd.astype(np.float32); c["c_kdecp"] = kdp.astype(np.float32); c["c_rmd"] = rmd.astype(np.float32)
    return c


def _core_inputs(c, shared, x_prompt, x_sample, mem_prompt, state_ret, state_conv, cache_mem_k, cache_mem_v):
    b, half = c // 2, c % 2
    m = dict(shared)
    m["xp"] = np.ascontiguousarray(x_prompt[b, half * TP:(half + 1) * TP])
    m["xprev"] = np.ascontiguousarray(x_prompt[b, 0:TP]) if half == 1 else np.zeros((TP, D), np.float32)
    m["xs"] = np.ascontiguousarray(x_sample[NSEQ * c:NSEQ * (c + 1)].reshape(TS, D))
    own = mem_prompt[b, 128 * half:128 * half + 128]
    oth = mem_prompt[b, 128 * (1 - half):128 * (1 - half) + 128]
    m["mem"] = np.ascontiguousarray(np.concatenate([own, oth], axis=0))
    m["sret"] = np.ascontiguousarray(state_ret[0, NSEQ * c:NSEQ * (c + 1)])
    m["sconv"] = np.ascontiguousarray(state_conv[0, NSEQ * c:NSEQ * (c + 1)].reshape(NSEQ * 2, 1024))
    m["ck"] = np.ascontiguousarray(cache_mem_k[0, NSEQ * c:NSEQ * (c + 1)].reshape(NSEQ, NMEM, D))
    m["cv"] = np.ascontiguousarray(cache_mem_v[0, NSEQ * c:NSEQ * (c + 1)].reshape(NSEQ, NMEM, D))
    m.update(_consts(half))
    return m


_NC_CACHE = {}


def kernel(x_prompt, x_sample, mem_prompt, state_ret, state_conv, cache_mem_k, cache_mem_v,
           ln_mix_g, w_in, conv_w, ret_gn_g, w_mix_out, ln_mem_g, ln_xa_g, w_xq, w_mk, w_mv,
           w_xo, ln_ffn_g, w_gate, w_up, w_down, final_g):
    f = lambda a: np.ascontiguousarray(np.asarray(a, dtype=np.float32))
    x_prompt, x_sample, mem_prompt = f(x_prompt), f(x_sample), f(mem_prompt)
    state_ret, state_conv, cache_mem_k, cache_mem_v = f(state_ret), f(state_conv), f(cache_mem_k), f(cache_mem_v)
    shared = {
        "ln_mix_g": f(ln_mix_g)[0], "w_in": f(w_in)[0], "conv_w": f(conv_w)[0], "ret_gn_g": f(ret_gn_g)[0],
        "w_mix_out": f(w_mix_out)[0], "ln_mem_g": f(ln_mem_g)[0], "ln_xa_g": f(ln_xa_g)[0], "w_xq": f(w_xq)[0],
        "w_mk": f(w_mk)[0], "w_mv": f(w_mv)[0], "w_xo": f(w_xo)[0], "ln_ffn_g": f(ln_ffn_g)[0],
        "w_gate": f(w_gate)[0], "w_up": f(w_up)[0], "w_down": f(w_down)[0], "final_g": f(final_g),
    }
    if "nc" not in _NC_CACHE:
        _NC_CACHE["nc"] = build_program()
    nc = _NC_CACHE["nc"]
    in_maps = [_core_inputs(c, shared, x_prompt, x_sample, mem_prompt, state_ret, state_conv, cache_mem_k, cache_mem_v)
               for c in range(8)]
    res = run_bass_kernel_spmd(nc, in_maps, core_ids=list(range(8)))
    R = res.results
    y_prompt = np.zeros((4, 2048, D), np.float32)
    y_sample = np.zeros((128, 8, D), np.float32)
    srp_o = np.zeros((1, 4, H, 128, 128), np.float32)
    srs_o = np.zeros((1, 128, H, 128, 128), np.float32)
    scp_o = np.zeros((1, 4, 2, 1024), np.float32)
    scs_o = np.zeros((1, 128, 2, 1024), np.float32)
    mk_o = np.zeros((1, 4, NMEM, 4, 512), np.float32)
    mv_o = np.zeros((1, 4, NMEM, 4, 512), np.float32)
    for c in range(8):
        b, half = c // 2, c % 2
        r = R[c]
        y_prompt[b, half * TP:(half + 1) * TP] = r["yp"]
        y_sample[NSEQ * c:NSEQ * (c + 1)] = r["ys"].reshape(NSEQ, 8, D)
        if half == 1:
            srp_o[0, b] = r["srp"]
            scp_o[0, b] = r["scp"]
        srs_o[0, NSEQ * c:NSEQ * (c + 1)] = r["srs"]
        scs_o[0, NSEQ * c:NSEQ * (c + 1)] = r["scs"].reshape(NSEQ, 2, 1024)
        mk_o[0, b, 128 * half:128 * half + 128] = r["mko"].reshape(128, 4, 512)
        mv_o[0, b, 128 * half:128 * half + 128] = r["mvo"].reshape(128, 4, 512)
    return (y_prompt, y_sample, srp_o, srs_o, scp_o, scs_o, mk_o, mv_o)
```

```python
import math
import os
from contextlib import ExitStack

import numpy as np
import concourse.bass as bass
import concourse.mybir as mybir
from concourse.bass_utils import run_bass_kernel_spmd

F32 = mybir.dt.float32
BF16 = mybir.dt.bfloat16
ALU = mybir.AluOpType
AF = mybir.ActivationFunctionType
AX = mybir.AxisListType

D = 2048
KC = 16
TP = 1024
TS = 128
T = TP + TS
NT = T // 128
NSEQ = 16
H = 8
DFF = 5632
NG = DFF // 512
NMEM = 256
RMS_EPS = 1e-6
GN_EPS = 1e-5
NSLOT = 4
TB = [(0, 512), (512, 512), (1024, 128)]
TBP = [(0, 512), (512, 512)]
LG = [math.log(1.0 - 2.0 ** (-5.0 - h)) for h in range(H)]

ENGS = ("pe", "act", "dve", "pool", "sp")
_PE_LABELS = [] if os.environ.get("KLABELS") else None


class Prog:
    def __init__(self, nc, sem_alloc):
        self.nc = nc
        self.sem_alloc = sem_alloc
        self.q = {e: [] for e in ENGS}
        self.cnt = {e: 0 for e in ENGS}
        self.epoch = {e: 0 for e in ENGS}
        self.sems = {}
        self.seen = {e: {} for e in ENGS}
        self.lastw = {}
        self.readers = {}
        self.nds = 12
        self.dq = {q: {"i": 0, "uses": [0] * self.nds} for q in ("sp", "pool", "act")}
        self.dma_tokens = []
        self.weight_tokens = set()
        self.EPOCH_MAX = 4000
        self.dead = False
        self.stop = float(os.environ.get("KSTOP", "99"))

    def phase(self, k):
        if k > self.stop:
            self.dead = True

    def sem(self, key):
        if key not in self.sems:
            self.sems[key] = self.sem_alloc("s_" + "_".join(str(k) for k in key))
        return self.sems[key]

    def _collect(self, eng, reads, writes):
        deps = {}

        def need(tok):
            if tok is None:
                return
            k, v = tok
            if k[0] == "e" and k[1] == "pe" and eng == "pe":
                return
            if v > deps.get(k, 0):
                deps[k] = v

        for t in reads:
            need(self.lastw.get(t))
        for t in writes:
            need(self.lastw.get(t))
            for r in self.readers.get(t, ()):
                need(r)
        waits = []
        for k, v in deps.items():
            if v > self.seen[eng].get(k, 0):
                self.seen[eng][k] = v
                waits.append((k, v))
        return waits

    def _record(self, tok, reads, writes):
        for t in reads:
            self.readers.setdefault(t, []).append(tok)
        for t in writes:
            self.lastw[t] = tok
            self.readers[t] = []

    def op(self, eng, fn, reads=(), writes=()):
        if self.dead:
            return None
        ps_reads = [t for t in reads if isinstance(t, tuple) and t[0] == "ps" and t not in writes]
        if ps_reads:
            writes = list(writes) + ps_reads
        waits = self._collect(eng, reads, writes)
        if self.cnt[eng] >= self.EPOCH_MAX:
            self.epoch[eng] += 1
            self.cnt[eng] = 0
        self.cnt[eng] += 1
        key = ("e", eng, self.epoch[eng])
        tok = (key, self.cnt[eng])
        self.q[eng].append((waits, fn, (key, 1)))
        self._record(tok, reads, writes)
        return tok

    def dma(self, queue, out, in_, reads=(), writes=(), track_out=False, slow=False):
        if self.dead:
            return None
        st = self.dq[queue]
        slot = st["i"] % self.nds
        st["i"] += 1
        key = ("d", queue, slot)
        waits = self._collect(queue, reads, writes)
        prev = st["uses"][slot] * 16
        if prev > self.seen[queue].get(key, 0):
            self.seen[queue][key] = prev
            waits.append((key, prev))
        st["uses"][slot] += 1
        tok = (key, st["uses"][slot] * 16)
        if slow:
            self.q[queue].append((waits, lambda e, o=out, i=in_: e.dma_start(out=o, in_=i, allow_slow_non_contiguous=True), (key, 16)))
        else:
            self.q[queue].append((waits, lambda e, o=out, i=in_: e.dma_start(out=o, in_=i), (key, 16)))
        self._record(tok, reads, writes)
        if track_out:
            self.dma_tokens.append(tok)
        return tok

    def fence(self):
        if self.dead:
            return
        toks = []
        for e in ENGS:
            if self.cnt[e] > 0:
                toks.append((("e", e, self.epoch[e]), self.cnt[e]))
        for q, st in self.dq.items():
            for s in range(self.nds):
                if st["uses"][s] > 0:
                    tk = (("d", q, s), st["uses"][s] * 16)
                    if tk in self.weight_tokens:
                        continue
                    toks.append(tk)
        for e in ENGS:
            waits = []
            for k, v in toks:
                if k[0] == "e" and k[1] == e and e == "pe":
                    continue
                if v > self.seen[e].get(k, 0):
                    self.seen[e][k] = v
                    waits.append((k, v))
            if waits:
                self.q[e].append((waits, None, None))
        keep_w = {k: v for k, v in self.lastw.items() if isinstance(k, tuple) and k[0] == "slot"}
        keep_r = {k: v for k, v in self.readers.items() if isinstance(k, tuple) and k[0] == "slot"}
        self.lastw = keep_w
        self.readers = keep_r

    def emit(self, block):
        nc = self.nc

        def run(eng_name):
            def body(e):
                for waits, fn, inc in self.q[eng_name]:
                    for k, v in waits:
                        e.wait_ge(self.sem(k), v)
                    if fn is not None:
                        ins = fn(e)
                        ins.then_inc(self.sem(inc[0]), inc[1])
            return body

        block.tensor(run("pe"))
        block.scalar(run("act"))
        block.vector(run("dve"))
        block.gpsimd(run("pool"))
        block.sync(run("sp"))


def build_program():
    nc = bass.Bass("TRN2", target_bir_lowering=False)

    def din(name, shape):
        return nc.dram_tensor(name, list(shape), F32, kind="ExternalInput").ap()

    def dout(name, shape):
        return nc.dram_tensor(name, list(shape), F32, kind="ExternalOutput").ap()

    xp = din("xp", [TP, D]); xprev = din("xprev", [TP, D]); xs = din("xs", [TS, D])
    mem = din("mem", [NMEM, D])
    sret = din("sret", [NSEQ, H, 128, 128]); sconv = din("sconv", [NSEQ * 2, 1024])
    ck = din("ck", [NSEQ, NMEM, D]); cv = din("cv", [NSEQ, NMEM, D])
    ln_mix_g = din("ln_mix_g", [D]); w_in = din("w_in", [D, 7168]); conv_w = din("conv_w", [3, 1024])
    ret_gn_g = din("ret_gn_g", [1024]); w_mix_out = din("w_mix_out", [D, D])
    ln_mem_g = din("ln_mem_g", [D]); ln_xa_g = din("ln_xa_g", [D])
    w_xq = din("w_xq", [D, D]); w_mk = din("w_mk", [D, D]); w_mv = din("w_mv", [D, D]); w_xo = din("w_xo", [D, D])
    ln_ffn_g = din("ln_ffn_g", [D]); w_gate = din("w_gate", [D, DFF]); w_up = din("w_up", [D, DFF])
    w_down = din("w_down", [DFF, D]); final_g = din("final_g", [D])
    c_ident = din("c_ident", [128, 128]); c_perm = din("c_perm", [128, 128])
    c_rot = din("c_rot", [128, 2, T]); c_rotp = din("c_rotp", [128, 2, TP])
    c_dmask = din("c_dmask", [128, H, 128]); c_smask = din("c_smask", [128, H, 128])
    c_qdec = din("c_qdec", [128, H, 128]); c_qdecs = din("c_qdecs", [128, H, 128])
    c_kdec = din("c_kdec", [128, H]); c_kdecp = din("c_kdecp", [128, H, 8]); c_rmd = din("c_rmd", [128, H, NSEQ + 1])

    yp = dout("yp", [TP, D]); ys = dout("ys", [TS, D])
    srp = dout("srp", [H, 128, 128]); srs = dout("srs", [NSEQ, H, 128, 128])
    scp = dout("scp", [2, 1024]); scs = dout("scs", [NSEQ * 2, 1024])
    mko = dout("mko", [128, D]); mvo = dout("mvo", [128, D])

    es = ExitStack()
    with es:
        def sb(name, shape, dt):
            return es.enter_context(nc.sbuf_tensor(name, list(shape), dt))

        RX = sb("RX", [128, NT * D], F32)
        RH = sb("RH", [128, KC * T], BF16)
        RA = sb("RA", [128, KC * T], BF16)
        SL = sb("SL", [128, NSLOT, 4096], BF16)
        identf = sb("identf", [128, 128], F32)
        identb = sb("identb", [128, 128], BF16)
        permb = sb("permb", [128, 128], BF16)
        onesf = sb("onesf", [128, 128], F32)
        gT = sb("gT", [128, 4, KC], F32)
        gnT = sb("gnT", [128, 8], F32)
        cwT = sb("cwT", [128, 3, 8], F32)
        stat = sb("stat", [128, 64], F32)
        Sf = sb("Sf", [128, H, 128], F32)
        MKV = sb("MKV", [128, 2, KC * 256], BF16)
        oo = sb("oo", [128, 2, T], F32)
        oall = oo[:, 0, :]
        osq = oo[:, 1, :]
        uprev = sb("uprev", [128, 8, 2], F32)
        PS = es.enter_context(nc.psum_tensor("PS", [128, 8, 512], F32))

        sem_list = []

        def sem_alloc(name):
            s = es.enter_context(nc.semaphore(name))
            sem_list.append(s)
            return s

        P = Prog(nc, sem_alloc)

        def psf(bank, n=512, off=0):
            return PS[:, bank, off:off + n]

        def psb(bank):
            return PS[:, bank, :].bitcast(BF16)

        def pe(mms, reads, writes):
            if _PE_LABELS is not None and not P.dead:
                import inspect
                fr = inspect.stack()[1]
                _PE_LABELS.append(("%s:%d" % (fr.function, fr.lineno), len(mms)))
            def fn(e, mms=mms):
                ins = None
                for m in mms:
                    if m[0] == "m":
                        ins = e.matmul(m[1], m[2], m[3], start=m[4], stop=m[5])
                    else:
                        ins = e.transpose(m[1], m[2], m[3])
                return ins
            return P.op("pe", fn, reads, writes)

        def dve(f, reads, writes):
            return P.op("dve", f, reads, writes)

        def act(f, reads, writes):
            return P.op("act", f, reads, writes)

        wv = lambda w: w.rearrange("(kc p) n -> p kc n", p=128)
        blocks = []

        def blk_std(w, col0):
            blocks.append((wv(w)[:, :, col0:col0 + 256], (16, 256)))
            return len(blocks) - 1

        def blk_a(w, kh, cb):
            blocks.append((wv(w)[:, 8 * kh:8 * kh + 8, 512 * cb:512 * cb + 512], (8, 512)))
            return len(blocks) - 1

        def blk_down(g, half):
            blocks.append((wv(w_down)[:, 4 * g:4 * g + 4, 1024 * half:1024 * half + 1024], (4, 1024)))
            return len(blocks) - 1

        CQ, CK, CV, CG, CBG, CCG, CHIN = 0, 1024, 2048, 3072, 4096, 5120, 6144
        W = {}
        W["pre_k"] = []; W["pre_v"] = []; W["pre_cg"] = []; W["pre_hin"] = []
        for i in range(4):
            W["pre_k"].append(blk_std(w_in, CK + 256 * i)); W["pre_v"].append(blk_std(w_in, CV + 256 * i))
            W["pre_cg"].append(blk_std(w_in, CCG + 256 * i)); W["pre_hin"].append(blk_std(w_in, CHIN + 256 * i))
        W["q"] = []; W["k"] = []; W["v"] = []; W["g"] = []; W["mkv"] = []
        for i in range(4):
            W["v"].append(blk_std(w_in, CV + 256 * i)); W["k"].append(blk_std(w_in, CK + 256 * i))
            W["mkv"].append((0, 2 * i, blk_std(w_mk, 256 * (2 * i))))
            W["q"].append(blk_std(w_in, CQ + 256 * i))
            W["mkv"].append((0, 2 * i + 1, blk_std(w_mk, 256 * (2 * i + 1))))
            W["g"].append(blk_std(w_in, CG + 256 * i))
            W["mkv"].append((1, 2 * i, blk_std(w_mv, 256 * (2 * i))))
            W["mkv"].append((1, 2 * i + 1, blk_std(w_mv, 256 * (2 * i + 1))))
        W["cg"] = []; W["hin"] = []; W["bg"] = []
        for i in range(4):
            W["cg"].append(blk_std(w_in, CCG + 256 * i)); W["hin"].append(blk_std(w_in, CHIN + 256 * i))
            W["bg"].append(blk_std(w_in, CBG + 256 * i))
        W["mixo"] = [[blk_a(w_mix_out, kh, cb) for kh in range(2)] for cb in range(4)]
        W["xq"] = [blk_std(w_xq, 256 * i) for i in range(8)]
        W["xo"] = [[blk_a(w_xo, kh, cb) for kh in range(2)] for cb in range(4)]
        W["gate"] = []; W["up"] = []; W["down"] = []
        for g in range(NG):
            gl, ul = [], []
            for j in range(2):
                gl.append(blk_std(w_gate, 512 * g + 256 * j)); ul.append(blk_std(w_up, 512 * g + 256 * j))
            W["gate"].append(gl); W["up"].append(ul)
            W["down"].append([blk_down(g, j) for j in range(2)])
        wstate = {"issued": 0, "cur": -1}

        def wslot(i):
            assert i == wstate["cur"] + 1 or i == wstate["cur"], (i, wstate)
            wstate["cur"] = i
            while wstate["issued"] < len(blocks) and wstate["issued"] <= i + NSLOT - 2:
                j = wstate["issued"]
                ap, shp = blocks[j]
                dst = SL[:, j % NSLOT, :].rearrange("p (a b) -> p a b", a=shp[0])
                wt = P.dma("pool", dst, ap, reads=([("hprev", 1)] if j < NSLOT - 1 else ()), writes=[("slot", j % NSLOT)])
                if wt is not None:
                    P.weight_tokens.add(wt)
                wstate["issued"] += 1
            shp = blocks[i][1]
            return SL[:, i % NSLOT, :].rearrange("p (a b) -> p a b", a=shp[0]), ("slot", i % NSLOT)

        P.dma("sp", identf[:], c_ident, writes=["identf"])
        P.dma("pool", identb[:], c_ident, writes=["identb"])
        P.dma("pool", permb[:], c_perm, writes=["permb"])
        gst = oo[:, :, :].rearrange("p a t -> p (a t)")[:, 0:128]
        for i, g in enumerate((ln_mix_g, ln_mem_g, ln_xa_g, ln_ffn_g)):
            P.dma("sp", gst[16 * i:16 * i + 16, :], g.rearrange("(kc p) -> kc p", p=128), writes=["gst"])
        P.dma("sp", gst[64:72, :], ret_gn_g.rearrange("(kc p) -> kc p", p=128), writes=["gst"])
        P.dma("sp", gst[72:96, :], conv_w.rearrange("k (kc p) -> (k kc) p", p=128), writes=["gst"])
        pe([("t", psf(0, 96), gst[0:96, :], identf[0:96, 0:96])], ["gst", "identf"], [("ps", 0)])
        act(lambda e: e.activation(out=gT[:, :, :].rearrange("p a k -> p (a k)"), in_=psf(0, 64), func=AF.Copy), [("ps", 0)], ["gT"])
        act(lambda e: e.activation(out=gnT[:], in_=psf(0, 8, 64), func=AF.Copy), [("ps", 0)], ["gnT"])
        act(lambda e: e.activation(out=cwT[:, :, :].rearrange("p k c -> p (k c)"), in_=psf(0, 24, 72), func=AF.Copy), [("ps", 0)], ["cwT"])
        dve(lambda e: e.memset(onesf[:], 1.0 / 128.0), [], ["onesf"])
        memx = MKV[:, :, :].rearrange("p a x -> p (a x)").bitcast(F32).rearrange("p (t d) -> p t d", t=2)

        rx_off = [0]

        def rx(n, dt=F32, shape=None):
            nf = n if dt == F32 else (n + 1) // 2
            o = rx_off[0]
            rx_off[0] += nf
            assert rx_off[0] <= NT * D, rx_off[0]
            v = RX[:, o:o + nf]
            if dt == BF16:
                v = v.bitcast(BF16)[:, 0:n]
            return v

        rot = rx(2 * T).rearrange("p (a t) -> p a t", a=2)
        dmask = rx(2 * 128).rearrange("p (h l) -> p h l", h=2)
        smask = rx(2 * 128).rearrange("p (h l) -> p h l", h=2)
        qdec = rx(2 * 128).rearrange("p (h l) -> p h l", h=2)
        qdecs = rx(2 * 128).rearrange("p (h l) -> p h l", h=2)
        kdec = rx(H)
        kdecp = rx(H * 8).rearrange("p (h c) -> p h c", h=H)
        rmd = rx(H * (NSEQ + 1)).rearrange("p (h s) -> p h s", h=H)
        qT = rx(2 * T, BF16).rearrange("p (a t) -> p a t", a=2)
        kT = rx(2 * T, BF16).rearrange("p (a t) -> p a t", a=2)
        sgT = rx(2 * T, BF16).rearrange("p (a t) -> p a t", a=2)
        vtok = rx(NT * 256, BF16).rearrange("p (t c) -> p t c", t=NT)
        raw = rx(T, BF16)
        t1 = rx(T)
        t2 = rx(T)
        stm = rx(2 * 128, BF16).rearrange("p (a l) -> p a l", a=2)
        kdt = rx(2 * 128, BF16).rearrange("p (a l) -> p a l", a=2)
        qds_ = rx(128, BF16)
        inn = rx(128)
        un0 = rx_off[0]
        xload = rx(2 * D).rearrange("p (a d) -> p a d", a=2)
        sqj = rx(D, BF16)
        enda = rx_off[0]
        rx_off[0] = un0
        SH = 8
        ssf2 = rx(2 * SH * 128).rearrange("p (a s e) -> p a s e", a=2, s=SH)
        ssb = rx(SH * 128, BF16).rearrange("p (s e) -> p s e", s=SH)
        Sprev = rx(2 * 8 * 128, BF16).rearrange("p (a c e) -> p a c e", a=2, c=8)
        Stmp = rx(2 * 128).rearrange("p (a e) -> p a e", a=2)
        qdall = rx(TP, BF16)
        vexp_off = rx_off[0]
        vexp = rx(NSEQ * 128, BF16).rearrange("p (s e) -> p s e", s=NSEQ)
        ktkd = rx(128, BF16)
        kdta = rx(8 * 128, BF16).rearrange("p (c d) -> p c d", c=8)
        osq_alt = RX[:, vexp_off:vexp_off + T]
        stma = rx(9 * 128, BF16).rearrange("p (c l) -> p c l", c=9)
        endb = rx_off[0]
        rx_off[0] = un0
        upad = rx(2 * (TP + 2))
        upads = rx(NSEQ * 10).rearrange("p (s k) -> p s k", s=NSEQ)
        cacc = rx(T)
        scT = rx(8 * 32).rearrange("p (c k) -> p c k", c=8)
        sc_in = rx(1024)
        sc_out = rx(1024)
        nbt = rx(32)
        endc = rx_off[0]
        rx_off[0] = max(enda, endb, endc)
        assert rx_off[0] <= NT * D, rx_off[0]

        P.dma("sp", kdec, c_kdec, writes=["kdec"])
        P.dma("sp", kdecp, c_kdecp, writes=["kdecp"])
        P.dma("sp", rmd, c_rmd, writes=["rmd"])

        hT = RH[:, :].rearrange("p (k t) -> p k t", k=KC)
        hprevT = RA[:, 0:KC * TP].rearrange("p (k t) -> p k t", k=KC)
        catT = RA[:, :].rearrange("p (k t) -> p k t", k=KC)
        xres = RX[:, :].rearrange("p (t d) -> p t d", t=NT)

        xn2 = oo[:, :, :].rearrange("p a t -> p (a t)").bitcast(BF16)[:, 0:2 * D].rearrange("p (a d) -> p a d", a=2)

        def norm_tile(i, xt, xr, gidx, dstT, dkey):
            j = i % 2
            so = 4 * j
            act(lambda e: e.activation(out=xn2[:, j, :], in_=xt, func=AF.Square, accum_out=stat[:, so:so + 1]), xr, [("xn", j), ("ss", j), "gst"])
            act(lambda e: e.activation(out=stat[:, so + 1:so + 2], in_=stat[:, so:so + 1], func=AF.Sqrt, bias=RMS_EPS, scale=1.0 / D), [("ss", j)], [("ms", j)])
            dve(lambda e: e.reciprocal(out=stat[:, so + 2:so + 3], in_=stat[:, so + 1:so + 2]), [("ms", j)], [("rstd", j)])
            if j == 0:
                act(lambda e: e.activation(out=xn2[:, j, :], in_=xt, func=AF.Copy, scale=stat[:, so + 2:so + 3]), xr + [("rstd", j)], [("xn", j)])
            else:
                dve(lambda e: e.tensor_scalar(out=xn2[:, j, :], in0=xt, scalar1=stat[:, so + 2:so + 3], scalar2=None, op0=ALU.mult),
                    xr + [("rstd", j)], [("xn", j)])
            b0 = 4 + 2 * j
            pbb = PS[:, b0:b0 + 2, :].rearrange("p b n -> p (b n)").bitcast(BF16)
            pe([("t", pbb[:, 128 * kc:128 * kc + 128], xn2[:, j, 128 * kc:128 * kc + 128], identb[:]) for kc in range(KC)],
               [("xn", j), "identb"], [("ps", b0), ("ps", b0 + 1)])
            dve(lambda e: e.tensor_tensor(out=dstT[:, :, 128 * i:128 * i + 128], in0=pbb.rearrange("p (k t) -> p k t", k=KC),
                                          in1=gT[:, gidx, :].unsqueeze(2).broadcast_to([128, KC, 128]), op=ALU.mult),
                [("ps", b0), ("ps", b0 + 1), "gT"], [(dkey, i)])

        def norm_tiles(src_aps, gidx, dstT, dkey):
            for i in range(len(src_aps)):
                xt = xload[:, i % 2, :]
                P.dma("sp", xt, src_aps[i], writes=[("xload", i % 2)])
                norm_tile(i, xt, [("xload", i % 2)], gidx, dstT, dkey)

        def proj_b(slot, slot_key, c, src, src_reads, tbs, banks):
            mms = []
            for (t0, n), b in zip(tbs, banks):
                pass
            for kc in range(KC):
                for (t0, n), b in zip(tbs, banks):
                    mms.append(("m", psf(b, n), slot[:, kc, 128 * c:128 * c + 128], src[:, kc, t0:t0 + n],
                                kc == 0, kc == KC - 1))
            return pe(mms, [slot_key] + src_reads, [("ps", b) for b in banks[:len(tbs)]])

        bank_sets = [[0, 1, 2], [3, 4, 5]]
        bs_i = [0]

        def next_banks():
            b = bank_sets[bs_i[0] % 2]
            bs_i[0] += 1
            return b

        def pview(banks, ntok):
            b0 = banks[0]
            return PS[:, b0:b0 + 3, :].rearrange("p b n -> p (b n)")[:, 0:ntok]

        def pkeys(banks, tbs):
            return [("ps", b) for b in banks[:len(tbs)]]

        def rotary_pre(banks, tbs, tab, scale):
            ntok = tbs[-1][0] + tbs[-1][1]
            pv = pview(banks, ntok)
            pk = pkeys(banks, tbs)
            act(lambda e: e.activation(out=raw[:, 0:ntok], in_=pv, func=AF.Copy), pk, ["raw"])
            dve(lambda e: e.scalar_tensor_tensor(out=t1[:, 0:ntok], in0=pv, scalar=scale, in1=tab[:, 0, 0:ntok],
                                                 op0=ALU.mult, op1=ALU.mult), pk + ["rot"], ["t1"])

        def rotary_post(banks, tbs, tab, scale, dst, dkey):
            ntok = tbs[-1][0] + tbs[-1][1]
            pv = pview(banks, ntok)
            pk = pkeys(banks, tbs)
            pe([("m", psf(b, n), permb[:], raw[:, t0:t0 + n], True, True) for (t0, n), b in zip(tbs, banks)],
               ["raw", "permb"], pk)
            dve(lambda e: e.scalar_tensor_tensor(out=t2[:, 0:ntok], in0=pv, scalar=scale, in1=tab[:, 1, 0:ntok],
                                                 op0=ALU.mult, op1=ALU.mult), pk + ["rot"], ["t2"])
            dve(lambda e: e.tensor_tensor(out=dst[:, 0:ntok], in0=t1[:, 0:ntok], in1=t2[:, 0:ntok], op=ALU.add),
                ["t1", "t2"], [("rotout", dkey)])

        def rotary(banks, tbs, tab, scale, dst, dkey):
            rotary_pre(banks, tbs, tab, scale)
            rotary_post(banks, tbs, tab, scale, dst, dkey)

        P.phase(1)
        P.dma("sp", rot[:, :, 0:TP], c_rotp, writes=["rot"])
        RAxn = RA
        xn_pre = RH[:, 0:8 * D].rearrange("p (t d) -> p t d", t=8)
        norm_tiles([xprev[128 * i:128 * i + 128, :] for i in range(8)], 0, hprevT, "hprev")
        for i in range(2):
            P.dma("sp", memx[:, i, :], mem[128 * i:128 * i + 128, :], writes=[("xl2", i)])
        P.phase(1.2)
        dve(lambda e: e.memset(Sf[:], 0.0), [], ["Sf"])
        kprevT = kT
        srcs = [xp[128 * i:128 * i + 128, :] for i in range(8)] + [xs[:, :]]
        main_i = [0]

        def main_norm_tiles(k):
            for _ in range(k):
                i = main_i[0]
                if i >= NT:
                    return
                main_i[0] += 1
                xt = xload[:, i % 2, :]
                P.dma("sp", xt, srcs[i], writes=[("xload", i % 2)])
                norm_tile(i, xt, [("xload", i % 2)], 0, hT, "hmain")

        P.phase(1.22)
        XB0, YB0 = [0, 1, 2], [3, 4, 5]
        cgp = stat[:, 8:8 + 16].rearrange("p (c k) -> p c k", c=8)
        for hp in range(4):
            slot, sk = wslot(W["pre_k"][hp])
            P.phase(1.25)
            hpr = [("hprev", i_) for i_ in range(8)]
            proj_b(slot, sk, 0, hprevT, hpr, TBP, XB0)
            proj_b(slot, sk, 1, hprevT, hpr, TBP, YB0)
            rotary(XB0, TBP, rot, 128.0 ** -0.5, kprevT[:, 0, :], ("k", 0))
            rotary_pre(YB0, TBP, rot, 128.0 ** -0.5)
            slot, sk = wslot(W["pre_v"][hp])
            for t in range(8):
                b = 6 + (t % 2)
                mms = [("m", psf(b, 256), hprevT[:, kc, 128 * t:128 * t + 128], slot[:, kc, :], kc == 0, kc == KC - 1)
                       for kc in range(KC)]
                pe(mms, [sk] + hpr, [("ps", b)])
                act(lambda e, b=b, t=t: e.activation(out=vtok[:, t, :], in_=psf(b, 256), func=AF.Copy),
                    [("ps", b)], [("vtok", t)])
            rotary_post(YB0, TBP, rot, 128.0 ** -0.5, kprevT[:, 1, :], ("k", 1))
            P.phase(1.6)
            main_norm_tiles(2)
            kdas = []
            for c in range(2):
                h = 2 * hp + c
                b = 6 + c
                pb = psb(b)
                pe([("t", pb[:, 128 * ch:128 * ch + 128], kprevT[:, c, 128 * ch:128 * ch + 128], identb[:]) for ch in range(8)],
                   [("rotout", ("k", c)), "identb"], [("ps", b)])
                kda = qT[:, c, 0:TP].rearrange("p (a d) -> p a d", a=8)
                kdas.append(kda)
                dve(lambda e, pb=pb, h=h, kda=kda: e.tensor_tensor(
                    out=kda, in0=pb[:, :].rearrange("p (a d) -> p a d", a=8),
                    in1=kdecp[:, h, :].unsqueeze(2).broadcast_to([128, 8, 128]), op=ALU.mult), [("ps", b), "kdecp"], [("kda", c)])

            def conv_tail(which, name):
                slot, sk = wslot(W[name][hp])
                for c in range(2):
                    ch = 2 * hp + c
                    b = 4 + (ch % 2)
                    mms = [("m", psf(b, 2), slot[:, kc, 128 * c:128 * c + 128], hprevT[:, kc, TP - 2:TP], kc == 0, kc == KC - 1)
                           for kc in range(KC)]
                    pe(mms, [sk] + [("hprev", i_) for i_ in range(8)], [("ps", b)])
                    if which == 0:
                        act(lambda e, b=b, ch=ch: e.activation(out=cgp[:, ch, :], in_=psf(b, 2), func=AF.Copy),
                            [("ps", b)], [("cgp", ch)])
                    else:
                        dve(lambda e, b=b, ch=ch: e.tensor_tensor(out=uprev[:, ch, :], in0=cgp[:, ch, :], in1=psf(b, 2),
                                                                  op=ALU.mult), [("ps", b), ("cgp", ch)], [("uprev", ch)])

            conv_tail(0, "pre_cg")
            for c in range(2):
                h = 2 * hp + c
                sb_bank = 2 + c
                kda = kdas[c]
                pe([("m", psf(sb_bank, 128), kda[:, ch, :], vtok[:, ch, 128 * c:128 * c + 128], ch == 0, ch == 7) for ch in range(8)],
                   [("kda", c)] + [("vtok", ch) for ch in range(8)], [("ps", sb_bank)])
                act(lambda e, h=h, sb_bank=sb_bank: e.activation(out=Sf[:, h, :], in_=psf(sb_bank, 128), func=AF.Copy),
                    [("ps", sb_bank)], [("Sf", h)])
            conv_tail(1, "pre_hin")
        main_norm_tiles(NT)
        slot_v0, sk_v0 = wslot(W["v"][0])
        for t in range(NT):
            b = 6 + (t % 2)
            mms = [("m", psf(b, 256), hT[:, kc, 128 * t:128 * t + 128], slot_v0[:, kc, :], kc == 0, kc == KC - 1)
                   for kc in range(KC)]
            pe(mms, [sk_v0] + [("hmain", i_) for i_ in range(NT)], [("ps", b)])
            act(lambda e, b=b, t=t: e.activation(out=vtok[:, t, :], in_=psf(b, 256), func=AF.Copy),
                [("ps", b)], [("vtok", t)])
        slot_k0, sk_k0 = wslot(W["k"][0])
        hmk = [("hmain", i_) for i_ in range(NT)]
        proj_b(slot_k0, sk_k0, 0, hT, hmk, TB, [0, 1, 2])
        proj_b(slot_k0, sk_k0, 1, hT, hmk, TB, [3, 4, 5])
        P.fence()

        P.phase(2)
        P.dma("sp", rot, c_rot, writes=["rot"])

        P.phase(3)
        def proj_v(slot, sk, extra=()):
            for t in range(NT):
                b = 6 + (t % 2)
                mms = [("m", psf(b, 256), hT[:, kc, 128 * t:128 * t + 128], slot[:, kc, :], kc == 0, kc == KC - 1)
                       for kc in range(KC)]
                pe(mms, [sk] + list(extra), [("ps", b)])
                act(lambda e, b=b, t=t: e.activation(out=vtok[:, t, :], in_=psf(b, 256), func=AF.Copy),
                    [("ps", b)], [("vtok", t)])

        def stage1_t(c, h):
            kk = [("rotout", ("k", c))]
            pb = psb(6)
            pe([("t", pb[:, 128 * ch:128 * ch + 128], kT[:, c, 128 * ch:128 * ch + 128], identb[:]) for ch in range(8)],
               kk + ["identb"], [("ps", 6)])
            act(lambda e, pb=pb, h=h: e.activation(out=kdta.rearrange("p c d -> p (c d)"), in_=pb[:, :], func=AF.Copy,
                                                   scale=kdec[:, h:h + 1]), [("ps", 6), "kdec"], ["kdta"])

        def stage1_a(c, h):
            pe([("m", psf(6 + ch // 4, 128, 128 * (ch % 4)), kdta[:, ch, :], vtok[:, ch, 128 * c:128 * c + 128], True, True) for ch in range(8)],
               ["kdta"] + [("vtok", ch) for ch in range(8)], [("ps", 6), ("ps", 7)])
            g128 = math.exp(128.0 * LG[h])
            act(lambda e: e.activation(out=Sprev[:, c, 0, :], in_=Sf[:, h, :], func=AF.Copy), [("Sf", h)], [("Sprev", c, 0)])
            for ch in range(8):
                ab = 6 + ch // 4
                src = Sf[:, h, :] if ch % 2 == 0 else Stmp[:, c, :]
                dst = Stmp[:, c, :] if ch % 2 == 0 else Sf[:, h, :]
                sk_, dk_ = (("Sf", h), ("Stmp", c)) if ch % 2 == 0 else (("Stmp", c), ("Sf", h))
                dve(lambda e, src=src, dst=dst, ab=ab, ch=ch: e.scalar_tensor_tensor(
                    out=dst, in0=src, scalar=g128, in1=psf(ab, 128, 128 * (ch % 4)), op0=ALU.mult, op1=ALU.add),
                    [("ps", ab), sk_], [dk_])
                if ch < 7:
                    act(lambda e, dst=dst, ch=ch: e.activation(out=Sprev[:, c, ch + 1, :], in_=dst, func=AF.Copy),
                        [dk_], [("Sprev", c, ch + 1)])
            P.dma("sp", srp[h], Sf[:, h, :], reads=[("Sf", h)], track_out=True)

        def stage3(c, h):
            oall = oo[:, c, :]
            qk = [("rotout", ("q", c))]
            kk = [("rotout", ("k", c))]
            for hf in range(2):
                P.dma("sp", ssf2[:, hf], sret[SH * hf:SH * hf + SH, h].rearrange("s d e -> d s e"), writes=[("ssf", hf)])
            pb = psb(3)
            pe([("t", pb[:, 0:128], kT[:, c, TP:T], identb[:])], kk + ["identb"], [("ps", 3)])
            act(lambda e, pb=pb: e.activation(out=ktkd, in_=pb[:, 0:128], func=AF.Copy, scale=rmd[:, h, NSEQ:NSEQ + 1]),
                [("ps", 3), "rmd"], ["ktkd"])
            dve(lambda e: e.tensor_tensor(
                out=vexp, in0=vtok[:, 8, 128 * c:128 * c + 128].unsqueeze(1).broadcast_to([128, NSEQ, 128]),
                in1=rmd[:, 0, 0:NSEQ].unsqueeze(2).broadcast_to([128, NSEQ, 128]), op=ALU.mult), [("vtok", 8), "rmd"], ["vexp"])
            dve(lambda e: e.tensor_tensor(
                out=qdall.rearrange("p (a l) -> p a l", l=128), in0=qT[:, c, 0:TP].rearrange("p (a l) -> p a l", l=128),
                in1=qdec[:, c, :].unsqueeze(1).broadcast_to([128, 8, 128]), op=ALU.mult), qk + ["qdec"], ["qdall"])
            dve(lambda e: e.tensor_tensor(out=qds_, in0=qT[:, c, TP:T], in1=qdecs[:, c, :], op=ALU.mult), qk + ["qdecs"], ["qds"])
            pe([("m", psf(6 + ch // 4, 128, 128 * (ch % 4)), kT[:, c, 128 * ch:128 * ch + 128], qT[:, c, 128 * ch:128 * ch + 128], True, True) for ch in range(8)]
               + [("m", psf(2, 128), kT[:, c, TP:T], qT[:, c, TP:T], True, True)], qk + kk, [("ps", 6), ("ps", 7), ("ps", 2)])
            for half in range(2):
                dve(lambda e, half=half: e.tensor_tensor(
                    out=stma[:, 4 * half:4 * half + 4, :], in0=psf(6 + half).rearrange("p (a l) -> p a l", l=128),
                    in1=dmask[:, c, :].unsqueeze(1).broadcast_to([128, 4, 128]), op=ALU.mult), [("ps", 6 + half), "dmask"], [("stma", half)])
            dve(lambda e: e.tensor_tensor(out=stma[:, 8, :], in0=psf(2, 128), in1=smask[:, c, :], op=ALU.mult), [("ps", 2), "smask"], [("stma", 2)])
            mms = []
            for ch in range(8):
                o_ = psf(ch // 4, 128, 128 * (ch % 4))
                mms.append(("m", o_, vtok[:, ch, 128 * c:128 * c + 128], stma[:, ch, :], True, False))
                mms.append(("m", o_, Sprev[:, c, ch, :], qdall[:, 128 * ch:128 * ch + 128], False, True))
            pe(mms, [("vtok", ch) for ch in range(8)] + [("stma", 0), ("stma", 1), "qdall"] + [("Sprev", c, ch) for ch in range(8)],
               [("ps", 0), ("ps", 1)])
            act(lambda e: e.activation(out=oall[:, 0:TP], in_=PS[:, 0:2, :].rearrange("p b n -> p (b n)"), func=AF.Copy),
                [("ps", 0), ("ps", 1)], [("oall", c, 0), ("xl2", 0)])
            pe([("m", psf(4, 128), vtok[:, 8, 128 * c:128 * c + 128], stma[:, 8, :], True, True)], [("vtok", 8), ("stma", 2)], [("ps", 4)])
            act(lambda e: e.activation(out=inn, in_=psf(4, 128), func=AF.Copy), [("ps", 4)], ["inn"])
            g8 = math.exp(8.0 * LG[h])
            for hf in range(2):
                s0 = SH * hf
                ssf = ssf2[:, hf]
                act(lambda e, ssf=ssf: e.activation(out=ssb.rearrange("p s e -> p (s e)"), in_=ssf.rearrange("p s e -> p (s e)"), func=AF.Copy),
                    [("ssf", hf)], ["ssb"])
                pe([("m", psf(5, 8, 8 * (s0 + s_)), ssb[:, s_, :], qds_[:, 8 * (s0 + s_):8 * (s0 + s_) + 8], True, True) for s_ in range(SH)],
                   ["ssb", "qds"], [("ps", 5)])
                for q4 in range(2):
                    b = 2 + q4
                    sq0 = s0 + 4 * q4
                    pe([("m", psf(b), ktkd, vexp[:, sq0:sq0 + 4, :].rearrange("p s e -> p (s e)"), True, True)], ["ktkd", "vexp"], [("ps", b)])
                    dve(lambda e, b=b, q4=q4, ssf=ssf: e.scalar_tensor_tensor(
                        out=ssf[:, 4 * q4:4 * q4 + 4, :].rearrange("p s e -> p (s e)"),
                        in0=ssf[:, 4 * q4:4 * q4 + 4, :].rearrange("p s e -> p (s e)"), scalar=g8, in1=psf(b),
                        op0=ALU.mult, op1=ALU.add), [("ps", b), ("ssf", hf)], [("ssf", hf)])
                P.dma("sp", srs[s0:s0 + SH, h].rearrange("s d e -> d s e"), ssf, reads=[("ssf", hf)], track_out=True)
            dve(lambda e: e.tensor_tensor(out=oall[:, TP:T], in0=psf(5, 128), in1=inn, op=ALU.add), [("ps", 5), "inn"], [("oall", c, 1), ("xl2", 0)])
            osq_, okeys = gn_bufs(c)
            dve(lambda e: e.tensor_tensor(out=osq_, in0=oall, in1=oall, op=ALU.mult), [("oall", c, 0), ("oall", c, 1)], okeys)

        def gn_bufs(c):
            if c == 0:
                return t2, ["t2"]
            return osq_alt, ["vexp", "ktkd", "kdta"]

        GXB, GYB = [0, 1, 2], [3, 4, 5]

        def gn_mm(c):
            oall = oo[:, c, :]
            osq_, okeys = gn_bufs(c)
            oa = [("oall", c, 0), ("oall", c, 1)]
            pe([("m", psf(b, n), onesf[:], oall[:, t0:t0 + n], True, True) for (t0, n), b in zip(TB, GXB)], oa + ["onesf"], pkeys(GXB, TB))
            pe([("m", psf(b, n), onesf[:], osq_[:, t0:t0 + n], True, True) for (t0, n), b in zip(TB, GYB)], okeys + ["onesf"], pkeys(GYB, TB))

        OAK = ["vexp", "ktkd", "kdta"]

        def gn_head(c):
            mv_, qv_ = pview(GXB, T), pview(GYB, T)
            act(lambda e: e.activation(out=t1, in_=mv_, func=AF.Copy), pkeys(GXB, TB), ["t1"])
            dve(lambda e: e.tensor_tensor(out=t2, in0=t1, in1=t1, op=ALU.mult), ["t1"], ["t2"])
            dve(lambda e: e.tensor_tensor(out=t2, in0=qv_, in1=t2, op=ALU.subtract), pkeys(GYB, TB) + ["t2"], ["t2"])

        def gn_head1_a():
            mv_ = pview(GXB, T)
            act(lambda e: e.activation(out=osq_alt, in_=mv_, func=AF.Copy), pkeys(GXB, TB), OAK)

        def gn_head1_b():
            qv_ = pview(GYB, T)
            dve(lambda e: e.tensor_tensor(out=t2, in0=osq_alt, in1=osq_alt, op=ALU.mult), OAK, ["t2"])
            dve(lambda e: e.tensor_tensor(out=t2, in0=qv_, in1=t2, op=ALU.subtract), pkeys(GYB, TB) + ["t2"], ["t2"])

        def gn_tail(c, h, mbuf=None, mkeys=None):
            oall = oo[:, c, :]
            oa = [("oall", c, 0), ("oall", c, 1)]
            mb = t1 if mbuf is None else mbuf
            mk_ = ["t1"] if mkeys is None else mkeys
            act(lambda e: e.activation(out=t2, in_=t2, func=AF.Ln, bias=GN_EPS, scale=1.0), ["t2"], ["t2"])
            act(lambda e: e.activation(out=t2, in_=t2, func=AF.Exp, scale=-0.5), ["t2"], ["t2"])
            dve(lambda e: e.tensor_tensor(out=t1, in0=oall, in1=mb, op=ALU.subtract), oa + mk_ + ["t1"], ["t1"])
            dve(lambda e: e.scalar_tensor_tensor(out=t1, in0=t1, scalar=gnT[:, h:h + 1], in1=t2, op0=ALU.mult, op1=ALU.mult),
                ["t1", "t2", "gnT"], ["t1"])
            dve(lambda e: e.tensor_tensor(out=catT[:, h, :], in0=t1, in1=sgT[:, c, :], op=ALU.mult), ["t1", ("sgT", c)], [("cat", h)])

        mkT = MKV[:, 0, :].rearrange("p (k t) -> p k t", k=KC)
        mvb = MKV[:, 1, :].rearrange("p (t d) -> p t d", t=2)
        csc = RA[:, 8 * T:16 * T]
        hmT = csc[:, 0:4096].rearrange("p (k t) -> p k t", k=KC)
        xn_mem = csc[:, 4096:8192].rearrange("p (t d) -> p t d", t=2)
        mst = csc[:, 8192:9216].bitcast(F32).rearrange("p (a c) -> p a c", a=2)
        def norm_mem():
            for i in range(2):
                xt = memx[:, i, :]
                act(lambda e, xt=xt, i=i: e.activation(out=xn_mem[:, i, :], in_=xt, func=AF.Square, accum_out=stat[:, 0:1]), [], [("xnm", i), "ss", ("xl2", i)])
                act(lambda e: e.activation(out=stat[:, 1:2], in_=stat[:, 0:1], func=AF.Sqrt, bias=RMS_EPS, scale=1.0 / D), ["ss"], ["ms"])
                dve(lambda e: e.reciprocal(out=stat[:, 2:3], in_=stat[:, 1:2]), ["ms"], ["rstd"])
                dve(lambda e, xt=xt, i=i: e.tensor_scalar(out=xn_mem[:, i, :], in0=xt, scalar1=stat[:, 2:3], scalar2=None,
                                                          op0=ALU.mult), ["rstd"], [("xnm", i), ("xl2", i)])
            for kc in range(KC):
                bank = 6 + (kc % 2)
                pb = psb(bank)
                pe([("t", pb[:, 128 * j:128 * j + 128], xn_mem[:, j, 128 * kc:128 * kc + 128], identb[:]) for j in range(2)],
                   [("xnm", 0), ("xnm", 1), "identb"], [("ps", bank)])
                act(lambda e, pb=pb, kc=kc: e.activation(out=hmT[:, kc, :], in_=pb[:, 0:256], func=AF.Copy, scale=gT[:, 1, kc:kc + 1]),
                    [("ps", bank), "gT"], [("hmT", kc)])

        hm_reads = [("hmT", kc) for kc in range(KC)]

        def memkv_block(which, i, blk):
            dst = mko if which == 0 else mvo
            slot, sk = wslot(blk)
            for t in range(2):
                b = 6 + t
                mms = [("m", psf(b, 256), hmT[:, kc, 128 * t:128 * t + 128], slot[:, kc, :], kc == 0, kc == KC - 1) for kc in range(KC)]
                pe(mms, [sk] + hm_reads, [("ps", b)])
                if t == 0:
                    j = i % 2
                    act(lambda e, b=b, j=j: e.activation(out=mst[:, j, :], in_=psf(b, 256), func=AF.Copy), [("ps", b)], [("mst", j)])
                    P.dma("sp", dst[:, 256 * i:256 * i + 256], mst[:, j, :], reads=[("mst", j)], track_out=True)
                if which == 1:
                    dve(lambda e, b=b, t=t, i=i: e.tensor_copy(out=mvb[:, t, 256 * i:256 * i + 256], in_=psf(b, 256)),
                        [("ps", b)], [("mvb", t, i), ("xl2", 0), ("xl2", 1)])
            if which == 0:
                for c in range(2):
                    b = 6 + c
                    mms = [("m", psf(b, 256), slot[:, kc, 128 * c:128 * c + 128], hmT[:, kc, :], kc == 0, kc == KC - 1) for kc in range(KC)]
                    pe(mms, [sk] + hm_reads, [("ps", b)])
                    act(lambda e, b=b, i=i, c=c: e.activation(out=mkT[:, 2 * i + c, :], in_=psf(b, 256), func=AF.Copy),
                        [("ps", b)], [("mkT", 2 * i + c), ("xl2", 0), ("xl2", 1)])

        XB_, YB_ = [0, 1, 2], [3, 4, 5]
        for hp in range(4):
            P.dma("sp", dmask, c_dmask[:, 2 * hp:2 * hp + 2, :], writes=["dmask"])
            P.dma("sp", smask, c_smask[:, 2 * hp:2 * hp + 2, :], writes=["smask"])
            P.dma("sp", qdec, c_qdec[:, 2 * hp:2 * hp + 2, :], writes=["qdec"])
            P.dma("sp", qdecs, c_qdecs[:, 2 * hp:2 * hp + 2, :], writes=["qdecs"])
            KS = 128.0 ** -0.5
            if hp > 0:
                slot, sk = wslot(W["k"][hp])
                proj_b(slot, sk, 0, hT, [], TB, XB_)
                proj_b(slot, sk, 1, hT, [], TB, YB_)
            rotary(XB_, TB, rot, KS, kT[:, 0, :], ("k", 0))
            rotary_pre(YB_, TB, rot, KS)
            if hp == 0:
                norm_mem()
            memkv_block(*W["mkv"][4 * hp + 0])
            rotary_post(YB_, TB, rot, KS, kT[:, 1, :], ("k", 1))
            slot, sk = wslot(W["q"][hp])
            stage1_t(0, 2 * hp)
            proj_b(slot, sk, 0, hT, [], TB, XB_)
            stage1_a(0, 2 * hp)
            proj_b(slot, sk, 1, hT, [], TB, YB_)
            rotary(XB_, TB, rot, 1.0, qT[:, 0, :], ("q", 0))
            rotary_pre(YB_, TB, rot, 1.0)
            stage1_t(1, 2 * hp + 1)
            memkv_block(*W["mkv"][4 * hp + 1])
            slot_g, sk_g = wslot(W["g"][hp])
            proj_b(slot_g, sk_g, 0, hT, [], TB, XB_)
            pv_ = pview(XB_, T)
            act(lambda e, pv_=pv_: e.activation(out=sgT[:, 0, :], in_=pv_, func=AF.Silu), pkeys(XB_, TB), [("sgT", 0)])
            stage1_a(1, 2 * hp + 1)
            rotary_post(YB_, TB, rot, 1.0, qT[:, 1, :], ("q", 1))
            proj_b(slot_g, sk_g, 1, hT, [], TB, YB_)
            pv_ = pview(YB_, T)
            act(lambda e, pv_=pv_: e.activation(out=sgT[:, 1, :], in_=pv_, func=AF.Silu), pkeys(YB_, TB), [("sgT", 1)])
            stage3(0, 2 * hp)
            memkv_block(*W["mkv"][4 * hp + 2])
            stage3(1, 2 * hp + 1)
            memkv_block(*W["mkv"][4 * hp + 3])
            gn_mm(0)
            gn_head(0)
            gn_mm(1)
            gn_head1_a()
            if hp < 3:
                slot, sk = wslot(W["v"][hp + 1])
                proj_v(slot, sk)
            gn_tail(0, 2 * hp)
            gn_head1_b()
            gn_tail(1, 2 * hp + 1, mbuf=osq_alt, mkeys=OAK)
        slot_cg0, sk_cg0 = wslot(W["cg"][0])
        cg0_banks = [next_banks(), next_banks()]
        for c in range(2):
            proj_b(slot_cg0, sk_cg0, c, hT, [], TB, cg0_banks[c])
        P.fence()

        P.phase(4)
        P.dma("sp", sc_in[0:32, :], sconv, writes=["sc_in"])

        def sconv_transposes():
            for chk in range(8):
                pe([("t", psf(6, 32), sc_in[0:32, 128 * chk:128 * chk + 128], identf[0:32, 0:32])], ["sc_in", "identf"], [("ps", 6)])
                act(lambda e, chk=chk: e.activation(out=scT[:, chk, :], in_=psf(6, 32), func=AF.Copy), [("ps", 6)], [("scT", chk)])

        for cp in range(4):
            if cp > 0:
                slot_cg, sk_cg = wslot(W["cg"][cp])
            cg_sb = [t1, t2]
            for c in range(2):
                if cp == 0:
                    banks = cg0_banks[c]
                else:
                    banks = next_banks()
                    proj_b(slot_cg, sk_cg, c, hT, [], TB, banks)
                for (t0, n), b in zip(TB, banks):
                    act(lambda e, b=b, n=n, t0=t0, c=c: e.activation(out=cg_sb[c][:, t0:t0 + n], in_=psf(b, n), func=AF.Copy),
                        [("ps", b)], [("cgsb", c, t0)])
            if cp == 0:
                sconv_transposes()
            slot_h, sk_h = wslot(W["hin"][cp])
            for c in range(2):
                chk = 2 * cp + c
                banks = next_banks()
                proj_b(slot_h, sk_h, c, hT, [], TB, banks)
                up = upad[:, c * (TP + 2):(c + 1) * (TP + 2)]
                dve(lambda e, up=up, chk=chk: e.tensor_copy(out=up[:, 0:2], in_=uprev[:, chk, :]), [("uprev", chk)], [("upad", c, -1)])
                for (t0, n), b in zip(TBP, banks):
                    dve(lambda e, b=b, n=n, t0=t0, c=c, up=up: e.tensor_tensor(
                        out=up[:, 2 + t0:2 + t0 + n], in0=cg_sb[c][:, t0:t0 + n], in1=psf(b, n), op=ALU.mult),
                        [("ps", b), ("cgsb", c, t0)], [("upad", c, t0)])
                ups = upads
                dve(lambda e, chk=chk: e.tensor_copy(out=upads[:, :, 0:2], in_=scT[:, chk, :].rearrange("p (s k) -> p s k", k=2)),
                    [("scT", chk)], [("upads", 0)])
                b = banks[2]
                dve(lambda e, b=b, c=c: e.tensor_tensor(
                    out=upads[:, :, 2:10], in0=cg_sb[c][:, TP:T].rearrange("p (s k) -> p s k", k=8),
                    in1=psf(b, 128).rearrange("p (s k) -> p s k", k=8), op=ALU.mult),
                    [("ps", b), ("cgsb", c, 1024)], [("upads", 1)])
                upr = [("upad", c, -1), ("upad", c, 0), ("upad", c, 512)]
                dve(lambda e, up=up, chk=chk: e.tensor_scalar(out=cacc[:, 0:TP], in0=up[:, 0:TP], scalar1=cwT[:, 0, chk:chk + 1],
                                                              scalar2=None, op0=ALU.mult), upr + ["cwT"], [("cacc", 0)])
                for k in (1, 2):
                    dve(lambda e, up=up, chk=chk, k=k: e.scalar_tensor_tensor(
                        out=cacc[:, 0:TP], in0=up[:, k:k + TP], scalar=cwT[:, k, chk:chk + 1], in1=cacc[:, 0:TP],
                        op0=ALU.mult, op1=ALU.add), upr + [("cacc", 0)], [("cacc", 0)])
                ca_s = cacc[:, TP:T].rearrange("p (s k) -> p s k", k=8)
                usr = [("upads", 0), ("upads", 1)]
                dve(lambda e, chk=chk: e.tensor_scalar(out=ca_s, in0=upads[:, :, 0:8], scalar1=cwT[:, 0, chk:chk + 1],
                                                       scalar2=None, op0=ALU.mult), usr + ["cwT"], [("cacc", 1)])
                for k in (1, 2):
                    dve(lambda e, chk=chk, k=k: e.scalar_tensor_tensor(
                        out=ca_s, in0=upads[:, :, k:k + 8], scalar=cwT[:, k, chk:chk + 1], in1=ca_s,
                        op0=ALU.mult, op1=ALU.add), usr + [("cacc", 1)], [("cacc", 1)])
                pe([("t", psf(7, 128)[0:2, :], up[:, TP:TP + 2], identf[:])], [("upad", c, 512), "identf"], [("ps", 7)])
                act(lambda e, chk=chk: e.activation(out=sc_out[0:2, 128 * chk:128 * chk + 128], in_=psf(7, 128)[0:2, :], func=AF.Copy),
                    [("ps", 7)], [("sc_out", chk)])
                dve(lambda e: e.tensor_copy(out=nbt.rearrange("p (s k) -> p s k", k=2), in_=upads[:, :, 8:10]), [("upads", 1)], ["nbt"])
                pe([("t", psf(7, 128)[0:32, :], nbt, identf[:])], ["nbt", "identf"], [("ps", 7)])
                act(lambda e, chk=chk: e.activation(out=sc_in[0:32, 128 * chk:128 * chk + 128], in_=psf(7, 128)[0:32, :], func=AF.Copy),
                    [("ps", 7)], [("sc_in2", chk), "sc_in"])
                dve(lambda e, c=c: e.tensor_copy(out=cg_sb[c][:, :], in_=cacc[:, :]), [("cacc", 0), ("cacc", 1), ("cgsb", c, 0), ("cgsb", c, 512), ("cgsb", c, 1024)],
                    [("cgsb", c, 0), ("cgsb", c, 512), ("cgsb", c, 1024)])
            slot_b, sk_b = wslot(W["bg"][cp])
            for c in range(2):
                chk = 2 * cp + c
                banks = next_banks()
                proj_b(slot_b, sk_b, c, hT, [], TB, banks)
                for (t0, n), b in zip(TB, banks):
                    dve(lambda e, b=b, n=n, t0=t0, c=c, chk=chk: e.tensor_tensor(
                        out=catT[:, 8 + chk, t0:t0 + n], in0=cg_sb[c][:, t0:t0 + n], in1=psf(b, n), op=ALU.mult),
                        [("ps", b), ("cgsb", c, t0)], [("cat", 8 + chk, t0)])
        P.dma("sp", scp, sc_out[0:2, :], reads=[("sc_out", i) for i in range(8)], track_out=True)
        P.dma("sp", scs, sc_in[0:32, :], reads=[("sc_in2", i) for i in range(8)], track_out=True)
        P.fence()

        P.phase(5)
        def out_proj(wkey, srcT, accumulate_first_from_dram, after_tile=None):
            if accumulate_first_from_dram is not None:
                for t in range(NT):
                    P.dma("sp", xres[:, t, :], accumulate_first_from_dram[t], writes=[("x", t, cb) for cb in range(4)])
            for cb in range(4):
                s0, k0 = wslot(W[wkey][cb][0])
                s1, k1 = wslot(W[wkey][cb][1])
                for t in range(NT):
                    b = t % 4
                    mms = []
                    for kc in range(KC):
                        s_ = s0 if kc < 8 else s1
                        mms.append(("m", psf(b), srcT[:, kc, 128 * t:128 * t + 128], s_[:, kc % 8, :], kc == 0, kc == KC - 1))
                    pe(mms, [k0, k1], [("ps", b)])
                    dve(lambda e, b=b, t=t, cb=cb: e.tensor_tensor(out=xres[:, t, 512 * cb:512 * cb + 512],
                                                                   in0=xres[:, t, 512 * cb:512 * cb + 512], in1=psf(b), op=ALU.add),
                        [("ps", b), ("x", t, cb)], [("x", t, cb)])
                    if cb == 3 and after_tile is not None and t >= 1:
                        after_tile(t - 1)
            if after_tile is not None:
                after_tile(NT - 1)

        def norm_hook(gidx):
            return lambda t: norm_tile(t, xres[:, t, :], [("x", t, cb) for cb in range(4)], gidx, hT, "hres")

        out_proj("mixo", catT, srcs, after_tile=norm_hook(2))
        P.fence()

        P.phase(6)
        qxT = RA[:, :].rearrange("p (k t) -> p k t", k=KC)
        for i in range(8):
            slot, sk = wslot(W["xq"][i])
            for c in range(2):
                banks = next_banks()
                proj_b(slot, sk, c, hT, [], TB, banks)
                for (t0, n), b in zip(TB, banks):
                    act(lambda e, b=b, n=n, t0=t0, i=i, c=c: e.activation(out=qxT[:, 2 * i + c, t0:t0 + n], in_=psf(b, n), func=AF.Copy),
                        [("ps", b)], [("qx", 2 * i + c, t0)])
        P.fence()

        P.phase(7)
        rh_off = [0]

        def rh(n, dt=BF16):
            nb = n if dt == BF16 else 2 * n
            o = rh_off[0]
            rh_off[0] += nb
            assert rh_off[0] <= KC * T, rh_off[0]
            v = RH[:, o:o + nb]
            if dt == F32:
                v = v.bitcast(F32)
            return v

        rh_xn0 = 0
        P.phase(8)
        oob2 = oo[:, :, :].rearrange("p a t -> p (a t)").bitcast(BF16)
        pn = oob2[:, 0:2048].rearrange("p (a t n) -> p a t n", a=2, t=4)
        sfb = Sf[:, :, :].rearrange("p h e -> p (h e)").bitcast(BF16)
        pT = sfb[:, 0:2048].rearrange("p (a c l) -> p a c l", a=2, c=2)
        rh_off[0] = 0
        kraw = rh(2 * 2 * D).rearrange("p (a t d) -> p a t d", a=2, t=2)
        vraw = rh(2 * 2 * D).rearrange("p (a t d) -> p a t d", a=2, t=2)
        kTs = rh(2 * 1024).rearrange("p (a x) -> p a x", a=2)
        for s_ in range(2):
            P.dma("pool", kraw[:, s_], ck[s_].rearrange("(t p) d -> p t d", p=128), writes=[("kraw", s_)])
            P.dma("pool", vraw[:, s_], cv[s_].rearrange("(t p) d -> p t d", p=128), writes=[("vraw", s_)])
        XS = 512.0 ** -0.5
        oT = qxT
        combos = [(hd, g4) for hd in range(4) for g4 in range(2)]

        def h2_a(i):
            hd, g4 = combos[i]
            b0 = 2 * (i % 2)
            mms = []
            for tt in range(4):
                t = 4 * g4 + tt
                o_ = psf(b0 + tt // 2, 256, 256 * (tt % 2))
                for dc in range(4):
                    mms.append(("m", o_, qxT[:, 4 * hd + dc, 128 * t:128 * t + 128], mkT[:, 4 * hd + dc, :], dc == 0, dc == 3))
            pe(mms, [], [("ps", b0), ("ps", b0 + 1)])

        def h2_b(i):
            hd, g4 = combos[i]
            pj = i % 2
            b0 = 2 * pj
            skeys = [("ps", b0), ("ps", b0 + 1)]
            so = 32 + 16 * pj
            sc4 = PS[:, b0:b0 + 2, :].rearrange("p b (t n) -> p (b t) n", t=2)
            dve(lambda e: e.reduce_max(out=stat[:, so:so + 4], in_=sc4, axis=AX.X), skeys, [("mx", pj)])
            dve(lambda e: e.tensor_scalar(out=stat[:, so + 4:so + 8], in0=stat[:, so:so + 4], scalar1=-XS, scalar2=None, op0=ALU.mult),
                [("mx", pj)], [("nmx", pj)])
            for tt in range(4):
                act(lambda e, tt=tt: e.activation(out=pn[:, pj, tt, :], in_=psf(b0 + tt // 2, 256, 256 * (tt % 2)), func=AF.Exp,
                                                  bias=stat[:, so + 4 + tt:so + 5 + tt], scale=XS, accum_out=stat[:, so + 8 + tt:so + 9 + tt]),
                    [("ps", b0 + tt // 2), ("nmx", pj)], [("pn", pj, tt), ("sm", pj, tt)])
            dve(lambda e: e.reciprocal(out=stat[:, so + 12:so + 16], in_=stat[:, so + 8:so + 12]), [("sm", pj, tt) for tt in range(4)], [("rs", pj)])
            dve(lambda e: e.tensor_tensor(out=pn[:, pj], in0=pn[:, pj], in1=stat[:, so + 12:so + 16].unsqueeze(2).broadcast_to([128, 4, 256]), op=ALU.mult),
                [("pn", pj, tt) for tt in range(4)] + [("rs", pj)], [("pn", pj, tt) for tt in range(4)])
            pb = psb(4 + pj)
            pe([("t", pb[:, 512 * nc_ + 128 * tt:512 * nc_ + 128 * tt + 128], pn[:, pj, tt, 128 * nc_:128 * nc_ + 128], identb[:]) for tt in range(4) for nc_ in range(2)],
               [("pn", pj, tt) for tt in range(4)] + ["identb"], [("ps", 4 + pj)])
            act(lambda e: e.activation(out=pT[:, pj].rearrange("p c l -> p (c l)"), in_=pb[:, :], func=AF.Copy), [("ps", 4 + pj)], [("pT", pj)])
            for e2 in range(2):
                mms = []
                for k_ in range(2):
                    ec = 2 * e2 + k_
                    for nc_ in range(2):
                        mms.append(("m", psf(6 + k_), mvb[:, nc_, 512 * hd + 128 * ec:512 * hd + 128 * ec + 128], pT[:, pj, nc_, :], nc_ == 0, nc_ == 1))
                pe(mms, [("pT", pj)], [("ps", 6), ("ps", 7)])
                act(lambda e, e2=e2: e.activation(out=oT[:, 4 * hd + 2 * e2:4 * hd + 2 * e2 + 2, 512 * g4:512 * g4 + 512],
                                                  in_=PS[:, 6:8, :], func=AF.Copy), [("ps", 6), ("ps", 7)], [("oT", hd, g4, e2)])

        h2_a(0)
        for i in range(len(combos)):
            if i + 1 < len(combos):
                h2_a(i + 1)
            h2_b(i)
        P.fence()

        P.phase(9)
        oob = oo[:, :, :].rearrange("p a t -> p (a t)").bitcast(BF16)
        ps_s = oob[:, 0:1024].rearrange("p (h n) -> p h n", h=4)
        pTs = oob[:, 1024:1088].rearrange("p (h c l) -> p h c l", h=4, c=2)
        scs_ = oo[:, :, :].rearrange("p a t -> p (a t)")[:, 1024:2048].rearrange("p (h n) -> p h n", h=4)
        def h3_load_k(s):
            a = s % 2
            P.dma("pool", kraw[:, a], ck[s].rearrange("(t p) d -> p t d", p=128), writes=[("kraw", a)])

        def h3_load_v(s):
            a = s % 2
            P.dma("pool", vraw[:, a], cv[s].rearrange("(t p) d -> p t d", p=128), writes=[("vraw", a)])

        def h3_load(s):
            h3_load_k(s)
            h3_load_v(s)

        def h3_a(s):
            a = s % 2
            sb0 = 2 + 2 * a

            def tr(hd):
                j = hd % 2
                pb = psb(j)
                mms = [("t", pb[:, 256 * dc + 128 * nt_:256 * dc + 128 * nt_ + 128], kraw[:, a, nt_, 512 * hd + 128 * dc:512 * hd + 128 * dc + 128], identb[:])
                       for dc in range(4) for nt_ in range(2)]
                pe(mms, [("kraw", a), "identb"], [("ps", j)])
                act(lambda e, pb=pb, j=j: e.activation(out=kTs[:, j, :], in_=pb[:, :], func=AF.Copy), [("ps", j)], [("kTs", j)])

            def sc_(hd):
                j = hd % 2
                sbk = sb0 + hd // 2
                so = 256 * (hd % 2)
                mms = [("m", psf(sbk, 256, so)[0:8, :], qxT[:, 4 * hd + dc, TP + 8 * s:TP + 8 * s + 8], kTs[:, j, 256 * dc:256 * dc + 256], dc == 0, dc == 3) for dc in range(4)]
                pe(mms, [("kTs", j)], [("ps", sbk)])

            tr(0); tr(1); sc_(0); tr(2); sc_(1); tr(3); sc_(2); sc_(3)

        def h3_b(s):
            a = s % 2
            sb0 = 2 + 2 * a
            skeys = [("ps", sb0), ("ps", sb0 + 1)]
            sc = PS[0:8, sb0:sb0 + 2, :].rearrange("p b n -> p (b n)")
            sc4 = PS[0:8, sb0:sb0 + 2, :].rearrange("p b (t n) -> p (b t) n", t=2)
            dve(lambda e: e.reduce_max(out=stat[0:8, 16:20], in_=sc4, axis=AX.X), skeys, ["mx"])
            dve(lambda e: e.tensor_tensor(out=scs_[0:8], in0=sc4, in1=stat[0:8, 16:20].unsqueeze(2).broadcast_to([8, 4, 256]), op=ALU.subtract),
                skeys + ["mx"], ["scs"])
            act(lambda e: e.activation(out=ps_s[0:8].rearrange("p h n -> p (h n)"), in_=scs_[0:8].rearrange("p h n -> p (h n)"), func=AF.Exp, scale=XS),
                ["scs"], ["pss"])
            dve(lambda e: e.reduce_sum(out=stat[0:8, 24:28], in_=ps_s[0:8], axis=AX.X), ["pss"], ["sm"])
            dve(lambda e: e.reciprocal(out=stat[0:8, 28:32], in_=stat[0:8, 24:28]), ["sm"], ["rs"])
            dve(lambda e: e.tensor_tensor(out=ps_s[0:8], in0=ps_s[0:8], in1=stat[0:8, 28:32].unsqueeze(2).broadcast_to([8, 4, 256]), op=ALU.mult),
                ["pss", "rs"], ["pss"])
            pb2 = psb(6)
            pe([("t", pb2[:, 16 * hd + 8 * nc_:16 * hd + 8 * nc_ + 8], ps_s[0:8, hd, 128 * nc_:128 * nc_ + 128], identb[0:8, 0:8]) for hd in range(4) for nc_ in range(2)],
               ["pss", "identb"], [("ps", 6)])
            act(lambda e: e.activation(out=pTs.rearrange("p h c l -> p (h c l)"), in_=pb2[:, 0:64], func=AF.Copy), [("ps", 6)], ["pTs"])
            mms = []
            for hd in range(4):
                for ec in range(4):
                    for nc_ in range(2):
                        mms.append(("m", psf(7, 8, 8 * (4 * hd + ec)), vraw[:, a, nc_, 512 * hd + 128 * ec:512 * hd + 128 * ec + 128], pTs[:, hd, nc_, :], nc_ == 0, nc_ == 1))
            pe(mms, [("vraw", a), "pTs"], [("ps", 7)])
            act(lambda e: e.activation(out=oT[:, :, TP + 8 * s:TP + 8 * s + 8], in_=psf(7, 128).rearrange("p (c l) -> p c l", c=16), func=AF.Copy),
                [("ps", 7)], [("oTs", s)])

        h3_a(0)
        for s in range(NSEQ):
            if s + 1 < NSEQ:
                h3_a(s + 1)
            if s + 2 < NSEQ:
                h3_load_k(s + 2)
            h3_b(s)
            if s + 2 < NSEQ:
                h3_load_v(s + 2)
        P.fence()

        P.phase(10)
        out_proj("xo", oT, None, after_tile=norm_hook(3))
        P.fence()

        P.phase(11)
        hid = RA[:, 0:2 * 4 * T].rearrange("p (a k t) -> p a k t", a=2, k=4)
        gsb = RA[:, 2 * 4 * T:2 * 4 * T + 2 * T].bitcast(F32)
        for g in range(NG):
            a = g % 2
            for j in range(2):
                sgj, kgj = wslot(W["gate"][g][j])
                suj, kuj = wslot(W["up"][g][j])
                for c in range(2):
                    kk_ = 2 * j + c
                    bg_ = next_banks()
                    hrd = ["hT"] if g == NG - 1 else []
                    proj_b(sgj, kgj, c, hT, hrd, TB, bg_)
                    for (t0, n), b in zip(TB, bg_):
                        act(lambda e, b=b, n=n, t0=t0: e.activation(out=gsb[:, t0:t0 + n], in_=psf(b, n), func=AF.Silu),
                            [("ps", b)], [("gsb", t0)])
                    bu_ = next_banks()
                    proj_b(suj, kuj, c, hT, hrd, TB, bu_)
                    for (t0, n), b in zip(TB, bu_):
                        dve(lambda e, b=b, n=n, t0=t0, a=a, kk_=kk_: e.tensor_tensor(out=hid[:, a, kk_, t0:t0 + n], in0=gsb[:, t0:t0 + n], in1=psf(b, n), op=ALU.mult),
                            [("ps", b), ("gsb", t0)], [("hid", a, kk_, t0)])
            hr = [("hid", a, k_, t0) for k_ in range(4) for t0, _ in TB]

            def down_tile(t, cb, sd, kd_, b):
                co = 512 * (cb % 2)
                mms = [("m", psf(b), hid[:, a, k_, 128 * t:128 * t + 128], sd[:, k_, co:co + 512], k_ == 0, k_ == 3) for k_ in range(4)]
                pe(mms, [kd_] + hr, [("ps", b)])
                dve(lambda e, b=b, t=t, cb=cb: e.tensor_tensor(out=xres[:, t, 512 * cb:512 * cb + 512],
                                                               in0=xres[:, t, 512 * cb:512 * cb + 512], in1=psf(b), op=ALU.add),
                    [("ps", b), ("x", t, cb)], [("x", t, cb)])

            if g < NG - 1:
                for cb in range(4):
                    if cb % 2 == 0:
                        sd, kd_ = wslot(W["down"][g][cb // 2])
                    for t in range(NT):
                        down_tile(t, cb, sd, kd_, 6 + (t % 2))
            else:
                sd0, kd0 = wslot(W["down"][g][0])
                sd1, kd1 = wslot(W["down"][g][1])
                for t in range(NT):
                    for cb in range(4):
                        down_tile(t, cb, (sd0, sd1)[cb // 2], (kd0, kd1)[cb // 2], 4 + cb)

        P.phase(12)
        gbc = RH[:, 0:2 * D].bitcast(F32)
        P.dma("sp", gbc, final_g.partition_broadcast(128), reads=[], writes=["gbc", "hT"])
        sq3 = RH[:, 2 * D:3 * D]
        yst = RH[:, 3 * D:7 * D].bitcast(F32).rearrange("p (a d) -> p a d", a=2)
        for t in range(NT):
            a = t % 2
            act(lambda e, t=t: e.activation(out=sq3, in_=xres[:, t, :], func=AF.Square, accum_out=stat[:, 0:1]), [("x", t, cb) for cb in range(4)] + ["gbc"], ["sq3", "ss"])
            act(lambda e: e.activation(out=stat[:, 1:2], in_=stat[:, 0:1], func=AF.Sqrt, bias=RMS_EPS, scale=1.0 / D), ["ss"], ["ms"])
            dve(lambda e: e.reciprocal(out=stat[:, 2:3], in_=stat[:, 1:2]), ["ms"], ["rstd"])
            dve(lambda e, t=t, a=a: e.scalar_tensor_tensor(out=yst[:, a, :], in0=xres[:, t, :], scalar=stat[:, 2:3], in1=gbc, op0=ALU.mult, op1=ALU.mult),
                ["rstd", "gbc"] + [("x", t, cb) for cb in range(4)], [("yst", a)])
            dst = yp[128 * t:128 * t + 128, :] if t < 8 else ys[:, :]
            P.dma("sp", dst, yst[:, a, :], reads=[("yst", a)], track_out=True)
        P.dead = False
        final_waits = []
        for k, v in P.dma_tokens:
            if v > P.seen["sp"].get(k, 0):
                P.seen["sp"][k] = v
                final_waits.append((k, v))
        P.q["sp"].append((final_waits, None, None))

        print("engine op counts", {e: (P.epoch[e], P.cnt[e]) for e in ENGS}, "n_sems", len(P.sems), flush=True)
        with nc.Block() as block:
            P.emit(block)
        print("n_sems", len(P.sems), flush=True)
    return nc


def _consts(half):
    c = {}
    c["c_ident"] = np.eye(128, dtype=np.float32)
    perm = np.zeros((128, 128), np.float32)
    for m in range(128):
        perm[(m + 64) % 128, m] = 1.0
    c["c_perm"] = perm
    inv = 10000.0 ** (-np.arange(64, dtype=np.float64) / 64.0)

    def rot_table(pos):
        ang = pos.astype(np.float64)[None, :] * inv[:, None]
        cos = np.cos(ang).astype(np.float32); sin = np.sin(ang).astype(np.float32)
        tab = np.zeros((128, 2, pos.shape[0]), np.float32)
        tab[:64, 0] = cos; tab[64:, 0] = cos
        tab[:64, 1] = -sin; tab[64:, 1] = sin
        return tab
    pos_main = np.concatenate([half * TP + np.arange(TP), np.tile(16384 + np.arange(8), NSEQ)])
    c["c_rot"] = rot_table(pos_main)
    c["c_rotp"] = rot_table(np.arange(TP))
    lg = np.array(LG, dtype=np.float64)
    m = np.arange(128)[:, None]; l = np.arange(128)[None, :]
    dm = np.zeros((128, H, 128)); sm = np.zeros((128, H, 128))
    for h in range(H):
        rel = l - m
        dm[:, h, :] = np.where(rel >= 0, np.exp(rel * lg[h]), 0.0)
        same = (m // 8) == (l // 8)
        sm[:, h, :] = np.where((rel >= 0) & same, np.exp(rel * lg[h]), 0.0)
    c["c_dmask"] = dm.astype(np.float32); c["c_smask"] = sm.astype(np.float32)
    qd = np.zeros((128, H, 128)); qds = np.zeros((128, H, 128))
    for h in range(H):
        qd[:, h, :] = np.exp((np.arange(128) + 1.0) * lg[h])[None, :]
        qds[:, h, :] = np.exp(((np.arange(128) % 8) + 1.0) * lg[h])[None, :]
    c["c_qdec"] = qd.astype(np.float32); c["c_qdecs"] = qds.astype(np.float32)
    kd = np.zeros((128, H)); kdp = np.zeros((128, H, 8)); rmd = np.zeros((128, H, NSEQ + 1))
    for h in range(H):
        kd[:, h] = np.exp((127.0 - np.arange(128)) * lg[h])
        for ch in range(8):
            kdp[:, h, ch] = np.exp((1023.0 - (128 * ch + np.arange(128))) * lg[h])
        for s in range(NSEQ):
            mm_ = np.arange(128)
            rmd[:, h, s] = np.where(mm_ // 8 == s, 1.0, 0.0)
        rmd[:, h, NSEQ] = np.exp((7.0 - (np.arange(128) % 8)) * lg[h])
    c["c_kdec"] = kd.astype(np.float32); c["c_kdecp"] = kdp.astype(np.float32); c["c_rmd"] = rmd.astype(np.float32)
    return c


def _core_inputs(c, shared, x_prompt, x_sample, mem_prompt, state_ret, state_conv, cache_mem_k, cache_mem_v):
    b, half = c // 2, c % 2
    m = dict(shared)
    m["xp"] = np.ascontiguousarray(x_prompt[b, half * TP:(half + 1) * TP])
    m["xprev"] = np.ascontiguousarray(x_prompt[b, 0:TP]) if half == 1 else np.zeros((TP, D), np.float32)
    m["xs"] = np.ascontiguousarray(x_sample[NSEQ * c:NSEQ * (c + 1)].reshape(TS, D))
    own = mem_prompt[b, 128 * half:128 * half + 128]
    oth = mem_prompt[b, 128 * (1 - half):128 * (1 - half) + 128]
    m["mem"] = np.ascontiguousarray(np.concatenate([own, oth], axis=0))
    m["sret"] = np.ascontiguousarray(state_ret[0, NSEQ * c:NSEQ * (c + 1)])
    m["sconv"] = np.ascontiguousarray(state_conv[0, NSEQ * c:NSEQ * (c + 1)].reshape(NSEQ * 2, 1024))
    m["ck"] = np.ascontiguousarray(cache_mem_k[0, NSEQ * c:NSEQ * (c + 1)].reshape(NSEQ, NMEM, D))
    m["cv"] = np.ascontiguousarray(cache_mem_v[0, NSEQ * c:NSEQ * (c + 1)].reshape(NSEQ, NMEM, D))
    m.update(_consts(half))
    return m


_NC_CACHE = {}


def kernel(x_prompt, x_sample, mem_prompt, state_ret, state_conv, cache_mem_k, cache_mem_v,
           ln_mix_g, w_in, conv_w, ret_gn_g, w_mix_out, ln_mem_g, ln_xa_g, w_xq, w_mk, w_mv,
           w_xo, ln_ffn_g, w_gate, w_up, w_down, final_g):
    f = lambda a: np.ascontiguousarray(np.asarray(a, dtype=np.float32))
    x_prompt, x_sample, mem_prompt = f(x_prompt), f(x_sample), f(mem_prompt)
    state_ret, state_conv, cache_mem_k, cache_mem_v = f(state_ret), f(state_conv), f(cache_mem_k), f(cache_mem_v)
    shared = {
        "ln_mix_g": f(ln_mix_g)[0], "w_in": f(w_in)[0], "conv_w": f(conv_w)[0], "ret_gn_g": f(ret_gn_g)[0],
        "w_mix_out": f(w_mix_out)[0], "ln_mem_g": f(ln_mem_g)[0], "ln_xa_g": f(ln_xa_g)[0], "w_xq": f(w_xq)[0],
        "w_mk": f(w_mk)[0], "w_mv": f(w_mv)[0], "w_xo": f(w_xo)[0], "ln_ffn_g": f(ln_ffn_g)[0],
        "w_gate": f(w_gate)[0], "w_up": f(w_up)[0], "w_down": f(w_down)[0], "final_g": f(final_g),
    }
    if "nc" not in _NC_CACHE:
        _NC_CACHE["nc"] = build_program()
    nc = _NC_CACHE["nc"]
    in_maps = [_core_inputs(c, shared, x_prompt, x_sample, mem_prompt, state_ret, state_conv, cache_mem_k, cache_mem_v)
               for c in range(8)]
    res = run_bass_kernel_spmd(nc, in_maps, core_ids=list(range(8)))
    R = res.results
    y_prompt = np.zeros((4, 2048, D), np.float32)
    y_sample = np.zeros((128, 8, D), np.float32)
    srp_o = np.zeros((1, 4, H, 128, 128), np.float32)
    srs_o = np.zeros((1, 128, H, 128, 128), np.float32)
    scp_o = np.zeros((1, 4, 2, 1024), np.float32)
    scs_o = np.zeros((1, 128, 2, 1024), np.float32)
    mk_o = np.zeros((1, 4, NMEM, 4, 512), np.float32)
    mv_o = np.zeros((1, 4, NMEM, 4, 512), np.float32)
    for c in range(8):
        b, half = c // 2, c % 2
        r = R[c]
        y_prompt[b, half * TP:(half + 1) * TP] = r["yp"]
        y_sample[NSEQ * c:NSEQ * (c + 1)] = r["ys"].reshape(NSEQ, 8, D)
        if half == 1:
            srp_o[0, b] = r["srp"]
            scp_o[0, b] = r["scp"]
        srs_o[0, NSEQ * c:NSEQ * (c + 1)] = r["srs"]
        scs_o[0, NSEQ * c:NSEQ * (c + 1)] = r["scs"].reshape(NSEQ, 2, 1024)
        mk_o[0, b, 128 * half:128 * half + 128] = r["mko"].reshape(128, 4, 512)
        mv_o[0, b, 128 * half:128 * half + 128] = r["mvo"].reshape(128, 4, 512)
    return (y_prompt, y_sample, srp_o, srs_o, scp_o, scs_o, mk_o, mv_o)
```

```python
import math
import os
from contextlib import ExitStack

import numpy as np
import concourse.bass as bass
import concourse.mybir as mybir
from concourse.bass_utils import run_bass_kernel_spmd

F32 = mybir.dt.float32
BF16 = mybir.dt.bfloat16
ALU = mybir.AluOpType
AF = mybir.ActivationFunctionType
AX = mybir.AxisListType

D = 2048
KC = 16
TP = 1024
TS = 128
T = TP + TS
NT = T // 128
NSEQ = 16
H = 8
DFF = 5632
NG = DFF // 512
NMEM = 256
RMS_EPS = 1e-6
GN_EPS = 1e-5
NSLOT = 4
TB = [(0, 512), (512, 512), (1024, 128)]
TBP = [(0, 512), (512, 512)]
LG = [math.log(1.0 - 2.0 ** (-5.0 - h)) for h in range(H)]

ENGS = ("pe", "act", "dve", "pool", "sp")
_PE_LABELS = [] if os.environ.get("KLABELS") else None


class Prog:
    def __init__(self, nc, sem_alloc):
        self.nc = nc
        self.sem_alloc = sem_alloc
        self.q = {e: [] for e in ENGS}
        self.cnt = {e: 0 for e in ENGS}
        self.epoch = {e: 0 for e in ENGS}
        self.sems = {}
        self.seen = {e: {} for e in ENGS}
        self.lastw = {}
        self.readers = {}
        self.nds = 12
        self.dq = {q: {"i": 0, "uses": [0] * self.nds} for q in ("sp", "pool", "act")}
        self.dma_tokens = []
        self.weight_tokens = set()
        self.EPOCH_MAX = 4000
        self.dead = False
        self.stop = float(os.environ.get("KSTOP", "99"))

    def phase(self, k):
        if k > self.stop:
            self.dead = True

    def sem(self, key):
        if key not in self.sems:
            self.sems[key] = self.sem_alloc("s_" + "_".join(str(k) for k in key))
        return self.sems[key]

    def _collect(self, eng, reads, writes):
        deps = {}

        def need(tok):
            if tok is None:
                return
            k, v = tok
            if k[0] == "e" and k[1] == "pe" and eng == "pe":
                return
            if v > deps.get(k, 0):
                deps[k] = v

        for t in reads:
            need(self.lastw.get(t))
        for t in writes:
            need(self.lastw.get(t))
            for r in self.readers.get(t, ()):
                need(r)
        waits = []
        for k, v in deps.items():
            if v > self.seen[eng].get(k, 0):
                self.seen[eng][k] = v
                waits.append((k, v))
        return waits

    def _record(self, tok, reads, writes):
        for t in reads:
            self.readers.setdefault(t, []).append(tok)
        for t in writes:
            self.lastw[t] = tok
            self.readers[t] = []

    def op(self, eng, fn, reads=(), writes=()):
        if self.dead:
            return None
        ps_reads = [t for t in reads if isinstance(t, tuple) and t[0] == "ps" and t not in writes]
        if ps_reads:
            writes = list(writes) + ps_reads
        waits = self._collect(eng, reads, writes)
        if self.cnt[eng] >= self.EPOCH_MAX:
            self.epoch[eng] += 1
            self.cnt[eng] = 0
        self.cnt[eng] += 1
        key = ("e", eng, self.epoch[eng])
        tok = (key, self.cnt[eng])
        self.q[eng].append((waits, fn, (key, 1)))
        self._record(tok, reads, writes)
        return tok

    def dma(self, queue, out, in_, reads=(), writes=(), track_out=False, slow=False):
        if self.dead:
            return None
        st = self.dq[queue]
        slot = st["i"] % self.nds
        st["i"] += 1
        key = ("d", queue, slot)
        waits = self._collect(queue, reads, writes)
        prev = st["uses"][slot] * 16
        if prev > self.seen[queue].get(key, 0):
            self.seen[queue][key] = prev
            waits.append((key, prev))
        st["uses"][slot] += 1
        tok = (key, st["uses"][slot] * 16)
        if slow:
            self.q[queue].append((waits, lambda e, o=out, i=in_: e.dma_start(out=o, in_=i, allow_slow_non_contiguous=True), (key, 16)))
        else:
            self.q[queue].append((waits, lambda e, o=out, i=in_: e.dma_start(out=o, in_=i), (key, 16)))
        self._record(tok, reads, writes)
        if track_out:
            self.dma_tokens.append(tok)
        return tok

    def fence(self):
        if self.dead:
            return
        toks = []
        for e in ENGS:
            if self.cnt[e] > 0:
                toks.append((("e", e, self.epoch[e]), self.cnt[e]))
        for q, st in self.dq.items():
            for s in range(self.nds):
                if st["uses"][s] > 0:
                    tk = (("d", q, s), st["uses"][s] * 16)
                    if tk in self.weight_tokens:
                        continue
                    toks.append(tk)
        for e in ENGS:
            waits = []
            for k, v in toks:
                if k[0] == "e" and k[1] == e and e == "pe":
                    continue
                if v > self.seen[e].get(k, 0):
                    self.seen[e][k] = v
                    waits.append((k, v))
            if waits:
                self.q[e].append((waits, None, None))
        keep_w = {k: v for k, v in self.lastw.items() if isinstance(k, tuple) and k[0] == "slot"}
        keep_r = {k: v for k, v in self.readers.items() if isinstance(k, tuple) and k[0] == "slot"}
        self.lastw = keep_w
        self.readers = keep_r

    def emit(self, block):
        nc = self.nc

        def run(eng_name):
            def body(e):
                for waits, fn, inc in self.q[eng_name]:
                    for k, v in waits:
                        e.wait_ge(self.sem(k), v)
                    if fn is not None:
                        ins = fn(e)
                        ins.then_inc(self.sem(inc[0]), inc[1])
            return body

        block.tensor(run("pe"))
        block.scalar(run("act"))
        block.vector(run("dve"))
        block.gpsimd(run("pool"))
        block.sync(run("sp"))


def build_program():
    nc = bass.Bass("TRN2", target_bir_lowering=False)

    def din(name, shape):
        return nc.dram_tensor(name, list(shape), F32, kind="ExternalInput").ap()

    def dout(name, shape):
        return nc.dram_tensor(name, list(shape), F32, kind="ExternalOutput").ap()

    xp = din("xp", [TP, D]); xprev = din("xprev", [TP, D]); xs = din("xs", [TS, D])
    mem = din("mem", [NMEM, D])
    sret = din("sret", [NSEQ, H, 128, 128]); sconv = din("sconv", [NSEQ * 2, 1024])
    ck = din("ck", [NSEQ, NMEM, D]); cv = din("cv", [NSEQ, NMEM, D])
    ln_mix_g = din("ln_mix_g", [D]); w_in = din("w_in", [D, 7168]); conv_w = din("conv_w", [3, 1024])
    ret_gn_g = din("ret_gn_g", [1024]); w_mix_out = din("w_mix_out", [D, D])
    ln_mem_g = din("ln_mem_g", [D]); ln_xa_g = din("ln_xa_g", [D])
    w_xq = din("w_xq", [D, D]); w_mk = din("w_mk", [D, D]); w_mv = din("w_mv", [D, D]); w_xo = din("w_xo", [D, D])
    ln_ffn_g = din("ln_ffn_g", [D]); w_gate = din("w_gate", [D, DFF]); w_up = din("w_up", [D, DFF])
    w_down = din("w_down", [DFF, D]); final_g = din("final_g", [D])
    c_ident = din("c_ident", [128, 128]); c_perm = din("c_perm", [128, 128])
    c_rot = din("c_rot", [128, 2, T]); c_rotp = din("c_rotp", [128, 2, TP])
    c_dmask = din("c_dmask", [128, H, 128]); c_smask = din("c_smask", [128, H, 128])
    c_qdec = din("c_qdec", [128, H, 128]); c_qdecs = din("c_qdecs", [128, H, 128])
    c_kdec = din("c_kdec", [128, H]); c_kdecp = din("c_kdecp", [128, H, 8]); c_rmd = din("c_rmd", [128, H, NSEQ + 1])

    yp = dout("yp", [TP, D]); ys = dout("ys", [TS, D])
    srp = dout("srp", [H, 128, 128]); srs = dout("srs", [NSEQ, H, 128, 128])
    scp = dout("scp", [2, 1024]); scs = dout("scs", [NSEQ * 2, 1024])
    mko = dout("mko", [128, D]); mvo = dout("mvo", [128, D])

    es = ExitStack()
    with es:
        def sb(name, shape, dt):
            return es.enter_context(nc.sbuf_tensor(name, list(shape), dt))

        RX = sb("RX", [128, NT * D], F32)
        RH = sb("RH", [128, KC * T], BF16)
        RA = sb("RA", [128, KC * T], BF16)
        SL = sb("SL", [128, NSLOT, 4096], BF16)
        identf = sb("identf", [128, 128], F32)
        identb = sb("identb", [128, 128], BF16)
        permb = sb("permb", [128, 128], BF16)
        onesf = sb("onesf", [128, 128], F32)
        gT = sb("gT", [128, 4, KC], F32)
        gnT = sb("gnT", [128, 8], F32)
        cwT = sb("cwT", [128, 3, 8], F32)
        stat = sb("stat", [128, 64], F32)
        Sf = sb("Sf", [128, H, 128], F32)
        MKV = sb("MKV", [128, 2, KC * 256], BF16)
        oo = sb("oo", [128, 2, T], F32)
        oall = oo[:, 0, :]
        osq = oo[:, 1, :]
        uprev = sb("uprev", [128, 8, 2], F32)
        PS = es.enter_context(nc.psum_tensor("PS", [128, 8, 512], F32))

        sem_list = []

        def sem_alloc(name):
            s = es.enter_context(nc.semaphore(name))
            sem_list.append(s)
            return s

        P = Prog(nc, sem_alloc)

        def psf(bank, n=512, off=0):
            return PS[:, bank, off:off + n]

        def psb(bank):
            return PS[:, bank, :].bitcast(BF16)

        def pe(mms, reads, writes):
            if _PE_LABELS is not None and not P.dead:
                import inspect
                fr = inspect.stack()[1]
                _PE_LABELS.append(("%s:%d" % (fr.function, fr.lineno), len(mms)))
            def fn(e, mms=mms):
                ins = None
                for m in mms:
                    if m[0] == "m":
                        ins = e.matmul(m[1], m[2], m[3], start=m[4], stop=m[5])
                    else:
                        ins = e.transpose(m[1], m[2], m[3])
                return ins
            return P.op("pe", fn, reads, writes)

        def dve(f, reads, writes):
            return P.op("dve", f, reads, writes)

        def act(f, reads, writes):
            return P.op("act", f, reads, writes)

        wv = lambda w: w.rearrange("(kc p) n -> p kc n", p=128)
        blocks = []

        def blk_std(w, col0):
            blocks.append((wv(w)[:, :, col0:col0 + 256], (16, 256)))
            return len(blocks) - 1

        def blk_a(w, kh, cb):
            blocks.append((wv(w)[:, 8 * kh:8 * kh + 8, 512 * cb:512 * cb + 512], (8, 512)))
            return len(blocks) - 1

        def blk_down(g, half):
            blocks.append((wv(w_down)[:, 4 * g:4 * g + 4, 1024 * half:1024 * half + 1024], (4, 1024)))
            return len(blocks) - 1

        CQ, CK, CV, CG, CBG, CCG, CHIN = 0, 1024, 2048, 3072, 4096, 5120, 6144
        W = {}
        W["pre_k"] = []; W["pre_v"] = []; W["pre_cg"] = []; W["pre_hin"] = []
        for i in range(4):
            W["pre_k"].append(blk_std(w_in, CK + 256 * i)); W["pre_v"].append(blk_std(w_in, CV + 256 * i))
            W["pre_cg"].append(blk_std(w_in, CCG + 256 * i)); W["pre_hin"].append(blk_std(w_in, CHIN + 256 * i))
        W["q"] = []; W["k"] = []; W["v"] = []; W["g"] = []; W["mkv"] = []
        for i in range(4):
            W["v"].append(blk_std(w_in, CV + 256 * i)); W["k"].append(blk_std(w_in, CK + 256 * i))
            W["mkv"].append((0, 2 * i, blk_std(w_mk, 256 * (2 * i))))
            W["q"].append(blk_std(w_in, CQ + 256 * i))
            W["mkv"].append((0, 2 * i + 1, blk_std(w_mk, 256 * (2 * i + 1))))
            W["g"].append(blk_std(w_in, CG + 256 * i))
            W["mkv"].append((1, 2 * i, blk_std(w_mv, 256 * (2 * i))))
            W["mkv"].append((1, 2 * i + 1, blk_std(w_mv, 256 * (2 * i + 1))))
        W["cg"] = []; W["hin"] = []; W["bg"] = []
        for i in range(4):
            W["cg"].append(blk_std(w_in, CCG + 256 * i)); W["hin"].append(blk_std(w_in, CHIN + 256 * i))
            W["bg"].append(blk_std(w_in, CBG + 256 * i))
        W["mixo"] = [[blk_a(w_mix_out, kh, cb) for kh in range(2)] for cb in range(4)]
        W["xq"] = [blk_std(w_xq, 256 * i) for i in range(8)]
        W["xo"] = [[blk_a(w_xo, kh, cb) for kh in range(2)] for cb in range(4)]
        W["gate"] = []; W["up"] = []; W["down"] = []
        for g in range(NG):
            gl, ul = [], []
            for j in range(2):
                gl.append(blk_std(w_gate, 512 * g + 256 * j)); ul.append(blk_std(w_up, 512 * g + 256 * j))
            W["gate"].append(gl); W["up"].append(ul)
            W["down"].append([blk_down(g, j) for j in range(2)])
        wstate = {"issued": 0, "cur": -1}

        def wslot(i):
            assert i == wstate["cur"] + 1 or i == wstate["cur"], (i, wstate)
            wstate["cur"] = i
            while wstate["issued"] < len(blocks) and wstate["issued"] <= i + NSLOT - 2:
                j = wstate["issued"]
                ap, shp = blocks[j]
                dst = SL[:, j % NSLOT, :].rearrange("p (a b) -> p a b", a=shp[0])
                wt = P.dma("pool", dst, ap, reads=([("hprev", 1)] if j < NSLOT - 1 else ()), writes=[("slot", j % NSLOT)])
                if wt is not None:
                    P.weight_tokens.add(wt)
                wstate["issued"] += 1
            shp = blocks[i][1]
            return SL[:, i % NSLOT, :].rearrange("p (a b) -> p a b", a=shp[0]), ("slot", i % NSLOT)

        P.dma("sp", identf[:], c_ident, writes=["identf"])
        P.dma("pool", identb[:], c_ident, writes=["identb"])
        P.dma("pool", permb[:], c_perm, writes=["permb"])
        gst = oo[:, :, :].rearrange("p a t -> p (a t)")[:, 0:128]
        for i, g in enumerate((ln_mix_g, ln_mem_g, ln_xa_g, ln_ffn_g)):
            P.dma("sp", gst[16 * i:16 * i + 16, :], g.rearrange("(kc p) -> kc p", p=128), writes=["gst"])
        P.dma("sp", gst[64:72, :], ret_gn_g.rearrange("(kc p) -> kc p", p=128), writes=["gst"])
        P.dma("sp", gst[72:96, :], conv_w.rearrange("k (kc p) -> (k kc) p", p=128), writes=["gst"])
        pe([("t", psf(0, 96), gst[0:96, :], identf[0:96, 0:96])], ["gst", "identf"], [("ps", 0)])
        act(lambda e: e.activation(out=gT[:, :, :].rearrange("p a k -> p (a k)"), in_=psf(0, 64), func=AF.Copy), [("ps", 0)], ["gT"])
        act(lambda e: e.activation(out=gnT[:], in_=psf(0, 8, 64), func=AF.Copy), [("ps", 0)], ["gnT"])
        act(lambda e: e.activation(out=cwT[:, :, :].rearrange("p k c -> p (k c)"), in_=psf(0, 24, 72), func=AF.Copy), [("ps", 0)], ["cwT"])
        dve(lambda e: e.memset(onesf[:], 1.0 / 128.0), [], ["onesf"])
        memx = MKV[:, :, :].rearrange("p a x -> p (a x)").bitcast(F32).rearrange("p (t d) -> p t d", t=2)

        rx_off = [0]

        def rx(n, dt=F32, shape=None):
            nf = n if dt == F32 else (n + 1) // 2
            o = rx_off[0]
            rx_off[0] += nf
            assert rx_off[0] <= NT * D, rx_off[0]
            v = RX[:, o:o + nf]
            if dt == BF16:
                v = v.bitcast(BF16)[:, 0:n]
            return v

        rot = rx(2 * T).rearrange("p (a t) -> p a t", a=2)
        dmask = rx(2 * 128).rearrange("p (h l) -> p h l", h=2)
        smask = rx(2 * 128).rearrange("p (h l) -> p h l", h=2)
        qdec = rx(2 * 128).rearrange("p (h l) -> p h l", h=2)
        qdecs = rx(2 * 128).rearrange("p (h l) -> p h l", h=2)
        kdec = rx(H)
        kdecp = rx(H * 8).rearrange("p (h c) -> p h c", h=H)
        rmd = rx(H * (NSEQ + 1)).rearrange("p (h s) -> p h s", h=H)
        qT = rx(2 * T, BF16).rearrange("p (a t) -> p a t", a=2)
        kT = rx(2 * T, BF16).rearrange("p (a t) -> p a t", a=2)
        sgT = rx(2 * T, BF16).rearrange("p (a t) -> p a t", a=2)
        vtok = rx(NT * 256, BF16).rearrange("p (t c) -> p t c", t=NT)
        raw = rx(T, BF16)
        t1 = rx(T)
        t2 = rx(T)
        stm = rx(2 * 128, BF16).rearrange("p (a l) -> p a l", a=2)
        kdt = rx(2 * 128, BF16).rearrange("p (a l) -> p a l", a=2)
        qds_ = rx(128, BF16)
        inn = rx(128)
        un0 = rx_off[0]
        xload = rx(2 * D).rearrange("p (a d) -> p a d", a=2)
        sqj = rx(D, BF16)
        enda = rx_off[0]
        rx_off[0] = un0
        SH = 8
        ssf2 = rx(2 * SH * 128).rearrange("p (a s e) -> p a s e", a=2, s=SH)
        ssb = rx(SH * 128, BF16).rearrange("p (s e) -> p s e", s=SH)
        Sprev = rx(2 * 8 * 128, BF16).rearrange("p (a c e) -> p a c e", a=2, c=8)
        Stmp = rx(2 * 128).rearrange("p (a e) -> p a e", a=2)
        qdall = rx(TP, BF16)
        vexp_off = rx_off[0]
        vexp = rx(NSEQ * 128, BF16).rearrange("p (s e) -> p s e", s=NSEQ)
        ktkd = rx(128, BF16)
        kdta = rx(8 * 128, BF16).rearrange("p (c d) -> p c d", c=8)
        osq_alt = RX[:, vexp_off:vexp_off + T]
        stma = rx(9 * 128, BF16).rearrange("p (c l) -> p c l", c=9)
        endb = rx_off[0]
        rx_off[0] = un0
        upad = rx(2 * (TP + 2))
        upads = rx(NSEQ * 10).rearrange("p (s k) -> p s k", s=NSEQ)
        cacc = rx(T)
        scT = rx(8 * 32).rearrange("p (c k) -> p c k", c=8)
        sc_in = rx(1024)
        sc_out = rx(1024)
        nbt = rx(32)
        endc = rx_off[0]
        rx_off[0] = max(enda, endb, endc)
        assert rx_off[0] <= NT * D, rx_off[0]

        P.dma("sp", kdec, c_kdec, writes=["kdec"])
        P.dma("sp", kdecp, c_kdecp, writes=["kdecp"])
        P.dma("sp", rmd, c_rmd, writes=["rmd"])

        hT = RH[:, :].rearrange("p (k t) -> p k t", k=KC)
        hprevT = RA[:, 0:KC * TP].rearrange("p (k t) -> p k t", k=KC)
        catT = RA[:, :].rearrange("p (k t) -> p k t", k=KC)
        xres = RX[:, :].rearrange("p (t d) -> p t d", t=NT)

        xn2 = oo[:, :, :].rearrange("p a t -> p (a t)").bitcast(BF16)[:, 0:2 * D].rearrange("p (a d) -> p a d", a=2)

        def norm_tile(i, xt, xr, gidx, dstT, dkey):
            j = i % 2
            so = 4 * j
            act(lambda e: e.activation(out=xn2[:, j, :], in_=xt, func=AF.Square, accum_out=stat[:, so:so + 1]), xr, [("xn", j), ("ss", j), "gst"])
            act(lambda e: e.activation(out=stat[:, so + 1:so + 2], in_=stat[:, so:so + 1], func=AF.Sqrt, bias=RMS_EPS, scale=1.0 / D), [("ss", j)], [("ms", j)])
            dve(lambda e: e.reciprocal(out=stat[:, so + 2:so + 3], in_=stat[:, so + 1:so + 2]), [("ms", j)], [("rstd", j)])
            if j == 0:
                act(lambda e: e.activation(out=xn2[:, j, :], in_=xt, func=AF.Copy, scale=stat[:, so + 2:so + 3]), xr + [("rstd", j)], [("xn", j)])
            else:
                dve(lambda e: e.tensor_scalar(out=xn2[:, j, :], in0=xt, scalar1=stat[:, so + 2:so + 3], scalar2=None, op0=ALU.mult),
                    xr + [("rstd", j)], [("xn", j)])
            b0 = 4 + 2 * j
            pbb = PS[:, b0:b0 + 2, :].rearrange("p b n -> p (b n)").bitcast(BF16)
            pe([("t", pbb[:, 128 * kc:128 * kc + 128], xn2[:, j, 128 * kc:128 * kc + 128], identb[:]) for kc in range(KC)],
               [("xn", j), "identb"], [("ps", b0), ("ps", b0 + 1)])
            dve(lambda e: e.tensor_tensor(out=dstT[:, :, 128 * i:128 * i + 128], in0=pbb.rearrange("p (k t) -> p k t", k=KC),
                                          in1=gT[:, gidx, :].unsqueeze(2).broadcast_to([128, KC, 128]), op=ALU.mult),
                [("ps", b0), ("ps", b0 + 1), "gT"], [(dkey, i)])

        def norm_tiles(src_aps, gidx, dstT, dkey):
            for i in range(len(src_aps)):
                xt = xload[:, i % 2, :]
                P.dma("sp", xt, src_aps[i], writes=[("xload", i % 2)])
                norm_tile(i, xt, [("xload", i % 2)], gidx, dstT, dkey)

        def proj_b(slot, slot_key, c, src, src_reads, tbs, banks):
            mms = []
            for (t0, n), b in zip(tbs, banks):
                pass
            for kc in range(KC):
                for (t0, n), b in zip(tbs, banks):
                    mms.append(("m", psf(b, n), slot[:, kc, 128 * c:128 * c + 128], src[:, kc, t0:t0 + n],
                                kc == 0, kc == KC - 1))
            return pe(mms, [slot_key] + src_reads, [("ps", b) for b in banks[:len(tbs)]])

        bank_sets = [[0, 1, 2], [3, 4, 5]]
        bs_i = [0]

        def next_banks():
            b = bank_sets[bs_i[0] % 2]
            bs_i[0] += 1
            return b

        def pview(banks, ntok):
            b0 = banks[0]
            return PS[:, b0:b0 + 3, :].rearrange("p b n -> p (b n)")[:, 0:ntok]

        def pkeys(banks, tbs):
            return [("ps", b) for b in banks[:len(tbs)]]

        def rotary_pre(banks, tbs, tab, scale):
            ntok = tbs[-1][0] + tbs[-1][1]
            pv = pview(banks, ntok)
            pk = pkeys(banks, tbs)
            act(lambda e: e.activation(out=raw[:, 0:ntok], in_=pv, func=AF.Copy), pk, ["raw"])
            dve(lambda e: e.scalar_tensor_tensor(out=t1[:, 0:ntok], in0=pv, scalar=scale, in1=tab[:, 0, 0:ntok],
                                                 op0=ALU.mult, op1=ALU.mult), pk + ["rot"], ["t1"])

        def rotary_post(banks, tbs, tab, scale, dst, dkey):
            ntok = tbs[-1][0] + tbs[-1][1]
            pv = pview(banks, ntok)
            pk = pkeys(banks, tbs)
            pe([("m", psf(b, n), permb[:], raw[:, t0:t0 + n], True, True) for (t0, n), b in zip(tbs, banks)],
               ["raw", "permb"], pk)
            dve(lambda e: e.scalar_tensor_tensor(out=t2[:, 0:ntok], in0=pv, scalar=scale, in1=tab[:, 1, 0:ntok],
                                                 op0=ALU.mult, op1=ALU.mult), pk + ["rot"], ["t2"])
            dve(lambda e: e.tensor_tensor(out=dst[:, 0:ntok], in0=t1[:, 0:ntok], in1=t2[:, 0:ntok], op=ALU.add),
                ["t1", "t2"], [("rotout", dkey)])

        def rotary(banks, tbs, tab, scale, dst, dkey):
            rotary_pre(banks, tbs, tab, scale)
            rotary_post(banks, tbs, tab, scale, dst, dkey)

        P.phase(1)
        P.dma("sp", rot[:, :, 0:TP], c_rotp, writes=["rot"])
        RAxn = RA
        xn_pre = RH[:, 0:8 * D].rearrange("p (t d) -> p t d", t=8)
        norm_tiles([xprev[128 * i:128 * i + 128, :] for i in range(8)], 0, hprevT, "hprev")
        for i in range(2):
            P.dma("sp", memx[:, i, :], mem[128 * i:128 * i + 128, :], writes=[("xl2", i)])
        P.phase(1.2)
        dve(lambda e: e.memset(Sf[:], 0.0), [], ["Sf"])
        kprevT = kT
        srcs = [xp[128 * i:128 * i + 128, :] for i in range(8)] + [xs[:, :]]
        main_i = [0]

        def main_norm_tiles(k):
            for _ in range(k):
                i = main_i[0]
                if i >= NT:
                    return
                main_i[0] += 1
                xt = xload[:, i % 2, :]
                P.dma("sp", xt, srcs[i], writes=[("xload", i % 2)])
                norm_tile(i, xt, [("xload", i % 2)], 0, hT, "hmain")

        P.phase(1.22)
        XB0, YB0 = [0, 1, 2], [3, 4, 5]
        cgp = stat[:, 8:8 + 16].rearrange("p (c k) -> p c k", c=8)
        for hp in range(4):
            slot, sk = wslot(W["pre_k"][hp])
            P.phase(1.25)
            hpr = [("hprev", i_) for i_ in range(8)]
            proj_b(slot, sk, 0, hprevT, hpr, TBP, XB0)
            proj_b(slot, sk, 1, hprevT, hpr, TBP, YB0)
            rotary(XB0, TBP, rot, 128.0 ** -0.5, kprevT[:, 0, :], ("k", 0))
            rotary_pre(YB0, TBP, rot, 128.0 ** -0.5)
            slot, sk = wslot(W["pre_v"][hp])
            for t in range(8):
                b = 6 + (t % 2)
                mms = [("m", psf(b, 256), hprevT[:, kc, 128 * t:128 * t + 128], slot[:, kc, :], kc == 0, kc == KC - 1)
                       for kc in range(KC)]
                pe(mms, [sk] + hpr, [("ps", b)])
                act(lambda e, b=b, t=t: e.activation(out=vtok[:, t, :], in_=psf(b, 256), func=AF.Copy),
                    [("ps", b)], [("vtok", t)])
            rotary_post(YB0, TBP, rot, 128.0 ** -0.5, kprevT[:, 1, :], ("k", 1))
            P.phase(1.6)
            main_norm_tiles(2)
            kdas = []
            for c in range(2):
                h = 2 * hp + c
                b = 6 + c
                pb = psb(b)
                pe([("t", pb[:, 128 * ch:128 * ch + 128], kprevT[:, c, 128 * ch:128 * ch + 128], identb[:]) for ch in range(8)],
                   [("rotout", ("k", c)), "identb"], [("ps", b)])
                kda = qT[:, c, 0:TP].rearrange("p (a d) -> p a d", a=8)
                kdas.append(kda)
                dve(lambda e, pb=pb, h=h, kda=kda: e.tensor_tensor(
                    out=kda, in0=pb[:, :].rearrange("p (a d) -> p a d", a=8),
                    in1=kdecp[:, h, :].unsqueeze(2).broadcast_to([128, 8, 128]), op=ALU.mult), [("ps", b), "kdecp"], [("kda", c)])

            def conv_tail(which, name):
                slot, sk = wslot(W[name][hp])
                for c in range(2):
                    ch = 2 * hp + c
                    b = 4 + (ch % 2)
                    mms = [("m", psf(b, 2), slot[:, kc, 128 * c:128 * c + 128], hprevT[:, kc, TP - 2:TP], kc == 0, kc == KC - 1)
                           for kc in range(KC)]
                    pe(mms, [sk] + [("hprev", i_) for i_ in range(8)], [("ps", b)])
                    if which == 0:
                        act(lambda e, b=b, ch=ch: e.activation(out=cgp[:, ch, :], in_=psf(b, 2), func=AF.Copy),
                            [("ps", b)], [("cgp", ch)])
                    else:
                        dve(lambda e, b=b, ch=ch: e.tensor_tensor(out=uprev[:, ch, :], in0=cgp[:, ch, :], in1=psf(b, 2),
                                                                  op=ALU.mult), [("ps", b), ("cgp", ch)], [("uprev", ch)])

            conv_tail(0, "pre_cg")
            for c in range(2):
                h = 2 * hp + c
                sb_bank = 2 + c
                kda = kdas[c]
                pe([("m", psf(sb_bank, 128), kda[:, ch, :], vtok[:, ch, 128 * c:128 * c + 128], ch == 0, ch == 7) for ch in range(8)],
                   [("kda", c)] + [("vtok", ch) for ch in range(8)], [("ps", sb_bank)])
                act(lambda e, h=h, sb_bank=sb_bank: e.activation(out=Sf[:, h, :], in_=psf(sb_bank, 128), func=AF.Copy),
                    [("ps", sb_bank)], [("Sf", h)])
            conv_tail(1, "pre_hin")
        main_norm_tiles(NT)
        slot_v0, sk_v0 = wslot(W["v"][0])
        for t in range(NT):
            b = 6 + (t % 2)
            mms = [("m", psf(b, 256), hT[:, kc, 128 * t:128 * t + 128], slot_v0[:, kc, :], kc == 0, kc == KC - 1)
                   for kc in range(KC)]
            pe(mms, [sk_v0] + [("hmain", i_) for i_ in range(NT)], [("ps", b)])
            act(lambda e, b=b, t=t: e.activation(out=vtok[:, t, :], in_=psf(b, 256), func=AF.Copy),
                [("ps", b)], [("vtok", t)])
        slot_k0, sk_k0 = wslot(W["k"][0])
        hmk = [("hmain", i_) for i_ in range(NT)]
        proj_b(slot_k0, sk_k0, 0, hT, hmk, TB, [0, 1, 2])
        proj_b(slot_k0, sk_k0, 1, hT, hmk, TB, [3, 4, 5])
        P.fence()

        P.phase(2)
        P.dma("sp", rot, c_rot, writes=["rot"])

        P.phase(3)
        def proj_v(slot, sk, extra=()):
            for t in range(NT):
                b = 6 + (t % 2)
                mms = [("m", psf(b, 256), hT[:, kc, 128 * t:128 * t + 128], slot[:, kc, :], kc == 0, kc == KC - 1)
                       for kc in range(KC)]
                pe(mms, [sk] + list(extra), [("ps", b)])
                act(lambda e, b=b, t=t: e.activation(out=vtok[:, t, :], in_=psf(b, 256), func=AF.Copy),
                    [("ps", b)], [("vtok", t)])

        def stage1_t(c, h):
            kk = [("rotout", ("k", c))]
            pb = psb(6)
            pe([("t", pb[:, 128 * ch:128 * ch + 128], kT[:, c, 128 * ch:128 * ch + 128], identb[:]) for ch in range(8)],
               kk + ["identb"], [("ps", 6)])
            act(lambda e, pb=pb, h=h: e.activation(out=kdta.rearrange("p c d -> p (c d)"), in_=pb[:, :], func=AF.Copy,
                                                   scale=kdec[:, h:h + 1]), [("ps", 6), "kdec"], ["kdta"])

        def stage1_a(c, h):
            pe([("m", psf(6 + ch // 4, 128, 128 * (ch % 4)), kdta[:, ch, :], vtok[:, ch, 128 * c:128 * c + 128], True, True) for ch in range(8)],
               ["kdta"] + [("vtok", ch) for ch in range(8)], [("ps", 6), ("ps", 7)])
            g128 = math.exp(128.0 * LG[h])
            act(lambda e: e.activation(out=Sprev[:, c, 0, :], in_=Sf[:, h, :], func=AF.Copy), [("Sf", h)], [("Sprev", c, 0)])
            for ch in range(8):
                ab = 6 + ch // 4
                src = Sf[:, h, :] if ch % 2 == 0 else Stmp[:, c, :]
                dst = Stmp[:, c, :] if ch % 2 == 0 else Sf[:, h, :]
                sk_, dk_ = (("Sf", h), ("Stmp", c)) if ch % 2 == 0 else (("Stmp", c), ("Sf", h))
                dve(lambda e, src=src, dst=dst, ab=ab, ch=ch: e.scalar_tensor_tensor(
                    out=dst, in0=src, scalar=g128, in1=psf(ab, 128, 128 * (ch % 4)), op0=ALU.mult, op1=ALU.add),
                    [("ps", ab), sk_], [dk_])
                if ch < 7:
                    act(lambda e, dst=dst, ch=ch: e.activation(out=Sprev[:, c, ch + 1, :], in_=dst, func=AF.Copy),
                        [dk_], [("Sprev", c, ch + 1)])
            P.dma("sp", srp[h], Sf[:, h, :], reads=[("Sf", h)], track_out=True)

        def stage3(c, h):
            oall = oo[:, c, :]
            qk = [("rotout", ("q", c))]
            kk = [("rotout", ("k", c))]
            for hf in range(2):
                P.dma("sp", ssf2[:, hf], sret[SH * hf:SH * hf + SH, h].rearrange("s d e -> d s e"), writes=[("ssf", hf)])
            pb = psb(3)
            pe([("t", pb[:, 0:128], kT[:, c, TP:T], identb[:])], kk + ["identb"], [("ps", 3)])
            act(lambda e, pb=pb: e.activation(out=ktkd, in_=pb[:, 0:128], func=AF.Copy, scale=rmd[:, h, NSEQ:NSEQ + 1]),
                [("ps", 3), "rmd"], ["ktkd"])
            dve(lambda e: e.tensor_tensor(
                out=vexp, in0=vtok[:, 8, 128 * c:128 * c + 128].unsqueeze(1).broadcast_to([128, NSEQ, 128]),
                in1=rmd[:, 0, 0:NSEQ].unsqueeze(2).broadcast_to([128, NSEQ, 128]), op=ALU.mult), [("vtok", 8), "rmd"], ["vexp"])
            dve(lambda e: e.tensor_tensor(
                out=qdall.rearrange("p (a l) -> p a l", l=128), in0=qT[:, c, 0:TP].rearrange("p (a l) -> p a l", l=128),
                in1=qdec[:, c, :].unsqueeze(1).broadcast_to([128, 8, 128]), op=ALU.mult), qk + ["qdec"], ["qdall"])
            dve(lambda e: e.tensor_tensor(out=qds_, in0=qT[:, c, TP:T], in1=qdecs[:, c, :], op=ALU.mult), qk + ["qdecs"], ["qds"])
            pe([("m", psf(6 + ch // 4, 128, 128 * (ch % 4)), kT[:, c, 128 * ch:128 * ch + 128], qT[:, c, 128 * ch:128 * ch + 128], True, True) for ch in range(8)]
               + [("m", psf(2, 128), kT[:, c, TP:T], qT[:, c, TP:T], True, True)], qk + kk, [("ps", 6), ("ps", 7), ("ps", 2)])
            for half in range(2):
                dve(lambda e, half=half: e.tensor_tensor(
                    out=stma[:, 4 * half:4 * half + 4, :], in0=psf(6 + half).rearrange("p (a l) -> p a l", l=128),
                    in1=dmask[:, c, :].unsqueeze(1).broadcast_to([128, 4, 128]), op=ALU.mult), [("ps", 6 + half), "dmask"], [("stma", half)])
            dve(lambda e: e.tensor_tensor(out=stma[:, 8, :], in0=psf(2, 128), in1=smask[:, c, :], op=ALU.mult), [("ps", 2), "smask"], [("stma", 2)])
            mms = []
            for ch in range(8):
                o_ = psf(ch // 4, 128, 128 * (ch % 4))
                mms.append(("m", o_, vtok[:, ch, 128 * c:128 * c + 128], stma[:, ch, :], True, False))
                mms.append(("m", o_, Sprev[:, c, ch, :], qdall[:, 128 * ch:128 * ch + 128], False, True))
            pe(mms, [("vtok", ch) for ch in range(8)] + [("stma", 0), ("stma", 1), "qdall"] + [("Sprev", c, ch) for ch in range(8)],
               [("ps", 0), ("ps", 1)])
            act(lambda e: e.activation(out=oall[:, 0:TP], in_=PS[:, 0:2, :].rearrange("p b n -> p (b n)"), func=AF.Copy),
                [("ps", 0), ("ps", 1)], [("oall", c, 0), ("xl2", 0)])
            pe([("m", psf(4, 128), vtok[:, 8, 128 * c:128 * c + 128], stma[:, 8, :], True, True)], [("vtok", 8), ("stma", 2)], [("ps", 4)])
            act(lambda e: e.activation(out=inn, in_=psf(4, 128), func=AF.Copy), [("ps", 4)], ["inn"])
            g8 = math.exp(8.0 * LG[h])
            for hf in range(2):
                s0 = SH * hf
                ssf = ssf2[:, hf]
                act(lambda e, ssf=ssf: e.activation(out=ssb.rearrange("p s e -> p (s e)"), in_=ssf.rearrange("p s e -> p (s e)"), func=AF.Copy),
                    [("ssf", hf)], ["ssb"])
                pe([("m", psf(5, 8, 8 * (s0 + s_)), ssb[:, s_, :], qds_[:, 8 * (s0 + s_):8 * (s0 + s_) + 8], True, True) for s_ in range(SH)],
                   ["ssb", "qds"], [("ps", 5)])
                for q4 in range(2):
                    b = 2 + q4
                    sq0 = s0 + 4 * q4
                    pe([("m", psf(b), ktkd, vexp[:, sq0:sq0 + 4, :].rearrange("p s e -> p (s e)"), True, True)], ["ktkd", "vexp"], [("ps", b)])
                    dve(lambda e, b=b, q4=q4, ssf=ssf: e.scalar_tensor_tensor(
                        out=ssf[:, 4 * q4:4 * q4 + 4, :].rearrange("p s e -> p (s e)"),
                        in0=ssf[:, 4 * q4:4 * q4 + 4, :].rearrange("p s e -> p (s e)"), scalar=g8, in1=psf(b),
                        op0=ALU.mult, op1=ALU.add), [("ps", b), ("ssf", hf)], [("ssf", hf)])
                P.dma("sp", srs[s0:s0 + SH, h].rearrange("s d e -> d s e"), ssf, reads=[("ssf", hf)], track_out=True)
            dve(lambda e: e.tensor_tensor(out=oall[:, TP:T], in0=psf(5, 128), in1=inn, op=ALU.add), [("ps", 5), "inn"], [("oall", c, 1), ("xl2", 0)])
            osq_, okeys = gn_bufs(c)
            dve(lambda e: e.tensor_tensor(out=osq_, in0=oall, in1=oall, op=ALU.mult), [("oall", c, 0), ("oall", c, 1)], okeys)

        def gn_bufs(c):
            if c == 0:
                return t2, ["t2"]
            return osq_alt, ["vexp", "ktkd", "kdta"]

        GXB, GYB = [0, 1, 2], [3, 4, 5]

        def gn_mm(c):
            oall = oo[:, c, :]
            osq_, okeys = gn_bufs(c)
            oa = [("oall", c, 0), ("oall", c, 1)]
            pe([("m", psf(b, n), onesf[:], oall[:, t0:t0 + n], True, True) for (t0, n), b in zip(TB, GXB)], oa + ["onesf"], pkeys(GXB, TB))
            pe([("m", psf(b, n), onesf[:], osq_[:, t0:t0 + n], True, True) for (t0, n), b in zip(TB, GYB)], okeys + ["onesf"], pkeys(GYB, TB))

        OAK = ["vexp", "ktkd", "kdta"]

        def gn_head(c):
            mv_, qv_ = pview(GXB, T), pview(GYB, T)
            act(lambda e: e.activation(out=t1, in_=mv_, func=AF.Copy), pkeys(GXB, TB), ["t1"])
            dve(lambda e: e.tensor_tensor(out=t2, in0=t1, in1=t1, op=ALU.mult), ["t1"], ["t2"])
            dve(lambda e: e.tensor_tensor(out=t2, in0=qv_, in1=t2, op=ALU.subtract), pkeys(GYB, TB) + ["t2"], ["t2"])

        def gn_head1_a():
            mv_ = pview(GXB, T)
            act(lambda e: e.activation(out=osq_alt, in_=mv_, func=AF.Copy), pkeys(GXB, TB), OAK)

        def gn_head1_b():
            qv_ = pview(GYB, T)
            dve(lambda e: e.tensor_tensor(out=t2, in0=osq_alt, in1=osq_alt, op=ALU.mult), OAK, ["t2"])
            dve(lambda e: e.tensor_tensor(out=t2, in0=qv_, in1=t2, op=ALU.subtract), pkeys(GYB, TB) + ["t2"], ["t2"])

        def gn_tail(c, h, mbuf=None, mkeys=None):
            oall = oo[:, c, :]
            oa = [("oall", c, 0), ("oall", c, 1)]
            mb = t1 if mbuf is None else mbuf
            mk_ = ["t1"] if mkeys is None else mkeys
            act(lambda e: e.activation(out=t2, in_=t2, func=AF.Ln, bias=GN_EPS, scale=1.0), ["t2"], ["t2"])
            act(lambda e: e.activation(out=t2, in_=t2, func=AF.Exp, scale=-0.5), ["t2"], ["t2"])
            dve(lambda e: e.tensor_tensor(out=t1, in0=oall, in1=mb, op=ALU.subtract), oa + mk_ + ["t1"], ["t1"])
            dve(lambda e: e.scalar_tensor_tensor(out=t1, in0=t1, scalar=gnT[:, h:h + 1], in1=t2, op0=ALU.mult, op1=ALU.mult),
                ["t1", "t2", "gnT"], ["t1"])
            dve(lambda e: e.tensor_tensor(out=catT[:, h, :], in0=t1, in1=sgT[:, c, :], op=ALU.mult), ["t1", ("sgT", c)], [("cat", h)])

        mkT = MKV[:, 0, :].rearrange("p (k t) -> p k t", k=KC)
        mvb = MKV[:, 1, :].rearrange("p (t d) -> p t d", t=2)
        csc = RA[:, 8 * T:16 * T]
        hmT = csc[:, 0:4096].rearrange("p (k t) -> p k t", k=KC)
        xn_mem = csc[:, 4096:8192].rearrange("p (t d) -> p t d", t=2)
        mst = csc[:, 8192:9216].bitcast(F32).rearrange("p (a c) -> p a c", a=2)
        def norm_mem():
            for i in range(2):
                xt = memx[:, i, :]
                act(lambda e, xt=xt, i=i: e.activation(out=xn_mem[:, i, :], in_=xt, func=AF.Square, accum_out=stat[:, 0:1]), [], [("xnm", i), "ss", ("xl2", i)])
                act(lambda e: e.activation(out=stat[:, 1:2], in_=stat[:, 0:1], func=AF.Sqrt, bias=RMS_EPS, scale=1.0 / D), ["ss"], ["ms"])
                dve(lambda e: e.reciprocal(out=stat[:, 2:3], in_=stat[:, 1:2]), ["ms"], ["rstd"])
                dve(lambda e, xt=xt, i=i: e.tensor_scalar(out=xn_mem[:, i, :], in0=xt, scalar1=stat[:, 2:3], scalar2=None,
                                                          op0=ALU.mult), ["rstd"], [("xnm", i), ("xl2", i)])
            for kc in range(KC):
                bank = 6 + (kc % 2)
                pb = psb(bank)
                pe([("t", pb[:, 128 * j:128 * j + 128], xn_mem[:, j, 128 * kc:128 * kc + 128], identb[:]) for j in range(2)],
                   [("xnm", 0), ("xnm", 1), "identb"], [("ps", bank)])
                act(lambda e, pb=pb, kc=kc: e.activation(out=hmT[:, kc, :], in_=pb[:, 0:256], func=AF.Copy, scale=gT[:, 1, kc:kc + 1]),
                    [("ps", bank), "gT"], [("hmT", kc)])

        hm_reads = [("hmT", kc) for kc in range(KC)]

        def memkv_block(which, i, blk):
            dst = mko if which == 0 else mvo
            slot, sk = wslot(blk)
            for t in range(2):
                b = 6 + t
                mms = [("m", psf(b, 256), hmT[:, kc, 128 * t:128 * t + 128], slot[:, kc, :], kc == 0, kc == KC - 1) for kc in range(KC)]
                pe(mms, [sk] + hm_reads, [("ps", b)])
                if t == 0:
                    j = i % 2
                    act(lambda e, b=b, j=j: e.activation(out=mst[:, j, :], in_=psf(b, 256), func=AF.Copy), [("ps", b)], [("mst", j)])
                    P.dma("sp", dst[:, 256 * i:256 * i + 256], mst[:, j, :], reads=[("mst", j)], track_out=True)
                if which == 1:
                    dve(lambda e, b=b, t=t, i=i: e.tensor_copy(out=mvb[:, t, 256 * i:256 * i + 256], in_=psf(b, 256)),
                        [("ps", b)], [("mvb", t, i), ("xl2", 0), ("xl2", 1)])
            if which == 0:
                for c in range(2):
                    b = 6 + c
                    mms = [("m", psf(b, 256), slot[:, kc, 128 * c:128 * c + 128], hmT[:, kc, :], kc == 0, kc == KC - 1) for kc in range(KC)]
                    pe(mms, [sk] + hm_reads, [("ps", b)])
                    act(lambda e, b=b, i=i, c=c: e.activation(out=mkT[:, 2 * i + c, :], in_=psf(b, 256), func=AF.Copy),
                        [("ps", b)], [("mkT", 2 * i + c), ("xl2", 0), ("xl2", 1)])

        XB_, YB_ = [0, 1, 2], [3, 4, 5]
        for hp in range(4):
            P.dma("sp", dmask, c_dmask[:, 2 * hp:2 * hp + 2, :], writes=["dmask"])
            P.dma("sp", smask, c_smask[:, 2 * hp:2 * hp + 2, :], writes=["smask"])
            P.dma("sp", qdec, c_qdec[:, 2 * hp:2 * hp + 2, :], writes=["qdec"])
            P.dma("sp", qdecs, c_qdecs[:, 2 * hp:2 * hp + 2, :], writes=["qdecs"])
            KS = 128.0 ** -0.5
            if hp > 0:
                slot, sk = wslot(W["k"][hp])
                proj_b(slot, sk, 0, hT, [], TB, XB_)
                proj_b(slot, sk, 1, hT, [], TB, YB_)
            rotary(XB_, TB, rot, KS, kT[:, 0, :], ("k", 0))
            rotary_pre(YB_, TB, rot, KS)
            if hp == 0:
                norm_mem()
            memkv_block(*W["mkv"][4 * hp + 0])
            rotary_post(YB_, TB, rot, KS, kT[:, 1, :], ("k", 1))
            slot, sk = wslot(W["q"][hp])
            stage1_t(0, 2 * hp)
            proj_b(slot, sk, 0, hT, [], TB, XB_)
            stage1_a(0, 2 * hp)
            proj_b(slot, sk, 1, hT, [], TB, YB_)
            rotary(XB_, TB, rot, 1.0, qT[:, 0, :], ("q", 0))
            rotary_pre(YB_, TB, rot, 1.0)
            stage1_t(1, 2 * hp + 1)
            memkv_block(*W["mkv"][4 * hp + 1])
            slot_g, sk_g = wslot(W["g"][hp])
            proj_b(slot_g, sk_g, 0, hT, [], TB, XB_)
            pv_ = pview(XB_, T)
            act(lambda e, pv_=pv_: e.activation(out=sgT[:, 0, :], in_=pv_, func=AF.Silu), pkeys(XB_, TB), [("sgT", 0)])
            stage1_a(1, 2 * hp + 1)
            rotary_post(YB_, TB, rot, 1.0, qT[:, 1, :], ("q", 1))
            proj_b(slot_g, sk_g, 1, hT, [], TB, YB_)
            pv_ = pview(YB_, T)
            act(lambda e, pv_=pv_: e.activation(out=sgT[:, 1, :], in_=pv_, func=AF.Silu), pkeys(YB_, TB), [("sgT", 1)])
            stage3(0, 2 * hp)
            memkv_block(*W["mkv"][4 * hp + 2])
            stage3(1, 2 * hp + 1)
            memkv_block(*W["mkv"][4 * hp + 3])
            gn_mm(0)
            gn_head(0)
            gn_mm(1)
            gn_head1_a()
            if hp < 3:
                slot, sk = wslot(W["v"][hp + 1])
                proj_v(slot, sk)
            gn_tail(0, 2 * hp)
            gn_head1_b()
            gn_tail(1, 2 * hp + 1, mbuf=osq_alt, mkeys=OAK)
        slot_cg0, sk_cg0 = wslot(W["cg"][0])
        cg0_banks = [next_banks(), next_banks()]
        for c in range(2):
            proj_b(slot_cg0, sk_cg0, c, hT, [], TB, cg0_banks[c])
        P.fence()

        P.phase(4)
        P.dma("sp", sc_in[0:32, :], sconv, writes=["sc_in"])

        def sconv_transposes():
            for chk in range(8):
                pe([("t", psf(6, 32), sc_in[0:32, 128 * chk:128 * chk + 128], identf[0:32, 0:32])], ["sc_in", "identf"], [("ps", 6)])
                act(lambda e, chk=chk: e.activation(out=scT[:, chk, :], in_=psf(6, 32), func=AF.Copy), [("ps", 6)], [("scT", chk)])

        for cp in range(4):
            if cp > 0:
                slot_cg, sk_cg = wslot(W["cg"][cp])
            cg_sb = [t1, t2]
            for c in range(2):
                if cp == 0:
                    banks = cg0_banks[c]
                else:
                    banks = next_banks()
                    proj_b(slot_cg, sk_cg, c, hT, [], TB, banks)
                for (t0, n), b in zip(TB, banks):
                    act(lambda e, b=b, n=n, t0=t0, c=c: e.activation(out=cg_sb[c][:, t0:t0 + n], in_=psf(b, n), func=AF.Copy),
                        [("ps", b)], [("cgsb", c, t0)])
            if cp == 0:
                sconv_transposes()
            slot_h, sk_h = wslot(W["hin"][cp])
            for c in range(2):
                chk = 2 * cp + c
                banks = next_banks()
                proj_b(slot_h, sk_h, c, hT, [], TB, banks)
                up = upad[:, c * (TP + 2):(c + 1) * (TP + 2)]
                dve(lambda e, up=up, chk=chk: e.tensor_copy(out=up[:, 0:2], in_=uprev[:, chk, :]), [("uprev", chk)], [("upad", c, -1)])
                for (t0, n), b in zip(TBP, banks):
                    dve(lambda e, b=b, n=n, t0=t0, c=c, up=up: e.tensor_tensor(
                        out=up[:, 2 + t0:2 + t0 + n], in0=cg_sb[c][:, t0:t0 + n], in1=psf(b, n), op=ALU.mult),
                        [("ps", b), ("cgsb", c, t0)], [("upad", c, t0)])
                ups = upads
                dve(lambda e, chk=chk: e.tensor_copy(out=upads[:, :, 0:2], in_=scT[:, chk, :].rearrange("p (s k) -> p s k", k=2)),
                    [("scT", chk)], [("upads", 0)])
                b = banks[2]
                dve(lambda e, b=b, c=c: e.tensor_tensor(
                    out=upads[:, :, 2:10], in0=cg_sb[c][:, TP:T].rearrange("p (s k) -> p s k", k=8),
                    in1=psf(b, 128).rearrange("p (s k) -> p s k", k=8), op=ALU.mult),
                    [("ps", b), ("cgsb", c, 1024)], [("upads", 1)])
                upr = [("upad", c, -1), ("upad", c, 0), ("upad", c, 512)]
                dve(lambda e, up=up, chk=chk: e.tensor_scalar(out=cacc[:, 0:TP], in0=up[:, 0:TP], scalar1=cwT[:, 0, chk:chk + 1],
                                                              scalar2=None, op0=ALU.mult), upr + ["cwT"], [("cacc", 0)])
                for k in (1, 2):
                    dve(lambda e, up=up, chk=chk, k=k: e.scalar_tensor_tensor(
                        out=cacc[:, 0:TP], in0=up[:, k:k + TP], scalar=cwT[:, k, chk:chk + 1], in1=cacc[:, 0:TP],
                        op0=ALU.mult, op1=ALU.add), upr + [("cacc", 0)], [("cacc", 0)])
                ca_s = cacc[:, TP:T].rearrange("p (s k) -> p s k", k=8)
                usr = [("upads", 0), ("upads", 1)]
                dve(lambda e, chk=chk: e.tensor_scalar(out=ca_s, in0=upads[:, :, 0:8], scalar1=cwT[:, 0, chk:chk + 1],
                                                       scalar2=None, op0=ALU.mult), usr + ["cwT"], [("cacc", 1)])
                for k in (1, 2):
                    dve(lambda e, chk=chk, k=k: e.scalar_tensor_tensor(
                        out=ca_s, in0=upads[:, :, k:k + 8], scalar=cwT[:, k, chk:chk + 1], in1=ca_s,
                        op0=ALU.mult, op1=ALU.add), usr + [("cacc", 1)], [("cacc", 1)])
                pe([("t", psf(7, 128)[0:2, :], up[:, TP:TP + 2], identf[:])], [("upad", c, 512), "identf"], [("ps", 7)])
                act(lambda e, chk=chk: e.activation(out=sc_out[0:2, 128 * chk:128 * chk + 128], in_=psf(7, 128)[0:2, :], func=AF.Copy),
                    [("ps", 7)], [("sc_out", chk)])
                dve(lambda e: e.tensor_copy(out=nbt.rearrange("p (s k) -> p s k", k=2), in_=upads[:, :, 8:10]), [("upads", 1)], ["nbt"])
                pe([("t", psf(7, 128)[0:32, :], nbt, identf[:])], ["nbt", "identf"], [("ps", 7)])
                act(lambda e, chk=chk: e.activation(out=sc_in[0:32, 128 * chk:128 * chk + 128], in_=psf(7, 128)[0:32, :], func=AF.Copy),
                    [("ps", 7)], [("sc_in2", chk), "sc_in"])
                dve(lambda e, c=c: e.tensor_copy(out=cg_sb[c][:, :], in_=cacc[:, :]), [("cacc", 0), ("cacc", 1), ("cgsb", c, 0), ("cgsb", c, 512), ("cgsb", c, 1024)],
                    [("cgsb", c, 0), ("cgsb", c, 512), ("cgsb", c, 1024)])
            slot_b, sk_b = wslot(W["bg"][cp])
            for c in range(2):
                chk = 2 * cp + c
                banks = next_banks()
                proj_b(slot_b, sk_b, c, hT, [], TB, banks)
                for (t0, n), b in zip(TB, banks):
                    dve(lambda e, b=b, n=n, t0=t0, c=c, chk=chk: e.tensor_tensor(
                        out=catT[:, 8 + chk, t0:t0 + n], in0=cg_sb[c][:, t0:t0 + n], in1=psf(b, n), op=ALU.mult),
                        [("ps", b), ("cgsb", c, t0)], [("cat", 8 + chk, t0)])
        P.dma("sp", scp, sc_out[0:2, :], reads=[("sc_out", i) for i in range(8)], track_out=True)
        P.dma("sp", scs, sc_in[0:32, :], reads=[("sc_in2", i) for i in range(8)], track_out=True)
        P.fence()

        P.phase(5)
        def out_proj(wkey, srcT, accumulate_first_from_dram, after_tile=None):
            if accumulate_first_from_dram is not None:
                for t in range(NT):
                    P.dma("sp", xres[:, t, :], accumulate_first_from_dram[t], writes=[("x", t, cb) for cb in range(4)])
            for cb in range(4):
                s0, k0 = wslot(W[wkey][cb][0])
                s1, k1 = wslot(W[wkey][cb][1])
                for t in range(NT):
                    b = t % 4
                    mms = []
                    for kc in range(KC):
                        s_ = s0 if kc < 8 else s1
                        mms.append(("m", psf(b), srcT[:, kc, 128 * t:128 * t + 128], s_[:, kc % 8, :], kc == 0, kc == KC - 1))
                    pe(mms, [k0, k1], [("ps", b)])
                    dve(lambda e, b=b, t=t, cb=cb: e.tensor_tensor(out=xres[:, t, 512 * cb:512 * cb + 512],
                                                                   in0=xres[:, t, 512 * cb:512 * cb + 512], in1=psf(b), op=ALU.add),
                        [("ps", b), ("x", t, cb)], [("x", t, cb)])
                    if cb == 3 and after_tile is not None and t >= 1:
                        after_tile(t - 1)
            if after_tile is not None:
                after_tile(NT - 1)

        def norm_hook(gidx):
            return lambda t: norm_tile(t, xres[:, t, :], [("x", t, cb) for cb in range(4)], gidx, hT, "hres")

        out_proj("mixo", catT, srcs, after_tile=norm_hook(2))
        P.fence()

        P.phase(6)
        qxT = RA[:, :].rearrange("p (k t) -> p k t", k=KC)
        for i in range(8):
            slot, sk = wslot(W["xq"][i])
            for c in range(2):
                banks = next_banks()
                proj_b(slot, sk, c, hT, [], TB, banks)
                for (t0, n), b in zip(TB, banks):
                    act(lambda e, b=b, n=n, t0=t0, i=i, c=c: e.activation(out=qxT[:, 2 * i + c, t0:t0 + n], in_=psf(b, n), func=AF.Copy),
                        [("ps", b)], [("qx", 2 * i + c, t0)])
        P.fence()

        P.phase(7)
        rh_off = [0]

        def rh(n, dt=BF16):
            nb = n if dt == BF16 else 2 * n
            o = rh_off[0]
            rh_off[0] += nb
            assert rh_off[0] <= KC * T, rh_off[0]
            v = RH[:, o:o + nb]
            if dt == F32:
                v = v.bitcast(F32)
            return v

        rh_xn0 = 0
        P.phase(8)
        oob2 = oo[:, :, :].rearrange("p a t -> p (a t)").bitcast(BF16)
        pn = oob2[:, 0:2048].rearrange("p (a t n) -> p a t n", a=2, t=4)
        sfb = Sf[:, :, :].rearrange("p h e -> p (h e)").bitcast(BF16)
        pT = sfb[:, 0:2048].rearrange("p (a c l) -> p a c l", a=2, c=2)
        rh_off[0] = 0
        kraw = rh(2 * 2 * D).rearrange("p (a t d) -> p a t d", a=2, t=2)
        vraw = rh(2 * 2 * D).rearrange("p (a t d) -> p a t d", a=2, t=2)
        kTs = rh(2 * 1024).rearrange("p (a x) -> p a x", a=2)
        for s_ in range(2):
            P.dma("pool", kraw[:, s_], ck[s_].rearrange("(t p) d -> p t d", p=128), writes=[("kraw", s_)])
            P.dma("pool", vraw[:, s_], cv[s_].rearrange("(t p) d -> p t d", p=128), writes=[("vraw", s_)])
        XS = 512.0 ** -0.5
        oT = qxT
        combos = [(hd, g4) for hd in range(4) for g4 in range(2)]

        def h2_a(i):
            hd, g4 = combos[i]
            b0 = 2 * (i % 2)
            mms = []
            for tt in range(4):
                t = 4 * g4 + tt
                o_ = psf(b0 + tt // 2, 256, 256 * (tt % 2))
                for dc in range(4):
                    mms.append(("m", o_, qxT[:, 4 * hd + dc, 128 * t:128 * t + 128], mkT[:, 4 * hd + dc, :], dc == 0, dc == 3))
            pe(mms, [], [("ps", b0), ("ps", b0 + 1)])

        def h2_b(i):
            hd, g4 = combos[i]
            pj = i % 2
            b0 = 2 * pj
            skeys = [("ps", b0), ("ps", b0 + 1)]
            so = 32 + 16 * pj
            sc4 = PS[:, b0:b0 + 2, :].rearrange("p b (t n) -> p (b t) n", t=2)
            dve(lambda e: e.reduce_max(out=stat[:, so:so + 4], in_=sc4, axis=AX.X), skeys, [("mx", pj)])
            dve(lambda e: e.tensor_scalar(out=stat[:, so + 4:so + 8], in0=stat[:, so:so + 4], scalar1=-XS, scalar2=None, op0=ALU.mult),
                [("mx", pj)], [("nmx", pj)])
            for tt in range(4):
                act(lambda e, tt=tt: e.activation(out=pn[:, pj, tt, :], in_=psf(b0 + tt // 2, 256, 256 * (tt % 2)), func=AF.Exp,
                                                  bias=stat[:, so + 4 + tt:so + 5 + tt], scale=XS, accum_out=stat[:, so + 8 + tt:so + 9 + tt]),
                    [("ps", b0 + tt // 2), ("nmx", pj)], [("pn", pj, tt), ("sm", pj, tt)])
            dve(lambda e: e.reciprocal(out=stat[:, so + 12:so + 16], in_=stat[:, so + 8:so + 12]), [("sm", pj, tt) for tt in range(4)], [("rs", pj)])
            dve(lambda e: e.tensor_tensor(out=pn[:, pj], in0=pn[:, pj], in1=stat[:, so + 12:so + 16].unsqueeze(2).broadcast_to([128, 4, 256]), op=ALU.mult),
                [("pn", pj, tt) for tt in range(4)] + [("rs", pj)], [("pn", pj, tt) for tt in range(4)])
            pb = psb(4 + pj)
            pe([("t", pb[:, 512 * nc_ + 128 * tt:512 * nc_ + 128 * tt + 128], pn[:, pj, tt, 128 * nc_:128 * nc_ + 128], identb[:]) for tt in range(4) for nc_ in range(2)],
               [("pn", pj, tt) for tt in range(4)] + ["identb"], [("ps", 4 + pj)])
            act(lambda e: e.activation(out=pT[:, pj].rearrange("p c l -> p (c l)"), in_=pb[:, :], func=AF.Copy), [("ps", 4 + pj)], [("pT", pj)])
            for e2 in range(2):
                mms = []
                for k_ in range(2):
                    ec = 2 * e2 + k_
                    for nc_ in range(2):
                        mms.append(("m", psf(6 + k_), mvb[:, nc_, 512 * hd + 128 * ec:512 * hd + 128 * ec + 128], pT[:, pj, nc_, :], nc_ == 0, nc_ == 1))
                pe(mms, [("pT", pj)], [("ps", 6), ("ps", 7)])
                act(lambda e, e2=e2: e.activation(out=oT[:, 4 * hd + 2 * e2:4 * hd + 2 * e2 + 2, 512 * g4:512 * g4 + 512],
                                                  in_=PS[:, 6:8, :], func=AF.Copy), [("ps", 6), ("ps", 7)], [("oT", hd, g4, e2)])

        h2_a(0)
        for i in range(len(combos)):
            if i + 1 < len(combos):
                h2_a(i + 1)
            h2_b(i)
        P.fence()

        P.phase(9)
        oob = oo[:, :, :].rearrange("p a t -> p (a t)").bitcast(BF16)
        ps_s = oob[:, 0:1024].rearrange("p (h n) -> p h n", h=4)
        pTs = oob[:, 1024:1088].rearrange("p (h c l) -> p h c l", h=4, c=2)
        scs_ = oo[:, :, :].rearrange("p a t -> p (a t)")[:, 1024:2048].rearrange("p (h n) -> p h n", h=4)
        def h3_load_k(s):
            a = s % 2
            P.dma("pool", kraw[:, a], ck[s].rearrange("(t p) d -> p t d", p=128), writes=[("kraw", a)])

        def h3_load_v(s):
            a = s % 2
            P.dma("pool", vraw[:, a], cv[s].rearrange("(t p) d -> p t d", p=128), writes=[("vraw", a)])

        def h3_load(s):
            h3_load_k(s)
            h3_load_v(s)

        def h3_a(s):
            a = s % 2
            sb0 = 2 + 2 * a

            def tr(hd):
                j = hd % 2
                pb = psb(j)
                mms = [("t", pb[:, 256 * dc + 128 * nt_:256 * dc + 128 * nt_ + 128], kraw[:, a, nt_, 512 * hd + 128 * dc:512 * hd + 128 * dc + 128], identb[:])
                       for dc in range(4) for nt_ in range(2)]
                pe(mms, [("kraw", a), "identb"], [("ps", j)])
                act(lambda e, pb=pb, j=j: e.activation(out=kTs[:, j, :], in_=pb[:, :], func=AF.Copy), [("ps", j)], [("kTs", j)])

            def sc_(hd):
                j = hd % 2
                sbk = sb0 + hd // 2
                so = 256 * (hd % 2)
                mms = [("m", psf(sbk, 256, so)[0:8, :], qxT[:, 4 * hd + dc, TP + 8 * s:TP + 8 * s + 8], kTs[:, j, 256 * dc:256 * dc + 256], dc == 0, dc == 3) for dc in range(4)]
                pe(mms, [("kTs", j)], [("ps", sbk)])

            tr(0); tr(1); sc_(0); tr(2); sc_(1); tr(3); sc_(2); sc_(3)

        def h3_b(s):
            a = s % 2
            sb0 = 2 + 2 * a
            skeys = [("ps", sb0), ("ps", sb0 + 1)]
            sc = PS[0:8, sb0:sb0 + 2, :].rearrange("p b n -> p (b n)")
            sc4 = PS[0:8, sb0:sb0 + 2, :].rearrange("p b (t n) -> p (b t) n", t=2)
            dve(lambda e: e.reduce_max(out=stat[0:8, 16:20], in_=sc4, axis=AX.X), skeys, ["mx"])
            dve(lambda e: e.tensor_tensor(out=scs_[0:8], in0=sc4, in1=stat[0:8, 16:20].unsqueeze(2).broadcast_to([8, 4, 256]), op=ALU.subtract),
                skeys + ["mx"], ["scs"])
            act(lambda e: e.activation(out=ps_s[0:8].rearrange("p h n -> p (h n)"), in_=scs_[0:8].rearrange("p h n -> p (h n)"), func=AF.Exp, scale=XS),
                ["scs"], ["pss"])
            dve(lambda e: e.reduce_sum(out=stat[0:8, 24:28], in_=ps_s[0:8], axis=AX.X), ["pss"], ["sm"])
            dve(lambda e: e.reciprocal(out=stat[0:8, 28:32], in_=stat[0:8, 24:28]), ["sm"], ["rs"])
            dve(lambda e: e.tensor_tensor(out=ps_s[0:8], in0=ps_s[0:8], in1=stat[0:8, 28:32].unsqueeze(2).broadcast_to([8, 4, 256]), op=ALU.mult),
                ["pss", "rs"], ["pss"])
            pb2 = psb(6)
            pe([("t", pb2[:, 16 * hd + 8 * nc_:16 * hd + 8 * nc_ + 8], ps_s[0:8, hd, 128 * nc_:128 * nc_ + 128], identb[0:8, 0:8]) for hd in range(4) for nc_ in range(2)],
               ["pss", "identb"], [("ps", 6)])
            act(lambda e: e.activation(out=pTs.rearrange("p h c l -> p (h c l)"), in_=pb2[:, 0:64], func=AF.Copy), [("ps", 6)], ["pTs"])
            mms = []
            for hd in range(4):
                for ec in range(4):
                    for nc_ in range(2):
                        mms.append(("m", psf(7, 8, 8 * (4 * hd + ec)), vraw[:, a, nc_, 512 * hd + 128 * ec:512 * hd + 128 * ec + 128], pTs[:, hd, nc_, :], nc_ == 0, nc_ == 1))
            pe(mms, [("vraw", a), "pTs"], [("ps", 7)])
            act(lambda e: e.activation(out=oT[:, :, TP + 8 * s:TP + 8 * s + 8], in_=psf(7, 128).rearrange("p (c l) -> p c l", c=16), func=AF.Copy),
                [("ps", 7)], [("oTs", s)])

        h3_a(0)
        for s in range(NSEQ):
            if s + 1 < NSEQ:
                h3_a(s + 1)
            if s + 2 < NSEQ:
                h3_load_k(s + 2)
            h3_b(s)
            if s + 2 < NSEQ:
                h3_load_v(s + 2)
        P.fence()

        P.phase(10)
        out_proj("xo", oT, None, after_tile=norm_hook(3))
        P.fence()

        P.phase(11)
        hid = RA[:, 0:2 * 4 * T].rearrange("p (a k t) -> p a k t", a=2, k=4)
        gsb = RA[:, 2 * 4 * T:2 * 4 * T + 2 * T].bitcast(F32)
        for g in range(NG):
            a = g % 2
            for j in range(2):
                sgj, kgj = wslot(W["gate"][g][j])
                suj, kuj = wslot(W["up"][g][j])
                for c in range(2):
                    kk_ = 2 * j + c
                    bg_ = next_banks()
                    hrd = ["hT"] if g == NG - 1 else []
                    proj_b(sgj, kgj, c, hT, hrd, TB, bg_)
                    for (t0, n), b in zip(TB, bg_):
                        act(lambda e, b=b, n=n, t0=t0: e.activation(out=gsb[:, t0:t0 + n], in_=psf(b, n), func=AF.Silu),
                            [("ps", b)], [("gsb", t0)])
                    bu_ = next_banks()
                    proj_b(suj, kuj, c, hT, hrd, TB, bu_)
                    for (t0, n), b in zip(TB, bu_):
                        dve(lambda e, b=b, n=n, t0=t0, a=a, kk_=kk_: e.tensor_tensor(out=hid[:, a, kk_, t0:t0 + n], in0=gsb[:, t0:t0 + n], in1=psf(b, n), op=ALU.mult),
                            [("ps", b), ("gsb", t0)], [("hid", a, kk_, t0)])
            hr = [("hid", a, k_, t0) for k_ in range(4) for t0, _ in TB]

            def down_tile(t, cb, sd, kd_, b):
                co = 512 * (cb % 2)
                mms = [("m", psf(b), hid[:, a, k_, 128 * t:128 * t + 128], sd[:, k_, co:co + 512], k_ == 0, k_ == 3) for k_ in range(4)]
                pe(mms, [kd_] + hr, [("ps", b)])
                dve(lambda e, b=b, t=t, cb=cb: e.tensor_tensor(out=xres[:, t, 512 * cb:512 * cb + 512],
                                                               in0=xres[:, t, 512 * cb:512 * cb + 512], in1=psf(b), op=ALU.add),
                    [("ps", b), ("x", t, cb)], [("x", t, cb)])

            if g < NG - 1:
                for cb in range(4):
                    if cb % 2 == 0:
                        sd, kd_ = wslot(W["down"][g][cb // 2])
                    for t in range(NT):
                        down_tile(t, cb, sd, kd_, 6 + (t % 2))
            else:
                sd0, kd0 = wslot(W["down"][g][0])
                sd1, kd1 = wslot(W["down"][g][1])
                gbc = RH[:, 0:2 * D].bitcast(F32)
                P.dma("sp", gbc, final_g.partition_broadcast(128), reads=[], writes=["gbc", "hT"])
                sq3 = RH[:, 2 * D:3 * D]
                yst = RH[:, 3 * D:7 * D].bitcast(F32).rearrange("p (a d) -> p a d", a=2)

                def final_tile(t):
                    a_ = t % 2
                    so = 4 * a_
                    xk = [("x", t, cb) for cb in range(4)]
                    act(lambda e: e.activation(out=sq3, in_=xres[:, t, :], func=AF.Square, accum_out=stat[:, so:so + 1]), xk + ["gbc"], ["sq3", ("fss", a_)])
                    act(lambda e: e.activation(out=stat[:, so + 1:so + 2], in_=stat[:, so:so + 1], func=AF.Sqrt, bias=RMS_EPS, scale=1.0 / D), [("fss", a_)], [("fms", a_)])
                    dve(lambda e: e.reciprocal(out=stat[:, so + 2:so + 3], in_=stat[:, so + 1:so + 2]), [("fms", a_)], [("frs", a_)])
                    dve(lambda e: e.scalar_tensor_tensor(out=yst[:, a_, :], in0=xres[:, t, :], scalar=stat[:, so + 2:so + 3], in1=gbc, op0=ALU.mult, op1=ALU.mult),
                        [("frs", a_), "gbc"] + xk, [("yst", a_)])
                    dst = yp[128 * t:128 * t + 128, :] if t < 8 else ys[:, :]
                    P.dma("sp", dst, yst[:, a_, :], reads=[("yst", a_)], track_out=True)

                for t in range(NT):
                    for cb in range(4):
                        down_tile(t, cb, (sd0, sd1)[cb // 2], (kd0, kd1)[cb // 2], 4 + cb)
                    if t >= 1:
                        final_tile(t - 1)
                final_tile(NT - 1)

        P.phase(12)
        P.dead = False
        final_waits = []
        for k, v in P.dma_tokens:
            if v > P.seen["sp"].get(k, 0):
                P.seen["sp"][k] = v
                final_waits.append((k, v))
        P.q["sp"].append((final_waits, None, None))

        print("engine op counts", {e: (P.epoch[e], P.cnt[e]) for e in ENGS}, "n_sems", len(P.sems), flush=True)
        with nc.Block() as block:
            P.emit(block)
        print("n_sems", len(P.sems), flush=True)
    return nc


def _consts(half):
    c = {}
    c["c_ident"] = np.eye(128, dtype=np.float32)
    perm = np.zeros((128, 128), np.float32)
    for m in range(128):
        perm[(m + 64) % 128, m] = 1.0
    c["c_perm"] = perm
    inv = 10000.0 ** (-np.arange(64, dtype=np.float64) / 64.0)

    def rot_table(pos):
        ang = pos.astype(np.float64)[None, :] * inv[:, None]
        cos = np.cos(ang).astype(np.float32); sin = np.sin(ang).astype(np.float32)
        tab = np.zeros((128, 2, pos.shape[0]), np.float32)
        tab[:64, 0] = cos; tab[64:, 0] = cos
        tab[:64, 1] = -sin; tab[64:, 1] = sin
        return tab
    pos_main = np.concatenate([half * TP + np.arange(TP), np.tile(16384 + np.arange(8), NSEQ)])
    c["c_rot"] = rot_table(pos_main)
    c["c_rotp"] = rot_table(np.arange(TP))
    lg = np.array(LG, dtype=np.float64)
    m = np.arange(128)[:, None]; l = np.arange(128)[None, :]
    dm = np.zeros((128, H, 128)); sm = np.zeros((128, H, 128))
    for h in range(H):
        rel = l - m
        dm[:, h, :] = np.where(rel >= 0, np.exp(rel * lg[h]), 0.0)
        same = (m // 8) == (l // 8)
        sm[:, h, :] = np.where((rel >= 0) & same, np.exp(rel * lg[h]), 0.0)
    c["c_dmask"] = dm.astype(np.float32); c["c_smask"] = sm.astype(np.float32)
    qd = np.zeros((128, H, 128)); qds = np.zeros((128, H, 128))
    for h in range(H):
        qd[:, h, :] = np.exp((np.arange(128) + 1.0) * lg[h])[None, :]
        qds[:, h, :] = np.exp(((np.arange(128) % 8) + 1.0) * lg[h])[None, :]
    c["c_qdec"] = qd.astype(np.float32); c["c_qdecs"] = qds.astype(np.float32)
    kd = np.zeros((128, H)); kdp = np.zeros((128, H, 8)); rmd = np.zeros((128, H, NSEQ + 1))
    for h in range(H):
        kd[:, h] = np.exp((127.0 - np.arange(128)) * lg[h])
        for ch in range(8):
            kdp[:, h, ch] = np.exp((1023.0 - (128 * ch + np.arange(128))) * lg[h])
        for s in range(NSEQ):
            mm_ = np.arange(128)
            rmd[:, h, s] = np.where(mm_ // 8 == s, 1.0, 0.0)
        rmd[:, h, NSEQ] = np.exp((7.0 - (np.arange(128) % 8)) * lg[h])
    c["c_kdec"] = kd.astype(np.float32); c["c_kdecp"] = kdp.astype(np.float32); c["c_rmd"] = rmd.astype(np.float32)
    return c


def _core_inputs(c, shared, x_prompt, x_sample, mem_prompt, state_ret, state_conv, cache_mem_k, cache_mem_v):
    b, half = c // 2, c % 2
    m = dict(shared)
    m["xp"] = np.ascontiguousarray(x_prompt[b, half * TP:(half + 1) * TP])
    m["xprev"] = np.ascontiguousarray(x_prompt[b, 0:TP]) if half == 1 else np.zeros((TP, D), np.float32)
    m["xs"] = np.ascontiguousarray(x_sample[NSEQ * c:NSEQ * (c + 1)].reshape(TS, D))
    own = mem_prompt[b, 128 * half:128 * half + 128]
    oth = mem_prompt[b, 128 * (1 - half):128 * (1 - half) + 128]
    m["mem"] = np.ascontiguousarray(np.concatenate([own, oth], axis=0))
    m["sret"] = np.ascontiguousarray(state_ret[0, NSEQ * c:NSEQ * (c + 1)])
    m["sconv"] = np.ascontiguousarray(state_conv[0, NSEQ * c:NSEQ * (c + 1)].reshape(NSEQ * 2, 1024))
    m["ck"] = np.ascontiguousarray(cache_mem_k[0, NSEQ * c:NSEQ * (c + 1)].reshape(NSEQ, NMEM, D))
    m["cv"] = np.ascontiguousarray(cache_mem_v[0, NSEQ * c:NSEQ * (c + 1)].reshape(NSEQ, NMEM, D))
    m.update(_consts(half))
    return m


_NC_CACHE = {}


def kernel(x_prompt, x_sample, mem_prompt, state_ret, state_conv, cache_mem_k, cache_mem_v,
           ln_mix_g, w_in, conv_w, ret_gn_g, w_mix_out, ln_mem_g, ln_xa_g, w_xq, w_mk, w_mv,
           w_xo, ln_ffn_g, w_gate, w_up, w_down, final_g):
    f = lambda a: np.ascontiguousarray(np.asarray(a, dtype=np.float32))
    x_prompt, x_sample, mem_prompt = f(x_prompt), f(x_sample), f(mem_prompt)
    state_ret, state_conv, cache_mem_k, cache_mem_v = f(state_ret), f(state_conv), f(cache_mem_k), f(cache_mem_v)
    shared = {
        "ln_mix_g": f(ln_mix_g)[0], "w_in": f(w_in)[0], "conv_w": f(conv_w)[0], "ret_gn_g": f(ret_gn_g)[0],
        "w_mix_out": f(w_mix_out)[0], "ln_mem_g": f(ln_mem_g)[0], "ln_xa_g": f(ln_xa_g)[0], "w_xq": f(w_xq)[0],
        "w_mk": f(w_mk)[0], "w_mv": f(w_mv)[0], "w_xo": f(w_xo)[0], "ln_ffn_g": f(ln_ffn_g)[0],
        "w_gate": f(w_gate)[0], "w_up": f(w_up)[0], "w_down": f(w_down)[0], "final_g": f(final_g),
    }
    if "nc" not in _NC_CACHE:
        _NC_CACHE["nc"] = build_program()
    nc = _NC_CACHE["nc"]
    in_maps = [_core_inputs(c, shared, x_prompt, x_sample, mem_prompt, state_ret, state_conv, cache_mem_k, cache_mem_v)
               for c in range(8)]
    res = run_bass_kernel_spmd(nc, in_maps, core_ids=list(range(8)))
    R = res.results
    y_prompt = np.zeros((4, 2048, D), np.float32)
    y_sample = np.zeros((128, 8, D), np.float32)
    srp_o = np.zeros((1, 4, H, 128, 128), np.float32)
    srs_o = np.zeros((1, 128, H, 128, 128), np.float32)
    scp_o = np.zeros((1, 4, 2, 1024), np.float32)
    scs_o = np.zeros((1, 128, 2, 1024), np.float32)
    mk_o = np.zeros((1, 4, NMEM, 4, 512), np.float32)
    mv_o = np.zeros((1, 4, NMEM, 4, 512), np.float32)
    for c in range(8):
        b, half = c // 2, c % 2
        r = R[c]
        y_prompt[b, half * TP:(half + 1) * TP] = r["yp"]
        y_sample[NSEQ * c:NSEQ * (c + 1)] = r["ys"].reshape(NSEQ, 8, D)
        if half == 1:
            srp_o[0, b] = r["srp"]
            scp_o[0, b] = r["scp"]
        srs_o[0, NSEQ * c:NSEQ * (c + 1)] = r["srs"]
        scs_o[0, NSEQ * c:NSEQ * (c + 1)] = r["scs"].reshape(NSEQ, 2, 1024)
        mk_o[0, b, 128 * half:128 * half + 128] = r["mko"].reshape(128, 4, 512)
        mv_o[0, b, 128 * half:128 * half + 128] = r["mvo"].reshape(128, 4, 512)
    return (y_prompt, y_sample, srp_o, srs_o, scp_o, scs_o, mk_o, mv_o)
```

```python
import math
import os
from contextlib import ExitStack

import numpy as np
import concourse.bass as bass
import concourse.mybir as mybir
from concourse.bass_utils import run_bass_kernel_spmd

F32 = mybir.dt.float32
BF16 = mybir.dt.bfloat16
ALU = mybir.AluOpType
AF = mybir.ActivationFunctionType
AX = mybir.AxisListType

D = 2048
KC = 16
TP = 1024
TS = 128
T = TP + TS
NT = T // 128
NSEQ = 16
H = 8
DFF = 5632
NG = DFF // 512
NMEM = 256
RMS_EPS = 1e-6
GN_EPS = 1e-5
NSLOT = 4
TB = [(0, 512), (512, 512), (1024, 128)]
TBP = [(0, 512), (512, 512)]
LG = [math.log(1.0 - 2.0 ** (-5.0 - h)) for h in range(H)]

ENGS = ("pe", "act", "dve", "pool", "sp")
_PE_LABELS = [] if os.environ.get("KLABELS") else None


class Prog:
    def __init__(self, nc, sem_alloc):
        self.nc = nc
        self.sem_alloc = sem_alloc
        self.q = {e: [] for e in ENGS}
        self.cnt = {e: 0 for e in ENGS}
        self.epoch = {e: 0 for e in ENGS}
        self.sems = {}
        self.seen = {e: {} for e in ENGS}
        self.lastw = {}
        self.readers = {}
        self.nds = 12
        self.dq = {q: {"i": 0, "uses": [0] * self.nds} for q in ("sp", "pool", "act")}
        self.dma_tokens = []
        self.weight_tokens = set()
        self.EPOCH_MAX = 4000
        self.dead = False
        self.stop = float(os.environ.get("KSTOP", "99"))

    def phase(self, k):
        if k > self.stop:
            self.dead = True

    def sem(self, key):
        if key not in self.sems:
            self.sems[key] = self.sem_alloc("s_" + "_".join(str(k) for k in key))
        return self.sems[key]

    def _collect(self, eng, reads, writes):
        deps = {}

        def need(tok):
            if tok is None:
                return
            k, v = tok
            if k[0] == "e" and k[1] == "pe" and eng == "pe":
                return
            if v > deps.get(k, 0):
                deps[k] = v

        for t in reads:
            need(self.lastw.get(t))
        for t in writes:
            need(self.lastw.get(t))
            for r in self.readers.get(t, ()):
                need(r)
        waits = []
        for k, v in deps.items():
            if v > self.seen[eng].get(k, 0):
                self.seen[eng][k] = v
                waits.append((k, v))
        return waits

    def _record(self, tok, reads, writes):
        for t in reads:
            self.readers.setdefault(t, []).append(tok)
        for t in writes:
            self.lastw[t] = tok
            self.readers[t] = []

    def op(self, eng, fn, reads=(), writes=()):
        if self.dead:
            return None
        ps_reads = [t for t in reads if isinstance(t, tuple) and t[0] == "ps" and t not in writes]
        if ps_reads:
            writes = list(writes) + ps_reads
        waits = self._collect(eng, reads, writes)
        if self.cnt[eng] >= self.EPOCH_MAX:
            self.epoch[eng] += 1
            self.cnt[eng] = 0
        self.cnt[eng] += 1
        key = ("e", eng, self.epoch[eng])
        tok = (key, self.cnt[eng])
        self.q[eng].append((waits, fn, (key, 1)))
        self._record(tok, reads, writes)
        return tok

    def dma(self, queue, out, in_, reads=(), writes=(), track_out=False, slow=False):
        if self.dead:
            return None
        st = self.dq[queue]
        slot = st["i"] % self.nds
        st["i"] += 1
        key = ("d", queue, slot)
        waits = self._collect(queue, reads, writes)
        prev = st["uses"][slot] * 16
        if prev > self.seen[queue].get(key, 0):
            self.seen[queue][key] = prev
            waits.append((key, prev))
        st["uses"][slot] += 1
        tok = (key, st["uses"][slot] * 16)
        if slow:
            self.q[queue].append((waits, lambda e, o=out, i=in_: e.dma_start(out=o, in_=i, allow_slow_non_contiguous=True), (key, 16)))
        else:
            self.q[queue].append((waits, lambda e, o=out, i=in_: e.dma_start(out=o, in_=i), (key, 16)))
        self._record(tok, reads, writes)
        if track_out:
            self.dma_tokens.append(tok)
        return tok

    def fence(self):
        if self.dead:
            return
        toks = []
        for e in ENGS:
            if self.cnt[e] > 0:
                toks.append((("e", e, self.epoch[e]), self.cnt[e]))
        for q, st in self.dq.items():
            for s in range(self.nds):
                if st["uses"][s] > 0:
                    tk = (("d", q, s), st["uses"][s] * 16)
                    if tk in self.weight_tokens:
                        continue
                    toks.append(tk)
        for e in ENGS:
            waits = []
            for k, v in toks:
                if k[0] == "e" and k[1] == e and e == "pe":
                    continue
                if v > self.seen[e].get(k, 0):
                    self.seen[e][k] = v
                    waits.append((k, v))
            if waits:
                self.q[e].append((waits, None, None))
        keep_w = {k: v for k, v in self.lastw.items() if isinstance(k, tuple) and k[0] == "slot"}
        keep_r = {k: v for k, v in self.readers.items() if isinstance(k, tuple) and k[0] == "slot"}
        self.lastw = keep_w
        self.readers = keep_r

    def emit(self, block):
        nc = self.nc

        def run(eng_name):
            def body(e):
                for waits, fn, inc in self.q[eng_name]:
                    for k, v in waits:
                        e.wait_ge(self.sem(k), v)
                    if fn is not None:
                        ins = fn(e)
                        ins.then_inc(self.sem(inc[0]), inc[1])
            return body

        block.tensor(run("pe"))
        block.scalar(run("act"))
        block.vector(run("dve"))
        block.gpsimd(run("pool"))
        block.sync(run("sp"))


def build_program():
    nc = bass.Bass("TRN2", target_bir_lowering=False)

    def din(name, shape):
        return nc.dram_tensor(name, list(shape), F32, kind="ExternalInput").ap()

    def dout(name, shape):
        return nc.dram_tensor(name, list(shape), F32, kind="ExternalOutput").ap()

    xp = din("xp", [TP, D]); xprev = din("xprev", [TP, D]); xs = din("xs", [TS, D])
    mem = din("mem", [NMEM, D])
    sret = din("sret", [NSEQ, H, 128, 128]); sconv = din("sconv", [NSEQ * 2, 1024])
    ck = din("ck", [NSEQ, NMEM, D]); cv = din("cv", [NSEQ, NMEM, D])
    ln_mix_g = din("ln_mix_g", [D]); w_in = din("w_in", [D, 7168]); conv_w = din("conv_w", [3, 1024])
    ret_gn_g = din("ret_gn_g", [1024]); w_mix_out = din("w_mix_out", [D, D])
    ln_mem_g = din("ln_mem_g", [D]); ln_xa_g = din("ln_xa_g", [D])
    w_xq = din("w_xq", [D, D]); w_mk = din("w_mk", [D, D]); w_mv = din("w_mv", [D, D]); w_xo = din("w_xo", [D, D])
    ln_ffn_g = din("ln_ffn_g", [D]); w_gate = din("w_gate", [D, DFF]); w_up = din("w_up", [D, DFF])
    w_down = din("w_down", [DFF, D]); final_g = din("final_g", [D])
    c_ident = din("c_ident", [128, 128]); c_perm = din("c_perm", [128, 128])
    c_rot = din("c_rot", [128, 2, T]); c_rotp = din("c_rotp", [128, 2, TP])
    c_dmask = din("c_dmask", [128, H, 128]); c_smask = din("c_smask", [128, H, 128])
    c_qdec = din("c_qdec", [128, H, 128]); c_qdecs = din("c_qdecs", [128, H, 128])
    c_kdec = din("c_kdec", [128, H]); c_kdecp = din("c_kdecp", [128, H, 8]); c_rmd = din("c_rmd", [128, H, NSEQ + 1])

    yp = dout("yp", [TP, D]); ys = dout("ys", [TS, D])
    srp = dout("srp", [H, 128, 128]); srs = dout("srs", [NSEQ, H, 128, 128])
    scp = dout("scp", [2, 1024]); scs = dout("scs", [NSEQ * 2, 1024])
    mko = dout("mko", [128, D]); mvo = dout("mvo", [128, D])

    es = ExitStack()
    with es:
        def sb(name, shape, dt):
            return es.enter_context(nc.sbuf_tensor(name, list(shape), dt))

        RX = sb("RX", [128, NT * D], F32)
        RH = sb("RH", [128, KC * T], BF16)
        RA = sb("RA", [128, KC * T], BF16)
        SL = sb("SL", [128, NSLOT, 4096], BF16)
        identf = sb("identf", [128, 128], F32)
        identb = sb("identb", [128, 128], BF16)
        permb = sb("permb", [128, 128], BF16)
        onesf = sb("onesf", [128, 128], F32)
        gT = sb("gT", [128, 4, KC], F32)
        gnT = sb("gnT", [128, 8], F32)
        cwT = sb("cwT", [128, 3, 8], F32)
        stat = sb("stat", [128, 64], F32)
        Sf = sb("Sf", [128, H, 128], F32)
        MKV = sb("MKV", [128, 2, KC * 256], BF16)
        oo = sb("oo", [128, 2, T], F32)
        oall = oo[:, 0, :]
        osq = oo[:, 1, :]
        uprev = sb("uprev", [128, 8, 2], F32)
        PS = es.enter_context(nc.psum_tensor("PS", [128, 8, 512], F32))

        sem_list = []

        def sem_alloc(name):
            s = es.enter_context(nc.semaphore(name))
            sem_list.append(s)
            return s

        P = Prog(nc, sem_alloc)

        def psf(bank, n=512, off=0):
            return PS[:, bank, off:off + n]

        def psb(bank):
            return PS[:, bank, :].bitcast(BF16)

        def pe(mms, reads, writes):
            if _PE_LABELS is not None and not P.dead:
                import inspect
                fr = inspect.stack()[1]
                _PE_LABELS.append(("%s:%d" % (fr.function, fr.lineno), len(mms)))
            def fn(e, mms=mms):
                ins = None
                for m in mms:
                    if m[0] == "m":
                        ins = e.matmul(m[1], m[2], m[3], start=m[4], stop=m[5])
                    else:
                        ins = e.transpose(m[1], m[2], m[3])
                return ins
            return P.op("pe", fn, reads, writes)

        def dve(f, reads, writes):
            return P.op("dve", f, reads, writes)

        def act(f, reads, writes):
            return P.op("act", f, reads, writes)

        wv = lambda w: w.rearrange("(kc p) n -> p kc n", p=128)
        blocks = []

        def blk_std(w, col0):
            blocks.append((wv(w)[:, :, col0:col0 + 256], (16, 256)))
            return len(blocks) - 1

        def blk_a(w, kh, cb):
            blocks.append((wv(w)[:, 8 * kh:8 * kh + 8, 512 * cb:512 * cb + 512], (8, 512)))
            return len(blocks) - 1

        def blk_down(g, half):
            blocks.append((wv(w_down)[:, 4 * g:4 * g + 4, 1024 * half:1024 * half + 1024], (4, 1024)))
            return len(blocks) - 1

        CQ, CK, CV, CG, CBG, CCG, CHIN = 0, 1024, 2048, 3072, 4096, 5120, 6144
        W = {}
        W["pre_k"] = []; W["pre_v"] = []; W["pre_cg"] = []; W["pre_hin"] = []
        for i in range(4):
            W["pre_k"].append(blk_std(w_in, CK + 256 * i)); W["pre_v"].append(blk_std(w_in, CV + 256 * i))
            W["pre_cg"].append(blk_std(w_in, CCG + 256 * i)); W["pre_hin"].append(blk_std(w_in, CHIN + 256 * i))
        W["q"] = []; W["k"] = []; W["v"] = []; W["g"] = []; W["mkv"] = []
        for i in range(4):
            W["v"].append(blk_std(w_in, CV + 256 * i)); W["k"].append(blk_std(w_in, CK + 256 * i))
            W["mkv"].append((0, 2 * i, blk_std(w_mk, 256 * (2 * i))))
            W["q"].append(blk_std(w_in, CQ + 256 * i))
            W["mkv"].append((0, 2 * i + 1, blk_std(w_mk, 256 * (2 * i + 1))))
            W["g"].append(blk_std(w_in, CG + 256 * i))
            W["mkv"].append((1, 2 * i, blk_std(w_mv, 256 * (2 * i))))
            W["mkv"].append((1, 2 * i + 1, blk_std(w_mv, 256 * (2 * i + 1))))
        W["cg"] = []; W["hin"] = []; W["bg"] = []
        for i in range(4):
            W["cg"].append(blk_std(w_in, CCG + 256 * i)); W["hin"].append(blk_std(w_in, CHIN + 256 * i))
            W["bg"].append(blk_std(w_in, CBG + 256 * i))
        W["mixo"] = [[blk_a(w_mix_out, kh, cb) for kh in range(2)] for cb in range(4)]
        W["xq"] = [blk_std(w_xq, 256 * i) for i in range(8)]
        W["xo"] = [[blk_a(w_xo, kh, cb) for kh in range(2)] for cb in range(4)]
        W["gate"] = []; W["up"] = []; W["down"] = []
        for g in range(NG):
            gl, ul = [], []
            for j in range(2):
                gl.append(blk_std(w_gate, 512 * g + 256 * j)); ul.append(blk_std(w_up, 512 * g + 256 * j))
            W["gate"].append(gl); W["up"].append(ul)
            W["down"].append([blk_down(g, j) for j in range(2)])
        wstate = {"issued": 0, "cur": -1}

        def wslot(i):
            assert i == wstate["cur"] + 1 or i == wstate["cur"], (i, wstate)
            wstate["cur"] = i
            while wstate["issued"] < len(blocks) and wstate["issued"] <= i + NSLOT - 2:
                j = wstate["issued"]
                ap, shp = blocks[j]
                dst = SL[:, j % NSLOT, :].rearrange("p (a b) -> p a b", a=shp[0])
                wt = P.dma("pool", dst, ap, reads=([("hprev", 1)] if j < NSLOT - 1 else ()), writes=[("slot", j % NSLOT)])
                if wt is not None:
                    P.weight_tokens.add(wt)
                wstate["issued"] += 1
            shp = blocks[i][1]
            return SL[:, i % NSLOT, :].rearrange("p (a b) -> p a b", a=shp[0]), ("slot", i % NSLOT)

        P.dma("sp", identf[:], c_ident, writes=["identf"])
        P.dma("pool", identb[:], c_ident, writes=["identb"])
        P.dma("pool", permb[:], c_perm, writes=["permb"])
        gst = oo[:, :, :].rearrange("p a t -> p (a t)")[:, 0:128]
        for i, g in enumerate((ln_mix_g, ln_mem_g, ln_xa_g, ln_ffn_g)):
            P.dma("sp", gst[16 * i:16 * i + 16, :], g.rearrange("(kc p) -> kc p", p=128), writes=["gst"])
        P.dma("sp", gst[64:72, :], ret_gn_g.rearrange("(kc p) -> kc p", p=128), writes=["gst"])
        P.dma("sp", gst[72:96, :], conv_w.rearrange("k (kc p) -> (k kc) p", p=128), writes=["gst"])
        pe([("t", psf(0, 96), gst[0:96, :], identf[0:96, 0:96])], ["gst", "identf"], [("ps", 0)])
        act(lambda e: e.activation(out=gT[:, :, :].rearrange("p a k -> p (a k)"), in_=psf(0, 64), func=AF.Copy), [("ps", 0)], ["gT"])
        act(lambda e: e.activation(out=gnT[:], in_=psf(0, 8, 64), func=AF.Copy), [("ps", 0)], ["gnT"])
        act(lambda e: e.activation(out=cwT[:, :, :].rearrange("p k c -> p (k c)"), in_=psf(0, 24, 72), func=AF.Copy), [("ps", 0)], ["cwT"])
        dve(lambda e: e.memset(onesf[:], 1.0 / 128.0), [], ["onesf"])
        memx = MKV[:, :, :].rearrange("p a x -> p (a x)").bitcast(F32).rearrange("p (t d) -> p t d", t=2)

        rx_off = [0]

        def rx(n, dt=F32, shape=None):
            nf = n if dt == F32 else (n + 1) // 2
            o = rx_off[0]
            rx_off[0] += nf
            assert rx_off[0] <= NT * D, rx_off[0]
            v = RX[:, o:o + nf]
            if dt == BF16:
                v = v.bitcast(BF16)[:, 0:n]
            return v

        rot = rx(2 * T).rearrange("p (a t) -> p a t", a=2)
        dmask = rx(2 * 128).rearrange("p (h l) -> p h l", h=2)
        smask = rx(2 * 128).rearrange("p (h l) -> p h l", h=2)
        qdec = rx(2 * 128).rearrange("p (h l) -> p h l", h=2)
        qdecs = rx(2 * 128).rearrange("p (h l) -> p h l", h=2)
        kdec = rx(H)
        kdecp = rx(H * 8).rearrange("p (h c) -> p h c", h=H)
        rmd = rx(H * (NSEQ + 1)).rearrange("p (h s) -> p h s", h=H)
        qT = rx(2 * T, BF16).rearrange("p (a t) -> p a t", a=2)
        kT = rx(2 * T, BF16).rearrange("p (a t) -> p a t", a=2)
        sgT = rx(2 * T, BF16).rearrange("p (a t) -> p a t", a=2)
        vtok = rx(NT * 256, BF16).rearrange("p (t c) -> p t c", t=NT)
        raw = rx(T, BF16)
        t1 = rx(T)
        t2 = rx(T)
        stm = rx(2 * 128, BF16).rearrange("p (a l) -> p a l", a=2)
        kdt = rx(2 * 128, BF16).rearrange("p (a l) -> p a l", a=2)
        qds_ = rx(128, BF16)
        inn = rx(128)
        un0 = rx_off[0]
        xload = rx(2 * D).rearrange("p (a d) -> p a d", a=2)
        sqj = rx(D, BF16)
        enda = rx_off[0]
        rx_off[0] = un0
        SH = 8
        ssf2 = rx(2 * SH * 128).rearrange("p (a s e) -> p a s e", a=2, s=SH)
        ssb = rx(SH * 128, BF16).rearrange("p (s e) -> p s e", s=SH)
        Sprev = rx(2 * 8 * 128, BF16).rearrange("p (a c e) -> p a c e", a=2, c=8)
        Stmp = rx(2 * 128).rearrange("p (a e) -> p a e", a=2)
        qdall = rx(TP, BF16)
        vexp_off = rx_off[0]
        vexp = rx(NSEQ * 128, BF16).rearrange("p (s e) -> p s e", s=NSEQ)
        ktkd = rx(128, BF16)
        kdta = rx(8 * 128, BF16).rearrange("p (c d) -> p c d", c=8)
        osq_alt = RX[:, vexp_off:vexp_off + T]
        stma = rx(9 * 128, BF16).rearrange("p (c l) -> p c l", c=9)
        endb = rx_off[0]
        rx_off[0] = un0
        upad = rx(2 * (TP + 2))
        upads = rx(NSEQ * 10).rearrange("p (s k) -> p s k", s=NSEQ)
        cacc = rx(T)
        scT = rx(8 * 32).rearrange("p (c k) -> p c k", c=8)
        sc_in = rx(1024)
        sc_out = rx(1024)
        nbt = rx(32)
        endc = rx_off[0]
        rx_off[0] = max(enda, endb, endc)
        assert rx_off[0] <= NT * D, rx_off[0]

        P.dma("sp", kdec, c_kdec, writes=["kdec"])
        P.dma("sp", kdecp, c_kdecp, writes=["kdecp"])
        P.dma("sp", rmd, c_rmd, writes=["rmd"])

        hT = RH[:, :].rearrange("p (k t) -> p k t", k=KC)
        hprevT = RA[:, 0:KC * TP].rearrange("p (k t) -> p k t", k=KC)
        catT = RA[:, :].rearrange("p (k t) -> p k t", k=KC)
        xres = RX[:, :].rearrange("p (t d) -> p t d", t=NT)

        xn2 = oo[:, :, :].rearrange("p a t -> p (a t)").bitcast(BF16)[:, 0:2 * D].rearrange("p (a d) -> p a d", a=2)

        def norm_tile(i, xt, xr, gidx, dstT, dkey):
            j = i % 2
            so = 4 * j
            act(lambda e: e.activation(out=xn2[:, j, :], in_=xt, func=AF.Square, accum_out=stat[:, so:so + 1]), xr, [("xn", j), ("ss", j), "gst"])
            act(lambda e: e.activation(out=stat[:, so + 1:so + 2], in_=stat[:, so:so + 1], func=AF.Sqrt, bias=RMS_EPS, scale=1.0 / D), [("ss", j)], [("ms", j)])
            dve(lambda e: e.reciprocal(out=stat[:, so + 2:so + 3], in_=stat[:, so + 1:so + 2]), [("ms", j)], [("rstd", j)])
            if j == 0:
                act(lambda e: e.activation(out=xn2[:, j, :], in_=xt, func=AF.Copy, scale=stat[:, so + 2:so + 3]), xr + [("rstd", j)], [("xn", j)])
            else:
                dve(lambda e: e.tensor_scalar(out=xn2[:, j, :], in0=xt, scalar1=stat[:, so + 2:so + 3], scalar2=None, op0=ALU.mult),
                    xr + [("rstd", j)], [("xn", j)])
            b0 = 4 + 2 * j
            pbb = PS[:, b0:b0 + 2, :].rearrange("p b n -> p (b n)").bitcast(BF16)
            pe([("t", pbb[:, 128 * kc:128 * kc + 128], xn2[:, j, 128 * kc:128 * kc + 128], identb[:]) for kc in range(KC)],
               [("xn", j), "identb"], [("ps", b0), ("ps", b0 + 1)])
            dve(lambda e: e.tensor_tensor(out=dstT[:, :, 128 * i:128 * i + 128], in0=pbb.rearrange("p (k t) -> p k t", k=KC),
                                          in1=gT[:, gidx, :].unsqueeze(2).broadcast_to([128, KC, 128]), op=ALU.mult),
                [("ps", b0), ("ps", b0 + 1), "gT"], [(dkey, i)])

        def norm_tiles(src_aps, gidx, dstT, dkey):
            for i in range(len(src_aps)):
                xt = xload[:, i % 2, :]
                P.dma("sp", xt, src_aps[i], writes=[("xload", i % 2)])
                norm_tile(i, xt, [("xload", i % 2)], gidx, dstT, dkey)

        def proj_b(slot, slot_key, c, src, src_reads, tbs, banks):
            mms = []
            for (t0, n), b in zip(tbs, banks):
                pass
            for kc in range(KC):
                for (t0, n), b in zip(tbs, banks):
                    mms.append(("m", psf(b, n), slot[:, kc, 128 * c:128 * c + 128], src[:, kc, t0:t0 + n],
                                kc == 0, kc == KC - 1))
            return pe(mms, [slot_key] + src_reads, [("ps", b) for b in banks[:len(tbs)]])

        bank_sets = [[0, 1, 2], [3, 4, 5]]
        bs_i = [0]

        def next_banks():
            b = bank_sets[bs_i[0] % 2]
            bs_i[0] += 1
            return b

        def pview(banks, ntok):
            b0 = banks[0]
            return PS[:, b0:b0 + 3, :].rearrange("p b n -> p (b n)")[:, 0:ntok]

        def pkeys(banks, tbs):
            return [("ps", b) for b in banks[:len(tbs)]]

        def rotary_pre(banks, tbs, tab, scale):
            ntok = tbs[-1][0] + tbs[-1][1]
            pv = pview(banks, ntok)
            pk = pkeys(banks, tbs)
            act(lambda e: e.activation(out=raw[:, 0:ntok], in_=pv, func=AF.Copy), pk, ["raw"])
            dve(lambda e: e.scalar_tensor_tensor(out=t1[:, 0:ntok], in0=pv, scalar=scale, in1=tab[:, 0, 0:ntok],
                                                 op0=ALU.mult, op1=ALU.mult), pk + ["rot"], ["t1"])

        def rotary_post(banks, tbs, tab, scale, dst, dkey):
            ntok = tbs[-1][0] + tbs[-1][1]
            pv = pview(banks, ntok)
            pk = pkeys(banks, tbs)
            pe([("m", psf(b, n), permb[:], raw[:, t0:t0 + n], True, True) for (t0, n), b in zip(tbs, banks)],
               ["raw", "permb"], pk)
            dve(lambda e: e.scalar_tensor_tensor(out=t2[:, 0:ntok], in0=pv, scalar=scale, in1=tab[:, 1, 0:ntok],
                                                 op0=ALU.mult, op1=ALU.mult), pk + ["rot"], ["t2"])
            dve(lambda e: e.tensor_tensor(out=dst[:, 0:ntok], in0=t1[:, 0:ntok], in1=t2[:, 0:ntok], op=ALU.add),
                ["t1", "t2"], [("rotout", dkey)])

        def rotary(banks, tbs, tab, scale, dst, dkey):
            rotary_pre(banks, tbs, tab, scale)
            rotary_post(banks, tbs, tab, scale, dst, dkey)

        P.phase(1)
        P.dma("sp", rot[:, :, 0:TP], c_rotp, writes=["rot"])
        RAxn = RA
        xn_pre = RH[:, 0:8 * D].rearrange("p (t d) -> p t d", t=8)
        norm_tiles([xprev[128 * i:128 * i + 128, :] for i in range(8)], 0, hprevT, "hprev")
        for i in range(2):
            P.dma("sp", memx[:, i, :], mem[128 * i:128 * i + 128, :], writes=[("xl2", i)])
        P.phase(1.2)
        dve(lambda e: e.memset(Sf[:], 0.0), [], ["Sf"])
        kprevT = kT
        srcs = [xp[128 * i:128 * i + 128, :] for i in range(8)] + [xs[:, :]]
        main_i = [0]

        def main_norm_tiles(k):
            for _ in range(k):
                i = main_i[0]
                if i >= NT:
                    return
                main_i[0] += 1
                xt = xload[:, i % 2, :]
                P.dma("sp", xt, srcs[i], writes=[("xload", i % 2)])
                norm_tile(i, xt, [("xload", i % 2)], 0, hT, "hmain")

        P.phase(1.22)
        XB0, YB0 = [0, 1, 2], [3, 4, 5]
        cgp = stat[:, 8:8 + 16].rearrange("p (c k) -> p c k", c=8)
        for hp in range(4):
            slot, sk = wslot(W["pre_k"][hp])
            P.phase(1.25)
            hpr = [("hprev", i_) for i_ in range(8)]
            proj_b(slot, sk, 0, hprevT, hpr, TBP, XB0)
            proj_b(slot, sk, 1, hprevT, hpr, TBP, YB0)
            rotary(XB0, TBP, rot, 128.0 ** -0.5, kprevT[:, 0, :], ("k", 0))
            rotary_pre(YB0, TBP, rot, 128.0 ** -0.5)
            slot, sk = wslot(W["pre_v"][hp])
            for t in range(8):
                b = 6 + (t % 2)
                mms = [("m", psf(b, 256), hprevT[:, kc, 128 * t:128 * t + 128], slot[:, kc, :], kc == 0, kc == KC - 1)
                       for kc in range(KC)]
                pe(mms, [sk] + hpr, [("ps", b)])
                act(lambda e, b=b, t=t: e.activation(out=vtok[:, t, :], in_=psf(b, 256), func=AF.Copy),
                    [("ps", b)], [("vtok", t)])
            rotary_post(YB0, TBP, rot, 128.0 ** -0.5, kprevT[:, 1, :], ("k", 1))
            P.phase(1.6)
            main_norm_tiles(2)
            kdas = []
            for c in range(2):
                h = 2 * hp + c
                b = 6 + c
                pb = psb(b)
                pe([("t", pb[:, 128 * ch:128 * ch + 128], kprevT[:, c, 128 * ch:128 * ch + 128], identb[:]) for ch in range(8)],
                   [("rotout", ("k", c)), "identb"], [("ps", b)])
                kda = qT[:, c, 0:TP].rearrange("p (a d) -> p a d", a=8)
                kdas.append(kda)
                dve(lambda e, pb=pb, h=h, kda=kda: e.tensor_tensor(
                    out=kda, in0=pb[:, :].rearrange("p (a d) -> p a d", a=8),
                    in1=kdecp[:, h, :].unsqueeze(2).broadcast_to([128, 8, 128]), op=ALU.mult), [("ps", b), "kdecp"], [("kda", c)])

            def conv_tail(which, name):
                slot, sk = wslot(W[name][hp])
                for c in range(2):
                    ch = 2 * hp + c
                    b = 4 + (ch % 2)
                    mms = [("m", psf(b, 2), slot[:, kc, 128 * c:128 * c + 128], hprevT[:, kc, TP - 2:TP], kc == 0, kc == KC - 1)
                           for kc in range(KC)]
                    pe(mms, [sk] + [("hprev", i_) for i_ in range(8)], [("ps", b)])
                    if which == 0:
                        act(lambda e, b=b, ch=ch: e.activation(out=cgp[:, ch, :], in_=psf(b, 2), func=AF.Copy),
                            [("ps", b)], [("cgp", ch)])
                    else:
                        dve(lambda e, b=b, ch=ch: e.tensor_tensor(out=uprev[:, ch, :], in0=cgp[:, ch, :], in1=psf(b, 2),
                                                                  op=ALU.mult), [("ps", b), ("cgp", ch)], [("uprev", ch)])

            conv_tail(0, "pre_cg")
            for c in range(2):
                h = 2 * hp + c
                sb_bank = 2 + c
                kda = kdas[c]
                pe([("m", psf(sb_bank, 128), kda[:, ch, :], vtok[:, ch, 128 * c:128 * c + 128], ch == 0, ch == 7) for ch in range(8)],
                   [("kda", c)] + [("vtok", ch) for ch in range(8)], [("ps", sb_bank)])
                act(lambda e, h=h, sb_bank=sb_bank: e.activation(out=Sf[:, h, :], in_=psf(sb_bank, 128), func=AF.Copy),
                    [("ps", sb_bank)], [("Sf", h)])
            conv_tail(1, "pre_hin")
        main_norm_tiles(NT)
        slot_v0, sk_v0 = wslot(W["v"][0])
        for t in range(NT):
            b = 6 + (t % 2)
            mms = [("m", psf(b, 256), hT[:, kc, 128 * t:128 * t + 128], slot_v0[:, kc, :], kc == 0, kc == KC - 1)
                   for kc in range(KC)]
            pe(mms, [sk_v0] + [("hmain", i_) for i_ in range(NT)], [("ps", b)])
            act(lambda e, b=b, t=t: e.activation(out=vtok[:, t, :], in_=psf(b, 256), func=AF.Copy),
                [("ps", b)], [("vtok", t)])
        slot_k0, sk_k0 = wslot(W["k"][0])
        hmk = [("hmain", i_) for i_ in range(NT)]
        proj_b(slot_k0, sk_k0, 0, hT, hmk, TB, [0, 1, 2])
        proj_b(slot_k0, sk_k0, 1, hT, hmk, TB, [3, 4, 5])
        P.fence()

        P.phase(2)
        P.dma("sp", rot, c_rot, writes=["rot"])

        P.phase(3)
        def proj_v(slot, sk, extra=()):
            for t in range(NT):
                b = 6 + (t % 2)
                mms = [("m", psf(b, 256), hT[:, kc, 128 * t:128 * t + 128], slot[:, kc, :], kc == 0, kc == KC - 1)
                       for kc in range(KC)]
                pe(mms, [sk] + list(extra), [("ps", b)])
                act(lambda e, b=b, t=t: e.activation(out=vtok[:, t, :], in_=psf(b, 256), func=AF.Copy),
                    [("ps", b)], [("vtok", t)])

        def stage1_t(c, h):
            kk = [("rotout", ("k", c))]
            pb = psb(6)
            pe([("t", pb[:, 128 * ch:128 * ch + 128], kT[:, c, 128 * ch:128 * ch + 128], identb[:]) for ch in range(8)],
               kk + ["identb"], [("ps", 6)])
            act(lambda e, pb=pb, h=h: e.activation(out=kdta.rearrange("p c d -> p (c d)"), in_=pb[:, :], func=AF.Copy,
                                                   scale=kdec[:, h:h + 1]), [("ps", 6), "kdec"], ["kdta"])

        def stage1_a(c, h):
            pe([("m", psf(6 + ch // 4, 128, 128 * (ch % 4)), kdta[:, ch, :], vtok[:, ch, 128 * c:128 * c + 128], True, True) for ch in range(8)],
               ["kdta"] + [("vtok", ch) for ch in range(8)], [("ps", 6), ("ps", 7)])
            g128 = math.exp(128.0 * LG[h])
            act(lambda e: e.activation(out=Sprev[:, c, 0, :], in_=Sf[:, h, :], func=AF.Copy), [("Sf", h)], [("Sprev", c, 0)])
            for ch in range(8):
                ab = 6 + ch // 4
                src = Sf[:, h, :] if ch % 2 == 0 else Stmp[:, c, :]
                dst = Stmp[:, c, :] if ch % 2 == 0 else Sf[:, h, :]
                sk_, dk_ = (("Sf", h), ("Stmp", c)) if ch % 2 == 0 else (("Stmp", c), ("Sf", h))
                dve(lambda e, src=src, dst=dst, ab=ab, ch=ch: e.scalar_tensor_tensor(
                    out=dst, in0=src, scalar=g128, in1=psf(ab, 128, 128 * (ch % 4)), op0=ALU.mult, op1=ALU.add),
                    [("ps", ab), sk_], [dk_])
                if ch < 7:
                    act(lambda e, dst=dst, ch=ch: e.activation(out=Sprev[:, c, ch + 1, :], in_=dst, func=AF.Copy),
                        [dk_], [("Sprev", c, ch + 1)])
            P.dma("sp", srp[h], Sf[:, h, :], reads=[("Sf", h)], track_out=True)

        def stage3(c, h):
            oall = oo[:, c, :]
            qk = [("rotout", ("q", c))]
            kk = [("rotout", ("k", c))]
            for hf in range(2):
                P.dma("sp", ssf2[:, hf], sret[SH * hf:SH * hf + SH, h].rearrange("s d e -> d s e"), writes=[("ssf", hf)])
            pb = psb(3)
            pe([("t", pb[:, 0:128], kT[:, c, TP:T], identb[:])], kk + ["identb"], [("ps", 3)])
            act(lambda e, pb=pb: e.activation(out=ktkd, in_=pb[:, 0:128], func=AF.Copy, scale=rmd[:, h, NSEQ:NSEQ + 1]),
                [("ps", 3), "rmd"], ["ktkd"])
            dve(lambda e: e.tensor_tensor(
                out=vexp, in0=vtok[:, 8, 128 * c:128 * c + 128].unsqueeze(1).broadcast_to([128, NSEQ, 128]),
                in1=rmd[:, 0, 0:NSEQ].unsqueeze(2).broadcast_to([128, NSEQ, 128]), op=ALU.mult), [("vtok", 8), "rmd"], ["vexp"])
            dve(lambda e: e.tensor_tensor(
                out=qdall.rearrange("p (a l) -> p a l", l=128), in0=qT[:, c, 0:TP].rearrange("p (a l) -> p a l", l=128),
                in1=qdec[:, c, :].unsqueeze(1).broadcast_to([128, 8, 128]), op=ALU.mult), qk + ["qdec"], ["qdall"])
            dve(lambda e: e.tensor_tensor(out=qds_, in0=qT[:, c, TP:T], in1=qdecs[:, c, :], op=ALU.mult), qk + ["qdecs"], ["qds"])
            pe([("m", psf(6 + ch // 4, 128, 128 * (ch % 4)), kT[:, c, 128 * ch:128 * ch + 128], qT[:, c, 128 * ch:128 * ch + 128], True, True) for ch in range(8)]
               + [("m", psf(2, 128), kT[:, c, TP:T], qT[:, c, TP:T], True, True)], qk + kk, [("ps", 6), ("ps", 7), ("ps", 2)])
            for half in range(2):
                dve(lambda e, half=half: e.tensor_tensor(
                    out=stma[:, 4 * half:4 * half + 4, :], in0=psf(6 + half).rearrange("p (a l) -> p a l", l=128),
                    in1=dmask[:, c, :].unsqueeze(1).broadcast_to([128, 4, 128]), op=ALU.mult), [("ps", 6 + half), "dmask"], [("stma", half)])
            dve(lambda e: e.tensor_tensor(out=stma[:, 8, :], in0=psf(2, 128), in1=smask[:, c, :], op=ALU.mult), [("ps", 2), "smask"], [("stma", 2)])
            mms = []
            for ch in range(8):
                o_ = psf(ch // 4, 128, 128 * (ch % 4))
                mms.append(("m", o_, vtok[:, ch, 128 * c:128 * c + 128], stma[:, ch, :], True, False))
                mms.append(("m", o_, Sprev[:, c, ch, :], qdall[:, 128 * ch:128 * ch + 128], False, True))
            pe(mms, [("vtok", ch) for ch in range(8)] + [("stma", 0), ("stma", 1), "qdall"] + [("Sprev", c, ch) for ch in range(8)],
               [("ps", 0), ("ps", 1)])
            act(lambda e: e.activation(out=oall[:, 0:TP], in_=PS[:, 0:2, :].rearrange("p b n -> p (b n)"), func=AF.Copy),
                [("ps", 0), ("ps", 1)], [("oall", c, 0), ("xl2", 0)])
            pe([("m", psf(4, 128), vtok[:, 8, 128 * c:128 * c + 128], stma[:, 8, :], True, True)], [("vtok", 8), ("stma", 2)], [("ps", 4)])
            act(lambda e: e.activation(out=inn, in_=psf(4, 128), func=AF.Copy), [("ps", 4)], ["inn"])
            g8 = math.exp(8.0 * LG[h])
            for hf in range(2):
                s0 = SH * hf
                ssf = ssf2[:, hf]
                act(lambda e, ssf=ssf: e.activation(out=ssb.rearrange("p s e -> p (s e)"), in_=ssf.rearrange("p s e -> p (s e)"), func=AF.Copy),
                    [("ssf", hf)], ["ssb"])
                pe([("m", psf(5, 8, 8 * (s0 + s_)), ssb[:, s_, :], qds_[:, 8 * (s0 + s_):8 * (s0 + s_) + 8], True, True) for s_ in range(SH)],
                   ["ssb", "qds"], [("ps", 5)])
                for q4 in range(2):
                    b = 2 + q4
                    sq0 = s0 + 4 * q4
                    pe([("m", psf(b), ktkd, vexp[:, sq0:sq0 + 4, :].rearrange("p s e -> p (s e)"), True, True)], ["ktkd", "vexp"], [("ps", b)])
                    dve(lambda e, b=b, q4=q4, ssf=ssf: e.scalar_tensor_tensor(
                        out=ssf[:, 4 * q4:4 * q4 + 4, :].rearrange("p s e -> p (s e)"),
                        in0=ssf[:, 4 * q4:4 * q4 + 4, :].rearrange("p s e -> p (s e)"), scalar=g8, in1=psf(b),
                        op0=ALU.mult, op1=ALU.add), [("ps", b), ("ssf", hf)], [("ssf", hf)])
                P.dma("sp", srs[s0:s0 + SH, h].rearrange("s d e -> d s e"), ssf, reads=[("ssf", hf)], track_out=True)
            dve(lambda e: e.tensor_tensor(out=oall[:, TP:T], in0=psf(5, 128), in1=inn, op=ALU.add), [("ps", 5), "inn"], [("oall", c, 1), ("xl2", 0)])
            osq_, okeys = gn_bufs(c)
            dve(lambda e: e.tensor_tensor(out=osq_, in0=oall, in1=oall, op=ALU.mult), [("oall", c, 0), ("oall", c, 1)], okeys)

        def gn_bufs(c):
            if c == 0:
                return t2, ["t2"]
            return osq_alt, ["vexp", "ktkd", "kdta"]

        GXB, GYB = [0, 1, 2], [3, 4, 5]

        def gn_mm(c):
            oall = oo[:, c, :]
            osq_, okeys = gn_bufs(c)
            oa = [("oall", c, 0), ("oall", c, 1)]
            pe([("m", psf(b, n), onesf[:], oall[:, t0:t0 + n], True, True) for (t0, n), b in zip(TB, GXB)], oa + ["onesf"], pkeys(GXB, TB))
            pe([("m", psf(b, n), onesf[:], osq_[:, t0:t0 + n], True, True) for (t0, n), b in zip(TB, GYB)], okeys + ["onesf"], pkeys(GYB, TB))

        OAK = ["vexp", "ktkd", "kdta"]

        def gn_head(c):
            mv_, qv_ = pview(GXB, T), pview(GYB, T)
            act(lambda e: e.activation(out=t1, in_=mv_, func=AF.Copy), pkeys(GXB, TB), ["t1"])
            dve(lambda e: e.tensor_tensor(out=t2, in0=t1, in1=t1, op=ALU.mult), ["t1"], ["t2"])
            dve(lambda e: e.tensor_tensor(out=t2, in0=qv_, in1=t2, op=ALU.subtract), pkeys(GYB, TB) + ["t2"], ["t2"])

        def gn_head1_a():
            mv_ = pview(GXB, T)
            act(lambda e: e.activation(out=osq_alt, in_=mv_, func=AF.Copy), pkeys(GXB, TB), OAK)

        def gn_head1_b():
            qv_ = pview(GYB, T)
            dve(lambda e: e.tensor_tensor(out=t2, in0=osq_alt, in1=osq_alt, op=ALU.mult), OAK, ["t2"])
            dve(lambda e: e.tensor_tensor(out=t2, in0=qv_, in1=t2, op=ALU.subtract), pkeys(GYB, TB) + ["t2"], ["t2"])

        def gn_tail(c, h, mbuf=None, mkeys=None):
            oall = oo[:, c, :]
            oa = [("oall", c, 0), ("oall", c, 1)]
            mb = t1 if mbuf is None else mbuf
            mk_ = ["t1"] if mkeys is None else mkeys
            act(lambda e: e.activation(out=t2, in_=t2, func=AF.Ln, bias=GN_EPS, scale=1.0), ["t2"], ["t2"])
            act(lambda e: e.activation(out=t2, in_=t2, func=AF.Exp, scale=-0.5), ["t2"], ["t2"])
            dve(lambda e: e.tensor_tensor(out=t1, in0=oall, in1=mb, op=ALU.subtract), oa + mk_ + ["t1"], ["t1"])
            dve(lambda e: e.scalar_tensor_tensor(out=t1, in0=t1, scalar=gnT[:, h:h + 1], in1=t2, op0=ALU.mult, op1=ALU.mult),
                ["t1", "t2", "gnT"], ["t1"])
            dve(lambda e: e.tensor_tensor(out=catT[:, h, :], in0=t1, in1=sgT[:, c, :], op=ALU.mult), ["t1", ("sgT", c)], [("cat", h)])

        mkT = MKV[:, 0, :].rearrange("p (k t) -> p k t", k=KC)
        mvb = MKV[:, 1, :].rearrange("p (t d) -> p t d", t=2)
        csc = RA[:, 8 * T:16 * T]
        hmT = csc[:, 0:4096].rearrange("p (k t) -> p k t", k=KC)
        xn_mem = csc[:, 4096:8192].rearrange("p (t d) -> p t d", t=2)
        mst = csc[:, 8192:9216].bitcast(F32).rearrange("p (a c) -> p a c", a=2)
        def norm_mem():
            for i in range(2):
                xt = memx[:, i, :]
                act(lambda e, xt=xt, i=i: e.activation(out=xn_mem[:, i, :], in_=xt, func=AF.Square, accum_out=stat[:, 0:1]), [], [("xnm", i), "ss", ("xl2", i)])
                act(lambda e: e.activation(out=stat[:, 1:2], in_=stat[:, 0:1], func=AF.Sqrt, bias=RMS_EPS, scale=1.0 / D), ["ss"], ["ms"])
                dve(lambda e: e.reciprocal(out=stat[:, 2:3], in_=stat[:, 1:2]), ["ms"], ["rstd"])
                dve(lambda e, xt=xt, i=i: e.tensor_scalar(out=xn_mem[:, i, :], in0=xt, scalar1=stat[:, 2:3], scalar2=None,
                                                          op0=ALU.mult), ["rstd"], [("xnm", i), ("xl2", i)])
            for kc in range(KC):
                bank = 6 + (kc % 2)
                pb = psb(bank)
                pe([("t", pb[:, 128 * j:128 * j + 128], xn_mem[:, j, 128 * kc:128 * kc + 128], identb[:]) for j in range(2)],
                   [("xnm", 0), ("xnm", 1), "identb"], [("ps", bank)])
                act(lambda e, pb=pb, kc=kc: e.activation(out=hmT[:, kc, :], in_=pb[:, 0:256], func=AF.Copy, scale=gT[:, 1, kc:kc + 1]),
                    [("ps", bank), "gT"], [("hmT", kc)])

        hm_reads = [("hmT", kc) for kc in range(KC)]

        def memkv_block(which, i, blk):
            dst = mko if which == 0 else mvo
            slot, sk = wslot(blk)
            for t in range(2):
                b = 6 + t
                mms = [("m", psf(b, 256), hmT[:, kc, 128 * t:128 * t + 128], slot[:, kc, :], kc == 0, kc == KC - 1) for kc in range(KC)]
                pe(mms, [sk] + hm_reads, [("ps", b)])
                if t == 0:
                    j = i % 2
                    act(lambda e, b=b, j=j: e.activation(out=mst[:, j, :], in_=psf(b, 256), func=AF.Copy), [("ps", b)], [("mst", j)])
                    P.dma("sp", dst[:, 256 * i:256 * i + 256], mst[:, j, :], reads=[("mst", j)], track_out=True)
                if which == 1:
                    dve(lambda e, b=b, t=t, i=i: e.tensor_copy(out=mvb[:, t, 256 * i:256 * i + 256], in_=psf(b, 256)),
                        [("ps", b)], [("mvb", t, i), ("xl2", 0), ("xl2", 1)])
            if which == 0:
                for c in range(2):
                    b = 6 + c
                    mms = [("m", psf(b, 256), slot[:, kc, 128 * c:128 * c + 128], hmT[:, kc, :], kc == 0, kc == KC - 1) for kc in range(KC)]
                    pe(mms, [sk] + hm_reads, [("ps", b)])
                    act(lambda e, b=b, i=i, c=c: e.activation(out=mkT[:, 2 * i + c, :], in_=psf(b, 256), func=AF.Copy),
                        [("ps", b)], [("mkT", 2 * i + c), ("xl2", 0), ("xl2", 1)])

        XB_, YB_ = [0, 1, 2], [3, 4, 5]
        for hp in range(4):
            P.dma("sp", dmask, c_dmask[:, 2 * hp:2 * hp + 2, :], writes=["dmask"])
            P.dma("sp", smask, c_smask[:, 2 * hp:2 * hp + 2, :], writes=["smask"])
            P.dma("sp", qdec, c_qdec[:, 2 * hp:2 * hp + 2, :], writes=["qdec"])
            P.dma("sp", qdecs, c_qdecs[:, 2 * hp:2 * hp + 2, :], writes=["qdecs"])
            KS = 128.0 ** -0.5
            if hp > 0:
                slot, sk = wslot(W["k"][hp])
                proj_b(slot, sk, 0, hT, [], TB, XB_)
                proj_b(slot, sk, 1, hT, [], TB, YB_)
            rotary(XB_, TB, rot, KS, kT[:, 0, :], ("k", 0))
            rotary_pre(YB_, TB, rot, KS)
            if hp == 0:
                norm_mem()
            memkv_block(*W["mkv"][4 * hp + 0])
            rotary_post(YB_, TB, rot, KS, kT[:, 1, :], ("k", 1))
            slot, sk = wslot(W["q"][hp])
            stage1_t(0, 2 * hp)
            proj_b(slot, sk, 0, hT, [], TB, XB_)
            stage1_a(0, 2 * hp)
            proj_b(slot, sk, 1, hT, [], TB, YB_)
            rotary(XB_, TB, rot, 1.0, qT[:, 0, :], ("q", 0))
            rotary_pre(YB_, TB, rot, 1.0)
            stage1_t(1, 2 * hp + 1)
            memkv_block(*W["mkv"][4 * hp + 1])
            slot_g, sk_g = wslot(W["g"][hp])
            proj_b(slot_g, sk_g, 0, hT, [], TB, XB_)
            pv_ = pview(XB_, T)
            act(lambda e, pv_=pv_: e.activation(out=sgT[:, 0, :], in_=pv_, func=AF.Silu), pkeys(XB_, TB), [("sgT", 0)])
            stage1_a(1, 2 * hp + 1)
            rotary_post(YB_, TB, rot, 1.0, qT[:, 1, :], ("q", 1))
            proj_b(slot_g, sk_g, 1, hT, [], TB, YB_)
            pv_ = pview(YB_, T)
            act(lambda e, pv_=pv_: e.activation(out=sgT[:, 1, :], in_=pv_, func=AF.Silu), pkeys(YB_, TB), [("sgT", 1)])
            stage3(0, 2 * hp)
            memkv_block(*W["mkv"][4 * hp + 2])
            stage3(1, 2 * hp + 1)
            memkv_block(*W["mkv"][4 * hp + 3])
            gn_mm(0)
            gn_head(0)
            gn_mm(1)
            gn_head1_a()
            if hp < 3:
                slot, sk = wslot(W["v"][hp + 1])
                proj_v(slot, sk)
            gn_tail(0, 2 * hp)
            gn_head1_b()
            gn_tail(1, 2 * hp + 1, mbuf=osq_alt, mkeys=OAK)
        slot_cg0, sk_cg0 = wslot(W["cg"][0])
        cg0_banks = [next_banks(), next_banks()]
        for c in range(2):
            proj_b(slot_cg0, sk_cg0, c, hT, [], TB, cg0_banks[c])
        P.fence()

        P.phase(4)
        P.dma("sp", sc_in[0:32, :], sconv, writes=["sc_in"])

        def sconv_transposes():
            for chk in range(8):
                pe([("t", psf(6, 32), sc_in[0:32, 128 * chk:128 * chk + 128], identf[0:32, 0:32])], ["sc_in", "identf"], [("ps", 6)])
                act(lambda e, chk=chk: e.activation(out=scT[:, chk, :], in_=psf(6, 32), func=AF.Copy), [("ps", 6)], [("scT", chk)])

        for cp in range(4):
            if cp > 0:
                slot_cg, sk_cg = wslot(W["cg"][cp])
            cg_sb = [t1, t2]
            for c in range(2):
                if cp == 0:
                    banks = cg0_banks[c]
                else:
                    banks = next_banks()
                    proj_b(slot_cg, sk_cg, c, hT, [], TB, banks)
                for (t0, n), b in zip(TB, banks):
                    act(lambda e, b=b, n=n, t0=t0, c=c: e.activation(out=cg_sb[c][:, t0:t0 + n], in_=psf(b, n), func=AF.Copy),
                        [("ps", b)], [("cgsb", c, t0)])
            if cp == 0:
                sconv_transposes()
            slot_h, sk_h = wslot(W["hin"][cp])
            for c in range(2):
                chk = 2 * cp + c
                banks = next_banks()
                proj_b(slot_h, sk_h, c, hT, [], TB, banks)
                up = upad[:, c * (TP + 2):(c + 1) * (TP + 2)]
                dve(lambda e, up=up, chk=chk: e.tensor_copy(out=up[:, 0:2], in_=uprev[:, chk, :]), [("uprev", chk)], [("upad", c, -1)])
                for (t0, n), b in zip(TBP, banks):
                    dve(lambda e, b=b, n=n, t0=t0, c=c, up=up: e.tensor_tensor(
                        out=up[:, 2 + t0:2 + t0 + n], in0=cg_sb[c][:, t0:t0 + n], in1=psf(b, n), op=ALU.mult),
                        [("ps", b), ("cgsb", c, t0)], [("upad", c, t0)])
                ups = upads
                dve(lambda e, chk=chk: e.tensor_copy(out=upads[:, :, 0:2], in_=scT[:, chk, :].rearrange("p (s k) -> p s k", k=2)),
                    [("scT", chk)], [("upads", 0)])
                b = banks[2]
                dve(lambda e, b=b, c=c: e.tensor_tensor(
                    out=upads[:, :, 2:10], in0=cg_sb[c][:, TP:T].rearrange("p (s k) -> p s k", k=8),
                    in1=psf(b, 128).rearrange("p (s k) -> p s k", k=8), op=ALU.mult),
                    [("ps", b), ("cgsb", c, 1024)], [("upads", 1)])
                upr = [("upad", c, -1), ("upad", c, 0), ("upad", c, 512)]
                dve(lambda e, up=up, chk=chk: e.tensor_scalar(out=cacc[:, 0:TP], in0=up[:, 0:TP], scalar1=cwT[:, 0, chk:chk + 1],
                                                              scalar2=None, op0=ALU.mult), upr + ["cwT"], [("cacc", 0)])
                for k in (1, 2):
                    dve(lambda e, up=up, chk=chk, k=k: e.scalar_tensor_tensor(
                        out=cacc[:, 0:TP], in0=up[:, k:k + TP], scalar=cwT[:, k, chk:chk + 1], in1=cacc[:, 0:TP],
                        op0=ALU.mult, op1=ALU.add), upr + [("cacc", 0)], [("cacc", 0)])
                ca_s = cacc[:, TP:T].rearrange("p (s k) -> p s k", k=8)
                usr = [("upads", 0), ("upads", 1)]
                dve(lambda e, chk=chk: e.tensor_scalar(out=ca_s, in0=upads[:, :, 0:8], scalar1=cwT[:, 0, chk:chk + 1],
                                                       scalar2=None, op0=ALU.mult), usr + ["cwT"], [("cacc", 1)])
                for k in (1, 2):
                    dve(lambda e, chk=chk, k=k: e.scalar_tensor_tensor(
                        out=ca_s, in0=upads[:, :, k:k + 8], scalar=cwT[:, k, chk:chk + 1], in1=ca_s,
                        op0=ALU.mult, op1=ALU.add), usr + [("cacc", 1)], [("cacc", 1)])
                pe([("t", psf(7, 128)[0:2, :], up[:, TP:TP + 2], identf[:])], [("upad", c, 512), "identf"], [("ps", 7)])
                act(lambda e, chk=chk: e.activation(out=sc_out[0:2, 128 * chk:128 * chk + 128], in_=psf(7, 128)[0:2, :], func=AF.Copy),
                    [("ps", 7)], [("sc_out", chk)])
                dve(lambda e: e.tensor_copy(out=nbt.rearrange("p (s k) -> p s k", k=2), in_=upads[:, :, 8:10]), [("upads", 1)], ["nbt"])
                pe([("t", psf(7, 128)[0:32, :], nbt, identf[:])], ["nbt", "identf"], [("ps", 7)])
                act(lambda e, chk=chk: e.activation(out=sc_in[0:32, 128 * chk:128 * chk + 128], in_=psf(7, 128)[0:32, :], func=AF.Copy),
                    [("ps", 7)], [("sc_in2", chk), "sc_in"])
                dve(lambda e, c=c: e.tensor_copy(out=cg_sb[c][:, :], in_=cacc[:, :]), [("cacc", 0), ("cacc", 1), ("cgsb", c, 0), ("cgsb", c, 512), ("cgsb", c, 1024)],
                    [("cgsb", c, 0), ("cgsb", c, 512), ("cgsb", c, 1024)])
            slot_b, sk_b = wslot(W["bg"][cp])
            for c in range(2):
                chk = 2 * cp + c
                banks = next_banks()
                proj_b(slot_b, sk_b, c, hT, [], TB, banks)
                for (t0, n), b in zip(TB, banks):
                    dve(lambda e, b=b, n=n, t0=t0, c=c, chk=chk: e.tensor_tensor(
                        out=catT[:, 8 + chk, t0:t0 + n], in0=cg_sb[c][:, t0:t0 + n], in1=psf(b, n), op=ALU.mult),
                        [("ps", b), ("cgsb", c, t0)], [("cat", 8 + chk, t0)])
        P.dma("sp", scp, sc_out[0:2, :], reads=[("sc_out", i) for i in range(8)], track_out=True)
        P.dma("sp", scs, sc_in[0:32, :], reads=[("sc_in2", i) for i in range(8)], track_out=True)
        P.fence()

        P.phase(5)
        def out_proj(wkey, srcT, accumulate_first_from_dram, after_tile=None):
            if accumulate_first_from_dram is not None:
                for t in range(NT):
                    P.dma("sp", xres[:, t, :], accumulate_first_from_dram[t], writes=[("x", t, cb) for cb in range(4)])
            for cb in range(4):
                s0, k0 = wslot(W[wkey][cb][0])
                s1, k1 = wslot(W[wkey][cb][1])
                for t in range(NT):
                    b = (t % 8) if (cb < 3 or after_tile is None) else (t % 4)
                    mms = []
                    for kc in range(KC):
                        s_ = s0 if kc < 8 else s1
                        mms.append(("m", psf(b), srcT[:, kc, 128 * t:128 * t + 128], s_[:, kc % 8, :], kc == 0, kc == KC - 1))
                    pe(mms, [k0, k1], [("ps", b)])
                    dve(lambda e, b=b, t=t, cb=cb: e.tensor_tensor(out=xres[:, t, 512 * cb:512 * cb + 512],
                                                                   in0=xres[:, t, 512 * cb:512 * cb + 512], in1=psf(b), op=ALU.add),
                        [("ps", b), ("x", t, cb)], [("x", t, cb)])
                    if cb == 3 and after_tile is not None and t >= 1:
                        after_tile(t - 1)
            if after_tile is not None:
                after_tile(NT - 1)

        def norm_hook(gidx):
            return lambda t: norm_tile(t, xres[:, t, :], [("x", t, cb) for cb in range(4)], gidx, hT, "hres")

        out_proj("mixo", catT, srcs, after_tile=norm_hook(2))
        P.fence()

        P.phase(6)
        qxT = RA[:, :].rearrange("p (k t) -> p k t", k=KC)
        for i in range(8):
            slot, sk = wslot(W["xq"][i])
            for c in range(2):
                banks = next_banks()
                proj_b(slot, sk, c, hT, [], TB, banks)
                for (t0, n), b in zip(TB, banks):
                    act(lambda e, b=b, n=n, t0=t0, i=i, c=c: e.activation(out=qxT[:, 2 * i + c, t0:t0 + n], in_=psf(b, n), func=AF.Copy),
                        [("ps", b)], [("qx", 2 * i + c, t0)])
        P.fence()

        P.phase(7)
        rh_off = [0]

        def rh(n, dt=BF16):
            nb = n if dt == BF16 else 2 * n
            o = rh_off[0]
            rh_off[0] += nb
            assert rh_off[0] <= KC * T, rh_off[0]
            v = RH[:, o:o + nb]
            if dt == F32:
                v = v.bitcast(F32)
            return v

        rh_xn0 = 0
        P.phase(8)
        oob2 = oo[:, :, :].rearrange("p a t -> p (a t)").bitcast(BF16)
        pn = oob2[:, 0:2048].rearrange("p (a t n) -> p a t n", a=2, t=4)
        sfb = Sf[:, :, :].rearrange("p h e -> p (h e)").bitcast(BF16)
        pT = sfb[:, 0:2048].rearrange("p (a c l) -> p a c l", a=2, c=2)
        rh_off[0] = 0
        kraw = rh(2 * 2 * D).rearrange("p (a t d) -> p a t d", a=2, t=2)
        vraw = rh(2 * 2 * D).rearrange("p (a t d) -> p a t d", a=2, t=2)
        kTs = rh(2 * 1024).rearrange("p (a x) -> p a x", a=2)
        for s_ in range(2):
            P.dma("pool", kraw[:, s_], ck[s_].rearrange("(t p) d -> p t d", p=128), writes=[("kraw", s_)])
            P.dma("pool", vraw[:, s_], cv[s_].rearrange("(t p) d -> p t d", p=128), writes=[("vraw", s_)])
        XS = 512.0 ** -0.5
        oT = qxT
        combos = [(hd, g4) for hd in range(4) for g4 in range(2)]

        def h2_a(i):
            hd, g4 = combos[i]
            b0 = 2 * (i % 2)
            mms = []
            for tt in range(4):
                t = 4 * g4 + tt
                o_ = psf(b0 + tt // 2, 256, 256 * (tt % 2))
                for dc in range(4):
                    mms.append(("m", o_, qxT[:, 4 * hd + dc, 128 * t:128 * t + 128], mkT[:, 4 * hd + dc, :], dc == 0, dc == 3))
            pe(mms, [], [("ps", b0), ("ps", b0 + 1)])

        def h2_b(i):
            hd, g4 = combos[i]
            pj = i % 2
            b0 = 2 * pj
            skeys = [("ps", b0), ("ps", b0 + 1)]
            so = 32 + 16 * pj
            sc4 = PS[:, b0:b0 + 2, :].rearrange("p b (t n) -> p (b t) n", t=2)
            dve(lambda e: e.reduce_max(out=stat[:, so:so + 4], in_=sc4, axis=AX.X), skeys, [("mx", pj)])
            dve(lambda e: e.tensor_scalar(out=stat[:, so + 4:so + 8], in0=stat[:, so:so + 4], scalar1=-XS, scalar2=None, op0=ALU.mult),
                [("mx", pj)], [("nmx", pj)])
            for tt in range(4):
                act(lambda e, tt=tt: e.activation(out=pn[:, pj, tt, :], in_=psf(b0 + tt // 2, 256, 256 * (tt % 2)), func=AF.Exp,
                                                  bias=stat[:, so + 4 + tt:so + 5 + tt], scale=XS, accum_out=stat[:, so + 8 + tt:so + 9 + tt]),
                    [("ps", b0 + tt // 2), ("nmx", pj)], [("pn", pj, tt), ("sm", pj, tt)])
            dve(lambda e: e.reciprocal(out=stat[:, so + 12:so + 16], in_=stat[:, so + 8:so + 12]), [("sm", pj, tt) for tt in range(4)], [("rs", pj)])
            dve(lambda e: e.tensor_tensor(out=pn[:, pj], in0=pn[:, pj], in1=stat[:, so + 12:so + 16].unsqueeze(2).broadcast_to([128, 4, 256]), op=ALU.mult),
                [("pn", pj, tt) for tt in range(4)] + [("rs", pj)], [("pn", pj, tt) for tt in range(4)])
            pb = psb(4 + pj)
            pe([("t", pb[:, 512 * nc_ + 128 * tt:512 * nc_ + 128 * tt + 128], pn[:, pj, tt, 128 * nc_:128 * nc_ + 128], identb[:]) for tt in range(4) for nc_ in range(2)],
               [("pn", pj, tt) for tt in range(4)] + ["identb"], [("ps", 4 + pj)])
            act(lambda e: e.activation(out=pT[:, pj].rearrange("p c l -> p (c l)"), in_=pb[:, :], func=AF.Copy), [("ps", 4 + pj)], [("pT", pj)])
            for e2 in range(2):
                mms = []
                for k_ in range(2):
                    ec = 2 * e2 + k_
                    for nc_ in range(2):
                        mms.append(("m", psf(6 + k_), mvb[:, nc_, 512 * hd + 128 * ec:512 * hd + 128 * ec + 128], pT[:, pj, nc_, :], nc_ == 0, nc_ == 1))
                pe(mms, [("pT", pj)], [("ps", 6), ("ps", 7)])
                act(lambda e, e2=e2: e.activation(out=oT[:, 4 * hd + 2 * e2:4 * hd + 2 * e2 + 2, 512 * g4:512 * g4 + 512],
                                                  in_=PS[:, 6:8, :], func=AF.Copy), [("ps", 6), ("ps", 7)], [("oT", hd, g4, e2)])

        h2_a(0)
        for i in range(len(combos)):
            if i + 1 < len(combos):
                h2_a(i + 1)
            h2_b(i)
        P.fence()

        P.phase(9)
        oob = oo[:, :, :].rearrange("p a t -> p (a t)").bitcast(BF16)
        ps_s = oob[:, 0:1024].rearrange("p (h n) -> p h n", h=4)
        pTs = oob[:, 1024:1088].rearrange("p (h c l) -> p h c l", h=4, c=2)
        scs_ = oo[:, :, :].rearrange("p a t -> p (a t)")[:, 1024:2048].rearrange("p (h n) -> p h n", h=4)
        def h3_load_k(s):
            a = s % 2
            P.dma("pool", kraw[:, a], ck[s].rearrange("(t p) d -> p t d", p=128), writes=[("kraw", a)])

        def h3_load_v(s):
            a = s % 2
            P.dma("pool", vraw[:, a], cv[s].rearrange("(t p) d -> p t d", p=128), writes=[("vraw", a)])

        def h3_load(s):
            h3_load_k(s)
            h3_load_v(s)

        def h3_a(s):
            a = s % 2
            sb0 = 2 + 2 * a

            def tr(hd):
                j = hd % 2
                pb = psb(j)
                mms = [("t", pb[:, 256 * dc + 128 * nt_:256 * dc + 128 * nt_ + 128], kraw[:, a, nt_, 512 * hd + 128 * dc:512 * hd + 128 * dc + 128], identb[:])
                       for dc in range(4) for nt_ in range(2)]
                pe(mms, [("kraw", a), "identb"], [("ps", j)])
                act(lambda e, pb=pb, j=j: e.activation(out=kTs[:, j, :], in_=pb[:, :], func=AF.Copy), [("ps", j)], [("kTs", j)])

            def sc_(hd):
                j = hd % 2
                sbk = sb0 + hd // 2
                so = 256 * (hd % 2)
                mms = [("m", psf(sbk, 256, so)[0:8, :], qxT[:, 4 * hd + dc, TP + 8 * s:TP + 8 * s + 8], kTs[:, j, 256 * dc:256 * dc + 256], dc == 0, dc == 3) for dc in range(4)]
                pe(mms, [("kTs", j)], [("ps", sbk)])

            tr(0); tr(1); sc_(0); tr(2); sc_(1); tr(3); sc_(2); sc_(3)

        def h3_b(s):
            a = s % 2
            sb0 = 2 + 2 * a
            skeys = [("ps", sb0), ("ps", sb0 + 1)]
            sc = PS[0:8, sb0:sb0 + 2, :].rearrange("p b n -> p (b n)")
            sc4 = PS[0:8, sb0:sb0 + 2, :].rearrange("p b (t n) -> p (b t) n", t=2)
            dve(lambda e: e.reduce_max(out=stat[0:8, 16:20], in_=sc4, axis=AX.X), skeys, ["mx"])
            dve(lambda e: e.tensor_tensor(out=scs_[0:8], in0=sc4, in1=stat[0:8, 16:20].unsqueeze(2).broadcast_to([8, 4, 256]), op=ALU.subtract),
                skeys + ["mx"], ["scs"])
            act(lambda e: e.activation(out=ps_s[0:8].rearrange("p h n -> p (h n)"), in_=scs_[0:8].rearrange("p h n -> p (h n)"), func=AF.Exp, scale=XS),
                ["scs"], ["pss"])
            dve(lambda e: e.reduce_sum(out=stat[0:8, 24:28], in_=ps_s[0:8], axis=AX.X), ["pss"], ["sm"])
            dve(lambda e: e.reciprocal(out=stat[0:8, 28:32], in_=stat[0:8, 24:28]), ["sm"], ["rs"])
            dve(lambda e: e.tensor_tensor(out=ps_s[0:8], in0=ps_s[0:8], in1=stat[0:8, 28:32].unsqueeze(2).broadcast_to([8, 4, 256]), op=ALU.mult),
                ["pss", "rs"], ["pss"])
            pb2 = psb(6)
            pe([("t", pb2[:, 16 * hd + 8 * nc_:16 * hd + 8 * nc_ + 8], ps_s[0:8, hd, 128 * nc_:128 * nc_ + 128], identb[0:8, 0:8]) for hd in range(4) for nc_ in range(2)],
               ["pss", "identb"], [("ps", 6)])
            act(lambda e: e.activation(out=pTs.rearrange("p h c l -> p (h c l)"), in_=pb2[:, 0:64], func=AF.Copy), [("ps", 6)], ["pTs"])
            mms = []
            for hd in range(4):
                for ec in range(4):
                    for nc_ in range(2):
                        mms.append(("m", psf(7, 8, 8 * (4 * hd + ec)), vraw[:, a, nc_, 512 * hd + 128 * ec:512 * hd + 128 * ec + 128], pTs[:, hd, nc_, :], nc_ == 0, nc_ == 1))
            pe(mms, [("vraw", a), "pTs"], [("ps", 7)])
            act(lambda e: e.activation(out=oT[:, :, TP + 8 * s:TP + 8 * s + 8], in_=psf(7, 128).rearrange("p (c l) -> p c l", c=16), func=AF.Copy),
                [("ps", 7)], [("oTs", s)])

        h3_a(0)
        for s in range(NSEQ):
            if s + 1 < NSEQ:
                h3_a(s + 1)
            if s + 2 < NSEQ:
                h3_load_k(s + 2)
            h3_b(s)
            if s + 2 < NSEQ:
                h3_load_v(s + 2)
        P.fence()

        P.phase(10)
        out_proj("xo", oT, None, after_tile=norm_hook(3))
        P.fence()

        P.phase(11)
        hid = RA[:, 0:2 * 4 * T].rearrange("p (a k t) -> p a k t", a=2, k=4)
        gsb = RA[:, 2 * 4 * T:2 * 4 * T + 2 * T].bitcast(F32)
        for g in range(NG):
            a = g % 2
            for j in range(2):
                sgj, kgj = wslot(W["gate"][g][j])
                suj, kuj = wslot(W["up"][g][j])
                for c in range(2):
                    kk_ = 2 * j + c
                    bg_ = next_banks()
                    hrd = ["hT"] if g == NG - 1 else []
                    proj_b(sgj, kgj, c, hT, hrd, TB, bg_)
                    for (t0, n), b in zip(TB, bg_):
                        act(lambda e, b=b, n=n, t0=t0: e.activation(out=gsb[:, t0:t0 + n], in_=psf(b, n), func=AF.Silu),
                            [("ps", b)], [("gsb", t0)])
                    bu_ = next_banks()
                    proj_b(suj, kuj, c, hT, hrd, TB, bu_)
                    for (t0, n), b in zip(TB, bu_):
                        dve(lambda e, b=b, n=n, t0=t0, a=a, kk_=kk_: e.tensor_tensor(out=hid[:, a, kk_, t0:t0 + n], in0=gsb[:, t0:t0 + n], in1=psf(b, n), op=ALU.mult),
                            [("ps", b), ("gsb", t0)], [("hid", a, kk_, t0)])
            hr = [("hid", a, k_, t0) for k_ in range(4) for t0, _ in TB]

            def down_tile(t, cb, sd, kd_, b):
                co = 512 * (cb % 2)
                mms = [("m", psf(b), hid[:, a, k_, 128 * t:128 * t + 128], sd[:, k_, co:co + 512], k_ == 0, k_ == 3) for k_ in range(4)]
                pe(mms, [kd_] + hr, [("ps", b)])
                dve(lambda e, b=b, t=t, cb=cb: e.tensor_tensor(out=xres[:, t, 512 * cb:512 * cb + 512],
                                                               in0=xres[:, t, 512 * cb:512 * cb + 512], in1=psf(b), op=ALU.add),
                    [("ps", b), ("x", t, cb)], [("x", t, cb)])

            if g < NG - 1:
                for cb in range(4):
                    if cb % 2 == 0:
                        sd, kd_ = wslot(W["down"][g][cb // 2])
                    for t in range(NT):
                        down_tile(t, cb, sd, kd_, 6 + (t % 2))
            else:
                sd0, kd0 = wslot(W["down"][g][0])
                sd1, kd1 = wslot(W["down"][g][1])
                gbc = RH[:, 0:2 * D].bitcast(F32)
                P.dma("sp", gbc, final_g.partition_broadcast(128), reads=[], writes=["gbc", "hT"])
                sq3 = RH[:, 2 * D:3 * D]
                yst = RH[:, 3 * D:7 * D].bitcast(F32).rearrange("p (a d) -> p a d", a=2)

                def final_tile(t):
                    a_ = t % 2
                    so = 4 * a_
                    xk = [("x", t, cb) for cb in range(4)]
                    act(lambda e: e.activation(out=sq3, in_=xres[:, t, :], func=AF.Square, accum_out=stat[:, so:so + 1]), xk + ["gbc"], ["sq3", ("fss", a_)])
                    act(lambda e: e.activation(out=stat[:, so + 1:so + 2], in_=stat[:, so:so + 1], func=AF.Sqrt, bias=RMS_EPS, scale=1.0 / D), [("fss", a_)], [("fms", a_)])
                    dve(lambda e: e.reciprocal(out=stat[:, so + 2:so + 3], in_=stat[:, so + 1:so + 2]), [("fms", a_)], [("frs", a_)])
                    dve(lambda e: e.scalar_tensor_tensor(out=yst[:, a_, :], in0=xres[:, t, :], scalar=stat[:, so + 2:so + 3], in1=gbc, op0=ALU.mult, op1=ALU.mult),
                        [("frs", a_), "gbc"] + xk, [("yst", a_)])
                    dst = yp[128 * t:128 * t + 128, :] if t < 8 else ys[:, :]
                    P.dma("sp", dst, yst[:, a_, :], reads=[("yst", a_)], track_out=True)

                for t in range(NT):
                    for cb in range(4):
                        down_tile(t, cb, (sd0, sd1)[cb // 2], (kd0, kd1)[cb // 2], 4 + cb)
                    if t >= 1:
                        final_tile(t - 1)
                final_tile(NT - 1)

        P.phase(12)
        P.dead = False
        final_waits = []
        for k, v in P.dma_tokens:
            if v > P.seen["sp"].get(k, 0):
                P.seen["sp"][k] = v
                final_waits.append((k, v))
        P.q["sp"].append((final_waits, None, None))

        print("engine op counts", {e: (P.epoch[e], P.cnt[e]) for e in ENGS}, "n_sems", len(P.sems), flush=True)
        with nc.Block() as block:
            P.emit(block)
        print("n_sems", len(P.sems), flush=True)
    return nc


def _consts(half):
    c = {}
    c["c_ident"] = np.eye(128, dtype=np.float32)
    perm = np.zeros((128, 128), np.float32)
    for m in range(128):
        perm[(m + 64) % 128, m] = 1.0
    c["c_perm"] = perm
    inv = 10000.0 ** (-np.arange(64, dtype=np.float64) / 64.0)

    def rot_table(pos):
        ang = pos.astype(np.float64)[None, :] * inv[:, None]
        cos = np.cos(ang).astype(np.float32); sin = np.sin(ang).astype(np.float32)
        tab = np.zeros((128, 2, pos.shape[0]), np.float32)
        tab[:64, 0] = cos; tab[64:, 0] = cos
        tab[:64, 1] = -sin; tab[64:, 1] = sin
        return tab
    pos_main = np.concatenate([half * TP + np.arange(TP), np.tile(16384 + np.arange(8), NSEQ)])
    c["c_rot"] = rot_table(pos_main)
    c["c_rotp"] = rot_table(np.arange(TP))
    lg = np.array(LG, dtype=np.float64)
    m = np.arange(128)[:, None]; l = np.arange(128)[None, :]
    dm = np.zeros((128, H, 128)); sm = np.zeros((128, H, 128))
    for h in range(H):
        rel = l - m
        dm[:, h, :] = np.where(rel >= 0, np.exp(rel * lg[h]), 0.0)
        same = (m // 8) == (l // 8)
        sm[:, h, :] = np.where((rel >= 0) & same, np.exp(rel * lg[h]), 0.0)
    c["c_dmask"] = dm.astype(np.float32); c["c_smask"] = sm.astype(np.float32)
    qd = np.zeros((128, H, 128)); qds = np.zeros((128, H, 128))
    for h in range(H):
        qd[:, h, :] = np.exp((np.arange(128) + 1.0) * lg[h])[None, :]
        qds[:, h, :] = np.exp(((np.arange(128) % 8) + 1.0) * lg[h])[None, :]
    c["c_qdec"] = qd.astype(np.float32); c["c_qdecs"] = qds.astype(np.float32)
    kd = np.zeros((128, H)); kdp = np.zeros((128, H, 8)); rmd = np.zeros((128, H, NSEQ + 1))
    for h in range(H):
        kd[:, h] = np.exp((127.0 - np.arange(128)) * lg[h])
        for ch in range(8):
            kdp[:, h, ch] = np.exp((1023.0 - (128 * ch + np.arange(128))) * lg[h])
        for s in range(NSEQ):
            mm_ = np.arange(128)
            rmd[:, h, s] = np.where(mm_ // 8 == s, 1.0, 0.0)
        rmd[:, h, NSEQ] = np.exp((7.0 - (np.arange(128) % 8)) * lg[h])
    c["c_kdec"] = kd.astype(np.float32); c["c_kdecp"] = kdp.astype(np.float32); c["c_rmd"] = rmd.astype(np.float32)
    return c


def _core_inputs(c, shared, x_prompt, x_sample, mem_prompt, state_ret, state_conv, cache_mem_k, cache_mem_v):
    b, half = c // 2, c % 2
    m = dict(shared)
    m["xp"] = np.ascontiguousarray(x_prompt[b, half * TP:(half + 1) * TP])
    m["xprev"] = np.ascontiguousarray(x_prompt[b, 0:TP]) if half == 1 else np.zeros((TP, D), np.float32)
    m["xs"] = np.ascontiguousarray(x_sample[NSEQ * c:NSEQ * (c + 1)].reshape(TS, D))
    own = mem_prompt[b, 128 * half:128 * half + 128]
    oth = mem_prompt[b, 128 * (1 - half):128 * (1 - half) + 128]
    m["mem"] = np.ascontiguousarray(np.concatenate([own, oth], axis=0))
    m["sret"] = np.ascontiguousarray(state_ret[0, NSEQ * c:NSEQ * (c + 1)])
    m["sconv"] = np.ascontiguousarray(state_conv[0, NSEQ * c:NSEQ * (c + 1)].reshape(NSEQ * 2, 1024))
    m["ck"] = np.ascontiguousarray(cache_mem_k[0, NSEQ * c:NSEQ * (c + 1)].reshape(NSEQ, NMEM, D))
    m["cv"] = np.ascontiguousarray(cache_mem_v[0, NSEQ * c:NSEQ * (c + 1)].reshape(NSEQ, NMEM, D))
    m.update(_consts(half))
    return m


_NC_CACHE = {}


def kernel(x_prompt, x_sample, mem_prompt, state_ret, state_conv, cache_mem_k, cache_mem_v,
           ln_mix_g, w_in, conv_w, ret_gn_g, w_mix_out, ln_mem_g, ln_xa_g, w_xq, w_mk, w_mv,
           w_xo, ln_ffn_g, w_gate, w_up, w_down, final_g):
    f = lambda a: np.ascontiguousarray(np.asarray(a, dtype=np.float32))
    x_prompt, x_sample, mem_prompt = f(x_prompt), f(x_sample), f(mem_prompt)
    state_ret, state_conv, cache_mem_k, cache_mem_v = f(state_ret), f(state_conv), f(cache_mem_k), f(cache_mem_v)
    shared = {
        "ln_mix_g": f(ln_mix_g)[0], "w_in": f(w_in)[0], "conv_w": f(conv_w)[0], "ret_gn_g": f(ret_gn_g)[0],
        "w_mix_out": f(w_mix_out)[0], "ln_mem_g": f(ln_mem_g)[0], "ln_xa_g": f(ln_xa_g)[0], "w_xq": f(w_xq)[0],
        "w_mk": f(w_mk)[0], "w_mv": f(w_mv)[0], "w_xo": f(w_xo)[0], "ln_ffn_g": f(ln_ffn_g)[0],
        "w_gate": f(w_gate)[0], "w_up": f(w_up)[0], "w_down": f(w_down)[0], "final_g": f(final_g),
    }
    if "nc" not in _NC_CACHE:
        _NC_CACHE["nc"] = build_program()
    nc = _NC_CACHE["nc"]
    in_maps = [_core_inputs(c, shared, x_prompt, x_sample, mem_prompt, state_ret, state_conv, cache_mem_k, cache_mem_v)
               for c in range(8)]
    res = run_bass_kernel_spmd(nc, in_maps, core_ids=list(range(8)))
    R = res.results
    y_prompt = np.zeros((4, 2048, D), np.float32)
    y_sample = np.zeros((128, 8, D), np.float32)
    srp_o = np.zeros((1, 4, H, 128, 128), np.float32)
    srs_o = np.zeros((1, 128, H, 128, 128), np.float32)
    scp_o = np.zeros((1, 4, 2, 1024), np.float32)
    scs_o = np.zeros((1, 128, 2, 1024), np.float32)
    mk_o = np.zeros((1, 4, NMEM, 4, 512), np.float32)
    mv_o = np.zeros((1, 4, NMEM, 4, 512), np.float32)
    for c in range(8):
        b, half = c // 2, c % 2
        r = R[c]
        y_prompt[b, half * TP:(half + 1) * TP] = r["yp"]
        y_sample[NSEQ * c:NSEQ * (c + 1)] = r["ys"].reshape(NSEQ, 8, D)
        if half == 1:
            srp_o[0, b] = r["srp"]
            scp_o[0, b] = r["scp"]
        srs_o[0, NSEQ * c:NSEQ * (c + 1)] = r["srs"]
        scs_o[0, NSEQ * c:NSEQ * (c + 1)] = r["scs"].reshape(NSEQ, 2, 1024)
        mk_o[0, b, 128 * half:128 * half + 128] = r["mko"].reshape(128, 4, 512)
        mv_o[0, b, 128 * half:128 * half + 128] = r["mvo"].reshape(128, 4, 512)
    return (y_prompt, y_sample, srp_o, srs_o, scp_o, scs_o, mk_o, mv_o)
```
